# Optimizing a Trainium2 kernel written in Bass

```python
import jax
import jax.numpy as jnp
from jax import lax
import numpy as np


D_MODEL = 2048
BATCH = 2
SEQ = 8192
DEPTH = 4

D_FF = 5504
NORM_EPS = 1e-6
ROPE_THETA = 10000.0
N_AB = (DEPTH + 1) // 2
N_C = DEPTH // 2
NEG = -1e30

ML_HEADS = 4
ML_HEAD_DIM = D_MODEL // 2 // ML_HEADS
ML_WIDTH = ML_HEADS * ML_HEAD_DIM
ML_CONV = 4
ML_CHUNK = 128

HG_HEADS = 8
HG_KEY_DIM = 128
HG_VAL_DIM = D_MODEL // 2 // HG_HEADS
HG_KEY_WIDTH = HG_HEADS * HG_KEY_DIM
HG_WIDTH = HG_HEADS * HG_VAL_DIM
HG_CHUNK = 64
HG_MAX_K = 0.999999

AB_SIZES = [ML_WIDTH, ML_WIDTH, ML_WIDTH, ML_HEADS, ML_HEADS, HG_KEY_WIDTH, HG_KEY_WIDTH, HG_WIDTH, HG_WIDTH]
AB_COLS = sum(AB_SIZES)

NSA_HEADS = 16
NSA_KV_GROUPS = 4
NSA_HEAD_DIM = D_MODEL // NSA_HEADS
NSA_KV_WIDTH = NSA_KV_GROUPS * NSA_HEAD_DIM
CMP_BLOCK = 32
CMP_STRIDE = 16
CMP_HIDDEN = 256
SEL_BLOCK = 64
SEL_TOPK = 16
WINDOW = 512
NSA_QBLOCK = 32
FORCE_BONUS = 1e4
C_SIZES = [NSA_HEADS * NSA_HEAD_DIM] + [NSA_KV_WIDTH] * 6 + [3 * NSA_HEADS]
C_COLS = sum(C_SIZES)

kernel_name = 'hybrid_mlstm_hgrn2_nsa_macaron'


def _split(a, sizes):
    offs = np.cumsum(sizes)[:-1].tolist()
    return jnp.split(a, offs, axis=-1)


def rmsnorm(x, w):
    xf = x.astype(jnp.float32)
    y = xf * lax.rsqrt(jnp.mean(xf * xf, axis=-1, keepdims=True) + NORM_EPS)
    return (y * w.astype(jnp.float32)).astype(x.dtype)


def swiglu(x, w_gate, w_up, w_down):
    return (jax.nn.silu(x @ w_gate) * (x @ w_up)) @ w_down


def rope(x, pos):
    half = x.shape[-1] // 2
    inv_freq = jnp.power(ROPE_THETA, -jnp.arange(half, dtype=jnp.float32) / half)
    ang = pos.astype(jnp.float32)[:, None] * inv_freq[None, :]
    cos, sin = jnp.cos(ang), jnp.sin(ang)
    xf = x.astype(jnp.float32)
    x1, x2 = xf[..., :half], xf[..., half:]
    return jnp.concatenate([x1 * cos - x2 * sin, x2 * cos + x1 * sin], axis=-1).astype(x.dtype)


def masked_softmax(s, valid):
    s = jnp.where(valid, s.astype(jnp.float32), NEG)
    m = jnp.max(s, axis=-1, keepdims=True)
    e = jnp.where(valid, jnp.exp(s - m), 0.0)
    z = jnp.sum(e, axis=-1, keepdims=True)
    return e / jnp.where(z > 0, z, 1.0)


def causal_dwconv(u, w, b):
    K, C = w.shape
    y = lax.conv_general_dilated(u, w[:, None, :].astype(u.dtype), window_strides=(1,), padding=[(K - 1, 0)],
                                 dimension_numbers=('NWC', 'WIO', 'NWC'), feature_group_count=C)
    return y + b


def mlstm_chunkwise(q, k, v, i_pre, f_pre):
    B, H, T, dh = q.shape
    L = ML_CHUNK
    nc = T // L
    f32 = jnp.float32
    out_dtype = q.dtype
    q, k, v = (a.astype(f32) for a in (q, k, v))
    log_f = jax.nn.log_sigmoid(f_pre.astype(f32))
    i_pre = i_pre.astype(f32)

    def chunks(a):
        return jnp.moveaxis(a.reshape(B, H, nc, L, *a.shape[3:]), 2, 0)

    causal = jnp.tril(jnp.ones((L, L), dtype=bool))

    def step(carry, xs):
        C, n, m = carry
        qj, kj, vj, ij, fj = xs
        b = jnp.cumsum(fj, axis=-1)
        dlog = jnp.where(causal, b[..., :, None] - b[..., None, :] + ij[..., None, :], NEG)
        inter = b + m[..., None]
        m_t = jnp.maximum(inter, jnp.max(dlog, axis=-1))
        w_intra = jnp.exp(dlog - m_t[..., None])
        w_inter = jnp.exp(inter - m_t)
        qk = jnp.einsum('bhld,bhsd->bhls', qj, kj) * w_intra
        num = jnp.einsum('bhls,bhsd->bhld', qk, vj) + w_inter[..., None] * jnp.einsum('bhvk,bhlk->bhlv', C, qj)
        den = jnp.sum(qk, axis=-1) + w_inter * jnp.einsum('bhk,bhlk->bhl', n, qj)
        h = num / jnp.maximum(jnp.abs(den), jnp.exp(-m_t))[..., None]
        g = b[..., -1]
        upd = g[..., None] - b + ij
        m_new = jnp.maximum(g + m, jnp.max(upd, axis=-1))
        a_prev = jnp.exp(g + m - m_new)
        a_s = jnp.exp(upd - m_new[..., None])
        C = a_prev[..., None, None] * C + jnp.einsum('bhs,bhsv,bhsk->bhvk', a_s, vj, kj)
        n = a_prev[..., None] * n + jnp.einsum('bhs,bhsk->bhk', a_s, kj)
        return (C, n, m_new), h

    init = (jnp.zeros((B, H, dh, dh), f32), jnp.zeros((B, H, dh), f32), jnp.zeros((B, H), f32))
    _, h = lax.scan(step, init, (chunks(q), chunks(k), chunks(v), chunks(i_pre), chunks(log_f)))
    return jnp.moveaxis(h, 0, 2).reshape(B, H, T, dh).astype(out_dtype)


def hgrn2_chunkwise(q, k, v, log_f):
    B, H, T, dk = q.shape
    dv = v.shape[-1]
    L = HG_CHUNK
    nc = T // L
    f32 = jnp.float32

    def chunks(a):
        return jnp.moveaxis(a.astype(f32).reshape(B, H, nc, L, a.shape[-1]), 2, 0)

    causal = jnp.tril(jnp.ones((L, L), dtype=bool))[:, :, None]

    def step(S, xs):
        qj, kj, vj, fj = xs
        b = jnp.cumsum(fj, axis=2)
        decay = jnp.exp(jnp.where(causal, b[:, :, :, None, :] - b[:, :, None, :, :], NEG))
        att = jnp.einsum('bhlc,bhlsc,bhsc->bhls', qj, decay, kj)
        o = jnp.einsum('bhls,bhsv->bhlv', att, vj) + jnp.einsum('bhlc,bhcv->bhlv', qj * jnp.exp(b), S)
        g = b[:, :, -1:, :]
        S = jnp.exp(g[:, :, 0, :])[..., None] * S + jnp.einsum('bhsc,bhsv->bhcv', kj * jnp.exp(g - b), vj)
        return S, o

    S0 = jnp.zeros((B, H, dk, dv), f32)
    _, o = lax.scan(step, S0, (chunks(q), chunks(k), chunks(v), chunks(log_f)))
    return jnp.moveaxis(o, 0, 2).reshape(B, H, T, dv).astype(v.dtype)


def ab_mixer(h, w_in, w_out, conv_w, conv_b, wq, wk, i_bias, f_bias, ml_norm, ml_skip, lb, hg_norm):
    B, T, _ = h.shape
    u, v, o_pre, i_pre, f_pre, hq, hf, hi, hg = _split(h @ w_in, AB_SIZES)
    c = jax.nn.silu(causal_dwconv(u, conv_w, conv_b))
    ch = c.reshape(B, T, ML_HEADS, ML_HEAD_DIM)
    q = jnp.einsum('bthd,hde->bhte', ch, wq)
    k = jnp.einsum('bthd,hde->bhte', ch, wk) * ML_HEAD_DIM ** -0.5
    vm = v.reshape(B, T, ML_HEADS, ML_HEAD_DIM).transpose(0, 2, 1, 3)
    hm = mlstm_chunkwise(q, k, vm, (i_pre + i_bias).transpose(0, 2, 1), (f_pre + f_bias).transpose(0, 2, 1))
    hm = rmsnorm(hm.transpose(0, 2, 1, 3), ml_norm).reshape(B, T, ML_WIDTH)
    y_ml = jax.nn.sigmoid(o_pre) * (hm + ml_skip * c)
    lbf = lb.astype(jnp.float32)
    zf = hf.astype(jnp.float32)
    k2 = (1.0 - lbf) * jax.nn.sigmoid(-zf)
    log_f = jnp.maximum(jnp.log1p(-jnp.minimum(k2, HG_MAX_K)), jax.nn.log_sigmoid(zf))

    def to_heads(a, n):
        return a.reshape(B, T, HG_HEADS, n).transpose(0, 2, 1, 3)

    o2 = hgrn2_chunkwise(to_heads(jax.nn.silu(hq), HG_KEY_DIM), to_heads(k2, HG_KEY_DIM),
                         to_heads(hi, HG_VAL_DIM), to_heads(log_f, HG_KEY_DIM))
    o2 = rmsnorm(o2.transpose(0, 2, 1, 3), hg_norm).reshape(B, T, HG_WIDTH)
    y_hg = o2.astype(h.dtype) * jax.nn.silu(hg)
    return jnp.concatenate([y_ml.astype(h.dtype), y_hg], axis=-1) @ w_out


def compress_blocks(kv, pos_emb, w1, b1, w2):
    B, G, T, d = kv.shape
    nc = (T - CMP_BLOCK) // CMP_STRIDE + 1
    idx = jnp.arange(nc)[:, None] * CMP_STRIDE + jnp.arange(CMP_BLOCK)[None, :]
    blocks = kv[:, :, idx, :] + pos_emb
    flat = blocks.reshape(B, G, nc, CMP_BLOCK * d)
    return jax.nn.silu(flat @ w1 + b1) @ w2


def nsa_attention(q, k_cmp, v_cmp, k_sel, v_sel, k_win, v_win, gates):
    B, G, R, T, d = q.shape
    f32 = jnp.float32
    nc = k_cmp.shape[2]
    ns = T // SEL_BLOCK
    n_sel = min(SEL_TOPK, ns)
    scale = d ** -0.5
    QB = NSA_QBLOCK
    cmp_end = jnp.arange(nc) * CMP_STRIDE + CMP_BLOCK - 1
    ci = jnp.arange(nc)[:, None] * CMP_STRIDE
    sj = jnp.arange(ns)[None, :] * SEL_BLOCK
    cover = ((ci < sj + SEL_BLOCK) & (ci + CMP_BLOCK > sj)).astype(f32)
    ksb = k_sel.reshape(B, G, ns, SEL_BLOCK, d)
    vsb = v_sel.reshape(B, G, ns, SEL_BLOCK, d)
    pad = ((0, 0), (0, 0), (WINDOW, 0), (0, 0))
    kwp, vwp = jnp.pad(k_win, pad), jnp.pad(v_win, pad)
    blk_ids = jnp.arange(ns)
    take = jax.vmap(jax.vmap(lambda blocks, ix: blocks[ix]))

    def one_block(bi):
        q0 = bi * QB
        t = q0 + jnp.arange(QB)
        qb = lax.dynamic_slice_in_dim(q, q0, QB, axis=3)
        gb = lax.dynamic_slice_in_dim(gates, q0, QB, axis=3).astype(f32)
        s_c = jnp.einsum('bgrqd,bgnd->bgrqn', qb, k_cmp, preferred_element_type=f32) * scale
        p_c = masked_softmax(s_c, cmp_end[None, :] <= t[:, None])
        o_c = jnp.einsum('bgrqn,bgnd->bgrqd', p_c, v_cmp.astype(f32))
        imp = jnp.einsum('bgrqn,ns->bgqs', p_c, cover)
        cur = (t // SEL_BLOCK)[:, None]
        forced = (blk_ids[None, :] == 0) | (blk_ids[None, :] == cur) | (blk_ids[None, :] == cur - 1)
        imp = jnp.where(blk_ids[None, :] * SEL_BLOCK <= t[:, None], imp + jnp.where(forced, FORCE_BONUS, 0.0), NEG)
        _, idx = lax.top_k(imp, n_sel)
        kg, vg = take(ksb, idx), take(vsb, idx)
        s_s = jnp.einsum('bgrqd,bgqnkd->bgrqnk', qb, kg, preferred_element_type=f32) * scale
        key_pos = idx[..., None] * SEL_BLOCK + jnp.arange(SEL_BLOCK)
        valid_s = (key_pos <= t[:, None, None])[:, :, None]
        p_s = masked_softmax(s_s.reshape(B, G, R, QB, -1), valid_s.reshape(B, G, 1, QB, -1)).reshape(s_s.shape)
        o_s = jnp.einsum('bgrqnk,bgqnkd->bgrqd', p_s, vg.astype(f32))
        kw = lax.dynamic_slice_in_dim(kwp, q0, WINDOW + QB, axis=2)
        vw = lax.dynamic_slice_in_dim(vwp, q0, WINDOW + QB, axis=2)
        kpos = q0 - WINDOW + jnp.arange(WINDOW + QB)
        valid_w = (kpos[None, :] >= 0) & (kpos[None, :] <= t[:, None]) & (kpos[None, :] > t[:, None] - WINDOW)
        s_w = jnp.einsum('bgrqd,bgkd->bgrqk', qb, kw, preferred_element_type=f32) * scale
        p_w = masked_softmax(s_w, valid_w)
        o_w = jnp.einsum('bgrqk,bgkd->bgrqd', p_w, vw.astype(f32))
        return gb[..., 0:1] * o_c + gb[..., 1:2] * o_s + gb[..., 2:3] * o_w

    out = lax.map(one_block, jnp.arange(T // QB))
    return jnp.moveaxis(out, 0, 3).reshape(B, G, R, T, d).astype(q.dtype)


def nsa_mixer(h, w_in, w_out, q_norm, k_norm, cmp_pos, cmp_w1, cmp_b1, cmp_w2, gate_bias):
    B, T, _ = h.shape
    d, G, R = NSA_HEAD_DIM, NSA_KV_GROUPS, NSA_HEADS // NSA_KV_GROUPS
    q, kc, vc, ks, vs, kw, vw, gp = _split(h @ w_in, C_SIZES)

    def heads(a, n):
        return a.reshape(B, T, n, d).transpose(0, 2, 1, 3)

    pos = jnp.arange(T)
    q = rope(rmsnorm(heads(q, NSA_HEADS), q_norm), pos).reshape(B, G, R, T, d)
    k_sel = rope(rmsnorm(heads(ks, G), k_norm[1]), pos)
    k_win = rope(rmsnorm(heads(kw, G), k_norm[2]), pos)
    k_cmp = compress_blocks(heads(kc, G), cmp_pos[0], cmp_w1[0], cmp_b1[0], cmp_w2[0])
    nc = k_cmp.shape[2]
    k_cmp = rope(rmsnorm(k_cmp, k_norm[0]), jnp.arange(nc) * CMP_STRIDE + CMP_BLOCK - 1)
    v_cmp = compress_blocks(heads(vc, G), cmp_pos[1], cmp_w1[1], cmp_b1[1], cmp_w2[1])
    gates = jax.nn.sigmoid(gp + gate_bias).reshape(B, T, G, R, 3).transpose(0, 2, 3, 1, 4)
    o = nsa_attention(q, k_cmp, v_cmp, k_sel, heads(vs, G), k_win, heads(vw, G), gates)
    o = o.reshape(B, NSA_HEADS, T, d).transpose(0, 2, 1, 3).reshape(B, T, NSA_HEADS * d)
    return o @ w_out


def setup_inputs(seed: int = 0) -> dict:
    key = jax.random.key(seed)
    ks = jax.random.split(key, 40)
    ctr = [0]

    def nrm(shape, scale):
        kk = ks[ctr[0]]
        ctr[0] += 1
        return jax.random.normal(kk, shape, jnp.float32) * scale

    def gain(shape):
        return 1.0 + nrm(shape, 0.02)

    D, F = D_MODEL, D_FF
    d = NSA_HEAD_DIM
    return {
        'x': nrm((BATCH, SEQ, D), 1.0),
        'ffn1_norm': gain((DEPTH, D)),
        'ffn1_w_gate': nrm((DEPTH, D, F), D ** -0.5),
        'ffn1_w_up': nrm((DEPTH, D, F), D ** -0.5),
        'ffn1_w_down': nrm((DEPTH, F, D), F ** -0.5),
        'mix_norm': gain((DEPTH, D)),
        'ffn2_norm': gain((DEPTH, D)),
        'ffn2_w_gate': nrm((DEPTH, D, F), D ** -0.5),
        'ffn2_w_up': nrm((DEPTH, D, F), D ** -0.5),
        'ffn2_w_down': nrm((DEPTH, F, D), F ** -0.5),
        'ab_w_in': nrm((N_AB, D, AB_COLS), D ** -0.5),
        'ab_w_out': nrm((N_AB, ML_WIDTH + HG_WIDTH, D), (ML_WIDTH + HG_WIDTH) ** -0.5),
        'ml_conv_w': nrm((N_AB, ML_CONV, ML_WIDTH), ML_CONV ** -0.5),
        'ml_conv_b': nrm((N_AB, ML_WIDTH), 0.02),
        'ml_wq': nrm((N_AB, ML_HEADS, ML_HEAD_DIM, ML_HEAD_DIM), ML_HEAD_DIM ** -0.5),
        'ml_wk': nrm((N_AB, ML_HEADS, ML_HEAD_DIM, ML_HEAD_DIM), ML_HEAD_DIM ** -0.5),
        'ml_i_bias': nrm((N_AB, ML_HEADS), 0.1),
        'ml_f_bias': jnp.linspace(3.0, 6.0, ML_HEADS, dtype=jnp.float32) + nrm((N_AB, ML_HEADS), 0.1),
        'ml_out_norm': gain((N_AB, ML_HEADS, ML_HEAD_DIM)),
        'ml_skip': gain((N_AB, ML_WIDTH)),
        'hg_lb_logits': nrm((N_AB, HG_KEY_WIDTH), 0.5),
        'hg_out_norm': gain((N_AB, HG_HEADS, HG_VAL_DIM)),
        'c_w_in': nrm((N_C, D, C_COLS), D ** -0.5),
        'c_w_out': nrm((N_C, NSA_HEADS * d, D), (NSA_HEADS * d) ** -0.5),
        'c_q_norm': gain((N_C, d)),
        'c_k_norm': gain((N_C, 3, d)),
        'c_cmp_pos': nrm((N_C, 2, CMP_BLOCK, d), 0.1),
        'c_cmp_w1': nrm((N_C, 2, CMP_BLOCK * d, CMP_HIDDEN), (CMP_BLOCK * d) ** -0.5),
        'c_cmp_b1': nrm((N_C, 2, CMP_HIDDEN), 0.02),
        'c_cmp_w2': nrm((N_C, 2, CMP_HIDDEN, d), CMP_HIDDEN ** -0.5),
        'c_gate_bias': nrm((N_C, 3 * NSA_HEADS), 0.1),
    }


def reference(x, ffn1_norm, ffn1_w_gate, ffn1_w_up, ffn1_w_down, mix_norm, ffn2_norm, ffn2_w_gate, ffn2_w_up,
              ffn2_w_down, ab_w_in, ab_w_out, ml_conv_w, ml_conv_b, ml_wq, ml_wk, ml_i_bias, ml_f_bias, ml_out_norm,
              ml_skip, hg_lb_logits, hg_out_norm, c_w_in, c_w_out, c_q_norm, c_k_norm, c_cmp_pos, c_cmp_w1, c_cmp_b1,
              c_cmp_w2, c_gate_bias):
    lb_soft = jax.nn.softmax(hg_lb_logits.astype(jnp.float32), axis=0)
    lb_all = jnp.cumsum(lb_soft, axis=0) - lb_soft[0]
    h = x
    for layer in range(DEPTH):
        h = h + 0.5 * swiglu(rmsnorm(h, ffn1_norm[layer]), ffn1_w_gate[layer], ffn1_w_up[layer], ffn1_w_down[layer])
        hn = rmsnorm(h, mix_norm[layer])
        j = layer // 2
        if layer % 2 == 0:
            h = h + ab_mixer(hn, ab_w_in[j], ab_w_out[j], ml_conv_w[j], ml_conv_b[j], ml_wq[j], ml_wk[j],
                             ml_i_bias[j], ml_f_bias[j], ml_out_norm[j], ml_skip[j], lb_all[j], hg_out_norm[j])
        else:
            h = h + nsa_mixer(hn, c_w_in[j], c_w_out[j], c_q_norm[j], c_k_norm[j], c_cmp_pos[j], c_cmp_w1[j],
                              c_cmp_b1[j], c_cmp_w2[j], c_gate_bias[j])
        h = h + 0.5 * swiglu(rmsnorm(h, ffn2_norm[layer]), ffn2_w_gate[layer], ffn2_w_up[layer], ffn2_w_down[layer])
    return h
```

```python
import numpy as np
import ml_dtypes
import concourse.bass as bass
import concourse.mybir as mybir
from concourse.bass_utils import run_bass_kernel_spmd

F32 = mybir.dt.float32
BF16 = mybir.dt.bfloat16
I32 = mybir.dt.int32
AF = mybir.ActivationFunctionType
ALU = mybir.AluOpType
AX = mybir.AxisListType

D_MODEL = 2048
D_FF = 5504
EPS = 1e-6
SAME_SYNC = True


class Tok:
    __slots__ = ("sem", "val", "key", "eng")

    def __init__(self, sem, val, key, eng):
        self.sem, self.val, self.key, self.eng = sem, val, key, eng


class Buf:
    __slots__ = ("name", "w", "r", "bank")

    def __init__(self, name):
        self.name, self.w, self.r, self.bank = name, None, {}, None


class DSem:
    __slots__ = ("h", "val", "key")

    def __init__(self, h, key):
        self.h, self.val, self.key = h, 0, key


class KB:
    def __init__(self, nc):
        self.nc = nc
        self.engs = dict(pe=nc.tensor, act=nc.scalar, dve=nc.vector, pool=nc.gpsimd, sp=nc.sync)
        self.sem = {e: nc.alloc_semaphore("sem_" + e) for e in self.engs}
        self.cnt = {e: 0 for e in self.engs}
        self.waited = {e: {} for e in self.engs}
        self.nds = 0
        self.out_toks = []

    def buf(self, name=""):
        return Buf(name)

    def bufs(self, n, name=""):
        return [Buf(f"{name}{i}") for i in range(n)]

    def dsem(self, name=None):
        self.nds += 1
        key = f"D{self.nds}"
        return DSem(self.nc.alloc_semaphore(name or key), key)

    def _wait(self, e, tok, raw=False):
        if tok is None:
            return
        if tok.eng == e and not (raw and SAME_SYNC and e != "pe"):
            return
        w = self.waited[e]
        if w.get(tok.key, 0) >= tok.val:
            return
        self.engs[e].wait_ge(tok.sem, tok.val)
        w[tok.key] = tok.val

    def op(self, e, fn, reads=(), writes=(), dsem=None):
        for b in reads:
            self._wait(e, b.w, raw=True)
        for b in writes:
            self._wait(e, b.w)
            for t in b.r.values():
                self._wait(e, t)
        banks = {}
        for b in list(reads) + list(writes):
            if b.bank is not None:
                banks[id(b.bank)] = b.bank
        for bk in banks.values():
            self._wait(e, bk.w)
        ins = fn(self.engs[e])
        if dsem is None:
            self.cnt[e] += 1
            ins.then_inc(self.sem[e], 1)
            tok = Tok(self.sem[e], self.cnt[e], "E" + e, e)
        else:
            dsem.val += 16
            ins.then_inc(dsem.h, 16)
            tok = Tok(dsem.h, dsem.val, dsem.key, None)
        for b in reads:
            b.r[tok.key] = tok
        for b in writes:
            b.w = tok
            b.r = {}
        for bk in banks.values():
            bk.w = tok
        return tok

    def finish(self, toks):
        for t in toks:
            self._wait("sp", t)


class Ring:
    def __init__(self, k, slots, name="ws"):
        self.k = k
        self.slots = slots
        self.ns = len(slots)
        self.b = k.bufs(self.ns, name)
        self.ds = [k.dsem() for _ in range(self.ns)]
        self.loads = []
        self.issued = 0
        self.consumed = 0

    def plan(self, dst_fn, src):
        self.loads.append((dst_fn, src))

    def _issue(self):
        if self.issued >= len(self.loads):
            return
        i = self.issued
        s = i % self.ns
        dst_fn, src = self.loads[i]
        slot = self.slots[s]
        self.k.op("pool", lambda e: e.dma_start(out=dst_fn(slot), in_=src), reads=(), writes=[self.b[s]],
                  dsem=self.ds[s])
        self.issued += 1

    def start(self):
        while self.issued < min(self.ns, len(self.loads)):
            self._issue()

    def get(self, off=0):
        i = self.consumed + off
        assert i < self.issued, "ring underflow"
        s = i % self.ns
        return self.slots[s], self.b[s]

    def done(self):
        self.consumed += 1
        self._issue()


def build_ffn(NT=2048, F=D_FF, preproj=False, TP=1024, NS=4):
    D = D_MODEL
    DC = D // 128
    FCn = F // 128
    assert F % 128 == 0 and NT % TP == 0 and TP % 512 == 0
    NTT = TP // 512
    nc = bass.Bass("TRN2", target_bir_lowering=False)
    k = KB(nc)
    hT_in = nc.dram_tensor("hT", [DC, 128, NT], F32, kind="ExternalInput").ap()
    nw = nc.dram_tensor("nw", [128, DC], F32, kind="ExternalInput").ap()
    wg = nc.dram_tensor("wg", [D, F], F32, kind="ExternalInput").ap()
    wu = nc.dram_tensor("wu", [D, F], F32, kind="ExternalInput").ap()
    wd = nc.dram_tensor("wd", [F, D], F32, kind="ExternalInput").ap()
    hT_out = nc.dram_tensor("hT_out", [DC, 128, NT], F32, kind="ExternalOutput").ap()
    if preproj:
        yT = nc.dram_tensor("yT", [DC, 128, NT], BF16, kind="ExternalInput").ap()
        wo = nc.dram_tensor("wo", [D, D], F32, kind="ExternalInput").ap()
        h2T = nc.dram_tensor("h2T", [DC, 128, NT], F32).ap()
        h_src = h2T
    else:
        h_src = hT_in
    wg_v = wg.rearrange("(dc p) f -> p dc f", p=128)
    wu_v = wu.rearrange("(dc p) f -> p dc f", p=128)
    wd_v = wd.rearrange("(fc p) m -> p fc m", p=128)

    actT = nc.alloc_sbuf_tensor("actT", [128, FCn, TP], BF16)
    xT = nc.alloc_sbuf_tensor("xT", [128, DC, TP], BF16)
    slots = [nc.alloc_sbuf_tensor(f"ws{i}", [128, 16, 256], BF16) for i in range(NS)]
    NSCR = 6
    scr = [nc.alloc_sbuf_tensor(f"scr{i}", [128, TP], F32) for i in range(NSCR)]
    rstd = nc.alloc_sbuf_tensor("rstd", [128, TP], F32)
    ones = nc.alloc_sbuf_tensor("ones", [128, 128], F32)
    nw_sb = nc.alloc_sbuf_tensor("nw_sb", [128, DC], F32)
    epst = nc.alloc_sbuf_tensor("epst", [128, 1], F32)
    ps = [nc.alloc_psum_tensor(f"ps{i}", [128, 512], F32) for i in range(8)]

    actT_b = k.bufs(FCn, "actT")
    xT_b = k.bufs(DC, "xT")
    scr_b = k.bufs(NSCR, "scr")
    scr_ds = [k.dsem() for _ in range(NSCR)]
    rstd_b = k.buf("rstd")
    ones_b = k.buf("ones")
    eps_b = k.buf("eps")
    nw_b = k.buf("nw")
    nw_ds = k.dsem()
    ps_b = k.bufs(8, "ps")
    ring = Ring(k, slots)
    npass = NT // TP
    h2_b = [[k.buf(f"h2_{p}_{dc}") for dc in range(DC)] for p in range(npass)]
    yT_ds = k.dsem()

    fblocks = []
    f0 = 0
    while f0 < F:
        fw = min(256, F - f0)
        fblocks.append((f0, fw))
        f0 += fw
    dsegs = []
    c0 = 0
    while c0 < FCn:
        n = min(16, FCn - c0)
        dsegs.append((c0, n))
        c0 += n
    NG = D // 256

    for p in range(npass):
        if preproj:
            for dc in range(DC):
                ring.plan(lambda s: s[:, :, 0:128], wo.rearrange("(yc p) m -> p yc m", p=128)[:, :, dc * 128:(dc + 1) * 128])
        for (f0, fw) in fblocks:
            ring.plan(lambda s, fw=fw: s[:, :, 0:fw], wg_v[:, :, f0:f0 + fw])
            ring.plan(lambda s, fw=fw: s[:, :, 0:fw], wu_v[:, :, f0:f0 + fw])
        for gi in range(NG):
            for (c0, n) in dsegs:
                ring.plan(lambda s, n=n: s[:, 0:n, :], wd_v[:, c0:c0 + n, gi * 256:(gi + 1) * 256])

    k.op("dve", lambda e: e.memset(ones[:], 1.0), writes=[ones_b])
    k.op("dve", lambda e: e.memset(epst[:], EPS), writes=[eps_b])
    k.op("sp", lambda e: e.dma_start(out=nw_sb[:], in_=nw), writes=[nw_b], dsem=nw_ds)
    ring.start()
    out_toks = []

    for p in range(npass):
        t0 = p * TP
        tsl = slice(t0, t0 + TP)
        if preproj:
            k.op("sp", lambda e: e.dma_start(out=actT[:, 0:DC, :], in_=yT.rearrange("yc p t -> p yc t")[:, :, tsl]),
                 writes=actT_b[0:DC], dsem=yT_ds)
        for dc in range(DC):
            hi = dc % 2
            ht = scr[hi]
            k.op("sp", lambda e: e.dma_start(out=ht[:], in_=hT_in[dc, :, tsl]), writes=[scr_b[hi]], dsem=scr_ds[hi])
            if preproj:
                slot, sb = ring.get()
                pb = 2 + 2 * (dc % 2)

                def mm(e):
                    for yc in range(DC):
                        for tt in range(NTT):
                            ins = e.matmul(ps[pb + tt][:], lhsT=slot[:, yc, 0:128],
                                           rhs=actT[:, yc, tt * 512:(tt + 1) * 512], start=(yc == 0), stop=(yc == DC - 1))
                    return ins
                k.op("pe", mm, reads=[sb] + actT_b[0:DC], writes=[ps_b[pb + tt] for tt in range(NTT)])
                ring.done()
                for tt in range(NTT):
                    k.op("dve", lambda e: e.tensor_tensor(out=ht[:, tt * 512:(tt + 1) * 512], in0=ps[pb + tt][:],
                                                          in1=ht[:, tt * 512:(tt + 1) * 512], op=ALU.add),
                         reads=[ps_b[pb + tt]], writes=[scr_b[hi]])
                k.op("sp", lambda e: e.dma_start(out=h2T[dc, :, tsl], in_=ht[:]), reads=[scr_b[hi]],
                     writes=[h2_b[p][dc]], dsem=scr_ds[hi])
            si = 2 + dc % 2
            sq = scr[si]
            k.op("act", lambda e: e.activation(out=sq[:], in_=ht[:], func=AF.Square), reads=[scr_b[hi]], writes=[scr_b[si]])

            def mm(e):
                for tt in range(NTT):
                    ins = e.matmul(ps[tt][:], lhsT=ones[:], rhs=sq[:, tt * 512:(tt + 1) * 512], start=(dc == 0),
                                   stop=(dc == DC - 1))
                return ins
            k.op("pe", mm, reads=[scr_b[si], ones_b], writes=[ps_b[tt] for tt in range(NTT)])
        for tt in range(NTT):
            k.op("act", lambda e: e.activation(out=rstd[:, tt * 512:(tt + 1) * 512], in_=ps[tt][:], func=AF.Sqrt,
                                               scale=1.0 / D, bias=epst[:, 0:1]),
                 reads=[ps_b[tt], eps_b], writes=[rstd_b])
        k.op("dve", lambda e: e.reciprocal(out=rstd[:], in_=rstd[:]), reads=[rstd_b], writes=[rstd_b])
        for dc in range(DC):
            hi = dc % 2
            ht = scr[hi]
            rd = [h2_b[p][dc]] if preproj else []
            k.op("sp", lambda e: e.dma_start(out=ht[:], in_=h_src[dc, :, tsl]), reads=rd, writes=[scr_b[hi]],
                 dsem=scr_ds[hi])
            k.op("dve", lambda e: e.scalar_tensor_tensor(out=xT[:, dc, :], in0=ht[:], scalar=nw_sb[:, dc:dc + 1],
                                                         in1=rstd[:], op0=ALU.mult, op1=ALU.mult),
                 reads=[scr_b[hi], rstd_b, nw_b], writes=[xT_b[dc]])
        ci = 0
        for (f0, fw) in fblocks:
            sg_, sgb = ring.get(0)
            su_, sub = ring.get(1)
            for j in range(fw // 128):
                fi = f0 // 128 + j
                par = ci % 2
                ci += 1
                gb = 4 * par
                ub = 4 * par + 2

                def mm(e, w_=None, b0=0):
                    for dc in range(DC):
                        for tt in range(NTT):
                            ins = e.matmul(ps[b0 + tt][:], lhsT=w_[:, dc, j * 128:(j + 1) * 128],
                                           rhs=xT[:, dc, tt * 512:(tt + 1) * 512], start=(dc == 0), stop=(dc == DC - 1))
                    return ins
                k.op("pe", lambda e: mm(e, sg_, gb), reads=[sgb] + xT_b, writes=[ps_b[gb + tt] for tt in range(NTT)])
                k.op("pe", lambda e: mm(e, su_, ub), reads=[sub] + xT_b, writes=[ps_b[ub + tt] for tt in range(NTT)])
                sgi = 4 + par
                sgt = scr[sgi]
                for tt in range(NTT):
                    k.op("act", lambda e: e.activation(out=sgt[:, tt * 512:(tt + 1) * 512], in_=ps[gb + tt][:], func=AF.Silu),
                         reads=[ps_b[gb + tt]], writes=[scr_b[sgi]])
                    k.op("dve", lambda e: e.tensor_tensor(out=actT[:, fi, tt * 512:(tt + 1) * 512],
                                                          in0=sgt[:, tt * 512:(tt + 1) * 512], in1=ps[ub + tt][:], op=ALU.mult),
                         reads=[scr_b[sgi], ps_b[ub + tt]], writes=[actT_b[fi]])
            ring.done()
            ring.done()
        for gi in range(NG):
            base = 4 * (gi % 2)
            for dmi in range(2):
                dmc = gi * 2 + dmi
                hi = dmi
                rd = [h2_b[p][dmc]] if preproj else []
                k.op("sp", lambda e: e.dma_start(out=scr[hi][:], in_=h_src[dmc, :, tsl]), reads=rd, writes=[scr_b[hi]],
                     dsem=scr_ds[hi])
            for (c0, n) in dsegs:
                slot, sb = ring.get()

                def mm(e):
                    for j in range(n):
                        fc = c0 + j
                        for dmi in range(2):
                            for tt in range(NTT):
                                ins = e.matmul(ps[base + 2 * dmi + tt][:], lhsT=slot[:, j, dmi * 128:(dmi + 1) * 128],
                                               rhs=actT[:, fc, tt * 512:(tt + 1) * 512], start=(fc == 0), stop=(fc == FCn - 1))
                    return ins
                k.op("pe", mm, reads=[sb] + actT_b[c0:c0 + n], writes=[ps_b[base + i] for i in range(4)])
                ring.done()
            for dmi in range(2):
                dmc = gi * 2 + dmi
                hi = dmi
                oi = 2 + dmi
                ot = scr[oi]
                for tt in range(NTT):
                    bk = base + 2 * dmi + tt
                    k.op("dve", lambda e: e.scalar_tensor_tensor(out=ot[:, tt * 512:(tt + 1) * 512], in0=ps[bk][:], scalar=0.5,
                                                                 in1=scr[hi][:, tt * 512:(tt + 1) * 512], op0=ALU.mult, op1=ALU.add),
                         reads=[ps_b[bk], scr_b[hi]], writes=[scr_b[oi]])
                tk = k.op("sp", lambda e: e.dma_start(out=hT_out[dmc, :, tsl], in_=ot[:]), reads=[scr_b[oi]], dsem=scr_ds[oi])
                out_toks.append(tk)
    k.finish(out_toks)
    return nc


class TT:
    def __init__(self, k, t, name):
        self.t = t
        self.b = k.buf(name)

    def __getitem__(self, idx):
        return self.t[idx]


def sb(k, name, shape, dt):
    return TT(k, k.nc.alloc_sbuf_tensor(name, list(shape), dt), name)


class PReg:
    _bankbufs = {}

    def __init__(self, k, bank, c0, c1, name):
        self.bank, self.c0, self.c1 = bank, c0, c1
        self.b = k.buf(name)
        key = (id(k), bank.name if hasattr(bank, "name") else id(bank))
        if key not in PReg._bankbufs:
            PReg._bankbufs[key] = k.buf("bank")
        self.b.bank = PReg._bankbufs[key]

    def ap(self, rows=slice(None), a=None, b=None):
        a = self.c0 if a is None else self.c0 + a
        b = self.c1 if b is None else self.c0 + b
        return self.bank[rows, a:b]


def make_consts(k, need_ident=True):
    nc = k.nc
    c = {}
    c["ones"] = sb(k, "c_ones", [128, 128], F32)
    k.op("dve", lambda e: e.memset(c["ones"][:], 1.0), writes=[c["ones"].b])
    c["ident"] = sb(k, "c_ident", [128, 128], F32)
    k.op("pool", lambda e: e.affine_select(out=c["ident"][:], in_=c["ones"][:], pattern=[[-1, 128]],
                                           compare_op=ALU.is_equal, fill=0.0, base=0, channel_multiplier=1),
         reads=[c["ones"].b], writes=[c["ident"].b])
    c["causal"] = sb(k, "c_causal", [128, 128], F32)
    k.op("pool", lambda e: e.affine_select(out=c["causal"][:], in_=c["ones"][:], pattern=[[1, 128]],
                                           compare_op=ALU.is_ge, fill=0.0, base=0, channel_multiplier=-1),
         reads=[c["ones"].b], writes=[c["causal"].b])
    c["eps"] = sb(k, "c_eps", [128, 1], F32)
    k.op("dve", lambda e: e.memset(c["eps"][:], EPS), writes=[c["eps"].b])
    c["one1"] = sb(k, "c_one1", [128, 1], F32)
    k.op("dve", lambda e: e.memset(c["one1"][:], 1.0), writes=[c["one1"].b])
    return c


def norm_supertile(k, c, hT_src, nw_sb, hTt, xT, ps_ss, rstd, scrsq, t0, TW, ds_h, h_reads=()):
    DC = 16
    k.op("sp", lambda e: e.dma_start(out=hTt[:, :, 0:TW], in_=hT_src.rearrange("dc p t -> p dc t")[:, :, t0:t0 + TW]),
         reads=list(h_reads), writes=[hTt.b], dsem=ds_h)
    for dc in range(DC):
        sq = scrsq[dc % 2]
        k.op("act", lambda e: e.activation(out=sq[:, 0:TW], in_=hTt[:, dc, 0:TW], func=AF.Square), reads=[hTt.b],
             writes=[sq.b])
        k.op("pe", lambda e: e.matmul(ps_ss.ap(b=TW), lhsT=c["ones"][:], rhs=sq[:, 0:TW], start=(dc == 0), stop=(dc == DC - 1)),
             reads=[sq.b, c["ones"].b], writes=[ps_ss.b])
    k.op("act", lambda e: e.activation(out=rstd[:, 0:TW], in_=ps_ss.ap(b=TW), func=AF.Sqrt, scale=1.0 / D_MODEL,
                                       bias=c["eps"][:, 0:1]), reads=[ps_ss.b, c["eps"].b], writes=[rstd.b])
    k.op("dve", lambda e: e.reciprocal(out=rstd[:, 0:TW], in_=rstd[:, 0:TW]), reads=[rstd.b], writes=[rstd.b])
    for dc in range(DC):
        k.op("dve", lambda e: e.scalar_tensor_tensor(out=xT[:, dc, 0:TW], in0=hTt[:, dc, 0:TW], scalar=nw_sb[:, dc:dc + 1],
                                                     in1=rstd[:, 0:TW], op0=ALU.mult, op1=ALU.mult),
             reads=[hTt.b, rstd.b, nw_sb.b], writes=[xT.b])


HG_MAX_K = 0.999999


class _Stop(Exception):
    pass


def build_ab(T=8192, layer_j=0, do_ml=2, do_hg=2, stop=99, debug=False):
    TW = 512
    NST = T // TW
    NCOL = 1794
    nc = bass.Bass("TRN2", target_bir_lowering=False)
    k = KB(nc)

    def dram(name, shape, dt=F32, kind="ExternalInput"):
        return nc.dram_tensor(name, list(shape), dt, kind=kind).ap()
    hT = dram("hT", [16, 128, T])
    nw = dram("nw", [128, 16])
    win = dram("win", [2048, NCOL])
    cw = dram("cw", [128, 2, 4])
    cb = dram("cb", [128, 2])
    wq = dram("wq", [256, 256])
    wk = dram("wk", [256, 256])
    gbias = dram("gb", [128, 2])
    mln = dram("mln", [128, 256])
    skp = dram("skp", [128, 256])
    lbl = dram("lbl", [128, 2, 2])
    hgn = dram("hgn", [128, 2, 128])
    yT = dram("yT", [4, 128, T], BF16, kind="ExternalOutput")
    dbg = dram("dbg", [128, 8192], F32, kind="ExternalOutput") if debug else None
    dbg_pos = [0]
    dbg_map = {}
    dbg_ds = k.dsem() if debug else None

    def dump(name, tt, ap, n):
        if not debug or name in dbg_map:
            return
        c0 = dbg_pos[0]
        dbg_pos[0] += n
        dbg_map[name] = (c0, n)
        k.op("sp", lambda e: e.dma_start(out=dbg[:, c0:c0 + n], in_=ap), reads=[tt.b], dsem=dbg_ds)
    nc._dbg_map = dbg_map

    c = make_consts(k)
    ps = [nc.alloc_psum_tensor(f"ps{i}", [128, 512], F32) for i in range(8)]
    acc = [PReg(k, ps[i], 0, 512, f"acc{i}") for i in range(2)]
    ps_ss = PReg(k, ps[2], 0, 512, "ss")
    regT = [PReg(k, ps[2], i * 128, (i + 1) * 128, f"regT{i}") for i in range(4)]
    regA = PReg(k, ps[3], 0, 8, "regA")
    regB = PReg(k, ps[3], 128, 257, "regB")
    regC = PReg(k, ps[3], 384, 512, "regC")
    regND = PReg(k, ps[4], 0, 264, "regND")
    regU = [PReg(k, ps[5 + i], 0, 264, f"regU{i}") for i in range(2)]
    r7A = PReg(k, ps[7], 0, 128, "r7A")
    r7B = PReg(k, ps[7], 128, 256, "r7B")
    r7C = PReg(k, ps[7], 256, 384, "r7C")
    r7D = PReg(k, ps[7], 384, 512, "r7D")

    win_sb = sb(k, "win_sb", [128, 16, NCOL], BF16)
    wds = [k.dsem() for _ in range(4)]
    cuts = [0, 512, 1024, 1536, NCOL]
    win_v = win.rearrange("(dc p) f -> p dc f", p=128)
    wtoks = []
    for i in range(4):
        a, b_ = cuts[i], cuts[i + 1]
        wtoks.append(k.op("pool", lambda e: e.dma_start(out=win_sb[:, :, a:b_], in_=win_v[:, :, a:b_]), writes=[],
                          dsem=wds[i]))
    wq_sb = sb(k, "wq_sb", [128, 2, 256], BF16)
    wk_sb = sb(k, "wk_sb", [128, 2, 256], BF16)
    pds = k.dsem()
    k.op("pool", lambda e: e.dma_start(out=wq_sb[:], in_=wq.rearrange("(d p) e -> p d e", p=128)), writes=[wq_sb.b], dsem=pds)
    k.op("pool", lambda e: e.dma_start(out=wk_sb[:], in_=wk.rearrange("(d p) e -> p d e", p=128)), writes=[wk_sb.b], dsem=k.dsem())

    def ld(name, shape, src):
        t = sb(k, name, shape, F32)
        k.op("sp", lambda e: e.dma_start(out=t[:], in_=src), writes=[t.b], dsem=k.dsem())
        return t
    nw_sb = ld("nw_sb", [128, 16], nw)
    cw_sb = ld("cw_sb", [128, 2, 4], cw)
    cb_sb = ld("cb_sb", [128, 2], cb)
    gb_sb = ld("gb_sb", [128, 2], gbias)
    mln_sb = ld("mln_sb", [128, 256], mln)
    skp_sb = ld("skp_sb", [128, 256], skp)
    lbl_sb = ld("lbl_sb", [128, 2, 2], lbl)
    hgn_sb = ld("hgn_sb", [128, 2, 128], hgn)
    for tkn in wtoks:
        k._wait("pe", tkn)

    oml = sb(k, "oml", [128, 2], F32)
    lbe = sb(k, "lbe", [128, 2, 2], F32)
    lbs = sb(k, "lbs", [128, 2], F32)
    k.op("act", lambda e: e.activation(out=lbe[:], in_=lbl_sb[:], func=AF.Exp), reads=[lbl_sb.b], writes=[lbe.b])
    k.op("dve", lambda e: e.tensor_tensor(out=lbs[:], in0=lbe[:, :, 0], in1=lbe[:, :, 1], op=ALU.add), reads=[lbe.b], writes=[lbs.b])
    k.op("dve", lambda e: e.reciprocal(out=lbs[:], in_=lbs[:]), reads=[lbs.b], writes=[lbs.b])
    for l in range(2):
        k.op("dve", lambda e: e.tensor_tensor(out=lbe[:, :, l], in0=lbe[:, :, l], in1=lbs[:], op=ALU.mult), reads=[lbe.b, lbs.b],
             writes=[lbe.b])
    k.op("dve", lambda e: e.tensor_copy(out=oml[:], in_=lbe[:, :, 0]), reads=[lbe.b], writes=[oml.b])
    for l in range(1, layer_j + 1):
        k.op("dve", lambda e: e.tensor_tensor(out=oml[:], in0=oml[:], in1=lbe[:, :, l], op=ALU.add), reads=[lbe.b, oml.b], writes=[oml.b])
    k.op("dve", lambda e: e.tensor_tensor(out=oml[:], in0=oml[:], in1=lbe[:, :, 0], op=ALU.subtract), reads=[lbe.b, oml.b], writes=[oml.b])
    k.op("dve", lambda e: e.tensor_scalar(out=oml[:], in0=oml[:], scalar1=-1.0, scalar2=1.0, op0=ALU.mult, op1=ALU.add),
         reads=[oml.b], writes=[oml.b])
    nfb = sb(k, "nfb", [128, 1], F32)
    k.op("dve", lambda e: e.tensor_scalar(out=nfb[:], in0=gb_sb[:, 1:2], scalar1=-1.0, scalar2=None, op0=ALU.mult),
         reads=[gb_sb.b], writes=[nfb.b])

    hTt = sb(k, "hTt", [128, 16, TW], F32)
    ds_h = k.dsem()
    xT = sb(k, "xT", [128, 16, TW], BF16)
    rstd = sb(k, "rstd", [128, TW], F32)
    scrsq = [sb(k, f"scrsq{i}", [128, TW], F32) for i in range(2)]
    ubuf = sb(k, "ubuf", [128, 2, TW + 3], F32)
    cacc = sb(k, "cacc", [128, TW], F32)
    cT = sb(k, "cT", [128, 2, TW], F32)
    cTb = sb(k, "cTb", [128, 2, TW], BF16)
    qT = sb(k, "qT", [128, 2, TW], F32)
    qTb = sb(k, "qTb", [128, 2, TW], BF16)
    kTb = sb(k, "kTb", [128, 2, TW], BF16)
    ktok = sb(k, "ktok", [128, 4, 256], F32)
    ctok = sb(k, "ctok", [128, 4, 256], F32)
    vext = sb(k, "vext", [128, 4, 264], BF16)
    osig = sb(k, "osig", [128, 4, 256], F32)
    hv = sb(k, "hv", [128, 4, 2, 128], BF16)
    hgs = sb(k, "hgs", [128, 4, 256], F32)
    ge1 = sb(k, "ge1", [128, 4], F32)
    logf = sb(k, "logf", [128, 4], F32)
    ig = sb(k, "ig", [128, 4], F32)
    lfb = sb(k, "lfb", [128, 128], F32)
    bias_s = sb(k, "bias_s", [128, 1], F32)
    DT = sb(k, "DT", [128, 128], F32)
    Eb = sb(k, "Eb", [128, 128], F32)
    Dm = sb(k, "Dm", [128, 128], F32)
    PT = sb(k, "PT", [128, 128], BF16)
    qs = sb(k, "qs", [128, 2, 128], BF16)
    small = sb(k, "small", [128, 8], F32)
    hn = sb(k, "hn", [128, 256], F32)
    junk = sb(k, "junk", [128, 256], F32)
    hm = sb(k, "hm", [128, 256], F32)
    t1 = sb(k, "t1", [128, 256], F32)
    yml = sb(k, "yml", [128, 256], F32)
    ka = sb(k, "ka", [128, 256], BF16)
    Cst = sb(k, "Cst", [128, 2, 264], F32)
    Cb = sb(k, "Cb", [128, 2, 264], BF16)
    ystage = sb(k, "ystage", [128, 4, TW], BF16)
    ds_y = k.dsem()
    resetm = sb(k, "resetm", [128, TW], F32)
    tA = sb(k, "tA", [128, TW], F32)
    tB = sb(k, "tB", [128, TW], F32)
    tC = sb(k, "tC", [128, TW], F32)
    k2 = sb(k, "k2", [128, TW], F32)
    lf1 = sb(k, "lf1", [128, TW], F32)
    lgf = sb(k, "lgf", [128, TW], F32)
    bt = sb(k, "bt", [128, TW], F32)
    brel = sb(k, "brel", [128, TW], F32)
    sqt = sb(k, "sqt", [128, TW], F32)
    kgT = sb(k, "kgT", [128, TW], F32)
    eg8 = sb(k, "eg8", [128, 8], F32)
    qz = [sb(k, f"qz{i}", [128, 4, 128], BF16) for i in range(2)]
    kz = [sb(k, f"kz{i}", [128, 4, 128], BF16) for i in range(2)]
    qbz = [sb(k, f"qbz{i}", [128, 4, 128], BF16) for i in range(2)]
    kgz = [sb(k, f"kgz{i}", [128, 4, 128], BF16) for i in range(2)]
    Am = sb(k, "Am", [128, 128], BF16)
    Sst = [sb(k, f"Sst{i}", [128, 128], F32) for i in range(2)]
    Sb = [[sb(k, f"Sb{i}_{j}", [128, 128], BF16) for j in range(2)] for i in range(2)]
    o_sb = sb(k, "o_sb", [128, 128], F32)
    o2n = sb(k, "o2n", [128, 128], F32)
    yh = sb(k, "yh", [128, 128], F32)

    for t_ in [ubuf, Cst, Cb, Sst[0], Sst[1], Sb[0][0], Sb[0][1], Sb[1][0], Sb[1][1]] + qz + kz + qbz + kgz:
        k.op("dve", lambda e: e.memset(t_[:], 0.0), writes=[t_.b])
    k.op("dve", lambda e: e.memset(vext[:], 1.0), writes=[vext.b])
    k.op("dve", lambda e: e.memset(resetm[:], 1.0), writes=[resetm.b])
    k.op("dve", lambda e: e.memset(resetm[:].rearrange("p (c l) -> p c l", l=64)[:, :, 0:1], 0.0), writes=[resetm.b])

    def v3(t, l=64):
        return t[:].rearrange("p (c l) -> p c l", l=l)

    def v4(t):
        return t[:].rearrange("p (a b l) -> p a b l", b=2, l=64)

    tcnt = [0]

    def transpose_to(dst_ap, dst_b, src_ap, src_b, eng="act"):
        r = regT[tcnt[0] % 4]
        tcnt[0] += 1
        k.op("pe", lambda e: e.transpose(r.ap(), src_ap, c["ident"][:]), reads=[src_b, c["ident"].b], writes=[r.b])
        if eng == "act":
            k.op("act", lambda e: e.copy(out=dst_ap, in_=r.ap()), reads=[r.b], writes=[dst_b])
        else:
            k.op("dve", lambda e: e.tensor_copy(out=dst_ap, in_=r.ap()), reads=[r.b], writes=[dst_b])

    acnt = [0]

    def next_acc():
        a = acc[acnt[0] % 2]
        acnt[0] += 1
        return a

    def inproj_fm(col0):
        a = next_acc()

        def mm(e):
            for dc in range(16):
                ins = e.matmul(a.ap(), lhsT=win_sb[:, dc, col0:col0 + 128], rhs=xT[:, dc, :], start=(dc == 0), stop=(dc == 15))
            return ins
        k.op("pe", mm, reads=[xT.b], writes=[a.b])
        return a

    def inproj_tm(tci, col0, ncol, out_reg=None, oc0=0):
        a = out_reg if out_reg is not None else next_acc()

        def mm(e):
            for dc in range(16):
                ins = e.matmul(a.ap(a=oc0, b=oc0 + ncol), lhsT=xT[:, dc, tci * 128:(tci + 1) * 128],
                               rhs=win_sb[:, dc, col0:col0 + ncol], start=(dc == 0), stop=(dc == 15))
            return ins
        k.op("pe", mm, reads=[xT.b], writes=[a.b])
        return a

    out_toks = []
    def body(st, t0):
        if stop <= 1:
            raise _Stop()
        norm_supertile(k, c, hT, nw_sb, hTt, xT, ps_ss, rstd, scrsq, t0, TW, ds_h)
        if stop <= 2:
            raise _Stop()
        body2(st, t0)

    def body2(st, t0):
        if st > 0:
            k.op("dve", lambda e: e.tensor_copy(out=ubuf[:, :, 0:3], in_=ubuf[:, :, TW:TW + 3]), reads=[ubuf.b], writes=[ubuf.b])
        for ch in range(2):
            a = inproj_fm(ch * 128)
            k.op("act", lambda e: e.copy(out=ubuf[:, ch, 3:3 + TW], in_=a.ap()), reads=[a.b], writes=[ubuf.b])
        if stop <= 2.2:
            raise _Stop()
        for ch in range(2):
            k.op("dve", lambda e: e.tensor_scalar(out=cacc[:], in0=ubuf[:, ch, 0:TW], scalar1=cw_sb[:, ch, 0:1], scalar2=None,
                                                  op0=ALU.mult), reads=[ubuf.b, cw_sb.b], writes=[cacc.b])
            for j in range(1, 4):
                k.op("dve", lambda e: e.scalar_tensor_tensor(out=cacc[:], in0=ubuf[:, ch, j:j + TW], scalar=cw_sb[:, ch, j:j + 1],
                                                             in1=cacc[:], op0=ALU.mult, op1=ALU.add),
                     reads=[ubuf.b, cw_sb.b, cacc.b], writes=[cacc.b])
            k.op("act", lambda e: e.activation(out=cT[:, ch, :], in_=cacc[:], func=AF.Silu, bias=cb_sb[:, ch:ch + 1], scale=1.0),
                 reads=[cacc.b, cb_sb.b], writes=[cT.b])
        k.op("dve", lambda e: e.tensor_copy(out=cTb[:], in_=cT[:]), reads=[cT.b], writes=[cTb.b])
        if stop <= 2.5:
            raise _Stop()
        for e_ in range(2):
            a = next_acc()

            def mm(e):
                for d in range(2):
                    ins = e.matmul(a.ap(), lhsT=wq_sb[:, d, e_ * 128:(e_ + 1) * 128], rhs=cTb[:, d, :], start=(d == 0), stop=(d == 1))
                return ins
            k.op("pe", mm, reads=[wq_sb.b, cTb.b], writes=[a.b])
            k.op("act", lambda e: e.copy(out=qT[:, e_, :], in_=a.ap()), reads=[a.b], writes=[qT.b])
            k.op("dve", lambda e: e.tensor_copy(out=qTb[:, e_, :], in_=qT[:, e_, :]), reads=[qT.b], writes=[qTb.b])
            a = next_acc()

            def mm2(e):
                for d in range(2):
                    ins = e.matmul(a.ap(), lhsT=wk_sb[:, d, e_ * 128:(e_ + 1) * 128], rhs=cTb[:, d, :], start=(d == 0), stop=(d == 1))
                return ins
            k.op("pe", mm2, reads=[wk_sb.b, cTb.b], writes=[a.b])
            k.op("dve", lambda e: e.tensor_scalar(out=kTb[:, e_, :], in0=a.ap(), scalar1=0.0625, scalar2=None, op0=ALU.mult),
                 reads=[a.b], writes=[kTb.b])
        if stop <= 3:
            raise _Stop()
        for tci in range(4):
            tsl = slice(tci * 128, (tci + 1) * 128)
            a = inproj_tm(tci, 768, 512)
            k.op("dve", lambda e: e.tensor_copy(out=vext[:, tci, 0:256], in_=a.ap(b=256)), reads=[a.b], writes=[vext.b])
            k.op("act", lambda e: e.activation(out=osig[:, tci, :], in_=a.ap(a=256, b=512), func=AF.Sigmoid), reads=[a.b],
                 writes=[osig.b])
            a = inproj_tm(tci, 1280, 512)
            k.op("dve", lambda e: e.tensor_copy(out=hv[:, tci, :, :].rearrange("p a b -> p (a b)"), in_=a.ap(b=256)), reads=[a.b],
                 writes=[hv.b])
            k.op("act", lambda e: e.activation(out=hgs[:, tci, :], in_=a.ap(a=256, b=512), func=AF.Silu), reads=[a.b],
                 writes=[hgs.b])
            inproj_tm(tci, 1792, 2, out_reg=regA, oc0=2 * tci)
            a = next_acc()

            def mm3(e):
                for d in range(2):
                    ins = e.matmul(a.ap(b=256), lhsT=cTb[:, d, tsl], rhs=wk_sb[:, d, :], start=(d == 0), stop=(d == 1))
                return ins
            k.op("pe", mm3, reads=[wk_sb.b, cTb.b], writes=[a.b])
            k.op("act", lambda e: e.mul(out=ktok[:, tci, :], in_=a.ap(b=256), mul=0.0625), reads=[a.b], writes=[ktok.b])
            for d in range(2):
                transpose_to(ctok[:, tci, d * 128:(d + 1) * 128], ctok.b, cT[:, d, tsl], cT.b, eng="dve")
        if stop <= 4:
            raise _Stop()
        gv = regA.ap().rearrange("p (a b) -> p a b", b=2)
        k.op("act", lambda e: e.activation(out=ge1[:], in_=gv[:, :, 1], func=AF.Exp, scale=-1.0, bias=nfb[:, 0:1]),
             reads=[regA.b, nfb.b], writes=[ge1.b])
        k.op("act", lambda e: e.activation(out=ge1[:], in_=ge1[:], func=AF.Ln, scale=1.0, bias=c["one1"][:, 0:1]),
             reads=[ge1.b, c["one1"].b], writes=[ge1.b])
        k.op("dve", lambda e: e.tensor_scalar(out=logf[:], in0=ge1[:], scalar1=-1.0, scalar2=None, op0=ALU.mult), reads=[ge1.b],
             writes=[logf.b])
        k.op("dve", lambda e: e.tensor_scalar(out=ig[:], in0=gv[:, :, 0], scalar1=gb_sb[:, 0:1], scalar2=None, op0=ALU.add),
             reads=[regA.b, gb_sb.b], writes=[ig.b])
        for tci in range(4 if do_ml >= 2 else 0):
            tsl = slice(tci * 128, (tci + 1) * 128)
            k.op("dve", lambda e: e.tensor_scalar(out=lfb[:], in0=c["ones"][:], scalar1=logf[:, tci:tci + 1], scalar2=None,
                                                  op0=ALU.mult), reads=[logf.b, c["ones"].b], writes=[lfb.b])

            def mmb(e):
                e.matmul(regB.ap(b=128), lhsT=lfb[:], rhs=c["causal"][:], start=True, stop=True)
                return e.matmul(regB.ap(a=128, b=129), lhsT=c["causal"][:], rhs=logf[:, tci:tci + 1], start=True, stop=True)
            k.op("pe", mmb, reads=[lfb.b, c["causal"].b, logf.b], writes=[regB.b])
            k.op("dve", lambda e: e.tensor_tensor(out=bias_s[:], in0=ig[:, tci:tci + 1], in1=regB.ap(a=128, b=129), op=ALU.subtract),
                 reads=[ig.b, regB.b], writes=[bias_s.b])
            k.op("act", lambda e: e.activation(out=DT[:], in_=regB.ap(b=128), func=AF.Exp, bias=bias_s[:, 0:1], scale=1.0),
                 reads=[regB.b, bias_s.b], writes=[DT.b])
            k.op("act", lambda e: e.activation(out=Eb[:], in_=regB.ap(b=128), func=AF.Exp), reads=[regB.b], writes=[Eb.b])
            k.op("dve", lambda e: e.tensor_copy(out=small[:, 0:1], in_=regB.ap(a=127, b=128)), reads=[regB.b], writes=[small.b])
            if stop <= 6.1:
                raise _Stop()
            k.op("pool", lambda e: e.tensor_tensor(out=Dm[:], in0=DT[:], in1=c["causal"][:], op=ALU.mult),
                 reads=[DT.b, c["causal"].b], writes=[Dm.b])
            if stop <= 6.2:
                raise _Stop()

            def mms(e):
                for e_ in range(2):
                    ins = e.matmul(regC.ap(), lhsT=kTb[:, e_, tsl], rhs=qTb[:, e_, tsl], start=(e_ == 0), stop=(e_ == 1))
                return ins
            k.op("pe", mms, reads=[kTb.b, qTb.b], writes=[regC.b])
            k.op("dve", lambda e: e.tensor_tensor(out=PT[:], in0=regC.ap(), in1=Dm[:], op=ALU.mult), reads=[regC.b, Dm.b],
                 writes=[PT.b])
            for e_ in range(2):
                k.op("dve", lambda e: e.tensor_tensor(out=qs[:, e_, :], in0=qT[:, e_, tsl], in1=Eb[:], op=ALU.mult),
                     reads=[qT.b, Eb.b], writes=[qs.b])

            def mmnd(e):
                e.matmul(regND.ap(), lhsT=PT[:], rhs=vext[:, tci, :], start=True, stop=False)
                e.matmul(regND.ap(), lhsT=qs[:, 0, :], rhs=Cb[:, 0, :], start=False, stop=False)
                return e.matmul(regND.ap(), lhsT=qs[:, 1, :], rhs=Cb[:, 1, :], start=False, stop=True)
            k.op("pe", mmnd, reads=[PT.b, vext.b, qs.b, Cb.b], writes=[regND.b])
            if debug and st == 0 and tci == 1:
                dump("Cst", Cst, Cst[:].rearrange("p a b -> p (a b)"), 528)
                dump("logf", logf, logf[:], 4)
                dump("ig", ig, ig[:], 4)
                dump("DT", DT, DT[:], 128)
                dump("Eb", Eb, Eb[:], 128)
                dump("ktok1", ktok, ktok[:, 1, :], 256)
                dump("qT", qT, qT[:, 0, 128:256], 128)
            if stop <= 6.3:
                raise _Stop()
            k.op("act", lambda e: e.activation(out=small[:, 1:2], in_=regND.ap(a=256, b=257), func=AF.Abs), reads=[regND.b],
                 writes=[small.b])
            k.op("dve", lambda e: e.tensor_scalar(out=small[:, 1:2], in0=small[:, 1:2], scalar1=1.0, scalar2=None, op0=ALU.max),
                 reads=[small.b], writes=[small.b])
            k.op("dve", lambda e: e.reciprocal(out=small[:, 1:2], in_=small[:, 1:2]), reads=[small.b], writes=[small.b])
            k.op("dve", lambda e: e.tensor_scalar(out=hn[:], in0=regND.ap(b=256), scalar1=small[:, 1:2], scalar2=None, op0=ALU.mult),
                 reads=[regND.b, small.b], writes=[hn.b])
            k.op("act", lambda e: e.activation(out=junk[:], in_=hn[:], func=AF.Square, accum_out=small[:, 2:3]), reads=[hn.b],
                 writes=[junk.b, small.b])
            k.op("act", lambda e: e.activation(out=small[:, 3:4], in_=small[:, 2:3], func=AF.Sqrt, scale=1.0 / 256, bias=c["eps"][:, 0:1]),
                 reads=[small.b, c["eps"].b], writes=[small.b])
            k.op("dve", lambda e: e.reciprocal(out=small[:, 3:4], in_=small[:, 3:4]), reads=[small.b], writes=[small.b])
            k.op("dve", lambda e: e.scalar_tensor_tensor(out=hm[:], in0=hn[:], scalar=small[:, 3:4], in1=mln_sb[:], op0=ALU.mult,
                                                         op1=ALU.mult), reads=[hn.b, small.b, mln_sb.b], writes=[hm.b])
            k.op("pool", lambda e: e.tensor_tensor(out=t1[:], in0=ctok[:, tci, :], in1=skp_sb[:], op=ALU.mult),
                 reads=[ctok.b, skp_sb.b], writes=[t1.b])
            k.op("dve", lambda e: e.tensor_tensor(out=t1[:], in0=t1[:], in1=hm[:], op=ALU.add), reads=[t1.b, hm.b], writes=[t1.b])
            k.op("dve", lambda e: e.tensor_tensor(out=yml[:], in0=t1[:], in1=osig[:, tci, :], op=ALU.mult), reads=[t1.b, osig.b],
                 writes=[yml.b])
            for d in range(2):
                transpose_to(ystage[:, d, tsl], ystage.b, yml[:, d * 128:(d + 1) * 128], yml.b, eng="act")
            if debug and st == 0 and tci == 1:
                dump("hn", hn, hn[:], 256)
                dump("small", small, small[:], 8)
                dump("hm", hm, hm[:], 256)
                dump("yml", yml, yml[:], 256)
            if stop <= 6.4:
                raise _Stop()
            k.op("act", lambda e: e.activation(out=small[:, 4:5], in_=bias_s[:], func=AF.Exp, bias=small[:, 0:1], scale=1.0),
                 reads=[bias_s.b, small.b], writes=[small.b])
            k.op("act", lambda e: e.activation(out=small[:, 5:6], in_=small[:, 0:1], func=AF.Exp), reads=[small.b], writes=[small.b])
            if stop <= 6.5:
                raise _Stop()
            k.op("dve", lambda e: e.tensor_scalar(out=ka[:], in0=ktok[:, tci, :], scalar1=small[:, 4:5], scalar2=None, op0=ALU.mult),
                 reads=[ktok.b, small.b], writes=[ka.b])
            if stop <= 6.6:
                raise _Stop()
            for kc in range(2):
                k.op("pe", lambda e: e.matmul(regU[kc].ap(), lhsT=ka[:, kc * 128:(kc + 1) * 128], rhs=vext[:, tci, :], start=True,
                                              stop=True), reads=[ka.b, vext.b], writes=[regU[kc].b])
                k.op("dve", lambda e: e.scalar_tensor_tensor(out=Cst[:, kc, :], in0=Cst[:, kc, :], scalar=small[:, 5:6],
                                                             in1=regU[kc].ap(), op0=ALU.mult, op1=ALU.add),
                     reads=[Cst.b, small.b, regU[kc].b], writes=[Cst.b])
            if stop <= 6.7 and kc == 1:
                raise _Stop()
            if stop <= 6.8:
                raise _Stop()
            k.op("act", lambda e: e.copy(out=Cb[:], in_=Cst[:]), reads=[Cst.b], writes=[Cb.b])
        for hd in range(2 if do_hg >= 1 else 0):
            az = inproj_fm(512 + hd * 128)
            aq = inproj_fm(256 + hd * 128)
            k.op("act", lambda e: e.activation(out=tA[:], in_=az.ap(), func=AF.Sigmoid, scale=-1.0), reads=[az.b], writes=[tA.b])
            k.op("act", lambda e: e.activation(out=tB[:], in_=az.ap(), func=AF.Exp, scale=-1.0), reads=[az.b], writes=[tB.b])
            k.op("act", lambda e: e.activation(out=sqt[:], in_=aq.ap(), func=AF.Silu), reads=[aq.b], writes=[sqt.b])
            k.op("dve", lambda e: e.tensor_scalar(out=k2[:], in0=tA[:], scalar1=oml[:, hd:hd + 1], scalar2=None, op0=ALU.mult),
                 reads=[tA.b, oml.b], writes=[k2.b])
            k.op("dve", lambda e: e.tensor_scalar(out=tA[:], in0=k2[:], scalar1=HG_MAX_K, scalar2=None, op0=ALU.min), reads=[k2.b],
                 writes=[tA.b])
            k.op("act", lambda e: e.activation(out=lf1[:], in_=tA[:], func=AF.Ln, scale=-1.0, bias=c["one1"][:, 0:1]),
                 reads=[tA.b, c["one1"].b], writes=[lf1.b])
            k.op("act", lambda e: e.activation(out=tB[:], in_=tB[:], func=AF.Ln, scale=1.0, bias=c["one1"][:, 0:1]),
                 reads=[tB.b, c["one1"].b], writes=[tB.b])
            k.op("dve", lambda e: e.scalar_tensor_tensor(out=lgf[:], in0=tB[:], scalar=-1.0, in1=lf1[:], op0=ALU.mult, op1=ALU.max),
                 reads=[tB.b, lf1.b], writes=[lgf.b])
            k.op("dve", lambda e: e.tensor_tensor_scan(out=bt[:], data0=resetm[:], data1=lgf[:], initial=0.0, op0=ALU.mult,
                                                       op1=ALU.add), reads=[resetm.b, lgf.b], writes=[bt.b])
            k.op("dve", lambda e: e.tensor_tensor(out=v3(brel), in0=v3(bt), in1=v3(bt)[:, :, 31:32].to_broadcast([128, 8, 64]),
                                                  op=ALU.subtract), reads=[bt.b], writes=[brel.b])
            k.op("act", lambda e: e.activation(out=tA[:], in_=brel[:], func=AF.Exp), reads=[brel.b], writes=[tA.b])
            k.op("act", lambda e: e.activation(out=tC[:], in_=brel[:], func=AF.Exp, scale=-1.0), reads=[brel.b], writes=[tC.b])
            for par in range(2):
                k.op("dve", lambda e: e.tensor_tensor(out=qz[par][:, :, par * 64:(par + 1) * 64], in0=v4(sqt)[:, :, par, :],
                                                      in1=v4(tA)[:, :, par, :], op=ALU.mult), reads=[sqt.b, tA.b], writes=[qz[par].b])
                k.op("dve", lambda e: e.tensor_tensor(out=kz[par][:, :, par * 64:(par + 1) * 64], in0=v4(k2)[:, :, par, :],
                                                      in1=v4(tC)[:, :, par, :], op=ALU.mult), reads=[k2.b, tC.b], writes=[kz[par].b])
            k.op("act", lambda e: e.activation(out=tA[:], in_=bt[:], func=AF.Exp), reads=[bt.b], writes=[tA.b])
            for par in range(2):
                k.op("dve", lambda e: e.tensor_tensor(out=qbz[par][:, :, par * 64:(par + 1) * 64], in0=v4(sqt)[:, :, par, :],
                                                      in1=v4(tA)[:, :, par, :], op=ALU.mult), reads=[sqt.b, tA.b], writes=[qbz[par].b])
            k.op("dve", lambda e: e.tensor_tensor(out=v3(tC), in0=v3(bt)[:, :, 63:64].to_broadcast([128, 8, 64]), in1=v3(bt),
                                                  op=ALU.subtract), reads=[bt.b], writes=[tC.b])
            k.op("act", lambda e: e.activation(out=tC[:], in_=tC[:], func=AF.Exp), reads=[tC.b], writes=[tC.b])
            k.op("dve", lambda e: e.tensor_tensor(out=kgT[:], in0=k2[:], in1=tC[:], op=ALU.mult), reads=[k2.b, tC.b], writes=[kgT.b])
            k.op("act", lambda e: e.activation(out=eg8[:], in_=v3(bt)[:, :, 63], func=AF.Exp), reads=[bt.b], writes=[eg8.b])
            for tl in range(4):
                r = regT[tcnt[0] % 4]
                tcnt[0] += 1
                k.op("pe", lambda e: e.transpose(r.ap(), kgT[:, tl * 128:(tl + 1) * 128], c["ident"][:]), reads=[kgT.b, c["ident"].b],
                     writes=[r.b])
                k.op("act", lambda e: e.copy(out=kgz[0][0:64, tl, :], in_=r.ap(rows=slice(0, 64))), reads=[r.b], writes=[kgz[0].b])
                k.op("act", lambda e: e.copy(out=kgz[1][64:128, tl, :], in_=r.ap(rows=slice(64, 128))), reads=[r.b], writes=[kgz[1].b])
            S = Sst[hd]
            for tl in range(4 if do_hg >= 2 else 0):
                def mma(e):
                    e.matmul(r7A.ap(), lhsT=kz[0][:, tl, :], rhs=qz[0][:, tl, :], start=True, stop=False)
                    return e.matmul(r7A.ap(), lhsT=kz[1][:, tl, :], rhs=qz[1][:, tl, :], start=False, stop=True)
                k.op("pe", mma, reads=[kz[0].b, kz[1].b, qz[0].b, qz[1].b], writes=[r7A.b])
                k.op("dve", lambda e: e.tensor_tensor(out=Am[:], in0=r7A.ap(), in1=c["causal"][:], op=ALU.mult),
                     reads=[r7A.b, c["causal"].b], writes=[Am.b])
                k.op("pe", lambda e: e.matmul(r7C.ap(), lhsT=kgz[0][:, tl, :], rhs=hv[:, tl, hd, :], start=True, stop=True),
                     reads=[kgz[0].b, hv.b], writes=[r7C.b])
                k.op("pe", lambda e: e.matmul(r7D.ap(), lhsT=kgz[1][:, tl, :], rhs=hv[:, tl, hd, :], start=True, stop=True),
                     reads=[kgz[1].b, hv.b], writes=[r7D.b])
                k.op("dve", lambda e: e.scalar_tensor_tensor(out=S[:], in0=S[:], scalar=eg8[:, 2 * tl:2 * tl + 1], in1=r7C.ap(),
                                                             op0=ALU.mult, op1=ALU.add), reads=[S.b, eg8.b, r7C.b], writes=[S.b])
                k.op("act", lambda e: e.copy(out=Sb[hd][1][:], in_=S[:]), reads=[S.b], writes=[Sb[hd][1].b])

                def mmo(e):
                    e.matmul(r7B.ap(), lhsT=Am[:], rhs=hv[:, tl, hd, :], start=True, stop=False)
                    e.matmul(r7B.ap(), lhsT=qbz[0][:, tl, :], rhs=Sb[hd][0][:], start=False, stop=False)
                    return e.matmul(r7B.ap(), lhsT=qbz[1][:, tl, :], rhs=Sb[hd][1][:], start=False, stop=True)
                k.op("pe", mmo, reads=[Am.b, hv.b, qbz[0].b, qbz[1].b, Sb[hd][0].b, Sb[hd][1].b], writes=[r7B.b])
                k.op("dve", lambda e: e.scalar_tensor_tensor(out=S[:], in0=S[:], scalar=eg8[:, 2 * tl + 1:2 * tl + 2], in1=r7D.ap(),
                                                             op0=ALU.mult, op1=ALU.add), reads=[S.b, eg8.b, r7D.b], writes=[S.b])
                k.op("act", lambda e: e.copy(out=Sb[hd][0][:], in_=S[:]), reads=[S.b], writes=[Sb[hd][0].b])
                k.op("act", lambda e: e.copy(out=o_sb[:], in_=r7B.ap()), reads=[r7B.b], writes=[o_sb.b])
                k.op("act", lambda e: e.activation(out=junk[:, 0:128], in_=o_sb[:], func=AF.Square, accum_out=small[:, 6:7]),
                     reads=[o_sb.b], writes=[junk.b, small.b])
                k.op("act", lambda e: e.activation(out=small[:, 7:8], in_=small[:, 6:7], func=AF.Sqrt, scale=1.0 / 128,
                                                   bias=c["eps"][:, 0:1]), reads=[small.b, c["eps"].b], writes=[small.b])
                k.op("dve", lambda e: e.reciprocal(out=small[:, 7:8], in_=small[:, 7:8]), reads=[small.b], writes=[small.b])
                k.op("dve", lambda e: e.scalar_tensor_tensor(out=o2n[:], in0=o_sb[:], scalar=small[:, 7:8], in1=hgn_sb[:, hd, :],
                                                             op0=ALU.mult, op1=ALU.mult), reads=[o_sb.b, small.b, hgn_sb.b], writes=[o2n.b])
                k.op("pool", lambda e: e.tensor_tensor(out=yh[:], in0=o2n[:], in1=hgs[:, tl, hd * 128:(hd + 1) * 128], op=ALU.mult),
                     reads=[o2n.b, hgs.b], writes=[yh.b])
                transpose_to(ystage[:, 2 + hd, tl * 128:(tl + 1) * 128], ystage.b, yh[:], yh.b, eng="act")
    for st in range(NST):
        t0 = st * TW
        try:
            body(st, t0)
        except _Stop:
            pass
        tk = k.op("sp", lambda e: e.dma_start(out=yT.rearrange("c p t -> p c t")[:, :, t0:t0 + TW], in_=ystage[:]), reads=[ystage.b],
                  dsem=ds_y)
        out_toks.append(tk)
    k.finish(out_toks)
    return nc


def fm(a):
    T, C = a.shape
    return np.ascontiguousarray(a.T.reshape(C // 128, 128, T))


def unfm(aT):
    n, p, T = aT.shape
    return np.ascontiguousarray(aT.reshape(n * p, T).T)


def pvec(w):
    return np.ascontiguousarray(w.reshape(-1, 128).T)


def rep(w):
    return np.ascontiguousarray(np.broadcast_to(w[None], (128,) + w.shape))


def ab_core_inputs(I, layer, hgp, hT_b):
    j = layer // 2
    h = hgp
    W = I["ab_w_in"][j]
    o_u, o_v, o_o, o_i, o_f, o_hq, o_hf, o_hi, o_hg = 0, 1024, 2048, 3072, 3076, 3080, 4104, 5128, 6152
    sl = lambda o, n, i: W[:, o + i * n:o + (i + 1) * n]
    win = np.concatenate([sl(o_u, 256, h), sl(o_hq, 256, h), sl(o_hf, 256, h), sl(o_v, 256, h), sl(o_o, 256, h),
                          sl(o_hi, 256, h), sl(o_hg, 256, h), W[:, o_i + h:o_i + h + 1], W[:, o_f + h:o_f + h + 1]], axis=1)
    cwf = I["ml_conv_w"][j][:, h * 256:(h + 1) * 256]
    cw = np.ascontiguousarray(cwf.reshape(4, 2, 128).transpose(2, 1, 0))
    cb = np.ascontiguousarray(I["ml_conv_b"][j][h * 256:(h + 1) * 256].reshape(2, 128).T)
    lbl = np.ascontiguousarray(I["hg_lb_logits"][:, h * 256:(h + 1) * 256].reshape(2, 2, 128).transpose(2, 1, 0))
    return {
        "hT": hT_b, "nw": pvec(I["mix_norm"][layer]), "win": np.ascontiguousarray(win), "cw": cw, "cb": cb,
        "wq": np.ascontiguousarray(I["ml_wq"][j][h]), "wk": np.ascontiguousarray(I["ml_wk"][j][h]),
        "gb": rep(np.array([I["ml_i_bias"][j][h], I["ml_f_bias"][j][h]], np.float32)),
        "mln": rep(I["ml_out_norm"][j][h]), "skp": rep(I["ml_skip"][j][h * 256:(h + 1) * 256]),
        "lbl": lbl, "hgn": rep(I["hg_out_norm"][j][2 * h:2 * h + 2]),
    }


import math
from contextlib import ExitStack

NSA_BIG = 200.0
ROPE_INVF = np.power(np.float32(10000.0), -np.arange(64, dtype=np.float32) / 64).astype(np.float32)


def kb_barrier(k):
    toks = [Tok(k.sem[e], k.cnt[e], "E" + e, e) for e in k.engs if k.cnt[e] > 0]
    toks += [Tok(d.h, d.val, d.key, None) for d in k._all_dsems if d.val > 0]
    for e in k.engs:
        for t in toks:
            if t.eng != e:
                k._wait(e, t)


def build_nsa(T=8192, stop=99, debug=False):
    TW = 256
    NST = T // TW
    NT = T // 128
    NCB = (T - 32) // 16 + 1
    NCT = (NCB + 127) // 128
    NCOL = 1292
    scale = 128 ** -0.5
    nc = bass.Bass("TRN2", target_bir_lowering=False)
    k = KB(nc)
    k._all_dsems = []
    _ds = k.dsem

    def dsem2(name=None):
        d = _ds(name)
        k._all_dsems.append(d)
        return d
    k.dsem = dsem2

    def dram(name, shape, dt=F32, kind="ExternalInput"):
        return nc.dram_tensor(name, list(shape), dt, kind=kind).ap()
    hT = dram("hT", [16, 128, T])
    nw = dram("nw", [128, 16])
    win = dram("win", [2048, NCOL])
    qnw = dram("qnw", [128, 512])
    knw = dram("knw", [128, 3, 128])
    posT = dram("posT", [128, 2, 32])
    w1 = dram("w1", [2, 4096, 256])
    b1 = dram("b1", [128, 2, 2])
    w2 = dram("w2", [2, 256, 128])
    gbias = dram("gbias", [128, 12])
    yT = dram("yT", [4, 128, T], BF16, kind="ExternalOutput")
    dbg = dram("dbg", [128, 8192], F32, kind="ExternalOutput") if debug else None
    dbg_pos = [0]
    dbg_map = {}
    nc._dbg_map = dbg_map

    def dump(name, tt, ap, n):
        if not debug or name in dbg_map:
            return
        c0 = dbg_pos[0]
        dbg_pos[0] += n
        dbg_map[name] = (c0, n)
        k.op("sp", lambda e: e.dma_start(out=dbg[:, c0:c0 + n], in_=ap), reads=[tt.b], dsem=k.dsem())

    c = make_consts(k)
    win_v = win.rearrange("(dc p) f -> p dc f", p=128)
    ps = [nc.alloc_psum_tensor(f"ps{i}", [128, 512], F32) for i in range(8)]
    accS = [PReg(k, ps[i], 0, 512, f"accS{i}") for i in range(2)]
    _oset = [PReg(k, ps[2 + h], 0, 130, f"o_{h}") for h in range(4)]
    oreg = [_oset, _oset]
    impT = PReg(k, ps[6], 0, 512, "impT")
    ps_ss = impT
    regT = [PReg(k, ps[7], i * 128, (i + 1) * 128, f"regT{i}") for i in range(4)]
    acnt = [0]
    tcnt = [0]

    def next_acc():
        a = accS[acnt[0] % 2]
        acnt[0] += 1
        return a

    def next_regT():
        r = regT[tcnt[0] % 4]
        tcnt[0] += 1
        return r

    def ld(name, shape, src, dt=F32, eng="sp"):
        t = sb(k, name, shape, dt)
        k.op(eng, lambda e: e.dma_start(out=t[:], in_=src), writes=[t.b], dsem=k.dsem())
        return t

    nw_sb = ld("nw_sb", [128, 16], nw)
    qnw_sb = ld("qnw_sb", [128, 512], qnw)
    knw_sb = ld("knw_sb", [128, 3, 128], knw)
    b1_sb = ld("b1_sb", [128, 2, 2], b1)
    gb_sb = ld("gb_sb", [128, 12], gbias)
    hTt = sb(k, "hTt", [128, 16, TW], F32)
    ds_h = k.dsem()
    xT = sb(k, "xT", [128, 16, TW], BF16)
    rstd = sb(k, "rstd", [128, TW], F32)
    scrsq = [sb(k, f"scrsq{i}", [128, TW], F32) for i in range(2)]
    kcmpT = sb(k, "kcmpT", [128, NCT * 128], BF16)
    vcmp = sb(k, "vcmp", [128, NCT, 130], BF16)
    cover = sb(k, "cover", [128, NCT, 128], BF16)
    identb = sb(k, "identb", [128, 128], BF16)
    k.op("dve", lambda e: e.tensor_copy(out=identb[:], in_=c["ident"][:]), reads=[c["ident"].b], writes=[identb.b])
    k.op("dve", lambda e: e.memset(kcmpT[:], 0.0), writes=[kcmpT.b])
    k.op("dve", lambda e: e.memset(vcmp[:], 0.0), writes=[vcmp.b])
    k.op("dve", lambda e: e.memset(vcmp[:, :, 128:129], 1.0), writes=[vcmp.b])
    invf = sb(k, "invf", [128, 64], F32)
    for i in range(64):
        k.op("dve", lambda e: e.memset(invf[:, i:i + 1], float(ROPE_INVF[i])), writes=[invf.b])
    pidx_i = sb(k, "pidx_i", [128, 1], I32)
    k.op("pool", lambda e: e.iota(pidx_i[:], pattern=[[0, 1]], base=0, channel_multiplier=1), writes=[pidx_i.b])
    pidx = sb(k, "pidx", [128, 1], F32)
    k.op("dve", lambda e: e.tensor_copy(out=pidx[:], in_=pidx_i[:]), reads=[pidx_i.b], writes=[pidx.b])
    pcol = sb(k, "pcol", [128, 1], F32)
    ang = sb(k, "ang", [128, 64], F32)
    rr = sb(k, "rr", [128, 64], F32)
    rf = sb(k, "rf", [128, 64], F32)
    ri = sb(k, "ri", [128, 64], I32)
    cos_t = sb(k, "cos_t", [128, 64], F32)
    sin_t = sb(k, "sin_t", [128, 64], F32)
    rtmp = sb(k, "rtmp", [128, 4, 64], F32)
    TWO_PI = 2 * math.pi

    def rope_tables(mult, add):
        k.op("dve", lambda e: e.tensor_scalar(out=pcol[:], in0=pidx[:], scalar1=float(mult), scalar2=float(add), op0=ALU.mult,
                                              op1=ALU.add), reads=[pidx.b], writes=[pcol.b])
        k.op("dve", lambda e: e.tensor_scalar(out=ang[:], in0=invf[:], scalar1=pcol[:, 0:1], scalar2=None, op0=ALU.mult),
             reads=[invf.b, pcol.b], writes=[ang.b])
        for (off, dst) in ((0.0, sin_t), (math.pi / 2, cos_t)):
            k.op("dve", lambda e: e.tensor_scalar(out=rr[:], in0=ang[:], scalar1=off, scalar2=None, op0=ALU.add), reads=[ang.b],
                 writes=[rr.b])
            k.op("dve", lambda e: e.tensor_scalar(out=rf[:], in0=rr[:], scalar1=1.0 / TWO_PI, scalar2=None, op0=ALU.mult),
                 reads=[rr.b], writes=[rf.b])
            k.op("dve", lambda e: e.tensor_copy(out=ri[:], in_=rf[:]), reads=[rf.b], writes=[ri.b])
            k.op("dve", lambda e: e.tensor_copy(out=rf[:], in_=ri[:]), reads=[ri.b], writes=[rf.b])
            k.op("dve", lambda e: e.scalar_tensor_tensor(out=rr[:], in0=rf[:], scalar=-TWO_PI, in1=rr[:], op0=ALU.mult, op1=ALU.add),
                 reads=[rf.b, rr.b], writes=[rr.b])
            k.op("dve", lambda e: e.tensor_scalar(out=rf[:], in0=rr[:], scalar1=math.pi, scalar2=None, op0=ALU.is_gt), reads=[rr.b],
                 writes=[rf.b])
            k.op("dve", lambda e: e.scalar_tensor_tensor(out=rr[:], in0=rf[:], scalar=-TWO_PI, in1=rr[:], op0=ALU.mult, op1=ALU.add),
                 reads=[rf.b, rr.b], writes=[rr.b])
            k.op("act", lambda e: e.activation(out=dst[:], in_=rr[:], func=AF.Sin), reads=[rr.b], writes=[dst.b])

    def apply_rope(dst, src, H):
        cb = cos_t[:].rearrange("p (o f) -> p o f", o=1).to_broadcast([128, H, 64])
        sbb = sin_t[:].rearrange("p (o f) -> p o f", o=1).to_broadcast([128, H, 64])
        x1, x2 = src[:, 0:H, 0:64], src[:, 0:H, 64:128]
        tm = rtmp[:, 0:H, :]
        k.op("dve", lambda e: e.tensor_tensor(out=tm, in0=x2, in1=sbb, op=ALU.mult), reads=[src.b, sin_t.b], writes=[rtmp.b])
        k.op("dve", lambda e: e.tensor_tensor(out=dst[:, 0:H, 0:64], in0=x1, in1=cb, op=ALU.mult), reads=[src.b, cos_t.b], writes=[dst.b])
        k.op("dve", lambda e: e.tensor_tensor(out=dst[:, 0:H, 0:64], in0=dst[:, 0:H, 0:64], in1=tm, op=ALU.subtract),
             reads=[dst.b, rtmp.b], writes=[dst.b])
        k.op("dve", lambda e: e.tensor_tensor(out=tm, in0=x1, in1=sbb, op=ALU.mult), reads=[src.b, sin_t.b], writes=[rtmp.b])
        k.op("dve", lambda e: e.tensor_tensor(out=dst[:, 0:H, 64:128], in0=x2, in1=cb, op=ALU.mult), reads=[src.b, cos_t.b], writes=[dst.b])
        k.op("dve", lambda e: e.tensor_tensor(out=dst[:, 0:H, 64:128], in0=dst[:, 0:H, 64:128], in1=tm, op=ALU.add),
             reads=[dst.b, rtmp.b], writes=[dst.b])

    small = sb(k, "small", [128, 16], F32)
    junk = sb(k, "junk", [128, 128], F32)
    kn = sb(k, "kn", [128, 4, 128], F32)
    kr = sb(k, "kr", [128, 4, 128], F32)

    def rms_heads(src_reg, col0, H, wfn, post_scale=1.0, nrows=128):
        rows = slice(0, nrows)
        for h in range(H):
            k.op("act", lambda e: e.activation(out=junk[rows, :], in_=src_reg.ap(rows, col0[h], col0[h] + 128), func=AF.Square,
                                               accum_out=small[rows, h:h + 1]), reads=[src_reg.b], writes=[junk.b, small.b])
        k.op("act", lambda e: e.activation(out=small[rows, 4:4 + H], in_=small[rows, 0:H], func=AF.Sqrt, scale=1.0 / 128,
                                           bias=c["eps"][rows, 0:1]), reads=[small.b, c["eps"].b], writes=[small.b])
        k.op("dve", lambda e: e.reciprocal(out=small[rows, 4:4 + H], in_=small[rows, 4:4 + H]), reads=[small.b], writes=[small.b])
        if post_scale != 1.0:
            k.op("dve", lambda e: e.tensor_scalar(out=small[rows, 4:4 + H], in0=small[rows, 4:4 + H], scalar1=post_scale, scalar2=None,
                                                  op0=ALU.mult), reads=[small.b], writes=[small.b])
        for h in range(H):
            wt, wap = wfn(h)
            k.op("dve", lambda e: e.scalar_tensor_tensor(out=kn[rows, h, :], in0=src_reg.ap(rows, col0[h], col0[h] + 128),
                                                         scalar=small[rows, 4 + h:5 + h], in1=wap, op0=ALU.mult, op1=ALU.mult),
                 reads=[src_reg.b, small.b, wt.b], writes=[kn.b])

    out_toks = []
    es = ExitStack()

    def sbs(name, shape, dt):
        return TT(k, es.enter_context(nc.sbuf_tensor(name, list(shape), dt)), name)

    win_a = sbs("win_a", [128, 16, 256], BF16)
    k.op("pool", lambda e: e.dma_start(out=win_a[:], in_=win_v[:, :, 0:256]), writes=[win_a.b], dsem=k.dsem())
    kvT = [sbs(f"kvT{i}", [128, T], BF16) for i in range(2)]
    w1_sb = sbs("w1_sb", [128, 32, 256], BF16)
    w2_sb = sbs("w2_sb", [128, 2, 2, 128], BF16)
    posT_sb = sbs("posT_sb", [128, 2, 32], BF16)
    hsil = sbs("hsil", [128, 2, 128], BF16)
    biasv = sbs("biasv", [128, 2], F32)
    k.op("pool", lambda e: e.dma_start(out=posT_sb[:], in_=posT), writes=[posT_sb.b], dsem=k.dsem())
    for kv in range(2):
        k.op("pool", lambda e: e.dma_start(out=w2_sb[:, kv, :, :], in_=w2[kv].rearrange("(c p) e -> p c e", p=128)), writes=[w2_sb.b],
             dsem=k.dsem())
    for st in range(NST):
        t0 = st * TW
        norm_supertile(k, c, hT, nw_sb, hTt, xT, ps_ss, rstd, scrsq, t0, TW, ds_h)
        for kv in range(2):
            a = next_acc()

            def mm(e):
                for dc in range(16):
                    ins = e.matmul(a.ap(b=TW), lhsT=win_a[:, dc, kv * 128:(kv + 1) * 128], rhs=xT[:, dc, :], start=(dc == 0), stop=(dc == 15))
                return ins
            k.op("pe", mm, reads=[win_a.b, xT.b], writes=[a.b])
            k.op("act", lambda e: e.copy(out=kvT[kv][:, t0:t0 + TW], in_=a.ap(b=TW)), reads=[a.b], writes=[kvT[kv].b])
    ds_w1 = k.dsem()
    for kv in range(2):
        k.op("pool", lambda e: e.dma_start(out=w1_sb[:], in_=w1[kv].rearrange("(l p) h -> p l h", p=128)), writes=[w1_sb.b], dsem=ds_w1)
        for hc in range(2):
            r = next_regT()

            def mmp(e):
                for l in range(32):
                    ins = e.matmul(r.ap(b=1), lhsT=w1_sb[:, l, hc * 128:(hc + 1) * 128], rhs=posT_sb[:, kv, l:l + 1], start=(l == 0),
                                   stop=(l == 31))
                return ins
            k.op("pe", mmp, reads=[w1_sb.b, posT_sb.b], writes=[r.b])
            k.op("dve", lambda e: e.tensor_tensor(out=biasv[:, hc:hc + 1], in0=r.ap(b=1), in1=b1_sb[:, kv, hc:hc + 1], op=ALU.add),
                 reads=[r.b, b1_sb.b], writes=[biasv.b])
        for nti in range(NCT):
            nn = min(128, NCB - 128 * nti)
            for hc in range(2):
                r = next_regT()

                def mmh(e):
                    for l in range(32):
                        s0 = 16 * 128 * nti + l
                        ins = e.matmul(r.ap(b=nn), lhsT=w1_sb[:, l, hc * 128:(hc + 1) * 128], rhs=kvT[kv][:, s0:s0 + 16 * (nn - 1) + 1:16],
                                       start=(l == 0), stop=(l == 31))
                    return ins
                k.op("pe", mmh, reads=[w1_sb.b, kvT[kv].b], writes=[r.b])
                k.op("act", lambda e: e.activation(out=hsil[:, hc, 0:nn], in_=r.ap(b=nn), func=AF.Silu, bias=biasv[:, hc:hc + 1], scale=1.0),
                     reads=[r.b, biasv.b], writes=[hsil.b])
            r = next_regT()

            def mmo(e):
                for hc in range(2):
                    ins = e.matmul(r.ap(rows=slice(0, nn)), lhsT=hsil[:, hc, 0:nn], rhs=w2_sb[:, kv, hc, :], start=(hc == 0), stop=(hc == 1))
                return ins
            k.op("pe", mmo, reads=[hsil.b, w2_sb.b], writes=[r.b])
            if kv == 0:
                rms_heads(r, [0], 1, lambda h: (knw_sb, knw_sb[0:nn, 0, :]), nrows=nn)
                rope_tables(16.0, 16.0 * 128 * nti + 31.0)
                apply_rope(kr, kn, 1)
                r2 = next_regT()
                k.op("pe", lambda e: e.transpose(r2.ap(b=nn), kr[0:nn, 0, :], c["ident"][0:nn, 0:nn]), reads=[kr.b, c["ident"].b],
                     writes=[r2.b])
                k.op("act", lambda e: e.copy(out=kcmpT[:, nti * 128:nti * 128 + nn], in_=r2.ap(b=nn)), reads=[r2.b], writes=[kcmpT.b])
            else:
                k.op("act", lambda e: e.copy(out=vcmp[0:nn, nti, 0:128], in_=r.ap(rows=slice(0, nn))), reads=[r.b], writes=[vcmp.b])
    if debug:
        dump("kcmpT", kcmpT, kcmpT[:, 0:128], 64) if False else None
    kb_barrier(k)
    es.close()
    es = ExitStack()
    if stop <= 1:
        tk = k.op("sp", lambda e: e.dma_start(out=yT[0, :, 0:NCT * 128], in_=kcmpT[:]), reads=[kcmpT.b], dsem=k.dsem())
        tk2 = k.op("sp", lambda e: e.dma_start(out=yT[1, :, 0:NCT * 130], in_=vcmp[:].rearrange("p a b -> p (a b)")), reads=[vcmp.b],
                   dsem=k.dsem())
        k.finish([tk, tk2])
        return nc

    ksT = sbs("ksT", [128, T], BF16)
    kwT = sbs("kwT", [128, T], BF16)
    vs_e = sbs("vs_e", [128, NT, 130], BF16)
    vw_e = sbs("vw_e", [128, NT, 130], BF16)
    k.op("dve", lambda e: e.memset(vs_e[:, :, 128:130], 1.0), writes=[vs_e.b])
    k.op("dve", lambda e: e.memset(vw_e[:, :, 128:130], 1.0), writes=[vw_e.b])
    esB = ExitStack()
    win_b = TT(k, esB.enter_context(nc.sbuf_tensor("win_b", [128, 16, 512], BF16)), "win_b")
    k.op("pool", lambda e: e.dma_start(out=win_b[:], in_=win_v[:, :, 256:768]), writes=[win_b.b], dsem=k.dsem())
    for st in range(NST):
        t0 = st * TW
        norm_supertile(k, c, hT, nw_sb, hTt, xT, ps_ss, rstd, scrsq, t0, TW, ds_h)
        for tci in range(TW // 128):
            ti = st * (TW // 128) + tci
            a = next_acc()

            def mm(e):
                for dc in range(16):
                    ins = e.matmul(a.ap(), lhsT=xT[:, dc, tci * 128:(tci + 1) * 128], rhs=win_b[:, dc, :], start=(dc == 0), stop=(dc == 15))
                return ins
            k.op("pe", mm, reads=[win_b.b, xT.b], writes=[a.b])
            k.op("act", lambda e: e.copy(out=vs_e[:, ti, 0:128], in_=a.ap(a=128, b=256)), reads=[a.b], writes=[vs_e.b])
            k.op("act", lambda e: e.copy(out=vw_e[:, ti, 0:128], in_=a.ap(a=384, b=512)), reads=[a.b], writes=[vw_e.b])
            rms_heads(a, [0, 256], 2, lambda h: (knw_sb, knw_sb[:, 1 + h, :]))
            rope_tables(1.0, float(ti * 128))
            apply_rope(kr, kn, 2)
            for h, dstT in ((0, ksT), (1, kwT)):
                r2 = next_regT()
                k.op("pe", lambda e: e.transpose(r2.ap(), kr[:, h, :], c["ident"][:]), reads=[kr.b, c["ident"].b], writes=[r2.b])
                k.op("act", lambda e: e.copy(out=dstT[:, ti * 128:(ti + 1) * 128], in_=r2.ap()), reads=[r2.b], writes=[dstT.b])
    kb_barrier(k)
    esB.close()
    if stop <= 2:
        tk = k.op("sp", lambda e: e.dma_start(out=yT[0, :, :], in_=ksT[:]), reads=[ksT.b], dsem=k.dsem())
        tk2 = k.op("sp", lambda e: e.dma_start(out=yT[1, :, :], in_=kwT[:]), reads=[kwT.b], dsem=k.dsem())
        tk3 = k.op("sp", lambda e: e.dma_start(out=yT[2, :, 0:NT * 128].rearrange("p (a b) -> p a b", b=128), in_=vs_e[:, :, 0:128]),
                   reads=[vs_e.b], dsem=k.dsem())
        k.finish([tk, tk2, tk3])
        return nc

    win_q = sbs("win_q", [128, 16, 524], BF16)
    k.op("pool", lambda e: e.dma_start(out=win_q[:], in_=win_v[:, :, 768:1292]), writes=[win_q.b], dsem=k.dsem())
    Esel = sbs("Esel", [128, T], BF16)
    k.op("dve", lambda e: e.memset(Esel[:], 1.0), writes=[Esel.b])
    k.op("pool", lambda e: e.affine_select(out=Esel[:], in_=Esel[:], pattern=[[1, T]], compare_op=ALU.is_ge, fill=0.0, base=0,
                                           channel_multiplier=-64), reads=[Esel.b], writes=[Esel.b])
    k.op("pool", lambda e: e.affine_select(out=Esel[:], in_=Esel[:], pattern=[[-1, T]], compare_op=ALU.is_ge, fill=0.0, base=63,
                                           channel_multiplier=64), reads=[Esel.b], writes=[Esel.b])
    f32a = sbs("f32a", [128, 512], F32)
    f32b = sbs("f32b", [128, 512], F32)
    ones4 = sbs("ones4", [128, 512], F32)
    zeros4 = sbs("zeros4", [128, 512], F32)
    k.op("dve", lambda e: e.memset(ones4[:], 1.0), writes=[ones4.b])
    k.op("dve", lambda e: e.memset(zeros4[:], 0.0), writes=[zeros4.b])
    for nti in range(NCT):
        k.op("pool", lambda e: e.affine_select(out=f32a[:, 0:128], in_=ones4[:, 0:128], pattern=[[64, 128]], compare_op=ALU.is_gt, fill=0.0,
                                               base=64 - 2048 * nti, channel_multiplier=-16), reads=[ones4.b], writes=[f32a.b])
        k.op("pool", lambda e: e.affine_select(out=f32a[:, 0:128], in_=f32a[:, 0:128], pattern=[[-64, 128]], compare_op=ALU.is_gt, fill=0.0,
                                               base=2048 * nti + 32, channel_multiplier=16), reads=[f32a.b], writes=[f32a.b])
        k.op("dve", lambda e: e.tensor_copy(out=cover[:, nti, :], in_=f32a[:, 0:128]), reads=[f32a.b], writes=[cover.b])
    cneg = sbs("cneg", [128, 512], BF16)
    wneg = sbs("wneg", [128, 512], BF16)
    k.op("pool", lambda e: e.affine_select(out=f32a[:].rearrange("p (h j) -> p h j", h=4), in_=zeros4[:].rearrange("p (h j) -> p h j", h=4),
                                           pattern=[[0, 4], [1, 128]], compare_op=ALU.is_ge, fill=-NSA_BIG, base=0, channel_multiplier=-1),
         reads=[zeros4.b], writes=[f32a.b])
    k.op("dve", lambda e: e.tensor_copy(out=cneg[:], in_=f32a[:]), reads=[f32a.b], writes=[cneg.b])
    k.op("pool", lambda e: e.affine_select(out=f32a[:].rearrange("p (h j) -> p h j", h=4), in_=zeros4[:].rearrange("p (h j) -> p h j", h=4),
                                           pattern=[[0, 4], [-1, 128]], compare_op=ALU.is_ge, fill=-NSA_BIG, base=-1, channel_multiplier=1),
         reads=[zeros4.b], writes=[f32a.b])
    k.op("dve", lambda e: e.tensor_copy(out=wneg[:], in_=f32a[:]), reads=[f32a.b], writes=[wneg.b])
    c1e4 = sbs("c1e4", [128, 128], F32)
    k.op("dve", lambda e: e.memset(c1e4[:], 1e4), writes=[c1e4.b])

    gsb = sbs("gsb", [128, 12], F32)
    qn4 = sbs("qn4", [128, 4, 128], F32)
    qr4 = sbs("qr4", [128, 4, 128], F32)
    qT = sbs("qT", [128, 512], BF16)
    Ef = sbs("Ef", [128, 512], F32)
    m01 = sbs("m01", [128, 512], F32)
    Pt = [sbs(f"Pt{i}", [128, 512], BF16) for i in range(2)]
    pcnt = [0]
    impS = sbs("impS", [128, 512], F32)
    imp = sbs("imp", [128, 128], F32)
    bon = sbs("bon", [128, 128], F32)
    impf = sbs("impf", [128, 128], F32)
    val01 = sbs("val01", [128, 128], F32)
    wk_ = sbs("wk_", [128, 128], F32)
    m8 = sbs("m8", [128, 8], F32)
    selm = sbs("selm", [128, 128], F32)
    nmT = sbs("nmT", [128, 512], BF16)
    zt = sbs("zt", [128, 3, 4], F32)
    wgt = sbs("wgt", [128, 4], F32)
    oacc = sbs("oacc", [128, 4, 128], F32)
    ystage = sbs("ystage", [128, 4, TW], BF16)
    ds_y = k.dsem()

    def exp_to_P(a, mask01=None):
        p = Pt[pcnt[0] % 2]
        pcnt[0] += 1
        if mask01 is None:
            k.op("act", lambda e: e.activation(out=p[:], in_=a.ap(), func=AF.Exp), reads=[a.b], writes=[p.b])
        else:
            k.op("act", lambda e: e.activation(out=Ef[:], in_=a.ap(), func=AF.Exp), reads=[a.b], writes=[Ef.b])
            k.op("dve", lambda e: e.tensor_tensor(out=p[:], in0=Ef[:], in1=mask01[:], op=ALU.mult), reads=[Ef.b, mask01.b], writes=[p.b])
        return p

    def pv(p, vt, vidx, oset, first, last):
        for h in range(4):
            k.op("pe", lambda e: e.matmul(oset[h].ap(), lhsT=p[:, h * 128:(h + 1) * 128], rhs=vt[:, vidx, :], start=first, stop=last),
                 reads=[p.b, vt.b], writes=[oset[h].b])

    def combine(oset, br, first):
        for h in range(4):
            k.op("dve", lambda e: e.tensor_scalar(out=zt[:, br, h:h + 1], in0=oset[h].ap(a=128, b=129), scalar1=1e-30, scalar2=None,
                                                  op0=ALU.max), reads=[oset[h].b], writes=[zt.b])
        k.op("dve", lambda e: e.reciprocal(out=zt[:, br, :], in_=zt[:, br, :]), reads=[zt.b], writes=[zt.b])
        k.op("dve", lambda e: e.tensor_tensor(out=wgt[:], in0=zt[:, br, :], in1=gsb[:, br:12:3], op=ALU.mult), reads=[zt.b, gsb.b],
             writes=[wgt.b])
        for h in range(4):
            if first:
                k.op("dve", lambda e: e.tensor_scalar(out=oacc[:, h, :], in0=oset[h].ap(b=128), scalar1=wgt[:, h:h + 1], scalar2=None,
                                                      op0=ALU.mult), reads=[oset[h].b, wgt.b], writes=[oacc.b])
            else:
                k.op("dve", lambda e: e.scalar_tensor_tensor(out=oacc[:, h, :], in0=oset[h].ap(b=128), scalar=wgt[:, h:h + 1],
                                                             in1=oacc[:, h, :], op0=ALU.mult, op1=ALU.add),
                     reads=[oset[h].b, wgt.b, oacc.b], writes=[oacc.b])

    for st in range(NST):
        t0s = st * TW
        norm_supertile(k, c, hT, nw_sb, hTt, xT, ps_ss, rstd, scrsq, t0s, TW, ds_h)
        for tci in range(TW // 128):
            qt = st * (TW // 128) + tci
            t0 = qt * 128
            tsl = slice(tci * 128, (tci + 1) * 128)
            a = next_acc()

            def mm(e):
                for dc in range(16):
                    ins = e.matmul(a.ap(), lhsT=xT[:, dc, tsl], rhs=win_q[:, dc, 0:512], start=(dc == 0), stop=(dc == 15))
                return ins
            k.op("pe", mm, reads=[win_q.b, xT.b], writes=[a.b])
            rg = next_regT()

            def mmg(e):
                for dc in range(16):
                    ins = e.matmul(rg.ap(b=12), lhsT=xT[:, dc, tsl], rhs=win_q[:, dc, 512:524], start=(dc == 0), stop=(dc == 15))
                return ins
            k.op("pe", mmg, reads=[win_q.b, xT.b], writes=[rg.b])
            k.op("dve", lambda e: e.tensor_tensor(out=gsb[:], in0=rg.ap(b=12), in1=gb_sb[:], op=ALU.add), reads=[rg.b, gb_sb.b], writes=[gsb.b])
            k.op("act", lambda e: e.activation(out=gsb[:], in_=gsb[:], func=AF.Sigmoid), reads=[gsb.b], writes=[gsb.b])
            rms_heads(a, [0, 128, 256, 384], 4, lambda h: (qnw_sb, qnw_sb[:, h * 128:(h + 1) * 128]), post_scale=scale)
            rope_tables(1.0, float(t0))
            apply_rope(qr4, kn, 4)
            for h in range(4):
                r2 = next_regT()
                k.op("pe", lambda e: e.transpose(r2.ap(), qr4[:, h, :], c["ident"][:]), reads=[qr4.b, c["ident"].b], writes=[r2.b])
                k.op("act", lambda e: e.copy(out=qT[:, h * 128:(h + 1) * 128], in_=r2.ap()), reads=[r2.b], writes=[qT.b])
            nmax = (t0 + 127 - 31) // 16
            ntiles = 0 if nmax < 0 else min(NCT, nmax // 128 + 1)
            oc = oreg[0]
            if ntiles == 0:
                k.op("dve", lambda e: e.memset(imp[:], 0.0), writes=[imp.b])
            for nt in range(ntiles):
                a = next_acc()
                k.op("pe", lambda e: e.matmul(a.ap(), lhsT=kcmpT[:, nt * 128:(nt + 1) * 128], rhs=qT[:], start=True, stop=True),
                     reads=[kcmpT.b, qT.b], writes=[a.b])
                k.op("pool", lambda e: e.affine_select(out=m01[:].rearrange("p (h j) -> p h j", h=4),
                                                       in_=ones4[:].rearrange("p (h j) -> p h j", h=4), pattern=[[0, 4], [1, 128]],
                                                       compare_op=ALU.is_ge, fill=0.0, base=t0 - 2048 * nt - 31, channel_multiplier=-16),
                     reads=[ones4.b], writes=[m01.b])
                p = exp_to_P(a, m01)
                pv(p, vcmp, nt, oc, nt == 0, nt == ntiles - 1)
                k.op("pe", lambda e: e.matmul(impT.ap(), lhsT=cover[:, nt, :], rhs=p[:], start=(nt == 0), stop=(nt == ntiles - 1)),
                     reads=[cover.b, p.b], writes=[impT.b])
            if ntiles > 0:
                combine(oc, 0, True)
                k.op("act", lambda e: e.copy(out=impS[:], in_=impT.ap()), reads=[impT.b], writes=[impS.b])
                for h in range(4):
                    r2 = next_regT()
                    k.op("pe", lambda e: e.transpose(r2.ap(), impS[:, h * 128:(h + 1) * 128], c["ident"][:]), reads=[impS.b, c["ident"].b],
                         writes=[r2.b])
                    if h == 0:
                        k.op("dve", lambda e: e.tensor_scalar(out=imp[:], in0=r2.ap(), scalar1=zt[:, 0, 0:1], scalar2=None, op0=ALU.mult),
                             reads=[r2.b, zt.b], writes=[imp.b])
                    else:
                        k.op("dve", lambda e: e.scalar_tensor_tensor(out=imp[:], in0=r2.ap(), scalar=zt[:, 0, h:h + 1], in1=imp[:],
                                                                     op0=ALU.mult, op1=ALU.add), reads=[r2.b, zt.b, imp.b], writes=[imp.b])
            else:
                k.op("dve", lambda e: e.memset(oacc[:], 0.0), writes=[oacc.b])
            k.op("pool", lambda e: e.affine_select(out=bon[:], in_=c1e4[:], pattern=[[-64, 128]], compare_op=ALU.is_ge, fill=0.0, base=t0,
                                                   channel_multiplier=1), reads=[c1e4.b], writes=[bon.b])
            k.op("pool", lambda e: e.affine_select(out=bon[:], in_=bon[:], pattern=[[64, 128]], compare_op=ALU.is_ge, fill=0.0,
                                                   base=127 - t0, channel_multiplier=-1), reads=[bon.b], writes=[bon.b])
            k.op("pool", lambda e: e.memset(bon[:, 0:1], 1e4), writes=[bon.b])
            k.op("dve", lambda e: e.tensor_tensor(out=impf[:], in0=imp[:], in1=bon[:], op=ALU.add), reads=[imp.b, bon.b], writes=[impf.b])
            k.op("pool", lambda e: e.affine_select(out=val01[:], in_=ones4[:, 0:128], pattern=[[-64, 128]], compare_op=ALU.is_ge, fill=0.0,
                                                   base=t0, channel_multiplier=1), reads=[ones4.b], writes=[val01.b])
            k.op("dve", lambda e: e.tensor_tensor(out=impf[:], in0=impf[:], in1=val01[:], op=ALU.mult), reads=[impf.b, val01.b],
                 writes=[impf.b])
            k.op("dve", lambda e: e.tensor_scalar(out=val01[:], in0=val01[:], scalar1=1e30, scalar2=-1e30, op0=ALU.mult, op1=ALU.add),
                 reads=[val01.b], writes=[val01.b])
            k.op("dve", lambda e: e.tensor_tensor(out=impf[:], in0=impf[:], in1=val01[:], op=ALU.add), reads=[impf.b, val01.b],
                 writes=[impf.b])
            k.op("dve", lambda e: e.max(out=m8[:], in_=impf[:]), reads=[impf.b], writes=[m8.b])
            k.op("dve", lambda e: e.match_replace(out=wk_[:], in_to_replace=m8[:], in_values=impf[:], imm_value=-3e38), reads=[impf.b, m8.b],
                 writes=[wk_.b])
            k.op("dve", lambda e: e.max(out=m8[:], in_=wk_[:]), reads=[wk_.b], writes=[m8.b])
            k.op("dve", lambda e: e.tensor_scalar(out=selm[:], in0=impf[:], scalar1=m8[:, 7:8], scalar2=None, op0=ALU.is_ge),
                 reads=[impf.b, m8.b], writes=[selm.b])
            k.op("dve", lambda e: e.tensor_scalar(out=selm[:], in0=selm[:], scalar1=-1.0, scalar2=NSA_BIG, op0=ALU.add, op1=ALU.mult),
                 reads=[selm.b], writes=[selm.b])
            r2 = next_regT()
            k.op("pe", lambda e: e.transpose(r2.ap(), selm[:], c["ident"][:]), reads=[selm.b, c["ident"].b], writes=[r2.b])
            k.op("act", lambda e: e.copy(out=nmT[:].rearrange("p (h j) -> p h j", h=4),
                                         in_=r2.ap().rearrange("p (o j) -> p o j", o=1).to_broadcast([128, 4, 128])), reads=[r2.b],
                 writes=[nmT.b])
            if debug and qt == (NT - 1):
                dump("imp", imp, imp[:], 128)
                dump("impf", impf, impf[:], 128)
                dump("selm", selm, selm[:], 128)
            osel = oreg[1]
            for kt in range(qt + 1):
                a = next_acc()

                def mms(e):
                    e.matmul(a.ap(), lhsT=ksT[:, kt * 128:(kt + 1) * 128], rhs=qT[:], start=True, stop=False)
                    ins = e.matmul(a.ap(), lhsT=Esel[:, kt * 128:(kt + 1) * 128], rhs=nmT[:], start=False, stop=(kt != qt))
                    if kt == qt:
                        ins = e.matmul(a.ap(), lhsT=identb[:], rhs=cneg[:], start=False, stop=True)
                    return ins
                k.op("pe", mms, reads=[ksT.b, qT.b, Esel.b, nmT.b, identb.b, cneg.b], writes=[a.b])
                p = exp_to_P(a)
                pv(p, vs_e, kt, osel, kt == 0, kt == qt)
            combine(osel, 1, False)
            owin = oreg[0]
            kts = list(range(max(0, qt - 4), qt + 1))
            for kt in kts:
                a = next_acc()

                def mmw(e):
                    need_mask = (kt == qt) or (kt == qt - 4)
                    ins = e.matmul(a.ap(), lhsT=kwT[:, kt * 128:(kt + 1) * 128], rhs=qT[:], start=True, stop=not need_mask)
                    if kt == qt:
                        ins = e.matmul(a.ap(), lhsT=identb[:], rhs=cneg[:], start=False, stop=True)
                    elif kt == qt - 4:
                        ins = e.matmul(a.ap(), lhsT=identb[:], rhs=wneg[:], start=False, stop=True)
                    return ins
                k.op("pe", mmw, reads=[kwT.b, qT.b, identb.b, cneg.b, wneg.b], writes=[a.b])
                p = exp_to_P(a)
                pv(p, vw_e, kt, owin, kt == kts[0], kt == kts[-1])
            combine(owin, 2, False)
            for h in range(4):
                r2 = next_regT()
                k.op("pe", lambda e: e.transpose(r2.ap(), oacc[:, h, :], c["ident"][:]), reads=[oacc.b, c["ident"].b], writes=[r2.b])
                k.op("act", lambda e: e.copy(out=ystage[:, h, tsl], in_=r2.ap()), reads=[r2.b], writes=[ystage.b])
        tk = k.op("sp", lambda e: e.dma_start(out=yT.rearrange("c p t -> p c t")[:, :, t0s:t0s + TW], in_=ystage[:]), reads=[ystage.b],
                  dsem=ds_y)
        out_toks.append(tk)
    k.finish(out_toks)
    return nc


def nsa_core_inputs(I, layer, g, hT_b):
    j = layer // 2
    W = I["c_w_in"][j]
    o_q, o_kc, o_vc, o_ks, o_vs, o_kw, o_vw, o_gp = 0, 2048, 2560, 3072, 3584, 4096, 4608, 5120
    sl = lambda o: W[:, o + g * 128:o + (g + 1) * 128]
    win = np.concatenate([sl(o_kc), sl(o_vc), sl(o_ks), sl(o_vs), sl(o_kw), sl(o_vw), W[:, g * 512:(g + 1) * 512],
                          W[:, o_gp + 12 * g:o_gp + 12 * (g + 1)]], axis=1)
    posT = np.ascontiguousarray(I["c_cmp_pos"][j].transpose(2, 0, 1))
    b1 = np.ascontiguousarray(I["c_cmp_b1"][j].reshape(2, 2, 128).transpose(2, 0, 1))
    return {
        "hT": hT_b, "nw": pvec(I["mix_norm"][layer]), "win": np.ascontiguousarray(win),
        "qnw": rep(np.tile(I["c_q_norm"][j], 4)), "knw": rep(I["c_k_norm"][j]), "posT": posT,
        "w1": np.ascontiguousarray(I["c_cmp_w1"][j]), "b1": b1, "w2": np.ascontiguousarray(I["c_cmp_w2"][j]),
        "gbias": rep(I["c_gate_bias"][j][12 * g:12 * (g + 1)]),
    }


_PROGS = {}


def _prog(name, fn):
    if name not in _PROGS:
        _PROGS[name] = fn()
    return _PROGS[name]


def _launch(nc, maps):
    res = run_bass_kernel_spmd(nc, maps, core_ids=list(range(8)))
    return res.results


def kernel(**I):
    I = {k_: np.ascontiguousarray(np.asarray(v)) for k_, v in I.items()}
    x = I["x"]
    B, S, D = x.shape
    NTC = S // 4
    hT = [np.ascontiguousarray(x[b].T.reshape(16, 128, S)) for b in range(B)]

    def tok_shard(arrs, c):
        b, q = divmod(c, 4)
        return np.ascontiguousarray(arrs[b][:, :, q * NTC:(q + 1) * NTC])

    def ffn_launch(hT, pre, layer, yT=None, wo=None):
        nc = _prog("ffn_pre" if yT is not None else "ffn", lambda: build_ffn(NT=NTC, preproj=yT is not None))
        maps = []
        for c in range(8):
            m = {"hT": tok_shard(hT, c), "nw": pvec(I[pre + "_norm"][layer]), "wg": I[pre + "_w_gate"][layer],
                 "wu": I[pre + "_w_up"][layer], "wd": I[pre + "_w_down"][layer]}
            if yT is not None:
                m["yT"] = tok_shard(yT, c)
                m["wo"] = wo
            maps.append(m)
        res = _launch(nc, maps)
        out = [np.empty((16, 128, S), np.float32) for _ in range(B)]
        for c in range(8):
            b, q = divmod(c, 4)
            out[b][:, :, q * NTC:(q + 1) * NTC] = res[c]["hT_out"]
        return out

    for layer in range(4):
        j = layer // 2
        hT = ffn_launch(hT, "ffn1", layer)
        yT = [np.empty((16, 128, S), ml_dtypes.bfloat16) for _ in range(B)]
        if layer % 2 == 0:
            nc = _prog(f"ab{j}", lambda: build_ab(T=S, layer_j=j))
            maps = [ab_core_inputs(I, layer, c % 4, hT[c // 4]) for c in range(8)]
            res = _launch(nc, maps)
            for c in range(8):
                b, g = divmod(c, 4)
                y = res[c]["yT"]
                yT[b][2 * g:2 * g + 2] = y[0:2]
                yT[b][8 + 2 * g:8 + 2 * g + 2] = y[2:4]
            wo = I["ab_w_out"][j]
        else:
            nc = _prog("nsa", lambda: build_nsa(T=S))
            maps = [nsa_core_inputs(I, layer, c % 4, hT[c // 4]) for c in range(8)]
            res = _launch(nc, maps)
            for c in range(8):
                b, g = divmod(c, 4)
                yT[b][4 * g:4 * g + 4] = res[c]["yT"]
            wo = I["c_w_out"][j]
        hT = ffn_launch(hT, "ffn2", layer, yT=yT, wo=wo)
    out = np.stack([hT[b].reshape(D, S).T for b in range(B)], axis=0)
    return np.ascontiguousarray(out.astype(np.float32))
```

```python
import numpy as np
import ml_dtypes
import concourse.bass as bass
import concourse.mybir as mybir
from concourse.bass_utils import run_bass_kernel_spmd

F32 = mybir.dt.float32
BF16 = mybir.dt.bfloat16
I32 = mybir.dt.int32
AF = mybir.ActivationFunctionType
ALU = mybir.AluOpType
AX = mybir.AxisListType

D_MODEL = 2048
D_FF = 5504
EPS = 1e-6
SAME_SYNC = True


class Tok:
    __slots__ = ("sem", "val", "key", "eng")

    def __init__(self, sem, val, key, eng):
        self.sem, self.val, self.key, self.eng = sem, val, key, eng


class Buf:
    __slots__ = ("name", "w", "r", "bank")

    def __init__(self, name):
        self.name, self.w, self.r, self.bank = name, None, {}, None


class DSem:
    __slots__ = ("h", "val", "key")

    def __init__(self, h, key):
        self.h, self.val, self.key = h, 0, key


class KB:
    def __init__(self, nc):
        self.nc = nc
        self.engs = dict(pe=nc.tensor, act=nc.scalar, dve=nc.vector, pool=nc.gpsimd, sp=nc.sync)
        self.sem = {e: nc.alloc_semaphore("sem_" + e) for e in self.engs}
        self.cnt = {e: 0 for e in self.engs}
        self.waited = {e: {} for e in self.engs}
        self.nds = 0
        self.out_toks = []

    def buf(self, name=""):
        return Buf(name)

    def bufs(self, n, name=""):
        return [Buf(f"{name}{i}") for i in range(n)]

    def dsem(self, name=None):
        self.nds += 1
        key = f"D{self.nds}"
        return DSem(self.nc.alloc_semaphore(name or key), key)

    def _wait(self, e, tok, raw=False):
        if tok is None:
            return
        if tok.eng == e and not (raw and SAME_SYNC and e != "pe"):
            return
        w = self.waited[e]
        if w.get(tok.key, 0) >= tok.val:
            return
        self.engs[e].wait_ge(tok.sem, tok.val)
        w[tok.key] = tok.val

    def op(self, e, fn, reads=(), writes=(), dsem=None):
        for b in reads:
            self._wait(e, b.w, raw=True)
        for b in writes:
            self._wait(e, b.w)
            for t in b.r.values():
                self._wait(e, t)
        banks = {}
        for b in list(reads) + list(writes):
            if b.bank is not None:
                banks[id(b.bank)] = b.bank
        for bk in banks.values():
            self._wait(e, bk.w)
        ins = fn(self.engs[e])
        if dsem is None:
            self.cnt[e] += 1
            ins.then_inc(self.sem[e], 1)
            tok = Tok(self.sem[e], self.cnt[e], "E" + e, e)
        else:
            dsem.val += 16
            ins.then_inc(dsem.h, 16)
            tok = Tok(dsem.h, dsem.val, dsem.key, None)
        for b in reads:
            b.r[tok.key] = tok
        for b in writes:
            b.w = tok
            b.r = {}
        for bk in banks.values():
            bk.w = tok
        return tok

    def finish(self, toks):
        for t in toks:
            self._wait("sp", t)


class Ring:
    def __init__(self, k, slots, name="ws"):
        self.k = k
        self.slots = slots
        self.ns = len(slots)
        self.b = k.bufs(self.ns, name)
        self.ds = [k.dsem() for _ in range(self.ns)]
        self.loads = []
        self.issued = 0
        self.consumed = 0

    def plan(self, dst_fn, src):
        self.loads.append((dst_fn, src))

    def _issue(self):
        if self.issued >= len(self.loads):
            return
        i = self.issued
        s = i % self.ns
        dst_fn, src = self.loads[i]
        slot = self.slots[s]
        self.k.op("pool", lambda e: e.dma_start(out=dst_fn(slot), in_=src), reads=(), writes=[self.b[s]],
                  dsem=self.ds[s])
        self.issued += 1

    def start(self):
        while self.issued < min(self.ns, len(self.loads)):
            self._issue()

    def get(self, off=0):
        i = self.consumed + off
        assert i < self.issued, "ring underflow"
        s = i % self.ns
        return self.slots[s], self.b[s]

    def done(self):
        self.consumed += 1
        self._issue()


def build_ffn(NT=2048, F=D_FF, preproj=False, TP=1024, NS=4):
    D = D_MODEL
    DC = D // 128
    FCn = F // 128
    assert F % 128 == 0 and NT % TP == 0 and TP % 512 == 0
    NTT = TP // 512
    nc = bass.Bass("TRN2", target_bir_lowering=False)
    k = KB(nc)
    hT_in = nc.dram_tensor("hT", [DC, 128, NT], F32, kind="ExternalInput").ap()
    nw = nc.dram_tensor("nw", [128, DC], F32, kind="ExternalInput").ap()
    wg = nc.dram_tensor("wg", [D, F], F32, kind="ExternalInput").ap()
    wu = nc.dram_tensor("wu", [D, F], F32, kind="ExternalInput").ap()
    wd = nc.dram_tensor("wd", [F, D], F32, kind="ExternalInput").ap()
    hT_out = nc.dram_tensor("hT_out", [DC, 128, NT], F32, kind="ExternalOutput").ap()
    if preproj:
        yT = nc.dram_tensor("yT", [DC, 128, NT], BF16, kind="ExternalInput").ap()
        wo = nc.dram_tensor("wo", [D, D], F32, kind="ExternalInput").ap()
        h2T = nc.dram_tensor("h2T", [DC, 128, NT], F32).ap()
        h_src = h2T
    else:
        h_src = hT_in
    wg_v = wg.rearrange("(dc p) f -> p dc f", p=128)
    wu_v = wu.rearrange("(dc p) f -> p dc f", p=128)
    wd_v = wd.rearrange("(fc p) m -> p fc m", p=128)

    actT = nc.alloc_sbuf_tensor("actT", [128, FCn, TP], BF16)
    xT = nc.alloc_sbuf_tensor("xT", [128, DC, TP], BF16)
    slots = [nc.alloc_sbuf_tensor(f"ws{i}", [128, 16, 256], BF16) for i in range(NS)]
    NSCR = 6
    scr = [nc.alloc_sbuf_tensor(f"scr{i}", [128, TP], F32) for i in range(NSCR)]
    rstd = nc.alloc_sbuf_tensor("rstd", [128, TP], F32)
    ones = nc.alloc_sbuf_tensor("ones", [128, 128], F32)
    nw_sb = nc.alloc_sbuf_tensor("nw_sb", [128, DC], F32)
    epst = nc.alloc_sbuf_tensor("epst", [128, 1], F32)
    ps = [nc.alloc_psum_tensor(f"ps{i}", [128, 512], F32) for i in range(8)]

    actT_b = k.bufs(FCn, "actT")
    xT_b = k.bufs(DC, "xT")
    scr_b = k.bufs(NSCR, "scr")
    scr_ds = [k.dsem() for _ in range(NSCR)]
    rstd_b = k.buf("rstd")
    ones_b = k.buf("ones")
    eps_b = k.buf("eps")
    nw_b = k.buf("nw")
    nw_ds = k.dsem()
    ps_b = k.bufs(8, "ps")
    ring = Ring(k, slots)
    npass = NT // TP
    h2_b = [[k.buf(f"h2_{p}_{dc}") for dc in range(DC)] for p in range(npass)]
    yT_ds = k.dsem()

    fblocks = []
    f0 = 0
    while f0 < F:
        fw = min(256, F - f0)
        fblocks.append((f0, fw))
        f0 += fw
    dsegs = []
    c0 = 0
    while c0 < FCn:
        n = min(16, FCn - c0)
        dsegs.append((c0, n))
        c0 += n
    NG = D // 256

    for p in range(npass):
        if preproj:
            for dc in range(DC):
                ring.plan(lambda s: s[:, :, 0:128], wo.rearrange("(yc p) m -> p yc m", p=128)[:, :, dc * 128:(dc + 1) * 128])
        for (f0, fw) in fblocks:
            ring.plan(lambda s, fw=fw: s[:, :, 0:fw], wg_v[:, :, f0:f0 + fw])
            ring.plan(lambda s, fw=fw: s[:, :, 0:fw], wu_v[:, :, f0:f0 + fw])
        for gi in range(NG):
            for (c0, n) in dsegs:
                ring.plan(lambda s, n=n: s[:, 0:n, :], wd_v[:, c0:c0 + n, gi * 256:(gi + 1) * 256])

    k.op("dve", lambda e: e.memset(ones[:], 1.0), writes=[ones_b])
    k.op("dve", lambda e: e.memset(epst[:], EPS), writes=[eps_b])
    k.op("sp", lambda e: e.dma_start(out=nw_sb[:], in_=nw), writes=[nw_b], dsem=nw_ds)
    ring.start()
    out_toks = []

    for p in range(npass):
        t0 = p * TP
        tsl = slice(t0, t0 + TP)
        if preproj:
            k.op("sp", lambda e: e.dma_start(out=actT[:, 0:DC, :], in_=yT.rearrange("yc p t -> p yc t")[:, :, tsl]),
                 writes=actT_b[0:DC], dsem=yT_ds)
        for dc in range(DC):
            hi = dc % 2
            ht = scr[hi]
            k.op("sp", lambda e: e.dma_start(out=ht[:], in_=hT_in[dc, :, tsl]), writes=[scr_b[hi]], dsem=scr_ds[hi])
            if preproj:
                slot, sb = ring.get()
                pb = 2 + 2 * (dc % 2)

                def mm(e):
                    for yc in range(DC):
                        for tt in range(NTT):
                            ins = e.matmul(ps[pb + tt][:], lhsT=slot[:, yc, 0:128],
                                           rhs=actT[:, yc, tt * 512:(tt + 1) * 512], start=(yc == 0), stop=(yc == DC - 1))
                    return ins
                k.op("pe", mm, reads=[sb] + actT_b[0:DC], writes=[ps_b[pb + tt] for tt in range(NTT)])
                ring.done()
                for tt in range(NTT):
                    k.op("dve", lambda e: e.tensor_tensor(out=ht[:, tt * 512:(tt + 1) * 512], in0=ps[pb + tt][:],
                                                          in1=ht[:, tt * 512:(tt + 1) * 512], op=ALU.add),
                         reads=[ps_b[pb + tt]], writes=[scr_b[hi]])
                k.op("sp", lambda e: e.dma_start(out=h2T[dc, :, tsl], in_=ht[:]), reads=[scr_b[hi]],
                     writes=[h2_b[p][dc]], dsem=scr_ds[hi])
            si = 2 + dc % 2
            sq = scr[si]
            k.op("act", lambda e: e.activation(out=sq[:], in_=ht[:], func=AF.Square), reads=[scr_b[hi]], writes=[scr_b[si]])

            def mm(e):
                for tt in range(NTT):
                    ins = e.matmul(ps[tt][:], lhsT=ones[:], rhs=sq[:, tt * 512:(tt + 1) * 512], start=(dc == 0),
                                   stop=(dc == DC - 1))
                return ins
            k.op("pe", mm, reads=[scr_b[si], ones_b], writes=[ps_b[tt] for tt in range(NTT)])
        for tt in range(NTT):
            k.op("act", lambda e: e.activation(out=rstd[:, tt * 512:(tt + 1) * 512], in_=ps[tt][:], func=AF.Sqrt,
                                               scale=1.0 / D, bias=epst[:, 0:1]),
                 reads=[ps_b[tt], eps_b], writes=[rstd_b])
        k.op("dve", lambda e: e.reciprocal(out=rstd[:], in_=rstd[:]), reads=[rstd_b], writes=[rstd_b])
        for dc in range(DC):
            hi = dc % 2
            ht = scr[hi]
            rd = [h2_b[p][dc]] if preproj else []
            k.op("sp", lambda e: e.dma_start(out=ht[:], in_=h_src[dc, :, tsl]), reads=rd, writes=[scr_b[hi]],
                 dsem=scr_ds[hi])
            k.op("dve", lambda e: e.scalar_tensor_tensor(out=xT[:, dc, :], in0=ht[:], scalar=nw_sb[:, dc:dc + 1],
                                                         in1=rstd[:], op0=ALU.mult, op1=ALU.mult),
                 reads=[scr_b[hi], rstd_b, nw_b], writes=[xT_b[dc]])
        ci = 0
        for (f0, fw) in fblocks:
            sg_, sgb = ring.get(0)
            su_, sub = ring.get(1)
            for j in range(fw // 128):
                fi = f0 // 128 + j
                par = ci % 2
                ci += 1
                gb = 4 * par
                ub = 4 * par + 2

                def mm(e, w_=None, b0=0):
                    for dc in range(DC):
                        for tt in range(NTT):
                            ins = e.matmul(ps[b0 + tt][:], lhsT=w_[:, dc, j * 128:(j + 1) * 128],
                                           rhs=xT[:, dc, tt * 512:(tt + 1) * 512], start=(dc == 0), stop=(dc == DC - 1))
                    return ins
                k.op("pe", lambda e: mm(e, sg_, gb), reads=[sgb] + xT_b, writes=[ps_b[gb + tt] for tt in range(NTT)])
                k.op("pe", lambda e: mm(e, su_, ub), reads=[sub] + xT_b, writes=[ps_b[ub + tt] for tt in range(NTT)])
                sgi = 4 + par
                sgt = scr[sgi]
                for tt in range(NTT):
                    k.op("act", lambda e: e.activation(out=sgt[:, tt * 512:(tt + 1) * 512], in_=ps[gb + tt][:], func=AF.Silu),
                         reads=[ps_b[gb + tt]], writes=[scr_b[sgi]])
                    k.op("dve", lambda e: e.tensor_tensor(out=actT[:, fi, tt * 512:(tt + 1) * 512],
                                                          in0=sgt[:, tt * 512:(tt + 1) * 512], in1=ps[ub + tt][:], op=ALU.mult),
                         reads=[scr_b[sgi], ps_b[ub + tt]], writes=[actT_b[fi]])
            ring.done()
            ring.done()
        for gi in range(NG):
            base = 4 * (gi % 2)
            for dmi in range(2):
                dmc = gi * 2 + dmi
                hi = dmi
                rd = [h2_b[p][dmc]] if preproj else []
                k.op("sp", lambda e: e.dma_start(out=scr[hi][:], in_=h_src[dmc, :, tsl]), reads=rd, writes=[scr_b[hi]],
                     dsem=scr_ds[hi])
            for (c0, n) in dsegs:
                slot, sb = ring.get()

                def mm(e):
                    for j in range(n):
                        fc = c0 + j
                        for dmi in range(2):
                            for tt in range(NTT):
                                ins = e.matmul(ps[base + 2 * dmi + tt][:], lhsT=slot[:, j, dmi * 128:(dmi + 1) * 128],
                                               rhs=actT[:, fc, tt * 512:(tt + 1) * 512], start=(fc == 0), stop=(fc == FCn - 1))
                    return ins
                k.op("pe", mm, reads=[sb] + actT_b[c0:c0 + n], writes=[ps_b[base + i] for i in range(4)])
                ring.done()
            for dmi in range(2):
                dmc = gi * 2 + dmi
                hi = dmi
                oi = 2 + dmi
                ot = scr[oi]
                for tt in range(NTT):
                    bk = base + 2 * dmi + tt
                    k.op("dve", lambda e: e.scalar_tensor_tensor(out=ot[:, tt * 512:(tt + 1) * 512], in0=ps[bk][:], scalar=0.5,
                                                                 in1=scr[hi][:, tt * 512:(tt + 1) * 512], op0=ALU.mult, op1=ALU.add),
                         reads=[ps_b[bk], scr_b[hi]], writes=[scr_b[oi]])
                tk = k.op("sp", lambda e: e.dma_start(out=hT_out[dmc, :, tsl], in_=ot[:]), reads=[scr_b[oi]], dsem=scr_ds[oi])
                out_toks.append(tk)
    k.finish(out_toks)
    return nc


class TT:
    def __init__(self, k, t, name):
        self.t = t
        self.b = k.buf(name)

    def __getitem__(self, idx):
        return self.t[idx]


def sb(k, name, shape, dt):
    return TT(k, k.nc.alloc_sbuf_tensor(name, list(shape), dt), name)


class PReg:
    _bankbufs = {}

    def __init__(self, k, bank, c0, c1, name):
        self.bank, self.c0, self.c1 = bank, c0, c1
        self.b = k.buf(name)
        key = (id(k), bank.name if hasattr(bank, "name") else id(bank))
        if key not in PReg._bankbufs:
            PReg._bankbufs[key] = k.buf("bank")
        self.b.bank = PReg._bankbufs[key]

    def ap(self, rows=slice(None), a=None, b=None):
        a = self.c0 if a is None else self.c0 + a
        b = self.c1 if b is None else self.c0 + b
        return self.bank[rows, a:b]


def make_consts(k, need_ident=True):
    nc = k.nc
    c = {}
    c["ones"] = sb(k, "c_ones", [128, 128], F32)
    k.op("dve", lambda e: e.memset(c["ones"][:], 1.0), writes=[c["ones"].b])
    c["ident"] = sb(k, "c_ident", [128, 128], F32)
    k.op("pool", lambda e: e.affine_select(out=c["ident"][:], in_=c["ones"][:], pattern=[[-1, 128]],
                                           compare_op=ALU.is_equal, fill=0.0, base=0, channel_multiplier=1),
         reads=[c["ones"].b], writes=[c["ident"].b])
    c["causal"] = sb(k, "c_causal", [128, 128], F32)
    k.op("pool", lambda e: e.affine_select(out=c["causal"][:], in_=c["ones"][:], pattern=[[1, 128]],
                                           compare_op=ALU.is_ge, fill=0.0, base=0, channel_multiplier=-1),
         reads=[c["ones"].b], writes=[c["causal"].b])
    c["eps"] = sb(k, "c_eps", [128, 1], F32)
    k.op("dve", lambda e: e.memset(c["eps"][:], EPS), writes=[c["eps"].b])
    c["one1"] = sb(k, "c_one1", [128, 1], F32)
    k.op("dve", lambda e: e.memset(c["one1"][:], 1.0), writes=[c["one1"].b])
    return c


def norm_supertile(k, c, hT_src, nw_sb, hTt, xT, ps_ss, rstd, scrsq, t0, TW, ds_h, h_reads=()):
    DC = 16
    k.op("sp", lambda e: e.dma_start(out=hTt[:, :, 0:TW], in_=hT_src.rearrange("dc p t -> p dc t")[:, :, t0:t0 + TW]),
         reads=list(h_reads), writes=[hTt.b], dsem=ds_h)
    for dc in range(DC):
        sq = scrsq[dc % 2]
        k.op("act", lambda e: e.activation(out=sq[:, 0:TW], in_=hTt[:, dc, 0:TW], func=AF.Square), reads=[hTt.b],
             writes=[sq.b])
        k.op("pe", lambda e: e.matmul(ps_ss.ap(b=TW), lhsT=c["ones"][:], rhs=sq[:, 0:TW], start=(dc == 0), stop=(dc == DC - 1)),
             reads=[sq.b, c["ones"].b], writes=[ps_ss.b])
    k.op("act", lambda e: e.activation(out=rstd[:, 0:TW], in_=ps_ss.ap(b=TW), func=AF.Sqrt, scale=1.0 / D_MODEL,
                                       bias=c["eps"][:, 0:1]), reads=[ps_ss.b, c["eps"].b], writes=[rstd.b])
    k.op("dve", lambda e: e.reciprocal(out=rstd[:, 0:TW], in_=rstd[:, 0:TW]), reads=[rstd.b], writes=[rstd.b])
    for dc in range(DC):
        k.op("dve", lambda e: e.scalar_tensor_tensor(out=xT[:, dc, 0:TW], in0=hTt[:, dc, 0:TW], scalar=nw_sb[:, dc:dc + 1],
                                                     in1=rstd[:, 0:TW], op0=ALU.mult, op1=ALU.mult),
             reads=[hTt.b, rstd.b, nw_sb.b], writes=[xT.b])


HG_MAX_K = 0.999999


class _Stop(Exception):
    pass


def build_ab(T=8192, layer_j=0, do_ml=2, do_hg=2, stop=99, debug=False):
    TW = 512
    NST = T // TW
    NCOL = 1794
    nc = bass.Bass("TRN2", target_bir_lowering=False)
    k = KB(nc)

    def dram(name, shape, dt=F32, kind="ExternalInput"):
        return nc.dram_tensor(name, list(shape), dt, kind=kind).ap()
    hT = dram("hT", [16, 128, T])
    nw = dram("nw", [128, 16])
    win = dram("win", [2048, NCOL])
    cw = dram("cw", [128, 2, 4])
    cb = dram("cb", [128, 2])
    wq = dram("wq", [256, 256])
    wk = dram("wk", [256, 256])
    gbias = dram("gb", [128, 2])
    mln = dram("mln", [128, 256])
    skp = dram("skp", [128, 256])
    lbl = dram("lbl", [128, 2, 2])
    hgn = dram("hgn", [128, 2, 128])
    yT = dram("yT", [4, 128, T], BF16, kind="ExternalOutput")
    dbg = dram("dbg", [128, 8192], F32, kind="ExternalOutput") if debug else None
    dbg_pos = [0]
    dbg_map = {}
    dbg_ds = k.dsem() if debug else None

    def dump(name, tt, ap, n):
        if not debug or name in dbg_map:
            return
        c0 = dbg_pos[0]
        dbg_pos[0] += n
        dbg_map[name] = (c0, n)
        k.op("sp", lambda e: e.dma_start(out=dbg[:, c0:c0 + n], in_=ap), reads=[tt.b], dsem=dbg_ds)
    nc._dbg_map = dbg_map

    c = make_consts(k)
    ps = [nc.alloc_psum_tensor(f"ps{i}", [128, 512], F32) for i in range(8)]
    acc = [PReg(k, ps[i], 0, 512, f"acc{i}") for i in range(2)]
    ps_ss = PReg(k, ps[2], 0, 512, "ss")
    regT = [PReg(k, ps[2], i * 128, (i + 1) * 128, f"regT{i}") for i in range(4)]
    regA = PReg(k, ps[3], 0, 8, "regA")
    regB = PReg(k, ps[3], 128, 257, "regB")
    regC = PReg(k, ps[3], 384, 512, "regC")
    regND = PReg(k, ps[4], 0, 264, "regND")
    regU = [PReg(k, ps[5 + i], 0, 264, f"regU{i}") for i in range(2)]
    r7A = PReg(k, ps[7], 0, 128, "r7A")
    r7B = PReg(k, ps[7], 128, 256, "r7B")
    r7C = PReg(k, ps[7], 256, 384, "r7C")
    r7D = PReg(k, ps[7], 384, 512, "r7D")

    win_sb = sb(k, "win_sb", [128, 16, NCOL], BF16)
    wds = [k.dsem() for _ in range(4)]
    cuts = [0, 512, 1024, 1536, NCOL]
    win_v = win.rearrange("(dc p) f -> p dc f", p=128)
    wtoks = []
    for i in range(4):
        a, b_ = cuts[i], cuts[i + 1]
        wtoks.append(k.op("pool", lambda e: e.dma_start(out=win_sb[:, :, a:b_], in_=win_v[:, :, a:b_]), writes=[],
                          dsem=wds[i]))
    wq_sb = sb(k, "wq_sb", [128, 2, 256], BF16)
    wk_sb = sb(k, "wk_sb", [128, 2, 256], BF16)
    pds = k.dsem()
    k.op("pool", lambda e: e.dma_start(out=wq_sb[:], in_=wq.rearrange("(d p) e -> p d e", p=128)), writes=[wq_sb.b], dsem=pds)
    k.op("pool", lambda e: e.dma_start(out=wk_sb[:], in_=wk.rearrange("(d p) e -> p d e", p=128)), writes=[wk_sb.b], dsem=k.dsem())

    def ld(name, shape, src):
        t = sb(k, name, shape, F32)
        k.op("sp", lambda e: e.dma_start(out=t[:], in_=src), writes=[t.b], dsem=k.dsem())
        return t
    nw_sb = ld("nw_sb", [128, 16], nw)
    cw_sb = ld("cw_sb", [128, 2, 4], cw)
    cb_sb = ld("cb_sb", [128, 2], cb)
    gb_sb = ld("gb_sb", [128, 2], gbias)
    mln_sb = ld("mln_sb", [128, 256], mln)
    skp_sb = ld("skp_sb", [128, 256], skp)
    lbl_sb = ld("lbl_sb", [128, 2, 2], lbl)
    hgn_sb = ld("hgn_sb", [128, 2, 128], hgn)
    for tkn in wtoks:
        k._wait("pe", tkn)

    oml = sb(k, "oml", [128, 2], F32)
    lbe = sb(k, "lbe", [128, 2, 2], F32)
    lbs = sb(k, "lbs", [128, 2], F32)
    k.op("act", lambda e: e.activation(out=lbe[:], in_=lbl_sb[:], func=AF.Exp), reads=[lbl_sb.b], writes=[lbe.b])
    k.op("dve", lambda e: e.tensor_tensor(out=lbs[:], in0=lbe[:, :, 0], in1=lbe[:, :, 1], op=ALU.add), reads=[lbe.b], writes=[lbs.b])
    k.op("dve", lambda e: e.reciprocal(out=lbs[:], in_=lbs[:]), reads=[lbs.b], writes=[lbs.b])
    for l in range(2):
        k.op("dve", lambda e: e.tensor_tensor(out=lbe[:, :, l], in0=lbe[:, :, l], in1=lbs[:], op=ALU.mult), reads=[lbe.b, lbs.b],
             writes=[lbe.b])
    k.op("dve", lambda e: e.tensor_copy(out=oml[:], in_=lbe[:, :, 0]), reads=[lbe.b], writes=[oml.b])
    for l in range(1, layer_j + 1):
        k.op("dve", lambda e: e.tensor_tensor(out=oml[:], in0=oml[:], in1=lbe[:, :, l], op=ALU.add), reads=[lbe.b, oml.b], writes=[oml.b])
    k.op("dve", lambda e: e.tensor_tensor(out=oml[:], in0=oml[:], in1=lbe[:, :, 0], op=ALU.subtract), reads=[lbe.b, oml.b], writes=[oml.b])
    k.op("dve", lambda e: e.tensor_scalar(out=oml[:], in0=oml[:], scalar1=-1.0, scalar2=1.0, op0=ALU.mult, op1=ALU.add),
         reads=[oml.b], writes=[oml.b])
    nfb = sb(k, "nfb", [128, 1], F32)
    k.op("dve", lambda e: e.tensor_scalar(out=nfb[:], in0=gb_sb[:, 1:2], scalar1=-1.0, scalar2=None, op0=ALU.mult),
         reads=[gb_sb.b], writes=[nfb.b])

    hTt = sb(k, "hTt", [128, 16, TW], F32)
    ds_h = k.dsem()
    xT = sb(k, "xT", [128, 16, TW], BF16)
    rstd = sb(k, "rstd", [128, TW], F32)
    scrsq = [sb(k, f"scrsq{i}", [128, TW], F32) for i in range(2)]
    ubuf = sb(k, "ubuf", [128, 2, TW + 3], F32)
    cacc = sb(k, "cacc", [128, TW], F32)
    cT = sb(k, "cT", [128, 2, TW], F32)
    cTb = sb(k, "cTb", [128, 2, TW], BF16)
    qT = sb(k, "qT", [128, 2, TW], F32)
    qTb = sb(k, "qTb", [128, 2, TW], BF16)
    kTb = sb(k, "kTb", [128, 2, TW], BF16)
    ktok = sb(k, "ktok", [128, 4, 256], F32)
    ctok = sb(k, "ctok", [128, 4, 256], F32)
    vext = sb(k, "vext", [128, 4, 264], BF16)
    osig = sb(k, "osig", [128, 4, 256], F32)
    hv = sb(k, "hv", [128, 4, 2, 128], BF16)
    hgs = sb(k, "hgs", [128, 4, 256], F32)
    ge1 = sb(k, "ge1", [128, 4], F32)
    logf = sb(k, "logf", [128, 4], F32)
    ig = sb(k, "ig", [128, 4], F32)
    lfb = sb(k, "lfb", [128, 128], F32)
    bias_s = sb(k, "bias_s", [128, 1], F32)
    DT = sb(k, "DT", [128, 128], F32)
    Eb = sb(k, "Eb", [128, 128], F32)
    Dm = sb(k, "Dm", [128, 128], F32)
    PT = sb(k, "PT", [128, 128], BF16)
    qs = sb(k, "qs", [128, 2, 128], BF16)
    small = sb(k, "small", [128, 8], F32)
    hn = sb(k, "hn", [128, 256], F32)
    junk = sb(k, "junk", [128, 256], F32)
    hm = sb(k, "hm", [128, 256], F32)
    t1 = sb(k, "t1", [128, 256], F32)
    yml = sb(k, "yml", [128, 256], F32)
    ka = sb(k, "ka", [128, 256], BF16)
    Cst = sb(k, "Cst", [128, 2, 264], F32)
    Cb = sb(k, "Cb", [128, 2, 264], BF16)
    ystage = sb(k, "ystage", [128, 4, TW], BF16)
    ds_y = k.dsem()
    resetm = sb(k, "resetm", [128, TW], F32)
    tA = sb(k, "tA", [128, TW], F32)
    tB = sb(k, "tB", [128, TW], F32)
    tC = sb(k, "tC", [128, TW], F32)
    k2 = sb(k, "k2", [128, TW], F32)
    lf1 = sb(k, "lf1", [128, TW], F32)
    lgf = sb(k, "lgf", [128, TW], F32)
    bt = sb(k, "bt", [128, TW], F32)
    brel = sb(k, "brel", [128, TW], F32)
    sqt = sb(k, "sqt", [128, TW], F32)
    kgT = sb(k, "kgT", [128, TW], F32)
    eg8 = sb(k, "eg8", [128, 8], F32)
    qz = [sb(k, f"qz{i}", [128, 4, 128], BF16) for i in range(2)]
    kz = [sb(k, f"kz{i}", [128, 4, 128], BF16) for i in range(2)]
    qbz = [sb(k, f"qbz{i}", [128, 4, 128], BF16) for i in range(2)]
    kgz = [sb(k, f"kgz{i}", [128, 4, 128], BF16) for i in range(2)]
    Am = sb(k, "Am", [128, 128], BF16)
    Sst = [sb(k, f"Sst{i}", [128, 128], F32) for i in range(2)]
    Sb = [[sb(k, f"Sb{i}_{j}", [128, 128], BF16) for j in range(2)] for i in range(2)]
    o_sb = sb(k, "o_sb", [128, 128], F32)
    o2n = sb(k, "o2n", [128, 128], F32)
    yh = sb(k, "yh", [128, 128], F32)

    for t_ in [ubuf, Cst, Cb, Sst[0], Sst[1], Sb[0][0], Sb[0][1], Sb[1][0], Sb[1][1]] + qz + kz + qbz + kgz:
        k.op("dve", lambda e: e.memset(t_[:], 0.0), writes=[t_.b])
    k.op("dve", lambda e: e.memset(vext[:], 1.0), writes=[vext.b])
    k.op("dve", lambda e: e.memset(resetm[:], 1.0), writes=[resetm.b])
    k.op("dve", lambda e: e.memset(resetm[:].rearrange("p (c l) -> p c l", l=64)[:, :, 0:1], 0.0), writes=[resetm.b])

    def v3(t, l=64):
        return t[:].rearrange("p (c l) -> p c l", l=l)

    def v4(t):
        return t[:].rearrange("p (a b l) -> p a b l", b=2, l=64)

    tcnt = [0]

    def transpose_to(dst_ap, dst_b, src_ap, src_b, eng="act"):
        r = regT[tcnt[0] % 4]
        tcnt[0] += 1
        k.op("pe", lambda e: e.transpose(r.ap(), src_ap, c["ident"][:]), reads=[src_b, c["ident"].b], writes=[r.b])
        if eng == "act":
            k.op("act", lambda e: e.copy(out=dst_ap, in_=r.ap()), reads=[r.b], writes=[dst_b])
        else:
            k.op("dve", lambda e: e.tensor_copy(out=dst_ap, in_=r.ap()), reads=[r.b], writes=[dst_b])

    acnt = [0]

    def next_acc():
        a = acc[acnt[0] % 2]
        acnt[0] += 1
        return a

    def inproj_fm(col0):
        a = next_acc()

        def mm(e):
            for dc in range(16):
                ins = e.matmul(a.ap(), lhsT=win_sb[:, dc, col0:col0 + 128], rhs=xT[:, dc, :], start=(dc == 0), stop=(dc == 15))
            return ins
        k.op("pe", mm, reads=[xT.b], writes=[a.b])
        return a

    def inproj_tm(tci, col0, ncol, out_reg=None, oc0=0):
        a = out_reg if out_reg is not None else next_acc()

        def mm(e):
            for dc in range(16):
                ins = e.matmul(a.ap(a=oc0, b=oc0 + ncol), lhsT=xT[:, dc, tci * 128:(tci + 1) * 128],
                               rhs=win_sb[:, dc, col0:col0 + ncol], start=(dc == 0), stop=(dc == 15))
            return ins
        k.op("pe", mm, reads=[xT.b], writes=[a.b])
        return a

    out_toks = []
    def body(st, t0):
        if stop <= 1:
            raise _Stop()
        norm_supertile(k, c, hT, nw_sb, hTt, xT, ps_ss, rstd, scrsq, t0, TW, ds_h)
        if stop <= 2:
            raise _Stop()
        body2(st, t0)

    def body2(st, t0):
        if st > 0:
            k.op("dve", lambda e: e.tensor_copy(out=ubuf[:, :, 0:3], in_=ubuf[:, :, TW:TW + 3]), reads=[ubuf.b], writes=[ubuf.b])
        for ch in range(2):
            a = inproj_fm(ch * 128)
            k.op("act", lambda e: e.copy(out=ubuf[:, ch, 3:3 + TW], in_=a.ap()), reads=[a.b], writes=[ubuf.b])
        if stop <= 2.2:
            raise _Stop()
        for ch in range(2):
            k.op("dve", lambda e: e.tensor_scalar(out=cacc[:], in0=ubuf[:, ch, 0:TW], scalar1=cw_sb[:, ch, 0:1], scalar2=None,
                                                  op0=ALU.mult), reads=[ubuf.b, cw_sb.b], writes=[cacc.b])
            for j in range(1, 4):
                k.op("dve", lambda e: e.scalar_tensor_tensor(out=cacc[:], in0=ubuf[:, ch, j:j + TW], scalar=cw_sb[:, ch, j:j + 1],
                                                             in1=cacc[:], op0=ALU.mult, op1=ALU.add),
                     reads=[ubuf.b, cw_sb.b, cacc.b], writes=[cacc.b])
            k.op("act", lambda e: e.activation(out=cT[:, ch, :], in_=cacc[:], func=AF.Silu, bias=cb_sb[:, ch:ch + 1], scale=1.0),
                 reads=[cacc.b, cb_sb.b], writes=[cT.b])
        k.op("dve", lambda e: e.tensor_copy(out=cTb[:], in_=cT[:]), reads=[cT.b], writes=[cTb.b])
        if stop <= 2.5:
            raise _Stop()
        for e_ in range(2):
            a = next_acc()

            def mm(e):
                for d in range(2):
                    ins = e.matmul(a.ap(), lhsT=wq_sb[:, d, e_ * 128:(e_ + 1) * 128], rhs=cTb[:, d, :], start=(d == 0), stop=(d == 1))
                return ins
            k.op("pe", mm, reads=[wq_sb.b, cTb.b], writes=[a.b])
            k.op("act", lambda e: e.copy(out=qT[:, e_, :], in_=a.ap()), reads=[a.b], writes=[qT.b])
            k.op("dve", lambda e: e.tensor_copy(out=qTb[:, e_, :], in_=qT[:, e_, :]), reads=[qT.b], writes=[qTb.b])
            a = next_acc()

            def mm2(e):
                for d in range(2):
                    ins = e.matmul(a.ap(), lhsT=wk_sb[:, d, e_ * 128:(e_ + 1) * 128], rhs=cTb[:, d, :], start=(d == 0), stop=(d == 1))
                return ins
            k.op("pe", mm2, reads=[wk_sb.b, cTb.b], writes=[a.b])
            k.op("dve", lambda e: e.tensor_scalar(out=kTb[:, e_, :], in0=a.ap(), scalar1=0.0625, scalar2=None, op0=ALU.mult),
                 reads=[a.b], writes=[kTb.b])
        if stop <= 3:
            raise _Stop()
        for tci in range(4):
            tsl = slice(tci * 128, (tci + 1) * 128)
            a = inproj_tm(tci, 768, 512)
            k.op("dve", lambda e: e.tensor_copy(out=vext[:, tci, 0:256], in_=a.ap(b=256)), reads=[a.b], writes=[vext.b])
            k.op("act", lambda e: e.activation(out=osig[:, tci, :], in_=a.ap(a=256, b=512), func=AF.Sigmoid), reads=[a.b],
                 writes=[osig.b])
            a = inproj_tm(tci, 1280, 512)
            k.op("dve", lambda e: e.tensor_copy(out=hv[:, tci, :, :].rearrange("p a b -> p (a b)"), in_=a.ap(b=256)), reads=[a.b],
                 writes=[hv.b])
            k.op("act", lambda e: e.activation(out=hgs[:, tci, :], in_=a.ap(a=256, b=512), func=AF.Silu), reads=[a.b],
                 writes=[hgs.b])
            inproj_tm(tci, 1792, 2, out_reg=regA, oc0=2 * tci)
            a = next_acc()

            def mm3(e):
                for d in range(2):
                    ins = e.matmul(a.ap(b=256), lhsT=cTb[:, d, tsl], rhs=wk_sb[:, d, :], start=(d == 0), stop=(d == 1))
                return ins
            k.op("pe", mm3, reads=[wk_sb.b, cTb.b], writes=[a.b])
            k.op("act", lambda e: e.mul(out=ktok[:, tci, :], in_=a.ap(b=256), mul=0.0625), reads=[a.b], writes=[ktok.b])
            for d in range(2):
                transpose_to(ctok[:, tci, d * 128:(d + 1) * 128], ctok.b, cT[:, d, tsl], cT.b, eng="dve")
        if stop <= 4:
            raise _Stop()
        gv = regA.ap().rearrange("p (a b) -> p a b", b=2)
        k.op("act", lambda e: e.activation(out=ge1[:], in_=gv[:, :, 1], func=AF.Exp, scale=-1.0, bias=nfb[:, 0:1]),
             reads=[regA.b, nfb.b], writes=[ge1.b])
        k.op("act", lambda e: e.activation(out=ge1[:], in_=ge1[:], func=AF.Ln, scale=1.0, bias=c["one1"][:, 0:1]),
             reads=[ge1.b, c["one1"].b], writes=[ge1.b])
        k.op("dve", lambda e: e.tensor_scalar(out=logf[:], in0=ge1[:], scalar1=-1.0, scalar2=None, op0=ALU.mult), reads=[ge1.b],
             writes=[logf.b])
        k.op("dve", lambda e: e.tensor_scalar(out=ig[:], in0=gv[:, :, 0], scalar1=gb_sb[:, 0:1], scalar2=None, op0=ALU.add),
             reads=[regA.b, gb_sb.b], writes=[ig.b])
        for tci in range(4 if do_ml >= 2 else 0):
            tsl = slice(tci * 128, (tci + 1) * 128)
            k.op("dve", lambda e: e.tensor_scalar(out=lfb[:], in0=c["ones"][:], scalar1=logf[:, tci:tci + 1], scalar2=None,
                                                  op0=ALU.mult), reads=[logf.b, c["ones"].b], writes=[lfb.b])

            def mmb(e):
                e.matmul(regB.ap(b=128), lhsT=lfb[:], rhs=c["causal"][:], start=True, stop=True)
                return e.matmul(regB.ap(a=128, b=129), lhsT=c["causal"][:], rhs=logf[:, tci:tci + 1], start=True, stop=True)
            k.op("pe", mmb, reads=[lfb.b, c["causal"].b, logf.b], writes=[regB.b])
            k.op("dve", lambda e: e.tensor_tensor(out=bias_s[:], in0=ig[:, tci:tci + 1], in1=regB.ap(a=128, b=129), op=ALU.subtract),
                 reads=[ig.b, regB.b], writes=[bias_s.b])
            k.op("act", lambda e: e.activation(out=DT[:], in_=regB.ap(b=128), func=AF.Exp, bias=bias_s[:, 0:1], scale=1.0),
                 reads=[regB.b, bias_s.b], writes=[DT.b])
            k.op("act", lambda e: e.activation(out=Eb[:], in_=regB.ap(b=128), func=AF.Exp), reads=[regB.b], writes=[Eb.b])
            k.op("dve", lambda e: e.tensor_copy(out=small[:, 0:1], in_=regB.ap(a=127, b=128)), reads=[regB.b], writes=[small.b])
            if stop <= 6.1:
                raise _Stop()
            k.op("pool", lambda e: e.tensor_tensor(out=Dm[:], in0=DT[:], in1=c["causal"][:], op=ALU.mult),
                 reads=[DT.b, c["causal"].b], writes=[Dm.b])
            if stop <= 6.2:
                raise _Stop()

            def mms(e):
                for e_ in range(2):
                    ins = e.matmul(regC.ap(), lhsT=kTb[:, e_, tsl], rhs=qTb[:, e_, tsl], start=(e_ == 0), stop=(e_ == 1))
                return ins
            k.op("pe", mms, reads=[kTb.b, qTb.b], writes=[regC.b])
            k.op("dve", lambda e: e.tensor_tensor(out=PT[:], in0=regC.ap(), in1=Dm[:], op=ALU.mult), reads=[regC.b, Dm.b],
                 writes=[PT.b])
            for e_ in range(2):
                k.op("dve", lambda e: e.tensor_tensor(out=qs[:, e_, :], in0=qT[:, e_, tsl], in1=Eb[:], op=ALU.mult),
                     reads=[qT.b, Eb.b], writes=[qs.b])

            def mmnd(e):
                e.matmul(regND.ap(), lhsT=PT[:], rhs=vext[:, tci, :], start=True, stop=False)
                e.matmul(regND.ap(), lhsT=qs[:, 0, :], rhs=Cb[:, 0, :], start=False, stop=False)
                return e.matmul(regND.ap(), lhsT=qs[:, 1, :], rhs=Cb[:, 1, :], start=False, stop=True)
            k.op("pe", mmnd, reads=[PT.b, vext.b, qs.b, Cb.b], writes=[regND.b])
            if debug and st == 0 and tci == 1:
                dump("Cst", Cst, Cst[:].rearrange("p a b -> p (a b)"), 528)
                dump("logf", logf, logf[:], 4)
                dump("ig", ig, ig[:], 4)
                dump("DT", DT, DT[:], 128)
                dump("Eb", Eb, Eb[:], 128)
                dump("ktok1", ktok, ktok[:, 1, :], 256)
                dump("qT", qT, qT[:, 0, 128:256], 128)
            if stop <= 6.3:
                raise _Stop()
            k.op("act", lambda e: e.activation(out=small[:, 1:2], in_=regND.ap(a=256, b=257), func=AF.Abs), reads=[regND.b],
                 writes=[small.b])
            k.op("dve", lambda e: e.tensor_scalar(out=small[:, 1:2], in0=small[:, 1:2], scalar1=1.0, scalar2=None, op0=ALU.max),
                 reads=[small.b], writes=[small.b])
            k.op("dve", lambda e: e.reciprocal(out=small[:, 1:2], in_=small[:, 1:2]), reads=[small.b], writes=[small.b])
            k.op("dve", lambda e: e.tensor_scalar(out=hn[:], in0=regND.ap(b=256), scalar1=small[:, 1:2], scalar2=None, op0=ALU.mult),
                 reads=[regND.b, small.b], writes=[hn.b])
            k.op("act", lambda e: e.activation(out=junk[:], in_=hn[:], func=AF.Square, accum_out=small[:, 2:3]), reads=[hn.b],
                 writes=[junk.b, small.b])
            k.op("act", lambda e: e.activation(out=small[:, 3:4], in_=small[:, 2:3], func=AF.Sqrt, scale=1.0 / 256, bias=c["eps"][:, 0:1]),
                 reads=[small.b, c["eps"].b], writes=[small.b])
            k.op("dve", lambda e: e.reciprocal(out=small[:, 3:4], in_=small[:, 3:4]), reads=[small.b], writes=[small.b])
            k.op("dve", lambda e: e.scalar_tensor_tensor(out=hm[:], in0=hn[:], scalar=small[:, 3:4], in1=mln_sb[:], op0=ALU.mult,
                                                         op1=ALU.mult), reads=[hn.b, small.b, mln_sb.b], writes=[hm.b])
            k.op("pool", lambda e: e.tensor_tensor(out=t1[:], in0=ctok[:, tci, :], in1=skp_sb[:], op=ALU.mult),
                 reads=[ctok.b, skp_sb.b], writes=[t1.b])
            k.op("dve", lambda e: e.tensor_tensor(out=t1[:], in0=t1[:], in1=hm[:], op=ALU.add), reads=[t1.b, hm.b], writes=[t1.b])
            k.op("dve", lambda e: e.tensor_tensor(out=yml[:], in0=t1[:], in1=osig[:, tci, :], op=ALU.mult), reads=[t1.b, osig.b],
                 writes=[yml.b])
            for d in range(2):
                transpose_to(ystage[:, d, tsl], ystage.b, yml[:, d * 128:(d + 1) * 128], yml.b, eng="act")
            if debug and st == 0 and tci == 1:
                dump("hn", hn, hn[:], 256)
                dump("small", small, small[:], 8)
                dump("hm", hm, hm[:], 256)
                dump("yml", yml, yml[:], 256)
            if stop <= 6.4:
                raise _Stop()
            k.op("act", lambda e: e.activation(out=small[:, 4:5], in_=bias_s[:], func=AF.Exp, bias=small[:, 0:1], scale=1.0),
                 reads=[bias_s.b, small.b], writes=[small.b])
            k.op("act", lambda e: e.activation(out=small[:, 5:6], in_=small[:, 0:1], func=AF.Exp), reads=[small.b], writes=[small.b])
            if stop <= 6.5:
                raise _Stop()
            k.op("dve", lambda e: e.tensor_scalar(out=ka[:], in0=ktok[:, tci, :], scalar1=small[:, 4:5], scalar2=None, op0=ALU.mult),
                 reads=[ktok.b, small.b], writes=[ka.b])
            if stop <= 6.6:
                raise _Stop()
            for kc in range(2):
                k.op("pe", lambda e: e.matmul(regU[kc].ap(), lhsT=ka[:, kc * 128:(kc + 1) * 128], rhs=vext[:, tci, :], start=True,
                                              stop=True), reads=[ka.b, vext.b], writes=[regU[kc].b])
                k.op("dve", lambda e: e.scalar_tensor_tensor(out=Cst[:, kc, :], in0=Cst[:, kc, :], scalar=small[:, 5:6],
                                                             in1=regU[kc].ap(), op0=ALU.mult, op1=ALU.add),
                     reads=[Cst.b, small.b, regU[kc].b], writes=[Cst.b])
            if stop <= 6.7 and kc == 1:
                raise _Stop()
            if stop <= 6.8:
                raise _Stop()
            k.op("act", lambda e: e.copy(out=Cb[:], in_=Cst[:]), reads=[Cst.b], writes=[Cb.b])
        for hd in range(2 if do_hg >= 1 else 0):
            az = inproj_fm(512 + hd * 128)
            aq = inproj_fm(256 + hd * 128)
            k.op("act", lambda e: e.activation(out=tA[:], in_=az.ap(), func=AF.Sigmoid, scale=-1.0), reads=[az.b], writes=[tA.b])
            k.op("act", lambda e: e.activation(out=tB[:], in_=az.ap(), func=AF.Exp, scale=-1.0), reads=[az.b], writes=[tB.b])
            k.op("act", lambda e: e.activation(out=sqt[:], in_=aq.ap(), func=AF.Silu), reads=[aq.b], writes=[sqt.b])
            k.op("dve", lambda e: e.tensor_scalar(out=k2[:], in0=tA[:], scalar1=oml[:, hd:hd + 1], scalar2=None, op0=ALU.mult),
                 reads=[tA.b, oml.b], writes=[k2.b])
            k.op("dve", lambda e: e.tensor_scalar(out=tA[:], in0=k2[:], scalar1=HG_MAX_K, scalar2=None, op0=ALU.min), reads=[k2.b],
                 writes=[tA.b])
            k.op("act", lambda e: e.activation(out=lf1[:], in_=tA[:], func=AF.Ln, scale=-1.0, bias=c["one1"][:, 0:1]),
                 reads=[tA.b, c["one1"].b], writes=[lf1.b])
            k.op("act", lambda e: e.activation(out=tB[:], in_=tB[:], func=AF.Ln, scale=1.0, bias=c["one1"][:, 0:1]),
                 reads=[tB.b, c["one1"].b], writes=[tB.b])
            k.op("dve", lambda e: e.scalar_tensor_tensor(out=lgf[:], in0=tB[:], scalar=-1.0, in1=lf1[:], op0=ALU.mult, op1=ALU.max),
                 reads=[tB.b, lf1.b], writes=[lgf.b])
            k.op("dve", lambda e: e.tensor_tensor_scan(out=bt[:], data0=resetm[:], data1=lgf[:], initial=0.0, op0=ALU.mult,
                                                       op1=ALU.add), reads=[resetm.b, lgf.b], writes=[bt.b])
            k.op("dve", lambda e: e.tensor_tensor(out=v3(brel), in0=v3(bt), in1=v3(bt)[:, :, 31:32].to_broadcast([128, 8, 64]),
                                                  op=ALU.subtract), reads=[bt.b], writes=[brel.b])
            k.op("act", lambda e: e.activation(out=tA[:], in_=brel[:], func=AF.Exp), reads=[brel.b], writes=[tA.b])
            k.op("act", lambda e: e.activation(out=tC[:], in_=brel[:], func=AF.Exp, scale=-1.0), reads=[brel.b], writes=[tC.b])
            for par in range(2):
                k.op("dve", lambda e: e.tensor_tensor(out=qz[par][:, :, par * 64:(par + 1) * 64], in0=v4(sqt)[:, :, par, :],
                                                      in1=v4(tA)[:, :, par, :], op=ALU.mult), reads=[sqt.b, tA.b], writes=[qz[par].b])
                k.op("dve", lambda e: e.tensor_tensor(out=kz[par][:, :, par * 64:(par + 1) * 64], in0=v4(k2)[:, :, par, :],
                                                      in1=v4(tC)[:, :, par, :], op=ALU.mult), reads=[k2.b, tC.b], writes=[kz[par].b])
            k.op("act", lambda e: e.activation(out=tA[:], in_=bt[:], func=AF.Exp), reads=[bt.b], writes=[tA.b])
            for par in range(2):
                k.op("dve", lambda e: e.tensor_tensor(out=qbz[par][:, :, par * 64:(par + 1) * 64], in0=v4(sqt)[:, :, par, :],
                                                      in1=v4(tA)[:, :, par, :], op=ALU.mult), reads=[sqt.b, tA.b], writes=[qbz[par].b])
            k.op("dve", lambda e: e.tensor_tensor(out=v3(tC), in0=v3(bt)[:, :, 63:64].to_broadcast([128, 8, 64]), in1=v3(bt),
                                                  op=ALU.subtract), reads=[bt.b], writes=[tC.b])
            k.op("act", lambda e: e.activation(out=tC[:], in_=tC[:], func=AF.Exp), reads=[tC.b], writes=[tC.b])
            k.op("dve", lambda e: e.tensor_tensor(out=kgT[:], in0=k2[:], in1=tC[:], op=ALU.mult), reads=[k2.b, tC.b], writes=[kgT.b])
            k.op("act", lambda e: e.activation(out=eg8[:], in_=v3(bt)[:, :, 63], func=AF.Exp), reads=[bt.b], writes=[eg8.b])
            for tl in range(4):
                r = regT[tcnt[0] % 4]
                tcnt[0] += 1
                k.op("pe", lambda e: e.transpose(r.ap(), kgT[:, tl * 128:(tl + 1) * 128], c["ident"][:]), reads=[kgT.b, c["ident"].b],
                     writes=[r.b])
                k.op("act", lambda e: e.copy(out=kgz[0][0:64, tl, :], in_=r.ap(rows=slice(0, 64))), reads=[r.b], writes=[kgz[0].b])
                k.op("act", lambda e: e.copy(out=kgz[1][64:128, tl, :], in_=r.ap(rows=slice(64, 128))), reads=[r.b], writes=[kgz[1].b])
            S = Sst[hd]
            for tl in range(4 if do_hg >= 2 else 0):
                def mma(e):
                    e.matmul(r7A.ap(), lhsT=kz[0][:, tl, :], rhs=qz[0][:, tl, :], start=True, stop=False)
                    return e.matmul(r7A.ap(), lhsT=kz[1][:, tl, :], rhs=qz[1][:, tl, :], start=False, stop=True)
                k.op("pe", mma, reads=[kz[0].b, kz[1].b, qz[0].b, qz[1].b], writes=[r7A.b])
                k.op("dve", lambda e: e.tensor_tensor(out=Am[:], in0=r7A.ap(), in1=c["causal"][:], op=ALU.mult),
                     reads=[r7A.b, c["causal"].b], writes=[Am.b])
                k.op("pe", lambda e: e.matmul(r7C.ap(), lhsT=kgz[0][:, tl, :], rhs=hv[:, tl, hd, :], start=True, stop=True),
                     reads=[kgz[0].b, hv.b], writes=[r7C.b])
                k.op("pe", lambda e: e.matmul(r7D.ap(), lhsT=kgz[1][:, tl, :], rhs=hv[:, tl, hd, :], start=True, stop=True),
                     reads=[kgz[1].b, hv.b], writes=[r7D.b])
                k.op("dve", lambda e: e.scalar_tensor_tensor(out=S[:], in0=S[:], scalar=eg8[:, 2 * tl:2 * tl + 1], in1=r7C.ap(),
                                                             op0=ALU.mult, op1=ALU.add), reads=[S.b, eg8.b, r7C.b], writes=[S.b])
                k.op("act", lambda e: e.copy(out=Sb[hd][1][:], in_=S[:]), reads=[S.b], writes=[Sb[hd][1].b])

                def mmo(e):
                    e.matmul(r7B.ap(), lhsT=Am[:], rhs=hv[:, tl, hd, :], start=True, stop=False)
                    e.matmul(r7B.ap(), lhsT=qbz[0][:, tl, :], rhs=Sb[hd][0][:], start=False, stop=False)
                    return e.matmul(r7B.ap(), lhsT=qbz[1][:, tl, :], rhs=Sb[hd][1][:], start=False, stop=True)
                k.op("pe", mmo, reads=[Am.b, hv.b, qbz[0].b, qbz[1].b, Sb[hd][0].b, Sb[hd][1].b], writes=[r7B.b])
                k.op("dve", lambda e: e.scalar_tensor_tensor(out=S[:], in0=S[:], scalar=eg8[:, 2 * tl + 1:2 * tl + 2], in1=r7D.ap(),
                                                             op0=ALU.mult, op1=ALU.add), reads=[S.b, eg8.b, r7D.b], writes=[S.b])
                k.op("act", lambda e: e.copy(out=Sb[hd][0][:], in_=S[:]), reads=[S.b], writes=[Sb[hd][0].b])
                k.op("act", lambda e: e.copy(out=o_sb[:], in_=r7B.ap()), reads=[r7B.b], writes=[o_sb.b])
                k.op("act", lambda e: e.activation(out=junk[:, 0:128], in_=o_sb[:], func=AF.Square, accum_out=small[:, 6:7]),
                     reads=[o_sb.b], writes=[junk.b, small.b])
                k.op("act", lambda e: e.activation(out=small[:, 7:8], in_=small[:, 6:7], func=AF.Sqrt, scale=1.0 / 128,
                                                   bias=c["eps"][:, 0:1]), reads=[small.b, c["eps"].b], writes=[small.b])
                k.op("dve", lambda e: e.reciprocal(out=small[:, 7:8], in_=small[:, 7:8]), reads=[small.b], writes=[small.b])
                k.op("dve", lambda e: e.scalar_tensor_tensor(out=o2n[:], in0=o_sb[:], scalar=small[:, 7:8], in1=hgn_sb[:, hd, :],
                                                             op0=ALU.mult, op1=ALU.mult), reads=[o_sb.b, small.b, hgn_sb.b], writes=[o2n.b])
                k.op("pool", lambda e: e.tensor_tensor(out=yh[:], in0=o2n[:], in1=hgs[:, tl, hd * 128:(hd + 1) * 128], op=ALU.mult),
                     reads=[o2n.b, hgs.b], writes=[yh.b])
                transpose_to(ystage[:, 2 + hd, tl * 128:(tl + 1) * 128], ystage.b, yh[:], yh.b, eng="act")
    for st in range(NST):
        t0 = st * TW
        try:
            body(st, t0)
        except _Stop:
            pass
        tk = k.op("sp", lambda e: e.dma_start(out=yT.rearrange("c p t -> p c t")[:, :, t0:t0 + TW], in_=ystage[:]), reads=[ystage.b],
                  dsem=ds_y)
        out_toks.append(tk)
    k.finish(out_toks)
    return nc


def fm(a):
    T, C = a.shape
    return np.ascontiguousarray(a.T.reshape(C // 128, 128, T))


def unfm(aT):
    n, p, T = aT.shape
    return np.ascontiguousarray(aT.reshape(n * p, T).T)


def pvec(w):
    return np.ascontiguousarray(w.reshape(-1, 128).T)


def rep(w):
    return np.ascontiguousarray(np.broadcast_to(w[None], (128,) + w.shape))


def ab_core_inputs(I, layer, hgp, hT_b):
    j = layer // 2
    h = hgp
    W = I["ab_w_in"][j]
    o_u, o_v, o_o, o_i, o_f, o_hq, o_hf, o_hi, o_hg = 0, 1024, 2048, 3072, 3076, 3080, 4104, 5128, 6152
    sl = lambda o, n, i: W[:, o + i * n:o + (i + 1) * n]
    win = np.concatenate([sl(o_u, 256, h), sl(o_hq, 256, h), sl(o_hf, 256, h), sl(o_v, 256, h), sl(o_o, 256, h),
                          sl(o_hi, 256, h), sl(o_hg, 256, h), W[:, o_i + h:o_i + h + 1], W[:, o_f + h:o_f + h + 1]], axis=1)
    cwf = I["ml_conv_w"][j][:, h * 256:(h + 1) * 256]
    cw = np.ascontiguousarray(cwf.reshape(4, 2, 128).transpose(2, 1, 0))
    cb = np.ascontiguousarray(I["ml_conv_b"][j][h * 256:(h + 1) * 256].reshape(2, 128).T)
    lbl = np.ascontiguousarray(I["hg_lb_logits"][:, h * 256:(h + 1) * 256].reshape(2, 2, 128).transpose(2, 1, 0))
    return {
        "hT": hT_b, "nw": pvec(I["mix_norm"][layer]), "win": np.ascontiguousarray(win), "cw": cw, "cb": cb,
        "wq": np.ascontiguousarray(I["ml_wq"][j][h]), "wk": np.ascontiguousarray(I["ml_wk"][j][h]),
        "gb": rep(np.array([I["ml_i_bias"][j][h], I["ml_f_bias"][j][h]], np.float32)),
        "mln": rep(I["ml_out_norm"][j][h]), "skp": rep(I["ml_skip"][j][h * 256:(h + 1) * 256]),
        "lbl": lbl, "hgn": rep(I["hg_out_norm"][j][2 * h:2 * h + 2]),
    }


import math
from contextlib import ExitStack

NSA_BIG = 200.0
ROPE_INVF = np.power(np.float32(10000.0), -np.arange(64, dtype=np.float32) / 64).astype(np.float32)


def kb_barrier(k):
    toks = [Tok(k.sem[e], k.cnt[e], "E" + e, e) for e in k.engs if k.cnt[e] > 0]
    toks += [Tok(d.h, d.val, d.key, None) for d in k._all_dsems if d.val > 0]
    for e in k.engs:
        for t in toks:
            if t.eng != e:
                k._wait(e, t)


def build_nsa(T=8192, stop=99, debug=False):
    TW = 256
    NST = T // TW
    NT = T // 128
    NCB = (T - 32) // 16 + 1
    NCT = (NCB + 127) // 128
    NCOL = 1292
    scale = 128 ** -0.5
    nc = bass.Bass("TRN2", target_bir_lowering=False)
    k = KB(nc)
    k._all_dsems = []
    _ds = k.dsem

    def dsem2(name=None):
        d = _ds(name)
        k._all_dsems.append(d)
        return d
    k.dsem = dsem2

    def dram(name, shape, dt=F32, kind="ExternalInput"):
        return nc.dram_tensor(name, list(shape), dt, kind=kind).ap()
    hT = dram("hT", [16, 128, T])
    nw = dram("nw", [128, 16])
    win = dram("win", [2048, NCOL])
    qnw = dram("qnw", [128, 512])
    knw = dram("knw", [128, 3, 128])
    posT = dram("posT", [128, 2, 32])
    w1 = dram("w1", [2, 4096, 256])
    b1 = dram("b1", [128, 2, 2])
    w2 = dram("w2", [2, 256, 128])
    gbias = dram("gbias", [128, 12])
    yT = dram("yT", [4, 128, T], BF16, kind="ExternalOutput")
    dbg = dram("dbg", [128, 8192], F32, kind="ExternalOutput") if debug else None
    dbg_pos = [0]
    dbg_map = {}
    nc._dbg_map = dbg_map

    def dump(name, tt, ap, n):
        if not debug or name in dbg_map:
            return
        c0 = dbg_pos[0]
        dbg_pos[0] += n
        dbg_map[name] = (c0, n)
        k.op("sp", lambda e: e.dma_start(out=dbg[:, c0:c0 + n], in_=ap), reads=[tt.b], dsem=k.dsem())

    c = make_consts(k)
    win_v = win.rearrange("(dc p) f -> p dc f", p=128)
    ps = [nc.alloc_psum_tensor(f"ps{i}", [128, 512], F32) for i in range(8)]
    accS = [PReg(k, ps[i], 0, 512, f"accS{i}") for i in range(2)]
    _oset = [PReg(k, ps[2 + h], 0, 130, f"o_{h}") for h in range(4)]
    oreg = [_oset, _oset]
    impT = PReg(k, ps[6], 0, 512, "impT")
    ps_ss = impT
    regT = [PReg(k, ps[7], i * 128, (i + 1) * 128, f"regT{i}") for i in range(4)]
    acnt = [0]
    tcnt = [0]

    def next_acc():
        a = accS[acnt[0] % 2]
        acnt[0] += 1
        return a

    def next_regT():
        r = regT[tcnt[0] % 4]
        tcnt[0] += 1
        return r

    def ld(name, shape, src, dt=F32, eng="sp"):
        t = sb(k, name, shape, dt)
        k.op(eng, lambda e: e.dma_start(out=t[:], in_=src), writes=[t.b], dsem=k.dsem())
        return t

    nw_sb = ld("nw_sb", [128, 16], nw)
    qnw_sb = ld("qnw_sb", [128, 512], qnw)
    knw_sb = ld("knw_sb", [128, 3, 128], knw)
    b1_sb = ld("b1_sb", [128, 2, 2], b1)
    gb_sb = ld("gb_sb", [128, 12], gbias)
    hTt = sb(k, "hTt", [128, 16, TW], F32)
    ds_h = k.dsem()
    xT = sb(k, "xT", [128, 16, TW], BF16)
    xT2 = sb(k, "xT2", [128, 16, TW], BF16)
    rstd = sb(k, "rstd", [128, TW], F32)
    scrsq = [sb(k, f"scrsq{i}", [128, TW], F32) for i in range(2)]
    kcmpT = sb(k, "kcmpT", [128, NCT * 128], BF16)
    vcmp = sb(k, "vcmp", [128, NCT, 130], BF16)
    cover = sb(k, "cover", [128, NCT, 128], BF16)
    identb = sb(k, "identb", [128, 128], BF16)
    k.op("dve", lambda e: e.tensor_copy(out=identb[:], in_=c["ident"][:]), reads=[c["ident"].b], writes=[identb.b])
    k.op("dve", lambda e: e.memset(kcmpT[:], 0.0), writes=[kcmpT.b])
    k.op("dve", lambda e: e.memset(vcmp[:], 0.0), writes=[vcmp.b])
    k.op("dve", lambda e: e.memset(vcmp[:, :, 128:129], 1.0), writes=[vcmp.b])
    invf = sb(k, "invf", [128, 64], F32)
    for i in range(64):
        k.op("dve", lambda e: e.memset(invf[:, i:i + 1], float(ROPE_INVF[i])), writes=[invf.b])
    pidx_i = sb(k, "pidx_i", [128, 1], I32)
    k.op("pool", lambda e: e.iota(pidx_i[:], pattern=[[0, 1]], base=0, channel_multiplier=1), writes=[pidx_i.b])
    pidx = sb(k, "pidx", [128, 1], F32)
    k.op("dve", lambda e: e.tensor_copy(out=pidx[:], in_=pidx_i[:]), reads=[pidx_i.b], writes=[pidx.b])
    pcol = sb(k, "pcol", [128, 1], F32)
    ang = sb(k, "ang", [128, 64], F32)
    rr = sb(k, "rr", [128, 64], F32)
    rf = sb(k, "rf", [128, 64], F32)
    ri = sb(k, "ri", [128, 64], I32)
    cos_t = sb(k, "cos_t", [128, 64], F32)
    sin_t = sb(k, "sin_t", [128, 64], F32)
    rtmp = sb(k, "rtmp", [128, 4, 64], F32)
    TWO_PI = 2 * math.pi

    def rope_tables(mult, add):
        k.op("dve", lambda e: e.tensor_scalar(out=pcol[:], in0=pidx[:], scalar1=float(mult), scalar2=float(add), op0=ALU.mult,
                                              op1=ALU.add), reads=[pidx.b], writes=[pcol.b])
        k.op("dve", lambda e: e.tensor_scalar(out=ang[:], in0=invf[:], scalar1=pcol[:, 0:1], scalar2=None, op0=ALU.mult),
             reads=[invf.b, pcol.b], writes=[ang.b])
        for (off, dst) in ((0.0, sin_t), (math.pi / 2, cos_t)):
            k.op("dve", lambda e: e.tensor_scalar(out=rr[:], in0=ang[:], scalar1=off, scalar2=None, op0=ALU.add), reads=[ang.b],
                 writes=[rr.b])
            k.op("dve", lambda e: e.tensor_scalar(out=rf[:], in0=rr[:], scalar1=1.0 / TWO_PI, scalar2=None, op0=ALU.mult),
                 reads=[rr.b], writes=[rf.b])
            k.op("dve", lambda e: e.tensor_copy(out=ri[:], in_=rf[:]), reads=[rf.b], writes=[ri.b])
            k.op("dve", lambda e: e.tensor_copy(out=rf[:], in_=ri[:]), reads=[ri.b], writes=[rf.b])
            k.op("dve", lambda e: e.scalar_tensor_tensor(out=rr[:], in0=rf[:], scalar=-TWO_PI, in1=rr[:], op0=ALU.mult, op1=ALU.add),
                 reads=[rf.b, rr.b], writes=[rr.b])
            k.op("dve", lambda e: e.tensor_scalar(out=rf[:], in0=rr[:], scalar1=math.pi, scalar2=None, op0=ALU.is_gt), reads=[rr.b],
                 writes=[rf.b])
            k.op("dve", lambda e: e.scalar_tensor_tensor(out=rr[:], in0=rf[:], scalar=-TWO_PI, in1=rr[:], op0=ALU.mult, op1=ALU.add),
                 reads=[rf.b, rr.b], writes=[rr.b])
            k.op("act", lambda e: e.activation(out=dst[:], in_=rr[:], func=AF.Sin), reads=[rr.b], writes=[dst.b])

    def apply_rope(dst, src, H):
        cb = cos_t[:].rearrange("p (o f) -> p o f", o=1).to_broadcast([128, H, 64])
        sbb = sin_t[:].rearrange("p (o f) -> p o f", o=1).to_broadcast([128, H, 64])
        x1, x2 = src[:, 0:H, 0:64], src[:, 0:H, 64:128]
        tm = rtmp[:, 0:H, :]
        k.op("dve", lambda e: e.tensor_tensor(out=tm, in0=x2, in1=sbb, op=ALU.mult), reads=[src.b, sin_t.b], writes=[rtmp.b])
        k.op("dve", lambda e: e.tensor_tensor(out=dst[:, 0:H, 0:64], in0=x1, in1=cb, op=ALU.mult), reads=[src.b, cos_t.b], writes=[dst.b])
        k.op("dve", lambda e: e.tensor_tensor(out=dst[:, 0:H, 0:64], in0=dst[:, 0:H, 0:64], in1=tm, op=ALU.subtract),
             reads=[dst.b, rtmp.b], writes=[dst.b])
        k.op("dve", lambda e: e.tensor_tensor(out=tm, in0=x1, in1=sbb, op=ALU.mult), reads=[src.b, sin_t.b], writes=[rtmp.b])
        k.op("dve", lambda e: e.tensor_tensor(out=dst[:, 0:H, 64:128], in0=x2, in1=cb, op=ALU.mult), reads=[src.b, cos_t.b], writes=[dst.b])
        k.op("dve", lambda e: e.tensor_tensor(out=dst[:, 0:H, 64:128], in0=dst[:, 0:H, 64:128], in1=tm, op=ALU.add),
             reads=[dst.b, rtmp.b], writes=[dst.b])

    small = sb(k, "small", [128, 16], F32)
    junk = sb(k, "junk", [128, 128], F32)
    kn = sb(k, "kn", [128, 4, 128], F32)
    kr = sb(k, "kr", [128, 4, 128], F32)

    def rms_heads(src_reg, col0, H, wfn, post_scale=1.0, nrows=128):
        rows = slice(0, nrows)
        for h in range(H):
            k.op("act", lambda e: e.activation(out=junk[rows, :], in_=src_reg.ap(rows, col0[h], col0[h] + 128), func=AF.Square,
                                               accum_out=small[rows, h:h + 1]), reads=[src_reg.b], writes=[junk.b, small.b])
        k.op("act", lambda e: e.activation(out=small[rows, 4:4 + H], in_=small[rows, 0:H], func=AF.Sqrt, scale=1.0 / 128,
                                           bias=c["eps"][rows, 0:1]), reads=[small.b, c["eps"].b], writes=[small.b])
        k.op("dve", lambda e: e.reciprocal(out=small[rows, 4:4 + H], in_=small[rows, 4:4 + H]), reads=[small.b], writes=[small.b])
        if post_scale != 1.0:
            k.op("dve", lambda e: e.tensor_scalar(out=small[rows, 4:4 + H], in0=small[rows, 4:4 + H], scalar1=post_scale, scalar2=None,
                                                  op0=ALU.mult), reads=[small.b], writes=[small.b])
        for h in range(H):
            wt, wap = wfn(h)
            k.op("dve", lambda e: e.scalar_tensor_tensor(out=kn[rows, h, :], in0=src_reg.ap(rows, col0[h], col0[h] + 128),
                                                         scalar=small[rows, 4 + h:5 + h], in1=wap, op0=ALU.mult, op1=ALU.mult),
                 reads=[src_reg.b, small.b, wt.b], writes=[kn.b])

    out_toks = []
    es = ExitStack()

    def sbs(name, shape, dt):
        return TT(k, es.enter_context(nc.sbuf_tensor(name, list(shape), dt)), name)

    win_a = sbs("win_a", [128, 16, 256], BF16)
    k.op("pool", lambda e: e.dma_start(out=win_a[:], in_=win_v[:, :, 0:256]), writes=[win_a.b], dsem=k.dsem())
    kvT = [sbs(f"kvT{i}", [128, T], BF16) for i in range(2)]
    w1_sb = sbs("w1_sb", [128, 32, 256], BF16)
    w2_sb = sbs("w2_sb", [128, 2, 2, 128], BF16)
    posT_sb = sbs("posT_sb", [128, 2, 32], BF16)
    hsil = sbs("hsil", [128, 2, 128], BF16)
    biasv = sbs("biasv", [128, 2], F32)
    k.op("pool", lambda e: e.dma_start(out=posT_sb[:], in_=posT), writes=[posT_sb.b], dsem=k.dsem())
    for kv in range(2):
        k.op("pool", lambda e: e.dma_start(out=w2_sb[:, kv, :, :], in_=w2[kv].rearrange("(c p) e -> p c e", p=128)), writes=[w2_sb.b],
             dsem=k.dsem())
    xT_dram = nc.dram_tensor("xT_scratch", [NST, 128, 16 * TW], BF16).ap()
    xd_b = k.bufs(NST, "xd")
    ds_xs = k.dsem()
    xTs = [xT, xT2]
    ds_xl = [k.dsem(), k.dsem()]

    def load_xT(st):
        xt = xTs[st % 2]
        k.op("sp", lambda e: e.dma_start(out=xt[:].rearrange("p a b -> p (a b)"), in_=xT_dram[st]), reads=[xd_b[st]], writes=[xt.b],
             dsem=ds_xl[st % 2])

    for st in range(NST):
        t0 = st * TW
        norm_supertile(k, c, hT, nw_sb, hTt, xT, ps_ss, rstd, scrsq, t0, TW, ds_h)
        k.op("sp", lambda e: e.dma_start(out=xT_dram[st], in_=xT[:].rearrange("p a b -> p (a b)")), reads=[xT.b], writes=[xd_b[st]],
             dsem=ds_xs)
        for kv in range(2):
            a = next_acc()

            def mm(e):
                for dc in range(16):
                    ins = e.matmul(a.ap(b=TW), lhsT=win_a[:, dc, kv * 128:(kv + 1) * 128], rhs=xT[:, dc, :], start=(dc == 0), stop=(dc == 15))
                return ins
            k.op("pe", mm, reads=[win_a.b, xT.b], writes=[a.b])
            k.op("act", lambda e: e.copy(out=kvT[kv][:, t0:t0 + TW], in_=a.ap(b=TW)), reads=[a.b], writes=[kvT[kv].b])
    ds_w1 = k.dsem()
    for kv in range(2):
        k.op("pool", lambda e: e.dma_start(out=w1_sb[:], in_=w1[kv].rearrange("(l p) h -> p l h", p=128)), writes=[w1_sb.b], dsem=ds_w1)
        for hc in range(2):
            r = next_regT()

            def mmp(e):
                for l in range(32):
                    ins = e.matmul(r.ap(b=1), lhsT=w1_sb[:, l, hc * 128:(hc + 1) * 128], rhs=posT_sb[:, kv, l:l + 1], start=(l == 0),
                                   stop=(l == 31))
                return ins
            k.op("pe", mmp, reads=[w1_sb.b, posT_sb.b], writes=[r.b])
            k.op("dve", lambda e: e.tensor_tensor(out=biasv[:, hc:hc + 1], in0=r.ap(b=1), in1=b1_sb[:, kv, hc:hc + 1], op=ALU.add),
                 reads=[r.b, b1_sb.b], writes=[biasv.b])
        for nti in range(NCT):
            nn = min(128, NCB - 128 * nti)
            for hc in range(2):
                r = next_regT()

                def mmh(e):
                    for l in range(32):
                        s0 = 16 * 128 * nti + l
                        ins = e.matmul(r.ap(b=nn), lhsT=w1_sb[:, l, hc * 128:(hc + 1) * 128], rhs=kvT[kv][:, s0:s0 + 16 * (nn - 1) + 1:16],
                                       start=(l == 0), stop=(l == 31))
                    return ins
                k.op("pe", mmh, reads=[w1_sb.b, kvT[kv].b], writes=[r.b])
                k.op("act", lambda e: e.activation(out=hsil[:, hc, 0:nn], in_=r.ap(b=nn), func=AF.Silu, bias=biasv[:, hc:hc + 1], scale=1.0),
                     reads=[r.b, biasv.b], writes=[hsil.b])
            r = next_regT()

            def mmo(e):
                for hc in range(2):
                    ins = e.matmul(r.ap(rows=slice(0, nn)), lhsT=hsil[:, hc, 0:nn], rhs=w2_sb[:, kv, hc, :], start=(hc == 0), stop=(hc == 1))
                return ins
            k.op("pe", mmo, reads=[hsil.b, w2_sb.b], writes=[r.b])
            if kv == 0:
                rms_heads(r, [0], 1, lambda h: (knw_sb, knw_sb[0:nn, 0, :]), nrows=nn)
                rope_tables(16.0, 16.0 * 128 * nti + 31.0)
                apply_rope(kr, kn, 1)
                r2 = next_regT()
                k.op("pe", lambda e: e.transpose(r2.ap(b=nn), kr[0:nn, 0, :], c["ident"][0:nn, 0:nn]), reads=[kr.b, c["ident"].b],
                     writes=[r2.b])
                k.op("act", lambda e: e.copy(out=kcmpT[:, nti * 128:nti * 128 + nn], in_=r2.ap(b=nn)), reads=[r2.b], writes=[kcmpT.b])
            else:
                k.op("act", lambda e: e.copy(out=vcmp[0:nn, nti, 0:128], in_=r.ap(rows=slice(0, nn))), reads=[r.b], writes=[vcmp.b])
    if debug:
        dump("kcmpT", kcmpT, kcmpT[:, 0:128], 64) if False else None
    kb_barrier(k)
    es.close()
    es = ExitStack()
    if stop <= 1:
        tk = k.op("sp", lambda e: e.dma_start(out=yT[0, :, 0:NCT * 128], in_=kcmpT[:]), reads=[kcmpT.b], dsem=k.dsem())
        tk2 = k.op("sp", lambda e: e.dma_start(out=yT[1, :, 0:NCT * 130], in_=vcmp[:].rearrange("p a b -> p (a b)")), reads=[vcmp.b],
                   dsem=k.dsem())
        k.finish([tk, tk2])
        return nc

    ksT = sbs("ksT", [128, T], BF16)
    kwT = sbs("kwT", [128, T], BF16)
    vs_e = sbs("vs_e", [128, NT, 130], BF16)
    vw_e = sbs("vw_e", [128, NT, 130], BF16)
    k.op("dve", lambda e: e.memset(vs_e[:, :, 128:130], 1.0), writes=[vs_e.b])
    k.op("dve", lambda e: e.memset(vw_e[:, :, 128:130], 1.0), writes=[vw_e.b])
    esB = ExitStack()
    win_b = TT(k, esB.enter_context(nc.sbuf_tensor("win_b", [128, 16, 512], BF16)), "win_b")
    k.op("pool", lambda e: e.dma_start(out=win_b[:], in_=win_v[:, :, 256:768]), writes=[win_b.b], dsem=k.dsem())
    load_xT(0)
    for st in range(NST):
        t0 = st * TW
        if st + 1 < NST:
            load_xT(st + 1)
        xc = xTs[st % 2]
        for tci in range(TW // 128):
            ti = st * (TW // 128) + tci
            a = next_acc()

            def mm(e):
                for dc in range(16):
                    ins = e.matmul(a.ap(), lhsT=xc[:, dc, tci * 128:(tci + 1) * 128], rhs=win_b[:, dc, :], start=(dc == 0), stop=(dc == 15))
                return ins
            k.op("pe", mm, reads=[win_b.b, xc.b], writes=[a.b])
            k.op("act", lambda e: e.copy(out=vs_e[:, ti, 0:128], in_=a.ap(a=128, b=256)), reads=[a.b], writes=[vs_e.b])
            k.op("act", lambda e: e.copy(out=vw_e[:, ti, 0:128], in_=a.ap(a=384, b=512)), reads=[a.b], writes=[vw_e.b])
            rms_heads(a, [0, 256], 2, lambda h: (knw_sb, knw_sb[:, 1 + h, :]))
            rope_tables(1.0, float(ti * 128))
            apply_rope(kr, kn, 2)
            for h, dstT in ((0, ksT), (1, kwT)):
                r2 = next_regT()
                k.op("pe", lambda e: e.transpose(r2.ap(), kr[:, h, :], c["ident"][:]), reads=[kr.b, c["ident"].b], writes=[r2.b])
                k.op("act", lambda e: e.copy(out=dstT[:, ti * 128:(ti + 1) * 128], in_=r2.ap()), reads=[r2.b], writes=[dstT.b])
    kb_barrier(k)
    esB.close()
    if stop <= 2:
        tk = k.op("sp", lambda e: e.dma_start(out=yT[0, :, :], in_=ksT[:]), reads=[ksT.b], dsem=k.dsem())
        tk2 = k.op("sp", lambda e: e.dma_start(out=yT[1, :, :], in_=kwT[:]), reads=[kwT.b], dsem=k.dsem())
        tk3 = k.op("sp", lambda e: e.dma_start(out=yT[2, :, 0:NT * 128].rearrange("p (a b) -> p a b", b=128), in_=vs_e[:, :, 0:128]),
                   reads=[vs_e.b], dsem=k.dsem())
        k.finish([tk, tk2, tk3])
        return nc

    win_q = sbs("win_q", [128, 16, 524], BF16)
    k.op("pool", lambda e: e.dma_start(out=win_q[:], in_=win_v[:, :, 768:1292]), writes=[win_q.b], dsem=k.dsem())
    Esel = sbs("Esel", [128, T], BF16)
    k.op("dve", lambda e: e.memset(Esel[:], 1.0), writes=[Esel.b])
    k.op("pool", lambda e: e.affine_select(out=Esel[:], in_=Esel[:], pattern=[[1, T]], compare_op=ALU.is_ge, fill=0.0, base=0,
                                           channel_multiplier=-64), reads=[Esel.b], writes=[Esel.b])
    k.op("pool", lambda e: e.affine_select(out=Esel[:], in_=Esel[:], pattern=[[-1, T]], compare_op=ALU.is_ge, fill=0.0, base=63,
                                           channel_multiplier=64), reads=[Esel.b], writes=[Esel.b])
    f32a = sbs("f32a", [128, 512], F32)
    f32b = sbs("f32b", [128, 512], F32)
    ones4 = sbs("ones4", [128, 512], F32)
    zeros4 = sbs("zeros4", [128, 512], F32)
    k.op("dve", lambda e: e.memset(ones4[:], 1.0), writes=[ones4.b])
    k.op("dve", lambda e: e.memset(zeros4[:], 0.0), writes=[zeros4.b])
    for nti in range(NCT):
        k.op("pool", lambda e: e.affine_select(out=f32a[:, 0:128], in_=ones4[:, 0:128], pattern=[[64, 128]], compare_op=ALU.is_gt, fill=0.0,
                                               base=64 - 2048 * nti, channel_multiplier=-16), reads=[ones4.b], writes=[f32a.b])
        k.op("pool", lambda e: e.affine_select(out=f32a[:, 0:128], in_=f32a[:, 0:128], pattern=[[-64, 128]], compare_op=ALU.is_gt, fill=0.0,
                                               base=2048 * nti + 32, channel_multiplier=16), reads=[f32a.b], writes=[f32a.b])
        k.op("dve", lambda e: e.tensor_copy(out=cover[:, nti, :], in_=f32a[:, 0:128]), reads=[f32a.b], writes=[cover.b])
    cneg = sbs("cneg", [128, 512], BF16)
    wneg = sbs("wneg", [128, 512], BF16)
    k.op("pool", lambda e: e.affine_select(out=f32a[:].rearrange("p (h j) -> p h j", h=4), in_=zeros4[:].rearrange("p (h j) -> p h j", h=4),
                                           pattern=[[0, 4], [1, 128]], compare_op=ALU.is_ge, fill=-NSA_BIG, base=0, channel_multiplier=-1),
         reads=[zeros4.b], writes=[f32a.b])
    k.op("dve", lambda e: e.tensor_copy(out=cneg[:], in_=f32a[:]), reads=[f32a.b], writes=[cneg.b])
    k.op("pool", lambda e: e.affine_select(out=f32a[:].rearrange("p (h j) -> p h j", h=4), in_=zeros4[:].rearrange("p (h j) -> p h j", h=4),
                                           pattern=[[0, 4], [-1, 128]], compare_op=ALU.is_ge, fill=-NSA_BIG, base=-1, channel_multiplier=1),
         reads=[zeros4.b], writes=[f32a.b])
    k.op("dve", lambda e: e.tensor_copy(out=wneg[:], in_=f32a[:]), reads=[f32a.b], writes=[wneg.b])
    c1e4 = sbs("c1e4", [128, 128], F32)
    k.op("dve", lambda e: e.memset(c1e4[:], 1e4), writes=[c1e4.b])

    gsb = sbs("gsb", [128, 12], F32)
    qn4 = sbs("qn4", [128, 4, 128], F32)
    qr4 = sbs("qr4", [128, 4, 128], F32)
    qT = sbs("qT", [128, 512], BF16)
    Ef = sbs("Ef", [128, 512], F32)
    m01 = sbs("m01", [128, 512], F32)
    Pt = [sbs(f"Pt{i}", [128, 512], BF16) for i in range(2)]
    pcnt = [0]
    impS = sbs("impS", [128, 512], F32)
    imp = sbs("imp", [128, 128], F32)
    bon = sbs("bon", [128, 128], F32)
    impf = sbs("impf", [128, 128], F32)
    val01 = sbs("val01", [128, 128], F32)
    wk_ = sbs("wk_", [128, 128], F32)
    m8 = sbs("m8", [128, 8], F32)
    selm = sbs("selm", [128, 128], F32)
    nmT = sbs("nmT", [128, 512], BF16)
    zt = sbs("zt", [128, 3, 4], F32)
    wgt = sbs("wgt", [128, 4], F32)
    oacc = sbs("oacc", [128, 4, 128], F32)
    ystage = sbs("ystage", [128, 4, TW], BF16)
    ds_y = k.dsem()

    def exp_to_P(a, mask01=None):
        p = Pt[pcnt[0] % 2]
        pcnt[0] += 1
        if mask01 is None:
            k.op("act", lambda e: e.activation(out=p[:], in_=a.ap(), func=AF.Exp), reads=[a.b], writes=[p.b])
        else:
            k.op("act", lambda e: e.activation(out=Ef[:], in_=a.ap(), func=AF.Exp), reads=[a.b], writes=[Ef.b])
            k.op("dve", lambda e: e.tensor_tensor(out=p[:], in0=Ef[:], in1=mask01[:], op=ALU.mult), reads=[Ef.b, mask01.b], writes=[p.b])
        return p

    def pv(p, vt, vidx, oset, first, last):
        for h in range(4):
            k.op("pe", lambda e: e.matmul(oset[h].ap(), lhsT=p[:, h * 128:(h + 1) * 128], rhs=vt[:, vidx, :], start=first, stop=last),
                 reads=[p.b, vt.b], writes=[oset[h].b])

    def run_pairs(jobs, oset):
        n = len(jobs)
        accs = [None] * n

        def emit_S(i):
            a = next_acc()
            accs[i] = a
            k.op("pe", lambda e: jobs[i]["mm"](e, a), reads=jobs[i]["reads"], writes=[a.b])
        if n:
            emit_S(0)
        for i in range(n):
            if i + 1 < n:
                emit_S(i + 1)
            msk = jobs[i]["pre"]() if "pre" in jobs[i] else None
            p = exp_to_P(accs[i], msk)
            pv(p, jobs[i]["vt"], jobs[i]["vidx"], oset, i == 0, i == n - 1)
            if "post" in jobs[i]:
                jobs[i]["post"](p)

    def combine(oset, br, first, gsb=None):
        for h in range(4):
            k.op("dve", lambda e: e.tensor_scalar(out=zt[:, br, h:h + 1], in0=oset[h].ap(a=128, b=129), scalar1=1e-30, scalar2=None,
                                                  op0=ALU.max), reads=[oset[h].b], writes=[zt.b])
        k.op("dve", lambda e: e.reciprocal(out=zt[:, br, :], in_=zt[:, br, :]), reads=[zt.b], writes=[zt.b])
        k.op("dve", lambda e: e.tensor_tensor(out=wgt[:], in0=zt[:, br, :], in1=gsb[:, br:12:3], op=ALU.mult), reads=[zt.b, gsb.b],
             writes=[wgt.b])
        for h in range(4):
            if first:
                k.op("dve", lambda e: e.tensor_scalar(out=oacc[:, h, :], in0=oset[h].ap(b=128), scalar1=wgt[:, h:h + 1], scalar2=None,
                                                      op0=ALU.mult), reads=[oset[h].b, wgt.b], writes=[oacc.b])
            else:
                k.op("dve", lambda e: e.scalar_tensor_tensor(out=oacc[:, h, :], in0=oset[h].ap(b=128), scalar=wgt[:, h:h + 1],
                                                             in1=oacc[:, h, :], op0=ALU.mult, op1=ALU.add),
                     reads=[oset[h].b, wgt.b, oacc.b], writes=[oacc.b])

    qTs = [qT, sbs("qT_b", [128, 512], BF16)]
    gsbs = [gsb, sbs("gsb_b", [128, 12], F32)]
    qacc = impT

    def q_prep_a(qt):
        st_, tci_ = divmod(qt, TW // 128)
        xc = xTs[st_ % 2]
        tsl_ = slice(tci_ * 128, (tci_ + 1) * 128)
        a = qacc
        g_ = gsbs[qt % 2]

        def mm(e):
            for dc in range(16):
                ins = e.matmul(a.ap(), lhsT=xc[:, dc, tsl_], rhs=win_q[:, dc, 0:512], start=(dc == 0), stop=(dc == 15))
            return ins
        k.op("pe", mm, reads=[win_q.b, xc.b], writes=[a.b])
        rg = next_regT()

        def mmg(e):
            for dc in range(16):
                ins = e.matmul(rg.ap(b=12), lhsT=xc[:, dc, tsl_], rhs=win_q[:, dc, 512:524], start=(dc == 0), stop=(dc == 15))
            return ins
        k.op("pe", mmg, reads=[win_q.b, xc.b], writes=[rg.b])
        k.op("dve", lambda e: e.tensor_tensor(out=g_[:], in0=rg.ap(b=12), in1=gb_sb[:], op=ALU.add), reads=[rg.b, gb_sb.b], writes=[g_.b])
        k.op("act", lambda e: e.activation(out=g_[:], in_=g_[:], func=AF.Sigmoid), reads=[g_.b], writes=[g_.b])
        rms_heads(a, [0, 128, 256, 384], 4, lambda h: (qnw_sb, qnw_sb[:, h * 128:(h + 1) * 128]), post_scale=scale)
        rope_tables(1.0, float(qt * 128))
        apply_rope(qr4, kn, 4)

    def q_prep_b(qt):
        q_ = qTs[qt % 2]
        for h in range(4):
            r2 = next_regT()
            k.op("pe", lambda e: e.transpose(r2.ap(), qr4[:, h, :], c["ident"][:]), reads=[qr4.b, c["ident"].b], writes=[r2.b])
            k.op("act", lambda e: e.copy(out=q_[:, h * 128:(h + 1) * 128], in_=r2.ap()), reads=[r2.b], writes=[q_.b])

    load_xT(0)
    if NST > 1:
        load_xT(1)
    q_prep_a(0)
    q_prep_b(0)
    for st in range(NST):
        t0s = st * TW
        for tci in range(TW // 128):
            qt = st * (TW // 128) + tci
            t0 = qt * 128
            tsl = slice(tci * 128, (tci + 1) * 128)
            qT = qTs[qt % 2]
            gsb = gsbs[qt % 2]
            nmax = (t0 + 127 - 31) // 16
            ntiles = 0 if nmax < 0 else min(NCT, nmax // 128 + 1)
            oc = oreg[0]
            if ntiles == 0:
                k.op("dve", lambda e: e.memset(imp[:], 0.0), writes=[imp.b])
            jobs = []
            for nt in range(ntiles):
                def mmc(e, a, nt=nt):
                    return e.matmul(a.ap(), lhsT=kcmpT[:, nt * 128:(nt + 1) * 128], rhs=qT[:], start=True, stop=True)

                def pre(nt=nt):
                    k.op("pool", lambda e: e.affine_select(out=m01[:].rearrange("p (h j) -> p h j", h=4),
                                                           in_=ones4[:].rearrange("p (h j) -> p h j", h=4), pattern=[[0, 4], [1, 128]],
                                                           compare_op=ALU.is_ge, fill=0.0, base=t0 - 2048 * nt - 31, channel_multiplier=-16),
                         reads=[ones4.b], writes=[m01.b])
                    return m01

                def post(p, nt=nt):
                    k.op("pe", lambda e: e.matmul(impT.ap(), lhsT=cover[:, nt, :], rhs=p[:], start=(nt == 0), stop=(nt == ntiles - 1)),
                         reads=[cover.b, p.b], writes=[impT.b])
                jobs.append(dict(mm=mmc, reads=[kcmpT.b, qT.b], vt=vcmp, vidx=nt, pre=pre, post=post))
            run_pairs(jobs, oc)
            if ntiles > 0:
                combine(oc, 0, True, gsb)
                k.op("act", lambda e: e.copy(out=impS[:], in_=impT.ap()), reads=[impT.b], writes=[impS.b])
                for h in range(4):
                    r2 = next_regT()
                    k.op("pe", lambda e: e.transpose(r2.ap(), impS[:, h * 128:(h + 1) * 128], c["ident"][:]), reads=[impS.b, c["ident"].b],
                         writes=[r2.b])
                    if h == 0:
                        k.op("dve", lambda e: e.tensor_scalar(out=imp[:], in0=r2.ap(), scalar1=zt[:, 0, 0:1], scalar2=None, op0=ALU.mult),
                             reads=[r2.b, zt.b], writes=[imp.b])
                    else:
                        k.op("dve", lambda e: e.scalar_tensor_tensor(out=imp[:], in0=r2.ap(), scalar=zt[:, 0, h:h + 1], in1=imp[:],
                                                                     op0=ALU.mult, op1=ALU.add), reads=[r2.b, zt.b, imp.b], writes=[imp.b])
            else:
                k.op("dve", lambda e: e.memset(oacc[:], 0.0), writes=[oacc.b])
            k.op("pool", lambda e: e.affine_select(out=bon[:], in_=c1e4[:], pattern=[[-64, 128]], compare_op=ALU.is_ge, fill=0.0, base=t0,
                                                   channel_multiplier=1), reads=[c1e4.b], writes=[bon.b])
            k.op("pool", lambda e: e.affine_select(out=bon[:], in_=bon[:], pattern=[[64, 128]], compare_op=ALU.is_ge, fill=0.0,
                                                   base=127 - t0, channel_multiplier=-1), reads=[bon.b], writes=[bon.b])
            k.op("pool", lambda e: e.memset(bon[:, 0:1], 1e4), writes=[bon.b])
            k.op("dve", lambda e: e.tensor_tensor(out=impf[:], in0=imp[:], in1=bon[:], op=ALU.add), reads=[imp.b, bon.b], writes=[impf.b])
            k.op("pool", lambda e: e.affine_select(out=val01[:], in_=ones4[:, 0:128], pattern=[[-64, 128]], compare_op=ALU.is_ge, fill=0.0,
                                                   base=t0, channel_multiplier=1), reads=[ones4.b], writes=[val01.b])
            k.op("dve", lambda e: e.tensor_tensor(out=impf[:], in0=impf[:], in1=val01[:], op=ALU.mult), reads=[impf.b, val01.b],
                 writes=[impf.b])
            k.op("dve", lambda e: e.tensor_scalar(out=val01[:], in0=val01[:], scalar1=1e30, scalar2=-1e30, op0=ALU.mult, op1=ALU.add),
                 reads=[val01.b], writes=[val01.b])
            k.op("dve", lambda e: e.tensor_tensor(out=impf[:], in0=impf[:], in1=val01[:], op=ALU.add), reads=[impf.b, val01.b],
                 writes=[impf.b])
            k.op("dve", lambda e: e.max(out=m8[:], in_=impf[:]), reads=[impf.b], writes=[m8.b])
            k.op("dve", lambda e: e.match_replace(out=wk_[:], in_to_replace=m8[:], in_values=impf[:], imm_value=-3e38), reads=[impf.b, m8.b],
                 writes=[wk_.b])
            k.op("dve", lambda e: e.max(out=m8[:], in_=wk_[:]), reads=[wk_.b], writes=[m8.b])
            k.op("dve", lambda e: e.tensor_scalar(out=selm[:], in0=impf[:], scalar1=m8[:, 7:8], scalar2=None, op0=ALU.is_ge),
                 reads=[impf.b, m8.b], writes=[selm.b])
            k.op("dve", lambda e: e.tensor_scalar(out=selm[:], in0=selm[:], scalar1=-1.0, scalar2=NSA_BIG, op0=ALU.add, op1=ALU.mult),
                 reads=[selm.b], writes=[selm.b])
            owin = oreg[0]
            kts = list(range(max(0, qt - 4), qt + 1))
            jobs = []
            for kt in kts:
                def mmw(e, a, kt=kt):
                    need_mask = (kt == qt) or (kt == qt - 4)
                    ins = e.matmul(a.ap(), lhsT=kwT[:, kt * 128:(kt + 1) * 128], rhs=qT[:], start=True, stop=not need_mask)
                    if kt == qt:
                        ins = e.matmul(a.ap(), lhsT=identb[:], rhs=cneg[:], start=False, stop=True)
                    elif kt == qt - 4:
                        ins = e.matmul(a.ap(), lhsT=identb[:], rhs=wneg[:], start=False, stop=True)
                    return ins
                jobs.append(dict(mm=mmw, reads=[kwT.b, qT.b, identb.b, cneg.b, wneg.b], vt=vw_e, vidx=kt))
            run_pairs(jobs, owin)
            combine(owin, 2, False, gsb)
            r2 = next_regT()
            k.op("pe", lambda e: e.transpose(r2.ap(), selm[:], c["ident"][:]), reads=[selm.b, c["ident"].b], writes=[r2.b])
            k.op("act", lambda e: e.copy(out=nmT[:].rearrange("p (h j) -> p h j", h=4),
                                         in_=r2.ap().rearrange("p (o j) -> p o j", o=1).to_broadcast([128, 4, 128])), reads=[r2.b],
                 writes=[nmT.b])
            if debug and qt == (NT - 1):
                dump("imp", imp, imp[:], 128)
                dump("impf", impf, impf[:], 128)
                dump("selm", selm, selm[:], 128)
            if qt + 1 < NT:
                if (qt + 1) % (TW // 128) == 0 and (qt + 1) // (TW // 128) + 1 < NST:
                    load_xT((qt + 1) // (TW // 128) + 1)
                q_prep_a(qt + 1)
            osel = oreg[1]
            jobs = []
            for kt in range(qt + 1):
                def mms(e, a, kt=kt):
                    e.matmul(a.ap(), lhsT=ksT[:, kt * 128:(kt + 1) * 128], rhs=qT[:], start=True, stop=False)
                    ins = e.matmul(a.ap(), lhsT=Esel[:, kt * 128:(kt + 1) * 128], rhs=nmT[:], start=False, stop=(kt != qt))
                    if kt == qt:
                        ins = e.matmul(a.ap(), lhsT=identb[:], rhs=cneg[:], start=False, stop=True)
                    return ins
                jobs.append(dict(mm=mms, reads=[ksT.b, qT.b, Esel.b, nmT.b, identb.b, cneg.b], vt=vs_e, vidx=kt))
            run_pairs(jobs, osel)
            combine(osel, 1, False, gsb)
            if qt + 1 < NT:
                q_prep_b(qt + 1)
            for h in range(4):
                r2 = next_regT()
                k.op("pe", lambda e: e.transpose(r2.ap(), oacc[:, h, :], c["ident"][:]), reads=[oacc.b, c["ident"].b], writes=[r2.b])
                k.op("act", lambda e: e.copy(out=ystage[:, h, tsl], in_=r2.ap()), reads=[r2.b], writes=[ystage.b])
        tk = k.op("sp", lambda e: e.dma_start(out=yT.rearrange("c p t -> p c t")[:, :, t0s:t0s + TW], in_=ystage[:]), reads=[ystage.b],
                  dsem=ds_y)
        out_toks.append(tk)
    k.finish(out_toks)
    return nc


def nsa_core_inputs(I, layer, g, hT_b):
    j = layer // 2
    W = I["c_w_in"][j]
    o_q, o_kc, o_vc, o_ks, o_vs, o_kw, o_vw, o_gp = 0, 2048, 2560, 3072, 3584, 4096, 4608, 5120
    sl = lambda o: W[:, o + g * 128:o + (g + 1) * 128]
    win = np.concatenate([sl(o_kc), sl(o_vc), sl(o_ks), sl(o_vs), sl(o_kw), sl(o_vw), W[:, g * 512:(g + 1) * 512],
                          W[:, o_gp + 12 * g:o_gp + 12 * (g + 1)]], axis=1)
    posT = np.ascontiguousarray(I["c_cmp_pos"][j].transpose(2, 0, 1))
    b1 = np.ascontiguousarray(I["c_cmp_b1"][j].reshape(2, 2, 128).transpose(2, 0, 1))
    return {
        "hT": hT_b, "nw": pvec(I["mix_norm"][layer]), "win": np.ascontiguousarray(win),
        "qnw": rep(np.tile(I["c_q_norm"][j], 4)), "knw": rep(I["c_k_norm"][j]), "posT": posT,
        "w1": np.ascontiguousarray(I["c_cmp_w1"][j]), "b1": b1, "w2": np.ascontiguousarray(I["c_cmp_w2"][j]),
        "gbias": rep(I["c_gate_bias"][j][12 * g:12 * (g + 1)]),
    }


_PROGS = {}


def _prog(name, fn):
    if name not in _PROGS:
        _PROGS[name] = fn()
    return _PROGS[name]


def _launch(nc, maps):
    res = run_bass_kernel_spmd(nc, maps, core_ids=list(range(8)))
    return res.results


def kernel(**I):
    I = {k_: np.ascontiguousarray(np.asarray(v)) for k_, v in I.items()}
    x = I["x"]
    B, S, D = x.shape
    NTC = S // 4
    hT = [np.ascontiguousarray(x[b].T.reshape(16, 128, S)) for b in range(B)]

    def tok_shard(arrs, c):
        b, q = divmod(c, 4)
        return np.ascontiguousarray(arrs[b][:, :, q * NTC:(q + 1) * NTC])

    def ffn_launch(hT, pre, layer, yT=None, wo=None):
        nc = _prog("ffn_pre" if yT is not None else "ffn", lambda: build_ffn(NT=NTC, preproj=yT is not None))
        maps = []
        for c in range(8):
            m = {"hT": tok_shard(hT, c), "nw": pvec(I[pre + "_norm"][layer]), "wg": I[pre + "_w_gate"][layer],
                 "wu": I[pre + "_w_up"][layer], "wd": I[pre + "_w_down"][layer]}
            if yT is not None:
                m["yT"] = tok_shard(yT, c)
                m["wo"] = wo
            maps.append(m)
        res = _launch(nc, maps)
        out = [np.empty((16, 128, S), np.float32) for _ in range(B)]
        for c in range(8):
            b, q = divmod(c, 4)
            out[b][:, :, q * NTC:(q + 1) * NTC] = res[c]["hT_out"]
        return out

    for layer in range(4):
        j = layer // 2
        hT = ffn_launch(hT, "ffn1", layer)
        yT = [np.empty((16, 128, S), ml_dtypes.bfloat16) for _ in range(B)]
        if layer % 2 == 0:
            nc = _prog(f"ab{j}", lambda: build_ab(T=S, layer_j=j))
            maps = [ab_core_inputs(I, layer, c % 4, hT[c // 4]) for c in range(8)]
            res = _launch(nc, maps)
            for c in range(8):
                b, g = divmod(c, 4)
                y = res[c]["yT"]
                yT[b][2 * g:2 * g + 2] = y[0:2]
                yT[b][8 + 2 * g:8 + 2 * g + 2] = y[2:4]
            wo = I["ab_w_out"][j]
        else:
            nc = _prog("nsa", lambda: build_nsa(T=S))
            maps = [nsa_core_inputs(I, layer, c % 4, hT[c // 4]) for c in range(8)]
            res = _launch(nc, maps)
            for c in range(8):
                b, g = divmod(c, 4)
                yT[b][4 * g:4 * g + 4] = res[c]["yT"]
            wo = I["c_w_out"][j]
        hT = ffn_launch(hT, "ffn2", layer, yT=yT, wo=wo)
    out = np.stack([hT[b].reshape(D, S).T for b in range(B)], axis=0)
    return np.ascontiguousarray(out.astype(np.float32))
```

```python
import numpy as np
import ml_dtypes
import concourse.bass as bass
import concourse.mybir as mybir
from concourse.bass_utils import run_bass_kernel_spmd

F32 = mybir.dt.float32
BF16 = mybir.dt.bfloat16
I32 = mybir.dt.int32
AF = mybir.ActivationFunctionType
ALU = mybir.AluOpType
AX = mybir.AxisListType

D_MODEL = 2048
D_FF = 5504
EPS = 1e-6
SAME_SYNC = True


class Tok:
    __slots__ = ("sem", "val", "key", "eng")

    def __init__(self, sem, val, key, eng):
        self.sem, self.val, self.key, self.eng = sem, val, key, eng


class Buf:
    __slots__ = ("name", "w", "r", "bank")

    def __init__(self, name):
        self.name, self.w, self.r, self.bank = name, None, {}, None


class DSem:
    __slots__ = ("h", "val", "key")

    def __init__(self, h, key):
        self.h, self.val, self.key = h, 0, key


class KB:
    def __init__(self, nc):
        self.nc = nc
        self.engs = dict(pe=nc.tensor, act=nc.scalar, dve=nc.vector, pool=nc.gpsimd, sp=nc.sync)
        self.sem = {e: nc.alloc_semaphore("sem_" + e) for e in self.engs}
        self.cnt = {e: 0 for e in self.engs}
        self.waited = {e: {} for e in self.engs}
        self.nds = 0
        self.out_toks = []
        self.after_op = None

    def buf(self, name=""):
        return Buf(name)

    def bufs(self, n, name=""):
        return [Buf(f"{name}{i}") for i in range(n)]

    def dsem(self, name=None):
        self.nds += 1
        key = f"D{self.nds}"
        return DSem(self.nc.alloc_semaphore(name or key), key)

    def _wait(self, e, tok, raw=False):
        if tok is None:
            return
        if tok.eng == e and not (raw and SAME_SYNC and e != "pe"):
            return
        w = self.waited[e]
        if w.get(tok.key, 0) >= tok.val:
            return
        self.engs[e].wait_ge(tok.sem, tok.val)
        w[tok.key] = tok.val

    def op(self, e, fn, reads=(), writes=(), dsem=None):
        for b in reads:
            self._wait(e, b.w, raw=True)
        for b in writes:
            self._wait(e, b.w)
            for t in b.r.values():
                self._wait(e, t)
        banks = {}
        for b in list(reads) + list(writes):
            if b.bank is not None:
                banks[id(b.bank)] = b.bank
        for bk in banks.values():
            self._wait(e, bk.w)
        ins = fn(self.engs[e])
        if dsem is None:
            self.cnt[e] += 1
            ins.then_inc(self.sem[e], 1)
            tok = Tok(self.sem[e], self.cnt[e], "E" + e, e)
        else:
            dsem.val += 16
            ins.then_inc(dsem.h, 16)
            tok = Tok(dsem.h, dsem.val, dsem.key, None)
        for b in reads:
            b.r[tok.key] = tok
        for b in writes:
            b.w = tok
            b.r = {}
        for bk in banks.values():
            bk.w = tok
        if self.after_op is not None:
            self.after_op()
        return tok

    def finish(self, toks):
        for t in toks:
            self._wait("sp", t)


import threading


def interleave(k, fns):
    n = len(fns)
    cv = threading.Condition()
    state = {"turn": 0, "alive": [True] * n, "err": None}
    tls = threading.local()

    def advance(i):
        for d in range(1, n + 1):
            j = (i + d) % n
            if state["alive"][j]:
                state["turn"] = j
                return
        state["turn"] = -1

    def yield_(i):
        with cv:
            advance(i)
            cv.notify_all()
            while state["turn"] != i:
                cv.wait()

    def runner(i):
        tls.idx = i
        with cv:
            while state["turn"] != i:
                cv.wait()
        try:
            fns[i]()
        except BaseException as e:
            state["err"] = e
        with cv:
            state["alive"][i] = False
            advance(i)
            cv.notify_all()

    old_hook = k.after_op
    k.after_op = lambda: yield_(tls.idx) if getattr(tls, "idx", None) is not None else None
    ths = [threading.Thread(target=runner, args=(i,)) for i in range(n)]
    for t in ths:
        t.start()
    for t in ths:
        t.join()
    k.after_op = old_hook
    if state["err"] is not None:
        raise state["err"]


class Ring:
    def __init__(self, k, slots, name="ws"):
        self.k = k
        self.slots = slots
        self.ns = len(slots)
        self.b = k.bufs(self.ns, name)
        self.ds = [k.dsem() for _ in range(self.ns)]
        self.loads = []
        self.issued = 0
        self.consumed = 0

    def plan(self, dst_fn, src):
        self.loads.append((dst_fn, src))

    def _issue(self):
        if self.issued >= len(self.loads):
            return
        i = self.issued
        s = i % self.ns
        dst_fn, src = self.loads[i]
        slot = self.slots[s]
        self.k.op("pool", lambda e: e.dma_start(out=dst_fn(slot), in_=src), reads=(), writes=[self.b[s]],
                  dsem=self.ds[s])
        self.issued += 1

    def start(self):
        while self.issued < min(self.ns, len(self.loads)):
            self._issue()

    def get(self, off=0):
        i = self.consumed + off
        assert i < self.issued, "ring underflow"
        s = i % self.ns
        return self.slots[s], self.b[s]

    def done(self):
        self.consumed += 1
        self._issue()


def build_ffn(NT=2048, F=D_FF, preproj=False, TP=1024, NS=4):
    D = D_MODEL
    DC = D // 128
    FCn = F // 128
    assert F % 128 == 0 and NT % TP == 0 and TP % 512 == 0
    NTT = TP // 512
    nc = bass.Bass("TRN2", target_bir_lowering=False)
    k = KB(nc)
    hT_in = nc.dram_tensor("hT", [DC, 128, NT], F32, kind="ExternalInput").ap()
    nw = nc.dram_tensor("nw", [128, DC], F32, kind="ExternalInput").ap()
    wg = nc.dram_tensor("wg", [D, F], F32, kind="ExternalInput").ap()
    wu = nc.dram_tensor("wu", [D, F], F32, kind="ExternalInput").ap()
    wd = nc.dram_tensor("wd", [F, D], F32, kind="ExternalInput").ap()
    hT_out = nc.dram_tensor("hT_out", [DC, 128, NT], F32, kind="ExternalOutput").ap()
    if preproj:
        yT = nc.dram_tensor("yT", [DC, 128, NT], BF16, kind="ExternalInput").ap()
        wo = nc.dram_tensor("wo", [D, D], F32, kind="ExternalInput").ap()
        h2T = nc.dram_tensor("h2T", [DC, 128, NT], F32).ap()
        h_src = h2T
    else:
        h_src = hT_in
    wg_v = wg.rearrange("(dc p) f -> p dc f", p=128)
    wu_v = wu.rearrange("(dc p) f -> p dc f", p=128)
    wd_v = wd.rearrange("(fc p) m -> p fc m", p=128)

    actT = nc.alloc_sbuf_tensor("actT", [128, FCn, TP], BF16)
    xT = nc.alloc_sbuf_tensor("xT", [128, DC, TP], BF16)
    slots = [nc.alloc_sbuf_tensor(f"ws{i}", [128, 16, 256], BF16) for i in range(NS)]
    NSCR = 6
    scr = [nc.alloc_sbuf_tensor(f"scr{i}", [128, TP], F32) for i in range(NSCR)]
    rstd = nc.alloc_sbuf_tensor("rstd", [128, TP], F32)
    ones = nc.alloc_sbuf_tensor("ones", [128, 128], F32)
    nw_sb = nc.alloc_sbuf_tensor("nw_sb", [128, DC], F32)
    epst = nc.alloc_sbuf_tensor("epst", [128, 1], F32)
    ps = [nc.alloc_psum_tensor(f"ps{i}", [128, 512], F32) for i in range(8)]

    actT_b = k.bufs(FCn, "actT")
    xT_b = k.bufs(DC, "xT")
    scr_b = k.bufs(NSCR, "scr")
    scr_ds = [k.dsem() for _ in range(NSCR)]
    rstd_b = k.buf("rstd")
    ones_b = k.buf("ones")
    eps_b = k.buf("eps")
    nw_b = k.buf("nw")
    nw_ds = k.dsem()
    ps_b = k.bufs(8, "ps")
    ring = Ring(k, slots)
    npass = NT // TP
    h2_b = [[k.buf(f"h2_{p}_{dc}") for dc in range(DC)] for p in range(npass)]
    yT_ds = k.dsem()

    fblocks = []
    f0 = 0
    while f0 < F:
        fw = min(256, F - f0)
        fblocks.append((f0, fw))
        f0 += fw
    dsegs = []
    c0 = 0
    while c0 < FCn:
        n = min(16, FCn - c0)
        dsegs.append((c0, n))
        c0 += n
    NG = D // 256

    for p in range(npass):
        if preproj:
            for dc in range(DC):
                ring.plan(lambda s: s[:, :, 0:128], wo.rearrange("(yc p) m -> p yc m", p=128)[:, :, dc * 128:(dc + 1) * 128])
        for (f0, fw) in fblocks:
            ring.plan(lambda s, fw=fw: s[:, :, 0:fw], wg_v[:, :, f0:f0 + fw])
            ring.plan(lambda s, fw=fw: s[:, :, 0:fw], wu_v[:, :, f0:f0 + fw])
        for gi in range(NG):
            for (c0, n) in dsegs:
                ring.plan(lambda s, n=n: s[:, 0:n, :], wd_v[:, c0:c0 + n, gi * 256:(gi + 1) * 256])

    k.op("dve", lambda e: e.memset(ones[:], 1.0), writes=[ones_b])
    k.op("dve", lambda e: e.memset(epst[:], EPS), writes=[eps_b])
    k.op("sp", lambda e: e.dma_start(out=nw_sb[:], in_=nw), writes=[nw_b], dsem=nw_ds)
    ring.start()
    out_toks = []

    for p in range(npass):
        t0 = p * TP
        tsl = slice(t0, t0 + TP)
        if preproj:
            k.op("sp", lambda e: e.dma_start(out=actT[:, 0:DC, :], in_=yT.rearrange("yc p t -> p yc t")[:, :, tsl]),
                 writes=actT_b[0:DC], dsem=yT_ds)
        for dc in range(DC):
            hi = dc % 2
            ht = scr[hi]
            k.op("sp", lambda e: e.dma_start(out=ht[:], in_=hT_in[dc, :, tsl]), writes=[scr_b[hi]], dsem=scr_ds[hi])
            if preproj:
                slot, sb = ring.get()
                pb = 2 + 2 * (dc % 2)

                def mm(e):
                    for yc in range(DC):
                        for tt in range(NTT):
                            ins = e.matmul(ps[pb + tt][:], lhsT=slot[:, yc, 0:128],
                                           rhs=actT[:, yc, tt * 512:(tt + 1) * 512], start=(yc == 0), stop=(yc == DC - 1))
                    return ins
                k.op("pe", mm, reads=[sb] + actT_b[0:DC], writes=[ps_b[pb + tt] for tt in range(NTT)])
                ring.done()
                for tt in range(NTT):
                    k.op("dve", lambda e: e.tensor_tensor(out=ht[:, tt * 512:(tt + 1) * 512], in0=ps[pb + tt][:],
                                                          in1=ht[:, tt * 512:(tt + 1) * 512], op=ALU.add),
                         reads=[ps_b[pb + tt]], writes=[scr_b[hi]])
                k.op("sp", lambda e: e.dma_start(out=h2T[dc, :, tsl], in_=ht[:]), reads=[scr_b[hi]],
                     writes=[h2_b[p][dc]], dsem=scr_ds[hi])
            si = 2 + dc % 2
            sq = scr[si]
            k.op("act", lambda e: e.activation(out=sq[:], in_=ht[:], func=AF.Square), reads=[scr_b[hi]], writes=[scr_b[si]])

            def mm(e):
                for tt in range(NTT):
                    ins = e.matmul(ps[tt][:], lhsT=ones[:], rhs=sq[:, tt * 512:(tt + 1) * 512], start=(dc == 0),
                                   stop=(dc == DC - 1))
                return ins
            k.op("pe", mm, reads=[scr_b[si], ones_b], writes=[ps_b[tt] for tt in range(NTT)])
        for tt in range(NTT):
            k.op("act", lambda e: e.activation(out=rstd[:, tt * 512:(tt + 1) * 512], in_=ps[tt][:], func=AF.Sqrt,
                                               scale=1.0 / D, bias=epst[:, 0:1]),
                 reads=[ps_b[tt], eps_b], writes=[rstd_b])
        k.op("dve", lambda e: e.reciprocal(out=rstd[:], in_=rstd[:]), reads=[rstd_b], writes=[rstd_b])
        for dc in range(DC):
            hi = dc % 2
            ht = scr[hi]
            rd = [h2_b[p][dc]] if preproj else []
            k.op("sp", lambda e: e.dma_start(out=ht[:], in_=h_src[dc, :, tsl]), reads=rd, writes=[scr_b[hi]],
                 dsem=scr_ds[hi])
            k.op("dve", lambda e: e.scalar_tensor_tensor(out=xT[:, dc, :], in0=ht[:], scalar=nw_sb[:, dc:dc + 1],
                                                         in1=rstd[:], op0=ALU.mult, op1=ALU.mult),
                 reads=[scr_b[hi], rstd_b, nw_b], writes=[xT_b[dc]])
        ci = 0
        for (f0, fw) in fblocks:
            sg_, sgb = ring.get(0)
            su_, sub = ring.get(1)
            for j in range(fw // 128):
                fi = f0 // 128 + j
                par = ci % 2
                ci += 1
                gb = 4 * par
                ub = 4 * par + 2

                def mm(e, w_=None, b0=0):
                    for dc in range(DC):
                        for tt in range(NTT):
                            ins = e.matmul(ps[b0 + tt][:], lhsT=w_[:, dc, j * 128:(j + 1) * 128],
                                           rhs=xT[:, dc, tt * 512:(tt + 1) * 512], start=(dc == 0), stop=(dc == DC - 1))
                    return ins
                k.op("pe", lambda e: mm(e, sg_, gb), reads=[sgb] + xT_b, writes=[ps_b[gb + tt] for tt in range(NTT)])
                k.op("pe", lambda e: mm(e, su_, ub), reads=[sub] + xT_b, writes=[ps_b[ub + tt] for tt in range(NTT)])
                sgi = 4 + par
                sgt = scr[sgi]
                for tt in range(NTT):
                    k.op("act", lambda e: e.activation(out=sgt[:, tt * 512:(tt + 1) * 512], in_=ps[gb + tt][:], func=AF.Silu),
                         reads=[ps_b[gb + tt]], writes=[scr_b[sgi]])
                    k.op("dve", lambda e: e.tensor_tensor(out=actT[:, fi, tt * 512:(tt + 1) * 512],
                                                          in0=sgt[:, tt * 512:(tt + 1) * 512], in1=ps[ub + tt][:], op=ALU.mult),
                         reads=[scr_b[sgi], ps_b[ub + tt]], writes=[actT_b[fi]])
            ring.done()
            ring.done()
        for gi in range(NG):
            base = 4 * (gi % 2)
            for dmi in range(2):
                dmc = gi * 2 + dmi
                hi = dmi
                rd = [h2_b[p][dmc]] if preproj else []
                k.op("sp", lambda e: e.dma_start(out=scr[hi][:], in_=h_src[dmc, :, tsl]), reads=rd, writes=[scr_b[hi]],
                     dsem=scr_ds[hi])
            for (c0, n) in dsegs:
                slot, sb = ring.get()

                def mm(e):
                    for j in range(n):
                        fc = c0 + j
                        for dmi in range(2):
                            for tt in range(NTT):
                                ins = e.matmul(ps[base + 2 * dmi + tt][:], lhsT=slot[:, j, dmi * 128:(dmi + 1) * 128],
                                               rhs=actT[:, fc, tt * 512:(tt + 1) * 512], start=(fc == 0), stop=(fc == FCn - 1))
                    return ins
                k.op("pe", mm, reads=[sb] + actT_b[c0:c0 + n], writes=[ps_b[base + i] for i in range(4)])
                ring.done()
            for dmi in range(2):
                dmc = gi * 2 + dmi
                hi = dmi
                oi = 2 + dmi
                ot = scr[oi]
                for tt in range(NTT):
                    bk = base + 2 * dmi + tt
                    k.op("dve", lambda e: e.scalar_tensor_tensor(out=ot[:, tt * 512:(tt + 1) * 512], in0=ps[bk][:], scalar=0.5,
                                                                 in1=scr[hi][:, tt * 512:(tt + 1) * 512], op0=ALU.mult, op1=ALU.add),
                         reads=[ps_b[bk], scr_b[hi]], writes=[scr_b[oi]])
                tk = k.op("sp", lambda e: e.dma_start(out=hT_out[dmc, :, tsl], in_=ot[:]), reads=[scr_b[oi]], dsem=scr_ds[oi])
                out_toks.append(tk)
    k.finish(out_toks)
    return nc


class TT:
    def __init__(self, k, t, name):
        self.t = t
        self.b = k.buf(name)

    def __getitem__(self, idx):
        return self.t[idx]


def sb(k, name, shape, dt):
    return TT(k, k.nc.alloc_sbuf_tensor(name, list(shape), dt), name)


class PReg:
    _bankbufs = {}

    def __init__(self, k, bank, c0, c1, name):
        self.bank, self.c0, self.c1 = bank, c0, c1
        self.b = k.buf(name)
        key = (id(k), bank.name if hasattr(bank, "name") else id(bank))
        if key not in PReg._bankbufs:
            PReg._bankbufs[key] = k.buf("bank")
        self.b.bank = PReg._bankbufs[key]

    def ap(self, rows=slice(None), a=None, b=None):
        a = self.c0 if a is None else self.c0 + a
        b = self.c1 if b is None else self.c0 + b
        return self.bank[rows, a:b]


def make_consts(k, need_ident=True):
    nc = k.nc
    c = {}
    c["ones"] = sb(k, "c_ones", [128, 128], F32)
    k.op("dve", lambda e: e.memset(c["ones"][:], 1.0), writes=[c["ones"].b])
    c["ident"] = sb(k, "c_ident", [128, 128], F32)
    k.op("pool", lambda e: e.affine_select(out=c["ident"][:], in_=c["ones"][:], pattern=[[-1, 128]],
                                           compare_op=ALU.is_equal, fill=0.0, base=0, channel_multiplier=1),
         reads=[c["ones"].b], writes=[c["ident"].b])
    c["causal"] = sb(k, "c_causal", [128, 128], F32)
    k.op("pool", lambda e: e.affine_select(out=c["causal"][:], in_=c["ones"][:], pattern=[[1, 128]],
                                           compare_op=ALU.is_ge, fill=0.0, base=0, channel_multiplier=-1),
         reads=[c["ones"].b], writes=[c["causal"].b])
    c["eps"] = sb(k, "c_eps", [128, 1], F32)
    k.op("dve", lambda e: e.memset(c["eps"][:], EPS), writes=[c["eps"].b])
    c["one1"] = sb(k, "c_one1", [128, 1], F32)
    k.op("dve", lambda e: e.memset(c["one1"][:], 1.0), writes=[c["one1"].b])
    return c


def norm_supertile(k, c, hT_src, nw_sb, hTt, xT, ps_ss, rstd, scrsq, t0, TW, ds_h, h_reads=()):
    DC = 16
    k.op("sp", lambda e: e.dma_start(out=hTt[:, :, 0:TW], in_=hT_src.rearrange("dc p t -> p dc t")[:, :, t0:t0 + TW]),
         reads=list(h_reads), writes=[hTt.b], dsem=ds_h)
    for dc in range(DC):
        sq = scrsq[dc % 2]
        k.op("act", lambda e: e.activation(out=sq[:, 0:TW], in_=hTt[:, dc, 0:TW], func=AF.Square), reads=[hTt.b],
             writes=[sq.b])
        k.op("pe", lambda e: e.matmul(ps_ss.ap(b=TW), lhsT=c["ones"][:], rhs=sq[:, 0:TW], start=(dc == 0), stop=(dc == DC - 1)),
             reads=[sq.b, c["ones"].b], writes=[ps_ss.b])
    k.op("act", lambda e: e.activation(out=rstd[:, 0:TW], in_=ps_ss.ap(b=TW), func=AF.Sqrt, scale=1.0 / D_MODEL,
                                       bias=c["eps"][:, 0:1]), reads=[ps_ss.b, c["eps"].b], writes=[rstd.b])
    k.op("dve", lambda e: e.reciprocal(out=rstd[:, 0:TW], in_=rstd[:, 0:TW]), reads=[rstd.b], writes=[rstd.b])
    for dc in range(DC):
        k.op("dve", lambda e: e.scalar_tensor_tensor(out=xT[:, dc, 0:TW], in0=hTt[:, dc, 0:TW], scalar=nw_sb[:, dc:dc + 1],
                                                     in1=rstd[:, 0:TW], op0=ALU.mult, op1=ALU.mult),
             reads=[hTt.b, rstd.b, nw_sb.b], writes=[xT.b])


HG_MAX_K = 0.999999


class _Stop(Exception):
    pass


def build_ab(T=8192, layer_j=0, do_ml=2, do_hg=2, stop=99, debug=False):
    TW = 512
    NST = T // TW
    NCOL = 1794
    nc = bass.Bass("TRN2", target_bir_lowering=False)
    k = KB(nc)

    def dram(name, shape, dt=F32, kind="ExternalInput"):
        return nc.dram_tensor(name, list(shape), dt, kind=kind).ap()
    hT = dram("hT", [16, 128, T])
    nw = dram("nw", [128, 16])
    win = dram("win", [2048, NCOL])
    cw = dram("cw", [128, 2, 4])
    cb = dram("cb", [128, 2])
    wq = dram("wq", [256, 256])
    wk = dram("wk", [256, 256])
    gbias = dram("gb", [128, 2])
    mln = dram("mln", [128, 256])
    skp = dram("skp", [128, 256])
    lbl = dram("lbl", [128, 2, 2])
    hgn = dram("hgn", [128, 2, 128])
    yT = dram("yT", [4, 128, T], BF16, kind="ExternalOutput")
    dbg = dram("dbg", [128, 8192], F32, kind="ExternalOutput") if debug else None
    dbg_pos = [0]
    dbg_map = {}
    dbg_ds = k.dsem() if debug else None

    def dump(name, tt, ap, n):
        if not debug or name in dbg_map:
            return
        c0 = dbg_pos[0]
        dbg_pos[0] += n
        dbg_map[name] = (c0, n)
        k.op("sp", lambda e: e.dma_start(out=dbg[:, c0:c0 + n], in_=ap), reads=[tt.b], dsem=dbg_ds)
    nc._dbg_map = dbg_map

    c = make_consts(k)
    ps = [nc.alloc_psum_tensor(f"ps{i}", [128, 512], F32) for i in range(8)]
    acc = [PReg(k, ps[i], 0, 512, f"acc{i}") for i in range(2)]
    ps_ss = PReg(k, ps[2], 0, 512, "ss")
    regT = [PReg(k, ps[2], i * 128, (i + 1) * 128, f"regT{i}") for i in range(4)]
    regA = PReg(k, ps[3], 0, 8, "regA")
    regB = PReg(k, ps[3], 128, 257, "regB")
    regC = PReg(k, ps[3], 384, 512, "regC")
    regND = PReg(k, ps[4], 0, 264, "regND")
    regU = [PReg(k, ps[5 + i], 0, 264, f"regU{i}") for i in range(2)]
    r7A = PReg(k, ps[7], 0, 128, "r7A")
    r7B = PReg(k, ps[7], 128, 256, "r7B")
    r7C = PReg(k, ps[7], 256, 384, "r7C")
    r7D = PReg(k, ps[7], 384, 512, "r7D")

    win_sb = sb(k, "win_sb", [128, 16, NCOL], BF16)
    wds = [k.dsem() for _ in range(4)]
    cuts = [0, 512, 1024, 1536, NCOL]
    win_v = win.rearrange("(dc p) f -> p dc f", p=128)
    wtoks = []
    for i in range(4):
        a, b_ = cuts[i], cuts[i + 1]
        wtoks.append(k.op("pool", lambda e: e.dma_start(out=win_sb[:, :, a:b_], in_=win_v[:, :, a:b_]), writes=[],
                          dsem=wds[i]))
    wq_sb = sb(k, "wq_sb", [128, 2, 256], BF16)
    wk_sb = sb(k, "wk_sb", [128, 2, 256], BF16)
    pds = k.dsem()
    k.op("pool", lambda e: e.dma_start(out=wq_sb[:], in_=wq.rearrange("(d p) e -> p d e", p=128)), writes=[wq_sb.b], dsem=pds)
    k.op("pool", lambda e: e.dma_start(out=wk_sb[:], in_=wk.rearrange("(d p) e -> p d e", p=128)), writes=[wk_sb.b], dsem=k.dsem())

    def ld(name, shape, src):
        t = sb(k, name, shape, F32)
        k.op("sp", lambda e: e.dma_start(out=t[:], in_=src), writes=[t.b], dsem=k.dsem())
        return t
    nw_sb = ld("nw_sb", [128, 16], nw)
    cw_sb = ld("cw_sb", [128, 2, 4], cw)
    cb_sb = ld("cb_sb", [128, 2], cb)
    gb_sb = ld("gb_sb", [128, 2], gbias)
    mln_sb = ld("mln_sb", [128, 256], mln)
    skp_sb = ld("skp_sb", [128, 256], skp)
    lbl_sb = ld("lbl_sb", [128, 2, 2], lbl)
    hgn_sb = ld("hgn_sb", [128, 2, 128], hgn)
    for tkn in wtoks:
        k._wait("pe", tkn)

    oml = sb(k, "oml", [128, 2], F32)
    lbe = sb(k, "lbe", [128, 2, 2], F32)
    lbs = sb(k, "lbs", [128, 2], F32)
    k.op("act", lambda e: e.activation(out=lbe[:], in_=lbl_sb[:], func=AF.Exp), reads=[lbl_sb.b], writes=[lbe.b])
    k.op("dve", lambda e: e.tensor_tensor(out=lbs[:], in0=lbe[:, :, 0], in1=lbe[:, :, 1], op=ALU.add), reads=[lbe.b], writes=[lbs.b])
    k.op("dve", lambda e: e.reciprocal(out=lbs[:], in_=lbs[:]), reads=[lbs.b], writes=[lbs.b])
    for l in range(2):
        k.op("dve", lambda e: e.tensor_tensor(out=lbe[:, :, l], in0=lbe[:, :, l], in1=lbs[:], op=ALU.mult), reads=[lbe.b, lbs.b],
             writes=[lbe.b])
    k.op("dve", lambda e: e.tensor_copy(out=oml[:], in_=lbe[:, :, 0]), reads=[lbe.b], writes=[oml.b])
    for l in range(1, layer_j + 1):
        k.op("dve", lambda e: e.tensor_tensor(out=oml[:], in0=oml[:], in1=lbe[:, :, l], op=ALU.add), reads=[lbe.b, oml.b], writes=[oml.b])
    k.op("dve", lambda e: e.tensor_tensor(out=oml[:], in0=oml[:], in1=lbe[:, :, 0], op=ALU.subtract), reads=[lbe.b, oml.b], writes=[oml.b])
    k.op("dve", lambda e: e.tensor_scalar(out=oml[:], in0=oml[:], scalar1=-1.0, scalar2=1.0, op0=ALU.mult, op1=ALU.add),
         reads=[oml.b], writes=[oml.b])
    nfb = sb(k, "nfb", [128, 1], F32)
    k.op("dve", lambda e: e.tensor_scalar(out=nfb[:], in0=gb_sb[:, 1:2], scalar1=-1.0, scalar2=None, op0=ALU.mult),
         reads=[gb_sb.b], writes=[nfb.b])

    hTt = sb(k, "hTt", [128, 16, TW], F32)
    ds_h = k.dsem()
    xT = sb(k, "xT", [128, 16, TW], BF16)
    rstd = sb(k, "rstd", [128, TW], F32)
    scrsq = [sb(k, f"scrsq{i}", [128, TW], F32) for i in range(2)]
    ubuf = sb(k, "ubuf", [128, 2, TW + 3], F32)
    cacc = sb(k, "cacc", [128, TW], F32)
    cT = sb(k, "cT", [128, 2, TW], F32)
    cTb = sb(k, "cTb", [128, 2, TW], BF16)
    qT = sb(k, "qT", [128, 2, TW], F32)
    qTb = sb(k, "qTb", [128, 2, TW], BF16)
    kTb = sb(k, "kTb", [128, 2, TW], BF16)
    ktok = sb(k, "ktok", [128, 4, 256], F32)
    ctok = sb(k, "ctok", [128, 4, 256], F32)
    vext = sb(k, "vext", [128, 4, 264], BF16)
    osig = sb(k, "osig", [128, 4, 256], F32)
    hv = sb(k, "hv", [128, 4, 2, 128], BF16)
    hgs = sb(k, "hgs", [128, 4, 256], F32)
    ge1 = sb(k, "ge1", [128, 4], F32)
    logf = sb(k, "logf", [128, 4], F32)
    ig = sb(k, "ig", [128, 4], F32)
    lfb = sb(k, "lfb", [128, 128], F32)
    bias_s = sb(k, "bias_s", [128, 1], F32)
    DT = sb(k, "DT", [128, 128], F32)
    Eb = sb(k, "Eb", [128, 128], F32)
    Dm = sb(k, "Dm", [128, 128], F32)
    PT = sb(k, "PT", [128, 128], BF16)
    qs = sb(k, "qs", [128, 2, 128], BF16)
    small = sb(k, "small", [128, 8], F32)
    small2 = sb(k, "small2", [128, 8], F32)
    junk2 = sb(k, "junk2", [128, 128], F32)
    hn = sb(k, "hn", [128, 256], F32)
    junk = sb(k, "junk", [128, 256], F32)
    hm = sb(k, "hm", [128, 256], F32)
    t1 = sb(k, "t1", [128, 256], F32)
    yml = sb(k, "yml", [128, 256], F32)
    ka = sb(k, "ka", [128, 256], BF16)
    Cst = sb(k, "Cst", [128, 2, 264], F32)
    Cb = sb(k, "Cb", [128, 2, 264], BF16)
    ystage = sb(k, "ystage", [128, 4, TW], BF16)
    ds_y = k.dsem()
    resetm = sb(k, "resetm", [128, TW], F32)
    tA = sb(k, "tA", [128, TW], F32)
    tB = sb(k, "tB", [128, TW], F32)
    tC = sb(k, "tC", [128, TW], F32)
    k2 = sb(k, "k2", [128, TW], F32)
    lf1 = sb(k, "lf1", [128, TW], F32)
    lgf = sb(k, "lgf", [128, TW], F32)
    bt = sb(k, "bt", [128, TW], F32)
    brel = sb(k, "brel", [128, TW], F32)
    sqt = sb(k, "sqt", [128, TW], F32)
    kgT = sb(k, "kgT", [128, TW], F32)
    eg8 = sb(k, "eg8", [128, 8], F32)
    qz = [sb(k, f"qz{i}", [128, 4, 128], BF16) for i in range(2)]
    kz = [sb(k, f"kz{i}", [128, 4, 128], BF16) for i in range(2)]
    qbz = [sb(k, f"qbz{i}", [128, 4, 128], BF16) for i in range(2)]
    kgz = [sb(k, f"kgz{i}", [128, 4, 128], BF16) for i in range(2)]
    Am = sb(k, "Am", [128, 128], BF16)
    Sst = [sb(k, f"Sst{i}", [128, 128], F32) for i in range(2)]
    Sb = [[sb(k, f"Sb{i}_{j}", [128, 128], BF16) for j in range(2)] for i in range(2)]
    o_sb = sb(k, "o_sb", [128, 128], F32)
    o2n = sb(k, "o2n", [128, 128], F32)
    yh = sb(k, "yh", [128, 128], F32)

    for t_ in [ubuf, Cst, Cb, Sst[0], Sst[1], Sb[0][0], Sb[0][1], Sb[1][0], Sb[1][1]] + qz + kz + qbz + kgz:
        k.op("dve", lambda e: e.memset(t_[:], 0.0), writes=[t_.b])
    k.op("dve", lambda e: e.memset(vext[:], 1.0), writes=[vext.b])
    k.op("dve", lambda e: e.memset(resetm[:], 1.0), writes=[resetm.b])
    k.op("dve", lambda e: e.memset(resetm[:].rearrange("p (c l) -> p c l", l=64)[:, :, 0:1], 0.0), writes=[resetm.b])

    def v3(t, l=64):
        return t[:].rearrange("p (c l) -> p c l", l=l)

    def v4(t):
        return t[:].rearrange("p (a b l) -> p a b l", b=2, l=64)

    tcnt = [0]

    def transpose_to(dst_ap, dst_b, src_ap, src_b, eng="act"):
        r = regT[tcnt[0] % 4]
        tcnt[0] += 1
        k.op("pe", lambda e: e.transpose(r.ap(), src_ap, c["ident"][:]), reads=[src_b, c["ident"].b], writes=[r.b])
        if eng == "act":
            k.op("act", lambda e: e.copy(out=dst_ap, in_=r.ap()), reads=[r.b], writes=[dst_b])
        else:
            k.op("dve", lambda e: e.tensor_copy(out=dst_ap, in_=r.ap()), reads=[r.b], writes=[dst_b])

    acnt = [0]

    def next_acc():
        a = acc[acnt[0] % 2]
        acnt[0] += 1
        return a

    def inproj_fm(col0):
        a = next_acc()

        def mm(e):
            for dc in range(16):
                ins = e.matmul(a.ap(), lhsT=win_sb[:, dc, col0:col0 + 128], rhs=xT[:, dc, :], start=(dc == 0), stop=(dc == 15))
            return ins
        k.op("pe", mm, reads=[xT.b], writes=[a.b])
        return a

    def inproj_tm(tci, col0, ncol, out_reg=None, oc0=0):
        a = out_reg if out_reg is not None else next_acc()

        def mm(e):
            for dc in range(16):
                ins = e.matmul(a.ap(a=oc0, b=oc0 + ncol), lhsT=xT[:, dc, tci * 128:(tci + 1) * 128],
                               rhs=win_sb[:, dc, col0:col0 + ncol], start=(dc == 0), stop=(dc == 15))
            return ins
        k.op("pe", mm, reads=[xT.b], writes=[a.b])
        return a

    out_toks = []
    def body(st, t0):
        if stop <= 1:
            raise _Stop()
        norm_supertile(k, c, hT, nw_sb, hTt, xT, ps_ss, rstd, scrsq, t0, TW, ds_h)
        if stop <= 2:
            raise _Stop()
        body2(st, t0)

    def body2(st, t0):
        if st > 0:
            k.op("dve", lambda e: e.tensor_copy(out=ubuf[:, :, 0:3], in_=ubuf[:, :, TW:TW + 3]), reads=[ubuf.b], writes=[ubuf.b])
        for ch in range(2):
            a = inproj_fm(ch * 128)
            k.op("act", lambda e: e.copy(out=ubuf[:, ch, 3:3 + TW], in_=a.ap()), reads=[a.b], writes=[ubuf.b])
        if stop <= 2.2:
            raise _Stop()
        for ch in range(2):
            k.op("dve", lambda e: e.tensor_scalar(out=cacc[:], in0=ubuf[:, ch, 0:TW], scalar1=cw_sb[:, ch, 0:1], scalar2=None,
                                                  op0=ALU.mult), reads=[ubuf.b, cw_sb.b], writes=[cacc.b])
            for j in range(1, 4):
                k.op("dve", lambda e: e.scalar_tensor_tensor(out=cacc[:], in0=ubuf[:, ch, j:j + TW], scalar=cw_sb[:, ch, j:j + 1],
                                                             in1=cacc[:], op0=ALU.mult, op1=ALU.add),
                     reads=[ubuf.b, cw_sb.b, cacc.b], writes=[cacc.b])
            k.op("act", lambda e: e.activation(out=cT[:, ch, :], in_=cacc[:], func=AF.Silu, bias=cb_sb[:, ch:ch + 1], scale=1.0),
                 reads=[cacc.b, cb_sb.b], writes=[cT.b])
        k.op("dve", lambda e: e.tensor_copy(out=cTb[:], in_=cT[:]), reads=[cT.b], writes=[cTb.b])
        if stop <= 2.5:
            raise _Stop()
        for e_ in range(2):
            a = next_acc()

            def mm(e):
                for d in range(2):
                    ins = e.matmul(a.ap(), lhsT=wq_sb[:, d, e_ * 128:(e_ + 1) * 128], rhs=cTb[:, d, :], start=(d == 0), stop=(d == 1))
                return ins
            k.op("pe", mm, reads=[wq_sb.b, cTb.b], writes=[a.b])
            k.op("act", lambda e: e.copy(out=qT[:, e_, :], in_=a.ap()), reads=[a.b], writes=[qT.b])
            k.op("dve", lambda e: e.tensor_copy(out=qTb[:, e_, :], in_=qT[:, e_, :]), reads=[qT.b], writes=[qTb.b])
            a = next_acc()

            def mm2(e):
                for d in range(2):
                    ins = e.matmul(a.ap(), lhsT=wk_sb[:, d, e_ * 128:(e_ + 1) * 128], rhs=cTb[:, d, :], start=(d == 0), stop=(d == 1))
                return ins
            k.op("pe", mm2, reads=[wk_sb.b, cTb.b], writes=[a.b])
            k.op("dve", lambda e: e.tensor_scalar(out=kTb[:, e_, :], in0=a.ap(), scalar1=0.0625, scalar2=None, op0=ALU.mult),
                 reads=[a.b], writes=[kTb.b])
        if stop <= 3:
            raise _Stop()
        for tci in range(4):
            tsl = slice(tci * 128, (tci + 1) * 128)
            a = inproj_tm(tci, 768, 512)
            k.op("dve", lambda e: e.tensor_copy(out=vext[:, tci, 0:256], in_=a.ap(b=256)), reads=[a.b], writes=[vext.b])
            k.op("act", lambda e: e.activation(out=osig[:, tci, :], in_=a.ap(a=256, b=512), func=AF.Sigmoid), reads=[a.b],
                 writes=[osig.b])
            a = inproj_tm(tci, 1280, 512)
            k.op("dve", lambda e: e.tensor_copy(out=hv[:, tci, :, :].rearrange("p a b -> p (a b)"), in_=a.ap(b=256)), reads=[a.b],
                 writes=[hv.b])
            k.op("act", lambda e: e.activation(out=hgs[:, tci, :], in_=a.ap(a=256, b=512), func=AF.Silu), reads=[a.b],
                 writes=[hgs.b])
            inproj_tm(tci, 1792, 2, out_reg=regA, oc0=2 * tci)
            a = next_acc()

            def mm3(e):
                for d in range(2):
                    ins = e.matmul(a.ap(b=256), lhsT=cTb[:, d, tsl], rhs=wk_sb[:, d, :], start=(d == 0), stop=(d == 1))
                return ins
            k.op("pe", mm3, reads=[wk_sb.b, cTb.b], writes=[a.b])
            k.op("act", lambda e: e.mul(out=ktok[:, tci, :], in_=a.ap(b=256), mul=0.0625), reads=[a.b], writes=[ktok.b])
            for d in range(2):
                transpose_to(ctok[:, tci, d * 128:(d + 1) * 128], ctok.b, cT[:, d, tsl], cT.b, eng="dve")
        if stop <= 4:
            raise _Stop()
        gv = regA.ap().rearrange("p (a b) -> p a b", b=2)
        k.op("act", lambda e: e.activation(out=ge1[:], in_=gv[:, :, 1], func=AF.Exp, scale=-1.0, bias=nfb[:, 0:1]),
             reads=[regA.b, nfb.b], writes=[ge1.b])
        k.op("act", lambda e: e.activation(out=ge1[:], in_=ge1[:], func=AF.Ln, scale=1.0, bias=c["one1"][:, 0:1]),
             reads=[ge1.b, c["one1"].b], writes=[ge1.b])
        k.op("dve", lambda e: e.tensor_scalar(out=logf[:], in0=ge1[:], scalar1=-1.0, scalar2=None, op0=ALU.mult), reads=[ge1.b],
             writes=[logf.b])
        k.op("dve", lambda e: e.tensor_scalar(out=ig[:], in0=gv[:, :, 0], scalar1=gb_sb[:, 0:1], scalar2=None, op0=ALU.add),
             reads=[regA.b, gb_sb.b], writes=[ig.b])
        def ml_stream():
            for tci in range(4 if do_ml >= 2 else 0):
                tsl = slice(tci * 128, (tci + 1) * 128)
                k.op("dve", lambda e: e.tensor_scalar(out=lfb[:], in0=c["ones"][:], scalar1=logf[:, tci:tci + 1], scalar2=None,
                                                      op0=ALU.mult), reads=[logf.b, c["ones"].b], writes=[lfb.b])

                def mmb(e):
                    e.matmul(regB.ap(b=128), lhsT=lfb[:], rhs=c["causal"][:], start=True, stop=True)
                    return e.matmul(regB.ap(a=128, b=129), lhsT=c["causal"][:], rhs=logf[:, tci:tci + 1], start=True, stop=True)
                k.op("pe", mmb, reads=[lfb.b, c["causal"].b, logf.b], writes=[regB.b])
                k.op("dve", lambda e: e.tensor_tensor(out=bias_s[:], in0=ig[:, tci:tci + 1], in1=regB.ap(a=128, b=129), op=ALU.subtract),
                     reads=[ig.b, regB.b], writes=[bias_s.b])
                k.op("act", lambda e: e.activation(out=DT[:], in_=regB.ap(b=128), func=AF.Exp, bias=bias_s[:, 0:1], scale=1.0),
                     reads=[regB.b, bias_s.b], writes=[DT.b])
                k.op("act", lambda e: e.activation(out=Eb[:], in_=regB.ap(b=128), func=AF.Exp), reads=[regB.b], writes=[Eb.b])
                k.op("dve", lambda e: e.tensor_copy(out=small[:, 0:1], in_=regB.ap(a=127, b=128)), reads=[regB.b], writes=[small.b])
                if stop <= 6.1:
                    raise _Stop()
                k.op("pool", lambda e: e.tensor_tensor(out=Dm[:], in0=DT[:], in1=c["causal"][:], op=ALU.mult),
                     reads=[DT.b, c["causal"].b], writes=[Dm.b])
                if stop <= 6.2:
                    raise _Stop()

                def mms(e):
                    for e_ in range(2):
                        ins = e.matmul(regC.ap(), lhsT=kTb[:, e_, tsl], rhs=qTb[:, e_, tsl], start=(e_ == 0), stop=(e_ == 1))
                    return ins
                k.op("pe", mms, reads=[kTb.b, qTb.b], writes=[regC.b])
                k.op("dve", lambda e: e.tensor_tensor(out=PT[:], in0=regC.ap(), in1=Dm[:], op=ALU.mult), reads=[regC.b, Dm.b],
                     writes=[PT.b])
                for e_ in range(2):
                    k.op("dve", lambda e: e.tensor_tensor(out=qs[:, e_, :], in0=qT[:, e_, tsl], in1=Eb[:], op=ALU.mult),
                         reads=[qT.b, Eb.b], writes=[qs.b])

                def mmnd(e):
                    e.matmul(regND.ap(), lhsT=PT[:], rhs=vext[:, tci, :], start=True, stop=False)
                    e.matmul(regND.ap(), lhsT=qs[:, 0, :], rhs=Cb[:, 0, :], start=False, stop=False)
                    return e.matmul(regND.ap(), lhsT=qs[:, 1, :], rhs=Cb[:, 1, :], start=False, stop=True)
                k.op("pe", mmnd, reads=[PT.b, vext.b, qs.b, Cb.b], writes=[regND.b])
                if debug and st == 0 and tci == 1:
                    dump("Cst", Cst, Cst[:].rearrange("p a b -> p (a b)"), 528)
                    dump("logf", logf, logf[:], 4)
                    dump("ig", ig, ig[:], 4)
                    dump("DT", DT, DT[:], 128)
                    dump("Eb", Eb, Eb[:], 128)
                    dump("ktok1", ktok, ktok[:, 1, :], 256)
                    dump("qT", qT, qT[:, 0, 128:256], 128)
                if stop <= 6.3:
                    raise _Stop()
                k.op("act", lambda e: e.activation(out=small[:, 1:2], in_=regND.ap(a=256, b=257), func=AF.Abs), reads=[regND.b],
                     writes=[small.b])
                k.op("dve", lambda e: e.tensor_scalar(out=small[:, 1:2], in0=small[:, 1:2], scalar1=1.0, scalar2=None, op0=ALU.max),
                     reads=[small.b], writes=[small.b])
                k.op("dve", lambda e: e.reciprocal(out=small[:, 1:2], in_=small[:, 1:2]), reads=[small.b], writes=[small.b])
                k.op("dve", lambda e: e.tensor_scalar(out=hn[:], in0=regND.ap(b=256), scalar1=small[:, 1:2], scalar2=None, op0=ALU.mult),
                     reads=[regND.b, small.b], writes=[hn.b])
                k.op("act", lambda e: e.activation(out=junk[:], in_=hn[:], func=AF.Square, accum_out=small[:, 2:3]), reads=[hn.b],
                     writes=[junk.b, small.b])
                k.op("act", lambda e: e.activation(out=small[:, 3:4], in_=small[:, 2:3], func=AF.Sqrt, scale=1.0 / 256, bias=c["eps"][:, 0:1]),
                     reads=[small.b, c["eps"].b], writes=[small.b])
                k.op("dve", lambda e: e.reciprocal(out=small[:, 3:4], in_=small[:, 3:4]), reads=[small.b], writes=[small.b])
                k.op("dve", lambda e: e.scalar_tensor_tensor(out=hm[:], in0=hn[:], scalar=small[:, 3:4], in1=mln_sb[:], op0=ALU.mult,
                                                             op1=ALU.mult), reads=[hn.b, small.b, mln_sb.b], writes=[hm.b])
                k.op("pool", lambda e: e.tensor_tensor(out=t1[:], in0=ctok[:, tci, :], in1=skp_sb[:], op=ALU.mult),
                     reads=[ctok.b, skp_sb.b], writes=[t1.b])
                k.op("dve", lambda e: e.tensor_tensor(out=t1[:], in0=t1[:], in1=hm[:], op=ALU.add), reads=[t1.b, hm.b], writes=[t1.b])
                k.op("dve", lambda e: e.tensor_tensor(out=yml[:], in0=t1[:], in1=osig[:, tci, :], op=ALU.mult), reads=[t1.b, osig.b],
                     writes=[yml.b])
                for d in range(2):
                    transpose_to(ystage[:, d, tsl], ystage.b, yml[:, d * 128:(d + 1) * 128], yml.b, eng="act")
                if debug and st == 0 and tci == 1:
                    dump("hn", hn, hn[:], 256)
                    dump("small", small, small[:], 8)
                    dump("hm", hm, hm[:], 256)
                    dump("yml", yml, yml[:], 256)
                if stop <= 6.4:
                    raise _Stop()
                k.op("act", lambda e: e.activation(out=small[:, 4:5], in_=bias_s[:], func=AF.Exp, bias=small[:, 0:1], scale=1.0),
                     reads=[bias_s.b, small.b], writes=[small.b])
                k.op("act", lambda e: e.activation(out=small[:, 5:6], in_=small[:, 0:1], func=AF.Exp), reads=[small.b], writes=[small.b])
                if stop <= 6.5:
                    raise _Stop()
                k.op("dve", lambda e: e.tensor_scalar(out=ka[:], in0=ktok[:, tci, :], scalar1=small[:, 4:5], scalar2=None, op0=ALU.mult),
                     reads=[ktok.b, small.b], writes=[ka.b])
                if stop <= 6.6:
                    raise _Stop()
                for kc in range(2):
                    k.op("pe", lambda e: e.matmul(regU[kc].ap(), lhsT=ka[:, kc * 128:(kc + 1) * 128], rhs=vext[:, tci, :], start=True,
                                                  stop=True), reads=[ka.b, vext.b], writes=[regU[kc].b])
                    k.op("dve", lambda e: e.scalar_tensor_tensor(out=Cst[:, kc, :], in0=Cst[:, kc, :], scalar=small[:, 5:6],
                                                                 in1=regU[kc].ap(), op0=ALU.mult, op1=ALU.add),
                         reads=[Cst.b, small.b, regU[kc].b], writes=[Cst.b])
                if stop <= 6.7 and kc == 1:
                    raise _Stop()
                if stop <= 6.8:
                    raise _Stop()
                k.op("act", lambda e: e.copy(out=Cb[:], in_=Cst[:]), reads=[Cst.b], writes=[Cb.b])

        def hg_stream():
            for hd in range(2 if do_hg >= 1 else 0):
                az = inproj_fm(512 + hd * 128)
                aq = inproj_fm(256 + hd * 128)
                k.op("act", lambda e: e.activation(out=tA[:], in_=az.ap(), func=AF.Sigmoid, scale=-1.0), reads=[az.b], writes=[tA.b])
                k.op("act", lambda e: e.activation(out=tB[:], in_=az.ap(), func=AF.Exp, scale=-1.0), reads=[az.b], writes=[tB.b])
                k.op("act", lambda e: e.activation(out=sqt[:], in_=aq.ap(), func=AF.Silu), reads=[aq.b], writes=[sqt.b])
                k.op("dve", lambda e: e.tensor_scalar(out=k2[:], in0=tA[:], scalar1=oml[:, hd:hd + 1], scalar2=None, op0=ALU.mult),
                     reads=[tA.b, oml.b], writes=[k2.b])
                k.op("dve", lambda e: e.tensor_scalar(out=tA[:], in0=k2[:], scalar1=HG_MAX_K, scalar2=None, op0=ALU.min), reads=[k2.b],
                     writes=[tA.b])
                k.op("act", lambda e: e.activation(out=lf1[:], in_=tA[:], func=AF.Ln, scale=-1.0, bias=c["one1"][:, 0:1]),
                     reads=[tA.b, c["one1"].b], writes=[lf1.b])
                k.op("act", lambda e: e.activation(out=tB[:], in_=tB[:], func=AF.Ln, scale=1.0, bias=c["one1"][:, 0:1]),
                     reads=[tB.b, c["one1"].b], writes=[tB.b])
                k.op("dve", lambda e: e.scalar_tensor_tensor(out=lgf[:], in0=tB[:], scalar=-1.0, in1=lf1[:], op0=ALU.mult, op1=ALU.max),
                     reads=[tB.b, lf1.b], writes=[lgf.b])
                k.op("dve", lambda e: e.tensor_tensor_scan(out=bt[:], data0=resetm[:], data1=lgf[:], initial=0.0, op0=ALU.mult,
                                                           op1=ALU.add), reads=[resetm.b, lgf.b], writes=[bt.b])
                k.op("dve", lambda e: e.tensor_tensor(out=v3(brel), in0=v3(bt), in1=v3(bt)[:, :, 31:32].to_broadcast([128, 8, 64]),
                                                      op=ALU.subtract), reads=[bt.b], writes=[brel.b])
                k.op("act", lambda e: e.activation(out=tA[:], in_=brel[:], func=AF.Exp), reads=[brel.b], writes=[tA.b])
                k.op("act", lambda e: e.activation(out=tC[:], in_=brel[:], func=AF.Exp, scale=-1.0), reads=[brel.b], writes=[tC.b])
                for par in range(2):
                    k.op("dve", lambda e: e.tensor_tensor(out=qz[par][:, :, par * 64:(par + 1) * 64], in0=v4(sqt)[:, :, par, :],
                                                          in1=v4(tA)[:, :, par, :], op=ALU.mult), reads=[sqt.b, tA.b], writes=[qz[par].b])
                    k.op("dve", lambda e: e.tensor_tensor(out=kz[par][:, :, par * 64:(par + 1) * 64], in0=v4(k2)[:, :, par, :],
                                                          in1=v4(tC)[:, :, par, :], op=ALU.mult), reads=[k2.b, tC.b], writes=[kz[par].b])
                k.op("act", lambda e: e.activation(out=tA[:], in_=bt[:], func=AF.Exp), reads=[bt.b], writes=[tA.b])
                for par in range(2):
                    k.op("dve", lambda e: e.tensor_tensor(out=qbz[par][:, :, par * 64:(par + 1) * 64], in0=v4(sqt)[:, :, par, :],
                                                          in1=v4(tA)[:, :, par, :], op=ALU.mult), reads=[sqt.b, tA.b], writes=[qbz[par].b])
                k.op("dve", lambda e: e.tensor_tensor(out=v3(tC), in0=v3(bt)[:, :, 63:64].to_broadcast([128, 8, 64]), in1=v3(bt),
                                                      op=ALU.subtract), reads=[bt.b], writes=[tC.b])
                k.op("act", lambda e: e.activation(out=tC[:], in_=tC[:], func=AF.Exp), reads=[tC.b], writes=[tC.b])
                k.op("dve", lambda e: e.tensor_tensor(out=kgT[:], in0=k2[:], in1=tC[:], op=ALU.mult), reads=[k2.b, tC.b], writes=[kgT.b])
                k.op("act", lambda e: e.activation(out=eg8[:], in_=v3(bt)[:, :, 63], func=AF.Exp), reads=[bt.b], writes=[eg8.b])
                for tl in range(4):
                    r = regT[tcnt[0] % 4]
                    tcnt[0] += 1
                    k.op("pe", lambda e: e.transpose(r.ap(), kgT[:, tl * 128:(tl + 1) * 128], c["ident"][:]), reads=[kgT.b, c["ident"].b],
                         writes=[r.b])
                    k.op("act", lambda e: e.copy(out=kgz[0][0:64, tl, :], in_=r.ap(rows=slice(0, 64))), reads=[r.b], writes=[kgz[0].b])
                    k.op("act", lambda e: e.copy(out=kgz[1][64:128, tl, :], in_=r.ap(rows=slice(64, 128))), reads=[r.b], writes=[kgz[1].b])
                S = Sst[hd]
                for tl in range(4 if do_hg >= 2 else 0):
                    def mma(e):
                        e.matmul(r7A.ap(), lhsT=kz[0][:, tl, :], rhs=qz[0][:, tl, :], start=True, stop=False)
                        return e.matmul(r7A.ap(), lhsT=kz[1][:, tl, :], rhs=qz[1][:, tl, :], start=False, stop=True)
                    k.op("pe", mma, reads=[kz[0].b, kz[1].b, qz[0].b, qz[1].b], writes=[r7A.b])
                    k.op("dve", lambda e: e.tensor_tensor(out=Am[:], in0=r7A.ap(), in1=c["causal"][:], op=ALU.mult),
                         reads=[r7A.b, c["causal"].b], writes=[Am.b])
                    k.op("pe", lambda e: e.matmul(r7C.ap(), lhsT=kgz[0][:, tl, :], rhs=hv[:, tl, hd, :], start=True, stop=True),
                         reads=[kgz[0].b, hv.b], writes=[r7C.b])
                    k.op("pe", lambda e: e.matmul(r7D.ap(), lhsT=kgz[1][:, tl, :], rhs=hv[:, tl, hd, :], start=True, stop=True),
                         reads=[kgz[1].b, hv.b], writes=[r7D.b])
                    k.op("dve", lambda e: e.scalar_tensor_tensor(out=S[:], in0=S[:], scalar=eg8[:, 2 * tl:2 * tl + 1], in1=r7C.ap(),
                                                                 op0=ALU.mult, op1=ALU.add), reads=[S.b, eg8.b, r7C.b], writes=[S.b])
                    k.op("act", lambda e: e.copy(out=Sb[hd][1][:], in_=S[:]), reads=[S.b], writes=[Sb[hd][1].b])

                    def mmo(e):
                        e.matmul(r7B.ap(), lhsT=Am[:], rhs=hv[:, tl, hd, :], start=True, stop=False)
                        e.matmul(r7B.ap(), lhsT=qbz[0][:, tl, :], rhs=Sb[hd][0][:], start=False, stop=False)
                        return e.matmul(r7B.ap(), lhsT=qbz[1][:, tl, :], rhs=Sb[hd][1][:], start=False, stop=True)
                    k.op("pe", mmo, reads=[Am.b, hv.b, qbz[0].b, qbz[1].b, Sb[hd][0].b, Sb[hd][1].b], writes=[r7B.b])
                    k.op("dve", lambda e: e.scalar_tensor_tensor(out=S[:], in0=S[:], scalar=eg8[:, 2 * tl + 1:2 * tl + 2], in1=r7D.ap(),
                                                                 op0=ALU.mult, op1=ALU.add), reads=[S.b, eg8.b, r7D.b], writes=[S.b])
                    k.op("act", lambda e: e.copy(out=Sb[hd][0][:], in_=S[:]), reads=[S.b], writes=[Sb[hd][0].b])
                    k.op("act", lambda e: e.copy(out=o_sb[:], in_=r7B.ap()), reads=[r7B.b], writes=[o_sb.b])
                    k.op("act", lambda e: e.activation(out=junk2[:], in_=o_sb[:], func=AF.Square, accum_out=small2[:, 6:7]),
                         reads=[o_sb.b], writes=[junk2.b, small2.b])
                    k.op("act", lambda e: e.activation(out=small2[:, 7:8], in_=small2[:, 6:7], func=AF.Sqrt, scale=1.0 / 128,
                                                       bias=c["eps"][:, 0:1]), reads=[small2.b, c["eps"].b], writes=[small2.b])
                    k.op("dve", lambda e: e.reciprocal(out=small2[:, 7:8], in_=small2[:, 7:8]), reads=[small2.b], writes=[small2.b])
                    k.op("dve", lambda e: e.scalar_tensor_tensor(out=o2n[:], in0=o_sb[:], scalar=small2[:, 7:8], in1=hgn_sb[:, hd, :],
                                                                 op0=ALU.mult, op1=ALU.mult), reads=[o_sb.b, small2.b, hgn_sb.b], writes=[o2n.b])
                    k.op("pool", lambda e: e.tensor_tensor(out=yh[:], in0=o2n[:], in1=hgs[:, tl, hd * 128:(hd + 1) * 128], op=ALU.mult),
                         reads=[o2n.b, hgs.b], writes=[yh.b])
                    transpose_to(ystage[:, 2 + hd, tl * 128:(tl + 1) * 128], ystage.b, yh[:], yh.b, eng="act")

        try:
            interleave(k, [ml_stream, hg_stream])
        except _Stop:
            pass

    for st in range(NST):
        t0 = st * TW
        try:
            body(st, t0)
        except _Stop:
            pass
        tk = k.op("sp", lambda e: e.dma_start(out=yT.rearrange("c p t -> p c t")[:, :, t0:t0 + TW], in_=ystage[:]), reads=[ystage.b],
                  dsem=ds_y)
        out_toks.append(tk)
    k.finish(out_toks)
    return nc


def fm(a):
    T, C = a.shape
    return np.ascontiguousarray(a.T.reshape(C // 128, 128, T))


def unfm(aT):
    n, p, T = aT.shape
    return np.ascontiguousarray(aT.reshape(n * p, T).T)


def pvec(w):
    return np.ascontiguousarray(w.reshape(-1, 128).T)


def rep(w):
    return np.ascontiguousarray(np.broadcast_to(w[None], (128,) + w.shape))


def ab_core_inputs(I, layer, hgp, hT_b):
    j = layer // 2
    h = hgp
    W = I["ab_w_in"][j]
    o_u, o_v, o_o, o_i, o_f, o_hq, o_hf, o_hi, o_hg = 0, 1024, 2048, 3072, 3076, 3080, 4104, 5128, 6152
    sl = lambda o, n, i: W[:, o + i * n:o + (i + 1) * n]
    win = np.concatenate([sl(o_u, 256, h), sl(o_hq, 256, h), sl(o_hf, 256, h), sl(o_v, 256, h), sl(o_o, 256, h),
                          sl(o_hi, 256, h), sl(o_hg, 256, h), W[:, o_i + h:o_i + h + 1], W[:, o_f + h:o_f + h + 1]], axis=1)
    cwf = I["ml_conv_w"][j][:, h * 256:(h + 1) * 256]
    cw = np.ascontiguousarray(cwf.reshape(4, 2, 128).transpose(2, 1, 0))
    cb = np.ascontiguousarray(I["ml_conv_b"][j][h * 256:(h + 1) * 256].reshape(2, 128).T)
    lbl = np.ascontiguousarray(I["hg_lb_logits"][:, h * 256:(h + 1) * 256].reshape(2, 2, 128).transpose(2, 1, 0))
    return {
        "hT": hT_b, "nw": pvec(I["mix_norm"][layer]), "win": np.ascontiguousarray(win), "cw": cw, "cb": cb,
        "wq": np.ascontiguousarray(I["ml_wq"][j][h]), "wk": np.ascontiguousarray(I["ml_wk"][j][h]),
        "gb": rep(np.array([I["ml_i_bias"][j][h], I["ml_f_bias"][j][h]], np.float32)),
        "mln": rep(I["ml_out_norm"][j][h]), "skp": rep(I["ml_skip"][j][h * 256:(h + 1) * 256]),
        "lbl": lbl, "hgn": rep(I["hg_out_norm"][j][2 * h:2 * h + 2]),
    }


import math
from contextlib import ExitStack

NSA_BIG = 200.0
ROPE_INVF = np.power(np.float32(10000.0), -np.arange(64, dtype=np.float32) / 64).astype(np.float32)


def kb_barrier(k):
    toks = [Tok(k.sem[e], k.cnt[e], "E" + e, e) for e in k.engs if k.cnt[e] > 0]
    toks += [Tok(d.h, d.val, d.key, None) for d in k._all_dsems if d.val > 0]
    for e in k.engs:
        for t in toks:
            if t.eng != e:
                k._wait(e, t)


def build_nsa(T=8192, stop=99, debug=False):
    TW = 256
    NST = T // TW
    NT = T // 128
    NCB = (T - 32) // 16 + 1
    NCT = (NCB + 127) // 128
    NCOL = 1292
    scale = 128 ** -0.5
    nc = bass.Bass("TRN2", target_bir_lowering=False)
    k = KB(nc)
    k._all_dsems = []
    _ds = k.dsem

    def dsem2(name=None):
        d = _ds(name)
        k._all_dsems.append(d)
        return d
    k.dsem = dsem2

    def dram(name, shape, dt=F32, kind="ExternalInput"):
        return nc.dram_tensor(name, list(shape), dt, kind=kind).ap()
    hT = dram("hT", [16, 128, T])
    nw = dram("nw", [128, 16])
    win = dram("win", [2048, NCOL])
    qnw = dram("qnw", [128, 512])
    knw = dram("knw", [128, 3, 128])
    posT = dram("posT", [128, 2, 32])
    w1 = dram("w1", [2, 4096, 256])
    b1 = dram("b1", [128, 2, 2])
    w2 = dram("w2", [2, 256, 128])
    gbias = dram("gbias", [128, 12])
    yT = dram("yT", [4, 128, T], BF16, kind="ExternalOutput")
    dbg = dram("dbg", [128, 8192], F32, kind="ExternalOutput") if debug else None
    dbg_pos = [0]
    dbg_map = {}
    nc._dbg_map = dbg_map

    def dump(name, tt, ap, n):
        if not debug or name in dbg_map:
            return
        c0 = dbg_pos[0]
        dbg_pos[0] += n
        dbg_map[name] = (c0, n)
        k.op("sp", lambda e: e.dma_start(out=dbg[:, c0:c0 + n], in_=ap), reads=[tt.b], dsem=k.dsem())

    c = make_consts(k)
    win_v = win.rearrange("(dc p) f -> p dc f", p=128)
    ps = [nc.alloc_psum_tensor(f"ps{i}", [128, 512], F32) for i in range(8)]
    accS = [PReg(k, ps[i], 0, 512, f"accS{i}") for i in range(2)]
    _oset = [PReg(k, ps[2 + h], 0, 130, f"o_{h}") for h in range(4)]
    oreg = [_oset, _oset]
    impT = PReg(k, ps[6], 0, 512, "impT")
    ps_ss = impT
    regT = [PReg(k, ps[7], i * 128, (i + 1) * 128, f"regT{i}") for i in range(4)]
    acnt = [0]
    tcnt = [0]

    def next_acc():
        a = accS[acnt[0] % 2]
        acnt[0] += 1
        return a

    def next_regT():
        r = regT[tcnt[0] % 4]
        tcnt[0] += 1
        return r

    def ld(name, shape, src, dt=F32, eng="sp"):
        t = sb(k, name, shape, dt)
        k.op(eng, lambda e: e.dma_start(out=t[:], in_=src), writes=[t.b], dsem=k.dsem())
        return t

    nw_sb = ld("nw_sb", [128, 16], nw)
    qnw_sb = ld("qnw_sb", [128, 512], qnw)
    knw_sb = ld("knw_sb", [128, 3, 128], knw)
    b1_sb = ld("b1_sb", [128, 2, 2], b1)
    gb_sb = ld("gb_sb", [128, 12], gbias)
    hTt = sb(k, "hTt", [128, 16, TW], F32)
    ds_h = k.dsem()
    xT = sb(k, "xT", [128, 16, TW], BF16)
    xT2 = sb(k, "xT2", [128, 16, TW], BF16)
    rstd = sb(k, "rstd", [128, TW], F32)
    scrsq = [sb(k, f"scrsq{i}", [128, TW], F32) for i in range(2)]
    kcmpT = sb(k, "kcmpT", [128, NCT * 128], BF16)
    vcmp = sb(k, "vcmp", [128, NCT, 130], BF16)
    cover = sb(k, "cover", [128, NCT, 128], BF16)
    identb = sb(k, "identb", [128, 128], BF16)
    k.op("dve", lambda e: e.tensor_copy(out=identb[:], in_=c["ident"][:]), reads=[c["ident"].b], writes=[identb.b])
    k.op("dve", lambda e: e.memset(kcmpT[:], 0.0), writes=[kcmpT.b])
    k.op("dve", lambda e: e.memset(vcmp[:], 0.0), writes=[vcmp.b])
    k.op("dve", lambda e: e.memset(vcmp[:, :, 128:129], 1.0), writes=[vcmp.b])
    invf = sb(k, "invf", [128, 64], F32)
    for i in range(64):
        k.op("dve", lambda e: e.memset(invf[:, i:i + 1], float(ROPE_INVF[i])), writes=[invf.b])
    pidx_i = sb(k, "pidx_i", [128, 1], I32)
    k.op("pool", lambda e: e.iota(pidx_i[:], pattern=[[0, 1]], base=0, channel_multiplier=1), writes=[pidx_i.b])
    pidx = sb(k, "pidx", [128, 1], F32)
    k.op("dve", lambda e: e.tensor_copy(out=pidx[:], in_=pidx_i[:]), reads=[pidx_i.b], writes=[pidx.b])
    pcol = sb(k, "pcol", [128, 1], F32)
    ang = sb(k, "ang", [128, 64], F32)
    rr = sb(k, "rr", [128, 64], F32)
    rf = sb(k, "rf", [128, 64], F32)
    ri = sb(k, "ri", [128, 64], I32)
    cos_t = sb(k, "cos_t", [128, 64], F32)
    sin_t = sb(k, "sin_t", [128, 64], F32)
    rtmp = sb(k, "rtmp", [128, 4, 64], F32)
    TWO_PI = 2 * math.pi

    def rope_tables(mult, add):
        k.op("dve", lambda e: e.tensor_scalar(out=pcol[:], in0=pidx[:], scalar1=float(mult), scalar2=float(add), op0=ALU.mult,
                                              op1=ALU.add), reads=[pidx.b], writes=[pcol.b])
        k.op("dve", lambda e: e.tensor_scalar(out=ang[:], in0=invf[:], scalar1=pcol[:, 0:1], scalar2=None, op0=ALU.mult),
             reads=[invf.b, pcol.b], writes=[ang.b])
        for (off, dst) in ((0.0, sin_t), (math.pi / 2, cos_t)):
            k.op("dve", lambda e: e.tensor_scalar(out=rr[:], in0=ang[:], scalar1=off, scalar2=None, op0=ALU.add), reads=[ang.b],
                 writes=[rr.b])
            k.op("dve", lambda e: e.tensor_scalar(out=rf[:], in0=rr[:], scalar1=1.0 / TWO_PI, scalar2=None, op0=ALU.mult),
                 reads=[rr.b], writes=[rf.b])
            k.op("dve", lambda e: e.tensor_copy(out=ri[:], in_=rf[:]), reads=[rf.b], writes=[ri.b])
            k.op("dve", lambda e: e.tensor_copy(out=rf[:], in_=ri[:]), reads=[ri.b], writes=[rf.b])
            k.op("dve", lambda e: e.scalar_tensor_tensor(out=rr[:], in0=rf[:], scalar=-TWO_PI, in1=rr[:], op0=ALU.mult, op1=ALU.add),
                 reads=[rf.b, rr.b], writes=[rr.b])
            k.op("dve", lambda e: e.tensor_scalar(out=rf[:], in0=rr[:], scalar1=math.pi, scalar2=None, op0=ALU.is_gt), reads=[rr.b],
                 writes=[rf.b])
            k.op("dve", lambda e: e.scalar_tensor_tensor(out=rr[:], in0=rf[:], scalar=-TWO_PI, in1=rr[:], op0=ALU.mult, op1=ALU.add),
                 reads=[rf.b, rr.b], writes=[rr.b])
            k.op("act", lambda e: e.activation(out=dst[:], in_=rr[:], func=AF.Sin), reads=[rr.b], writes=[dst.b])

    def apply_rope(dst, src, H):
        cb = cos_t[:].rearrange("p (o f) -> p o f", o=1).to_broadcast([128, H, 64])
        sbb = sin_t[:].rearrange("p (o f) -> p o f", o=1).to_broadcast([128, H, 64])
        x1, x2 = src[:, 0:H, 0:64], src[:, 0:H, 64:128]
        tm = rtmp[:, 0:H, :]
        k.op("dve", lambda e: e.tensor_tensor(out=tm, in0=x2, in1=sbb, op=ALU.mult), reads=[src.b, sin_t.b], writes=[rtmp.b])
        k.op("dve", lambda e: e.tensor_tensor(out=dst[:, 0:H, 0:64], in0=x1, in1=cb, op=ALU.mult), reads=[src.b, cos_t.b], writes=[dst.b])
        k.op("dve", lambda e: e.tensor_tensor(out=dst[:, 0:H, 0:64], in0=dst[:, 0:H, 0:64], in1=tm, op=ALU.subtract),
             reads=[dst.b, rtmp.b], writes=[dst.b])
        k.op("dve", lambda e: e.tensor_tensor(out=tm, in0=x1, in1=sbb, op=ALU.mult), reads=[src.b, sin_t.b], writes=[rtmp.b])
        k.op("dve", lambda e: e.tensor_tensor(out=dst[:, 0:H, 64:128], in0=x2, in1=cb, op=ALU.mult), reads=[src.b, cos_t.b], writes=[dst.b])
        k.op("dve", lambda e: e.tensor_tensor(out=dst[:, 0:H, 64:128], in0=dst[:, 0:H, 64:128], in1=tm, op=ALU.add),
             reads=[dst.b, rtmp.b], writes=[dst.b])

    small = sb(k, "small", [128, 16], F32)
    junk = sb(k, "junk", [128, 128], F32)
    kn = sb(k, "kn", [128, 4, 128], F32)
    kr = sb(k, "kr", [128, 4, 128], F32)

    def rms_heads(src_reg, col0, H, wfn, post_scale=1.0, nrows=128):
        rows = slice(0, nrows)
        for h in range(H):
            k.op("act", lambda e: e.activation(out=junk[rows, :], in_=src_reg.ap(rows, col0[h], col0[h] + 128), func=AF.Square,
                                               accum_out=small[rows, h:h + 1]), reads=[src_reg.b], writes=[junk.b, small.b])
        k.op("act", lambda e: e.activation(out=small[rows, 4:4 + H], in_=small[rows, 0:H], func=AF.Sqrt, scale=1.0 / 128,
                                           bias=c["eps"][rows, 0:1]), reads=[small.b, c["eps"].b], writes=[small.b])
        k.op("dve", lambda e: e.reciprocal(out=small[rows, 4:4 + H], in_=small[rows, 4:4 + H]), reads=[small.b], writes=[small.b])
        if post_scale != 1.0:
            k.op("dve", lambda e: e.tensor_scalar(out=small[rows, 4:4 + H], in0=small[rows, 4:4 + H], scalar1=post_scale, scalar2=None,
                                                  op0=ALU.mult), reads=[small.b], writes=[small.b])
        for h in range(H):
            wt, wap = wfn(h)
            k.op("dve", lambda e: e.scalar_tensor_tensor(out=kn[rows, h, :], in0=src_reg.ap(rows, col0[h], col0[h] + 128),
                                                         scalar=small[rows, 4 + h:5 + h], in1=wap, op0=ALU.mult, op1=ALU.mult),
                 reads=[src_reg.b, small.b, wt.b], writes=[kn.b])

    out_toks = []
    es = ExitStack()

    def sbs(name, shape, dt):
        return TT(k, es.enter_context(nc.sbuf_tensor(name, list(shape), dt)), name)

    win_a = sbs("win_a", [128, 16, 256], BF16)
    k.op("pool", lambda e: e.dma_start(out=win_a[:], in_=win_v[:, :, 0:256]), writes=[win_a.b], dsem=k.dsem())
    kvT = [sbs(f"kvT{i}", [128, T], BF16) for i in range(2)]
    w1_sb = sbs("w1_sb", [128, 32, 256], BF16)
    w2_sb = sbs("w2_sb", [128, 2, 2, 128], BF16)
    posT_sb = sbs("posT_sb", [128, 2, 32], BF16)
    hsil = sbs("hsil", [128, 2, 128], BF16)
    biasv = sbs("biasv", [128, 2], F32)
    k.op("pool", lambda e: e.dma_start(out=posT_sb[:], in_=posT), writes=[posT_sb.b], dsem=k.dsem())
    for kv in range(2):
        k.op("pool", lambda e: e.dma_start(out=w2_sb[:, kv, :, :], in_=w2[kv].rearrange("(c p) e -> p c e", p=128)), writes=[w2_sb.b],
             dsem=k.dsem())
    xT_dram = nc.dram_tensor("xT_scratch", [NST, 128, 16 * TW], BF16).ap()
    xd_b = k.bufs(NST, "xd")
    ds_xs = k.dsem()
    xTs = [xT, xT2]
    ds_xl = [k.dsem(), k.dsem()]

    def load_xT(st):
        xt = xTs[st % 2]
        k.op("sp", lambda e: e.dma_start(out=xt[:].rearrange("p a b -> p (a b)"), in_=xT_dram[st]), reads=[xd_b[st]], writes=[xt.b],
             dsem=ds_xl[st % 2])

    for st in range(NST):
        t0 = st * TW
        norm_supertile(k, c, hT, nw_sb, hTt, xT, ps_ss, rstd, scrsq, t0, TW, ds_h)
        k.op("sp", lambda e: e.dma_start(out=xT_dram[st], in_=xT[:].rearrange("p a b -> p (a b)")), reads=[xT.b], writes=[xd_b[st]],
             dsem=ds_xs)
        for kv in range(2):
            a = next_acc()

            def mm(e):
                for dc in range(16):
                    ins = e.matmul(a.ap(b=TW), lhsT=win_a[:, dc, kv * 128:(kv + 1) * 128], rhs=xT[:, dc, :], start=(dc == 0), stop=(dc == 15))
                return ins
            k.op("pe", mm, reads=[win_a.b, xT.b], writes=[a.b])
            k.op("act", lambda e: e.copy(out=kvT[kv][:, t0:t0 + TW], in_=a.ap(b=TW)), reads=[a.b], writes=[kvT[kv].b])
    ds_w1 = k.dsem()
    for kv in range(2):
        k.op("pool", lambda e: e.dma_start(out=w1_sb[:], in_=w1[kv].rearrange("(l p) h -> p l h", p=128)), writes=[w1_sb.b], dsem=ds_w1)
        for hc in range(2):
            r = next_regT()

            def mmp(e):
                for l in range(32):
                    ins = e.matmul(r.ap(b=1), lhsT=w1_sb[:, l, hc * 128:(hc + 1) * 128], rhs=posT_sb[:, kv, l:l + 1], start=(l == 0),
                                   stop=(l == 31))
                return ins
            k.op("pe", mmp, reads=[w1_sb.b, posT_sb.b], writes=[r.b])
            k.op("dve", lambda e: e.tensor_tensor(out=biasv[:, hc:hc + 1], in0=r.ap(b=1), in1=b1_sb[:, kv, hc:hc + 1], op=ALU.add),
                 reads=[r.b, b1_sb.b], writes=[biasv.b])
        for nti in range(NCT):
            nn = min(128, NCB - 128 * nti)
            for hc in range(2):
                r = next_regT()

                def mmh(e):
                    for l in range(32):
                        s0 = 16 * 128 * nti + l
                        ins = e.matmul(r.ap(b=nn), lhsT=w1_sb[:, l, hc * 128:(hc + 1) * 128], rhs=kvT[kv][:, s0:s0 + 16 * (nn - 1) + 1:16],
                                       start=(l == 0), stop=(l == 31))
                    return ins
                k.op("pe", mmh, reads=[w1_sb.b, kvT[kv].b], writes=[r.b])
                k.op("act", lambda e: e.activation(out=hsil[:, hc, 0:nn], in_=r.ap(b=nn), func=AF.Silu, bias=biasv[:, hc:hc + 1], scale=1.0),
                     reads=[r.b, biasv.b], writes=[hsil.b])
            r = next_regT()

            def mmo(e):
                for hc in range(2):
                    ins = e.matmul(r.ap(rows=slice(0, nn)), lhsT=hsil[:, hc, 0:nn], rhs=w2_sb[:, kv, hc, :], start=(hc == 0), stop=(hc == 1))
                return ins
            k.op("pe", mmo, reads=[hsil.b, w2_sb.b], writes=[r.b])
            if kv == 0:
                rms_heads(r, [0], 1, lambda h: (knw_sb, knw_sb[0:nn, 0, :]), nrows=nn)
                rope_tables(16.0, 16.0 * 128 * nti + 31.0)
                apply_rope(kr, kn, 1)
                r2 = next_regT()
                k.op("pe", lambda e: e.transpose(r2.ap(b=nn), kr[0:nn, 0, :], c["ident"][0:nn, 0:nn]), reads=[kr.b, c["ident"].b],
                     writes=[r2.b])
                k.op("act", lambda e: e.copy(out=kcmpT[:, nti * 128:nti * 128 + nn], in_=r2.ap(b=nn)), reads=[r2.b], writes=[kcmpT.b])
            else:
                k.op("act", lambda e: e.copy(out=vcmp[0:nn, nti, 0:128], in_=r.ap(rows=slice(0, nn))), reads=[r.b], writes=[vcmp.b])
    if debug:
        dump("kcmpT", kcmpT, kcmpT[:, 0:128], 64) if False else None
    kb_barrier(k)
    es.close()
    es = ExitStack()
    if stop <= 1:
        tk = k.op("sp", lambda e: e.dma_start(out=yT[0, :, 0:NCT * 128], in_=kcmpT[:]), reads=[kcmpT.b], dsem=k.dsem())
        tk2 = k.op("sp", lambda e: e.dma_start(out=yT[1, :, 0:NCT * 130], in_=vcmp[:].rearrange("p a b -> p (a b)")), reads=[vcmp.b],
                   dsem=k.dsem())
        k.finish([tk, tk2])
        return nc

    ksT = sbs("ksT", [128, T], BF16)
    kwT = sbs("kwT", [128, T], BF16)
    vs_e = sbs("vs_e", [128, NT, 130], BF16)
    vw_e = sbs("vw_e", [128, NT, 130], BF16)
    k.op("dve", lambda e: e.memset(vs_e[:, :, 128:130], 1.0), writes=[vs_e.b])
    k.op("dve", lambda e: e.memset(vw_e[:, :, 128:130], 1.0), writes=[vw_e.b])
    esB = ExitStack()
    win_b = TT(k, esB.enter_context(nc.sbuf_tensor("win_b", [128, 16, 512], BF16)), "win_b")
    k.op("pool", lambda e: e.dma_start(out=win_b[:], in_=win_v[:, :, 256:768]), writes=[win_b.b], dsem=k.dsem())
    load_xT(0)
    for st in range(NST):
        t0 = st * TW
        if st + 1 < NST:
            load_xT(st + 1)
        xc = xTs[st % 2]
        for tci in range(TW // 128):
            ti = st * (TW // 128) + tci
            a = next_acc()

            def mm(e):
                for dc in range(16):
                    ins = e.matmul(a.ap(), lhsT=xc[:, dc, tci * 128:(tci + 1) * 128], rhs=win_b[:, dc, :], start=(dc == 0), stop=(dc == 15))
                return ins
            k.op("pe", mm, reads=[win_b.b, xc.b], writes=[a.b])
            k.op("act", lambda e: e.copy(out=vs_e[:, ti, 0:128], in_=a.ap(a=128, b=256)), reads=[a.b], writes=[vs_e.b])
            k.op("act", lambda e: e.copy(out=vw_e[:, ti, 0:128], in_=a.ap(a=384, b=512)), reads=[a.b], writes=[vw_e.b])
            rms_heads(a, [0, 256], 2, lambda h: (knw_sb, knw_sb[:, 1 + h, :]))
            rope_tables(1.0, float(ti * 128))
            apply_rope(kr, kn, 2)
            for h, dstT in ((0, ksT), (1, kwT)):
                r2 = next_regT()
                k.op("pe", lambda e: e.transpose(r2.ap(), kr[:, h, :], c["ident"][:]), reads=[kr.b, c["ident"].b], writes=[r2.b])
                k.op("act", lambda e: e.copy(out=dstT[:, ti * 128:(ti + 1) * 128], in_=r2.ap()), reads=[r2.b], writes=[dstT.b])
    kb_barrier(k)
    esB.close()
    if stop <= 2:
        tk = k.op("sp", lambda e: e.dma_start(out=yT[0, :, :], in_=ksT[:]), reads=[ksT.b], dsem=k.dsem())
        tk2 = k.op("sp", lambda e: e.dma_start(out=yT[1, :, :], in_=kwT[:]), reads=[kwT.b], dsem=k.dsem())
        tk3 = k.op("sp", lambda e: e.dma_start(out=yT[2, :, 0:NT * 128].rearrange("p (a b) -> p a b", b=128), in_=vs_e[:, :, 0:128]),
                   reads=[vs_e.b], dsem=k.dsem())
        k.finish([tk, tk2, tk3])
        return nc

    win_q = sbs("win_q", [128, 16, 524], BF16)
    k.op("pool", lambda e: e.dma_start(out=win_q[:], in_=win_v[:, :, 768:1292]), writes=[win_q.b], dsem=k.dsem())
    Esel = sbs("Esel", [128, T], BF16)
    k.op("dve", lambda e: e.memset(Esel[:], 1.0), writes=[Esel.b])
    k.op("pool", lambda e: e.affine_select(out=Esel[:], in_=Esel[:], pattern=[[1, T]], compare_op=ALU.is_ge, fill=0.0, base=0,
                                           channel_multiplier=-64), reads=[Esel.b], writes=[Esel.b])
    k.op("pool", lambda e: e.affine_select(out=Esel[:], in_=Esel[:], pattern=[[-1, T]], compare_op=ALU.is_ge, fill=0.0, base=63,
                                           channel_multiplier=64), reads=[Esel.b], writes=[Esel.b])
    f32a = sbs("f32a", [128, 512], F32)
    f32b = sbs("f32b", [128, 512], F32)
    ones4 = sbs("ones4", [128, 512], F32)
    zeros4 = sbs("zeros4", [128, 512], F32)
    k.op("dve", lambda e: e.memset(ones4[:], 1.0), writes=[ones4.b])
    k.op("dve", lambda e: e.memset(zeros4[:], 0.0), writes=[zeros4.b])
    for nti in range(NCT):
        k.op("pool", lambda e: e.affine_select(out=f32a[:, 0:128], in_=ones4[:, 0:128], pattern=[[64, 128]], compare_op=ALU.is_gt, fill=0.0,
                                               base=64 - 2048 * nti, channel_multiplier=-16), reads=[ones4.b], writes=[f32a.b])
        k.op("pool", lambda e: e.affine_select(out=f32a[:, 0:128], in_=f32a[:, 0:128], pattern=[[-64, 128]], compare_op=ALU.is_gt, fill=0.0,
                                               base=2048 * nti + 32, channel_multiplier=16), reads=[f32a.b], writes=[f32a.b])
        k.op("dve", lambda e: e.tensor_copy(out=cover[:, nti, :], in_=f32a[:, 0:128]), reads=[f32a.b], writes=[cover.b])
    cneg = sbs("cneg", [128, 512], BF16)
    wneg = sbs("wneg", [128, 512], BF16)
    k.op("pool", lambda e: e.affine_select(out=f32a[:].rearrange("p (h j) -> p h j", h=4), in_=zeros4[:].rearrange("p (h j) -> p h j", h=4),
                                           pattern=[[0, 4], [1, 128]], compare_op=ALU.is_ge, fill=-NSA_BIG, base=0, channel_multiplier=-1),
         reads=[zeros4.b], writes=[f32a.b])
    k.op("dve", lambda e: e.tensor_copy(out=cneg[:], in_=f32a[:]), reads=[f32a.b], writes=[cneg.b])
    k.op("pool", lambda e: e.affine_select(out=f32a[:].rearrange("p (h j) -> p h j", h=4), in_=zeros4[:].rearrange("p (h j) -> p h j", h=4),
                                           pattern=[[0, 4], [-1, 128]], compare_op=ALU.is_ge, fill=-NSA_BIG, base=-1, channel_multiplier=1),
         reads=[zeros4.b], writes=[f32a.b])
    k.op("dve", lambda e: e.tensor_copy(out=wneg[:], in_=f32a[:]), reads=[f32a.b], writes=[wneg.b])
    c1e4 = sbs("c1e4", [128, 128], F32)
    k.op("dve", lambda e: e.memset(c1e4[:], 1e4), writes=[c1e4.b])

    gsb = sbs("gsb", [128, 12], F32)
    qn4 = sbs("qn4", [128, 4, 128], F32)
    qr4 = sbs("qr4", [128, 4, 128], F32)
    qT = sbs("qT", [128, 512], BF16)
    Ef = sbs("Ef", [128, 512], F32)
    m01 = sbs("m01", [128, 512], F32)
    Pt = [sbs(f"Pt{i}", [128, 512], BF16) for i in range(2)]
    pcnt = [0]
    impS = sbs("impS", [128, 512], F32)
    imp = sbs("imp", [128, 128], F32)
    bon = sbs("bon", [128, 128], F32)
    impf = sbs("impf", [128, 128], F32)
    val01 = sbs("val01", [128, 128], F32)
    wk_ = sbs("wk_", [128, 128], F32)
    m8 = sbs("m8", [128, 8], F32)
    selm = sbs("selm", [128, 128], F32)
    nmT = sbs("nmT", [128, 512], BF16)
    zt = sbs("zt", [128, 3, 4], F32)
    wgt = sbs("wgt", [128, 4], F32)
    oacc = sbs("oacc", [128, 4, 128], F32)
    ystage = sbs("ystage", [128, 4, TW], BF16)
    ds_y = k.dsem()

    def exp_to_P(a, mask01=None):
        p = Pt[pcnt[0] % 2]
        pcnt[0] += 1
        if mask01 is None:
            k.op("act", lambda e: e.activation(out=p[:], in_=a.ap(), func=AF.Exp), reads=[a.b], writes=[p.b])
        else:
            k.op("act", lambda e: e.activation(out=Ef[:], in_=a.ap(), func=AF.Exp), reads=[a.b], writes=[Ef.b])
            k.op("dve", lambda e: e.tensor_tensor(out=p[:], in0=Ef[:], in1=mask01[:], op=ALU.mult), reads=[Ef.b, mask01.b], writes=[p.b])
        return p

    def pv(p, vt, vidx, oset, first, last):
        for h in range(4):
            k.op("pe", lambda e: e.matmul(oset[h].ap(), lhsT=p[:, h * 128:(h + 1) * 128], rhs=vt[:, vidx, :], start=first, stop=last),
                 reads=[p.b, vt.b], writes=[oset[h].b])

    def run_pairs(jobs, oset):
        n = len(jobs)
        accs = [None] * n

        def emit_S(i):
            a = next_acc()
            accs[i] = a
            k.op("pe", lambda e: jobs[i]["mm"](e, a), reads=jobs[i]["reads"], writes=[a.b])
        if n:
            emit_S(0)
        for i in range(n):
            if i + 1 < n:
                emit_S(i + 1)
            msk = jobs[i]["pre"]() if "pre" in jobs[i] else None
            p = exp_to_P(accs[i], msk)
            pv(p, jobs[i]["vt"], jobs[i]["vidx"], oset, i == 0, i == n - 1)
            if "post" in jobs[i]:
                jobs[i]["post"](p)

    def combine(oset, br, first, gsb=None):
        for h in range(4):
            k.op("dve", lambda e: e.tensor_scalar(out=zt[:, br, h:h + 1], in0=oset[h].ap(a=128, b=129), scalar1=1e-30, scalar2=None,
                                                  op0=ALU.max), reads=[oset[h].b], writes=[zt.b])
        k.op("dve", lambda e: e.reciprocal(out=zt[:, br, :], in_=zt[:, br, :]), reads=[zt.b], writes=[zt.b])
        k.op("dve", lambda e: e.tensor_tensor(out=wgt[:], in0=zt[:, br, :], in1=gsb[:, br:12:3], op=ALU.mult), reads=[zt.b, gsb.b],
             writes=[wgt.b])
        for h in range(4):
            if first:
                k.op("dve", lambda e: e.tensor_scalar(out=oacc[:, h, :], in0=oset[h].ap(b=128), scalar1=wgt[:, h:h + 1], scalar2=None,
                                                      op0=ALU.mult), reads=[oset[h].b, wgt.b], writes=[oacc.b])
            else:
                k.op("dve", lambda e: e.scalar_tensor_tensor(out=oacc[:, h, :], in0=oset[h].ap(b=128), scalar=wgt[:, h:h + 1],
                                                             in1=oacc[:, h, :], op0=ALU.mult, op1=ALU.add),
                     reads=[oset[h].b, wgt.b, oacc.b], writes=[oacc.b])

    qTs = [qT, sbs("qT_b", [128, 512], BF16)]
    gsbs = [gsb, sbs("gsb_b", [128, 12], F32)]
    qacc = impT

    def q_prep_a(qt):
        st_, tci_ = divmod(qt, TW // 128)
        xc = xTs[st_ % 2]
        tsl_ = slice(tci_ * 128, (tci_ + 1) * 128)
        a = qacc
        g_ = gsbs[qt % 2]

        def mm(e):
            for dc in range(16):
                ins = e.matmul(a.ap(), lhsT=xc[:, dc, tsl_], rhs=win_q[:, dc, 0:512], start=(dc == 0), stop=(dc == 15))
            return ins
        k.op("pe", mm, reads=[win_q.b, xc.b], writes=[a.b])
        rg = next_regT()

        def mmg(e):
            for dc in range(16):
                ins = e.matmul(rg.ap(b=12), lhsT=xc[:, dc, tsl_], rhs=win_q[:, dc, 512:524], start=(dc == 0), stop=(dc == 15))
            return ins
        k.op("pe", mmg, reads=[win_q.b, xc.b], writes=[rg.b])
        k.op("dve", lambda e: e.tensor_tensor(out=g_[:], in0=rg.ap(b=12), in1=gb_sb[:], op=ALU.add), reads=[rg.b, gb_sb.b], writes=[g_.b])
        k.op("act", lambda e: e.activation(out=g_[:], in_=g_[:], func=AF.Sigmoid), reads=[g_.b], writes=[g_.b])
        rms_heads(a, [0, 128, 256, 384], 4, lambda h: (qnw_sb, qnw_sb[:, h * 128:(h + 1) * 128]), post_scale=scale)
        rope_tables(1.0, float(qt * 128))
        apply_rope(qr4, kn, 4)

    def q_prep_b(qt):
        q_ = qTs[qt % 2]
        for h in range(4):
            r2 = next_regT()
            k.op("pe", lambda e: e.transpose(r2.ap(), qr4[:, h, :], c["ident"][:]), reads=[qr4.b, c["ident"].b], writes=[r2.b])
            k.op("act", lambda e: e.copy(out=q_[:, h * 128:(h + 1) * 128], in_=r2.ap()), reads=[r2.b], writes=[q_.b])

    load_xT(0)
    if NST > 1:
        load_xT(1)
    q_prep_a(0)
    q_prep_b(0)
    for st in range(NST):
        t0s = st * TW
        for tci in range(TW // 128):
            qt = st * (TW // 128) + tci
            t0 = qt * 128
            tsl = slice(tci * 128, (tci + 1) * 128)
            qT = qTs[qt % 2]
            gsb = gsbs[qt % 2]
            nmax = (t0 + 127 - 31) // 16
            ntiles = 0 if nmax < 0 else min(NCT, nmax // 128 + 1)
            oc = oreg[0]
            if ntiles == 0:
                k.op("dve", lambda e: e.memset(imp[:], 0.0), writes=[imp.b])
            jobs = []
            for nt in range(ntiles):
                def mmc(e, a, nt=nt):
                    return e.matmul(a.ap(), lhsT=kcmpT[:, nt * 128:(nt + 1) * 128], rhs=qT[:], start=True, stop=True)

                def pre(nt=nt):
                    k.op("pool", lambda e: e.affine_select(out=m01[:].rearrange("p (h j) -> p h j", h=4),
                                                           in_=ones4[:].rearrange("p (h j) -> p h j", h=4), pattern=[[0, 4], [1, 128]],
                                                           compare_op=ALU.is_ge, fill=0.0, base=t0 - 2048 * nt - 31, channel_multiplier=-16),
                         reads=[ones4.b], writes=[m01.b])
                    return m01

                def post(p, nt=nt):
                    k.op("pe", lambda e: e.matmul(impT.ap(), lhsT=cover[:, nt, :], rhs=p[:], start=(nt == 0), stop=(nt == ntiles - 1)),
                         reads=[cover.b, p.b], writes=[impT.b])
                jobs.append(dict(mm=mmc, reads=[kcmpT.b, qT.b], vt=vcmp, vidx=nt, pre=pre, post=post))
            run_pairs(jobs, oc)
            if ntiles > 0:
                combine(oc, 0, True, gsb)
                k.op("act", lambda e: e.copy(out=impS[:], in_=impT.ap()), reads=[impT.b], writes=[impS.b])
                for h in range(4):
                    r2 = next_regT()
                    k.op("pe", lambda e: e.transpose(r2.ap(), impS[:, h * 128:(h + 1) * 128], c["ident"][:]), reads=[impS.b, c["ident"].b],
                         writes=[r2.b])
                    if h == 0:
                        k.op("dve", lambda e: e.tensor_scalar(out=imp[:], in0=r2.ap(), scalar1=zt[:, 0, 0:1], scalar2=None, op0=ALU.mult),
                             reads=[r2.b, zt.b], writes=[imp.b])
                    else:
                        k.op("dve", lambda e: e.scalar_tensor_tensor(out=imp[:], in0=r2.ap(), scalar=zt[:, 0, h:h + 1], in1=imp[:],
                                                                     op0=ALU.mult, op1=ALU.add), reads=[r2.b, zt.b, imp.b], writes=[imp.b])
            else:
                k.op("dve", lambda e: e.memset(oacc[:], 0.0), writes=[oacc.b])
            k.op("pool", lambda e: e.affine_select(out=bon[:], in_=c1e4[:], pattern=[[-64, 128]], compare_op=ALU.is_ge, fill=0.0, base=t0,
                                                   channel_multiplier=1), reads=[c1e4.b], writes=[bon.b])
            k.op("pool", lambda e: e.affine_select(out=bon[:], in_=bon[:], pattern=[[64, 128]], compare_op=ALU.is_ge, fill=0.0,
                                                   base=127 - t0, channel_multiplier=-1), reads=[bon.b], writes=[bon.b])
            k.op("pool", lambda e: e.memset(bon[:, 0:1], 1e4), writes=[bon.b])
            k.op("dve", lambda e: e.tensor_tensor(out=impf[:], in0=imp[:], in1=bon[:], op=ALU.add), reads=[imp.b, bon.b], writes=[impf.b])
            k.op("pool", lambda e: e.affine_select(out=val01[:], in_=ones4[:, 0:128], pattern=[[-64, 128]], compare_op=ALU.is_ge, fill=0.0,
                                                   base=t0, channel_multiplier=1), reads=[ones4.b], writes=[val01.b])
            k.op("dve", lambda e: e.tensor_tensor(out=impf[:], in0=impf[:], in1=val01[:], op=ALU.mult), reads=[impf.b, val01.b],
                 writes=[impf.b])
            k.op("dve", lambda e: e.tensor_scalar(out=val01[:], in0=val01[:], scalar1=1e30, scalar2=-1e30, op0=ALU.mult, op1=ALU.add),
                 reads=[val01.b], writes=[val01.b])
            k.op("dve", lambda e: e.tensor_tensor(out=impf[:], in0=impf[:], in1=val01[:], op=ALU.add), reads=[impf.b, val01.b],
                 writes=[impf.b])
            k.op("dve", lambda e: e.max(out=m8[:], in_=impf[:]), reads=[impf.b], writes=[m8.b])
            k.op("dve", lambda e: e.match_replace(out=wk_[:], in_to_replace=m8[:], in_values=impf[:], imm_value=-3e38), reads=[impf.b, m8.b],
                 writes=[wk_.b])
            k.op("dve", lambda e: e.max(out=m8[:], in_=wk_[:]), reads=[wk_.b], writes=[m8.b])
            k.op("dve", lambda e: e.tensor_scalar(out=selm[:], in0=impf[:], scalar1=m8[:, 7:8], scalar2=None, op0=ALU.is_ge),
                 reads=[impf.b, m8.b], writes=[selm.b])
            k.op("dve", lambda e: e.tensor_scalar(out=selm[:], in0=selm[:], scalar1=-1.0, scalar2=NSA_BIG, op0=ALU.add, op1=ALU.mult),
                 reads=[selm.b], writes=[selm.b])
            owin = oreg[0]
            kts = list(range(max(0, qt - 4), qt + 1))
            jobs = []
            for kt in kts:
                def mmw(e, a, kt=kt):
                    need_mask = (kt == qt) or (kt == qt - 4)
                    ins = e.matmul(a.ap(), lhsT=kwT[:, kt * 128:(kt + 1) * 128], rhs=qT[:], start=True, stop=not need_mask)
                    if kt == qt:
                        ins = e.matmul(a.ap(), lhsT=identb[:], rhs=cneg[:], start=False, stop=True)
                    elif kt == qt - 4:
                        ins = e.matmul(a.ap(), lhsT=identb[:], rhs=wneg[:], start=False, stop=True)
                    return ins
                jobs.append(dict(mm=mmw, reads=[kwT.b, qT.b, identb.b, cneg.b, wneg.b], vt=vw_e, vidx=kt))
            run_pairs(jobs, owin)
            combine(owin, 2, False, gsb)
            r2 = next_regT()
            k.op("pe", lambda e: e.transpose(r2.ap(), selm[:], c["ident"][:]), reads=[selm.b, c["ident"].b], writes=[r2.b])
            k.op("act", lambda e: e.copy(out=nmT[:].rearrange("p (h j) -> p h j", h=4),
                                         in_=r2.ap().rearrange("p (o j) -> p o j", o=1).to_broadcast([128, 4, 128])), reads=[r2.b],
                 writes=[nmT.b])
            if debug and qt == (NT - 1):
                dump("imp", imp, imp[:], 128)
                dump("impf", impf, impf[:], 128)
                dump("selm", selm, selm[:], 128)
            if qt + 1 < NT:
                if (qt + 1) % (TW // 128) == 0 and (qt + 1) // (TW // 128) + 1 < NST:
                    load_xT((qt + 1) // (TW // 128) + 1)
                q_prep_a(qt + 1)
            osel = oreg[1]
            jobs = []
            for kt in range(qt + 1):
                def mms(e, a, kt=kt):
                    e.matmul(a.ap(), lhsT=ksT[:, kt * 128:(kt + 1) * 128], rhs=qT[:], start=True, stop=False)
                    ins = e.matmul(a.ap(), lhsT=Esel[:, kt * 128:(kt + 1) * 128], rhs=nmT[:], start=False, stop=(kt != qt))
                    if kt == qt:
                        ins = e.matmul(a.ap(), lhsT=identb[:], rhs=cneg[:], start=False, stop=True)
                    return ins
                jobs.append(dict(mm=mms, reads=[ksT.b, qT.b, Esel.b, nmT.b, identb.b, cneg.b], vt=vs_e, vidx=kt))
            run_pairs(jobs, osel)
            combine(osel, 1, False, gsb)
            if qt + 1 < NT:
                q_prep_b(qt + 1)
            for h in range(4):
                r2 = next_regT()
                k.op("pe", lambda e: e.transpose(r2.ap(), oacc[:, h, :], c["ident"][:]), reads=[oacc.b, c["ident"].b], writes=[r2.b])
                k.op("act", lambda e: e.copy(out=ystage[:, h, tsl], in_=r2.ap()), reads=[r2.b], writes=[ystage.b])
        tk = k.op("sp", lambda e: e.dma_start(out=yT.rearrange("c p t -> p c t")[:, :, t0s:t0s + TW], in_=ystage[:]), reads=[ystage.b],
                  dsem=ds_y)
        out_toks.append(tk)
    k.finish(out_toks)
    return nc


def nsa_core_inputs(I, layer, g, hT_b):
    j = layer // 2
    W = I["c_w_in"][j]
    o_q, o_kc, o_vc, o_ks, o_vs, o_kw, o_vw, o_gp = 0, 2048, 2560, 3072, 3584, 4096, 4608, 5120
    sl = lambda o: W[:, o + g * 128:o + (g + 1) * 128]
    win = np.concatenate([sl(o_kc), sl(o_vc), sl(o_ks), sl(o_vs), sl(o_kw), sl(o_vw), W[:, g * 512:(g + 1) * 512],
                          W[:, o_gp + 12 * g:o_gp + 12 * (g + 1)]], axis=1)
    posT = np.ascontiguousarray(I["c_cmp_pos"][j].transpose(2, 0, 1))
    b1 = np.ascontiguousarray(I["c_cmp_b1"][j].reshape(2, 2, 128).transpose(2, 0, 1))
    return {
        "hT": hT_b, "nw": pvec(I["mix_norm"][layer]), "win": np.ascontiguousarray(win),
        "qnw": rep(np.tile(I["c_q_norm"][j], 4)), "knw": rep(I["c_k_norm"][j]), "posT": posT,
        "w1": np.ascontiguousarray(I["c_cmp_w1"][j]), "b1": b1, "w2": np.ascontiguousarray(I["c_cmp_w2"][j]),
        "gbias": rep(I["c_gate_bias"][j][12 * g:12 * (g + 1)]),
    }


_PROGS = {}


def _prog(name, fn):
    if name not in _PROGS:
        _PROGS[name] = fn()
    return _PROGS[name]


def _launch(nc, maps):
    res = run_bass_kernel_spmd(nc, maps, core_ids=list(range(8)))
    return res.results


def kernel(**I):
    I = {k_: np.ascontiguousarray(np.asarray(v)) for k_, v in I.items()}
    x = I["x"]
    B, S, D = x.shape
    NTC = S // 4
    hT = [np.ascontiguousarray(x[b].T.reshape(16, 128, S)) for b in range(B)]

    def tok_shard(arrs, c):
        b, q = divmod(c, 4)
        return np.ascontiguousarray(arrs[b][:, :, q * NTC:(q + 1) * NTC])

    def ffn_launch(hT, pre, layer, yT=None, wo=None):
        nc = _prog("ffn_pre" if yT is not None else "ffn", lambda: build_ffn(NT=NTC, preproj=yT is not None))
        maps = []
        for c in range(8):
            m = {"hT": tok_shard(hT, c), "nw": pvec(I[pre + "_norm"][layer]), "wg": I[pre + "_w_gate"][layer],
                 "wu": I[pre + "_w_up"][layer], "wd": I[pre + "_w_down"][layer]}
            if yT is not None:
                m["yT"] = tok_shard(yT, c)
                m["wo"] = wo
            maps.append(m)
        res = _launch(nc, maps)
        out = [np.empty((16, 128, S), np.float32) for _ in range(B)]
        for c in range(8):
            b, q = divmod(c, 4)
            out[b][:, :, q * NTC:(q + 1) * NTC] = res[c]["hT_out"]
        return out

    for layer in range(4):
        j = layer // 2
        hT = ffn_launch(hT, "ffn1", layer)
        yT = [np.empty((16, 128, S), ml_dtypes.bfloat16) for _ in range(B)]
        if layer % 2 == 0:
            nc = _prog(f"ab{j}", lambda: build_ab(T=S, layer_j=j))
            maps = [ab_core_inputs(I, layer, c % 4, hT[c // 4]) for c in range(8)]
            res = _launch(nc, maps)
            for c in range(8):
                b, g = divmod(c, 4)
                y = res[c]["yT"]
                yT[b][2 * g:2 * g + 2] = y[0:2]
                yT[b][8 + 2 * g:8 + 2 * g + 2] = y[2:4]
            wo = I["ab_w_out"][j]
        else:
            nc = _prog("nsa", lambda: build_nsa(T=S))
            maps = [nsa_core_inputs(I, layer, c % 4, hT[c // 4]) for c in range(8)]
            res = _launch(nc, maps)
            for c in range(8):
                b, g = divmod(c, 4)
                yT[b][4 * g:4 * g + 4] = res[c]["yT"]
            wo = I["c_w_out"][j]
        hT = ffn_launch(hT, "ffn2", layer, yT=yT, wo=wo)
    out = np.stack([hT[b].reshape(D, S).T for b in range(B)], axis=0)
    return np.ascontiguousarray(out.astype(np.float32))
```

```python
import numpy as np
import ml_dtypes
import concourse.bass as bass
import concourse.mybir as mybir
from concourse.bass_utils import run_bass_kernel_spmd

F32 = mybir.dt.float32
BF16 = mybir.dt.bfloat16
I32 = mybir.dt.int32
AF = mybir.ActivationFunctionType
ALU = mybir.AluOpType
AX = mybir.AxisListType

D_MODEL = 2048
D_FF = 5504
EPS = 1e-6
SAME_SYNC = True


class Tok:
    __slots__ = ("sem", "val", "key", "eng")

    def __init__(self, sem, val, key, eng):
        self.sem, self.val, self.key, self.eng = sem, val, key, eng


class Buf:
    __slots__ = ("name", "w", "r", "bank")

    def __init__(self, name):
        self.name, self.w, self.r, self.bank = name, None, {}, None


class DSem:
    __slots__ = ("h", "val", "key")

    def __init__(self, h, key):
        self.h, self.val, self.key = h, 0, key


class KB:
    def __init__(self, nc):
        self.nc = nc
        self.engs = dict(pe=nc.tensor, act=nc.scalar, dve=nc.vector, pool=nc.gpsimd, sp=nc.sync)
        self.sem = {e: nc.alloc_semaphore("sem_" + e) for e in self.engs}
        self.cnt = {e: 0 for e in self.engs}
        self.waited = {e: {} for e in self.engs}
        self.nds = 0
        self.out_toks = []
        self.after_op = None

    def buf(self, name=""):
        return Buf(name)

    def bufs(self, n, name=""):
        return [Buf(f"{name}{i}") for i in range(n)]

    def dsem(self, name=None):
        self.nds += 1
        key = f"D{self.nds}"
        return DSem(self.nc.alloc_semaphore(name or key), key)

    def _wait(self, e, tok, raw=False):
        if tok is None:
            return
        if tok.eng == e and not (raw and SAME_SYNC and e != "pe"):
            return
        w = self.waited[e]
        if w.get(tok.key, 0) >= tok.val:
            return
        self.engs[e].wait_ge(tok.sem, tok.val)
        w[tok.key] = tok.val

    def op(self, e, fn, reads=(), writes=(), dsem=None):
        for b in reads:
            self._wait(e, b.w, raw=True)
        for b in writes:
            self._wait(e, b.w)
            for t in b.r.values():
                self._wait(e, t)
        banks = {}
        for b in list(reads) + list(writes):
            if b.bank is not None:
                banks[id(b.bank)] = b.bank
        for bk in banks.values():
            self._wait(e, bk.w)
        ins = fn(self.engs[e])
        if dsem is None:
            self.cnt[e] += 1
            ins.then_inc(self.sem[e], 1)
            tok = Tok(self.sem[e], self.cnt[e], "E" + e, e)
        else:
            dsem.val += 16
            ins.then_inc(dsem.h, 16)
            tok = Tok(dsem.h, dsem.val, dsem.key, None)
        for b in reads:
            b.r[tok.key] = tok
        for b in writes:
            b.w = tok
            b.r = {}
        for bk in banks.values():
            bk.w = tok
        if self.after_op is not None:
            self.after_op()
        return tok

    def finish(self, toks):
        for t in toks:
            self._wait("sp", t)


import threading


def interleave(k, fns):
    n = len(fns)
    cv = threading.Condition()
    state = {"turn": 0, "alive": [True] * n, "err": None}
    tls = threading.local()

    def advance(i):
        for d in range(1, n + 1):
            j = (i + d) % n
            if state["alive"][j]:
                state["turn"] = j
                return
        state["turn"] = -1

    def yield_(i):
        with cv:
            advance(i)
            cv.notify_all()
            while state["turn"] != i:
                cv.wait()

    def runner(i):
        tls.idx = i
        with cv:
            while state["turn"] != i:
                cv.wait()
        try:
            fns[i]()
        except BaseException as e:
            state["err"] = e
        with cv:
            state["alive"][i] = False
            advance(i)
            cv.notify_all()

    old_hook = k.after_op
    k.after_op = lambda: yield_(tls.idx) if getattr(tls, "idx", None) is not None else None
    ths = [threading.Thread(target=runner, args=(i,)) for i in range(n)]
    for t in ths:
        t.start()
    for t in ths:
        t.join()
    k.after_op = old_hook
    if state["err"] is not None:
        raise state["err"]


class Ring:
    def __init__(self, k, slots, name="ws"):
        self.k = k
        self.slots = slots
        self.ns = len(slots)
        self.b = k.bufs(self.ns, name)
        self.ds = [k.dsem() for _ in range(self.ns)]
        self.loads = []
        self.issued = 0
        self.consumed = 0

    def plan(self, dst_fn, src):
        self.loads.append((dst_fn, src))

    def _issue(self):
        if self.issued >= len(self.loads):
            return
        i = self.issued
        s = i % self.ns
        dst_fn, src = self.loads[i]
        slot = self.slots[s]
        self.k.op("pool", lambda e: e.dma_start(out=dst_fn(slot), in_=src), reads=(), writes=[self.b[s]],
                  dsem=self.ds[s])
        self.issued += 1

    def start(self):
        while self.issued < min(self.ns, len(self.loads)):
            self._issue()

    def get(self, off=0):
        i = self.consumed + off
        assert i < self.issued, "ring underflow"
        s = i % self.ns
        return self.slots[s], self.b[s]

    def done(self):
        self.consumed += 1
        self._issue()


def build_ffn(NT=2048, F=D_FF, preproj=False, TP=1024, NS=4):
    D = D_MODEL
    DC = D // 128
    FCn = F // 128
    assert F % 128 == 0 and NT % TP == 0 and TP % 512 == 0
    NTT = TP // 512
    nc = bass.Bass("TRN2", target_bir_lowering=False)
    k = KB(nc)
    hT_in = nc.dram_tensor("hT", [DC, 128, NT], F32, kind="ExternalInput").ap()
    nw = nc.dram_tensor("nw", [128, DC], F32, kind="ExternalInput").ap()
    wg = nc.dram_tensor("wg", [D, F], F32, kind="ExternalInput").ap()
    wu = nc.dram_tensor("wu", [D, F], F32, kind="ExternalInput").ap()
    wd = nc.dram_tensor("wd", [F, D], F32, kind="ExternalInput").ap()
    hT_out = nc.dram_tensor("hT_out", [DC, 128, NT], F32, kind="ExternalOutput").ap()
    if preproj:
        yT = nc.dram_tensor("yT", [DC, 128, NT], BF16, kind="ExternalInput").ap()
        wo = nc.dram_tensor("wo", [D, D], F32, kind="ExternalInput").ap()
        h2T = nc.dram_tensor("h2T", [DC, 128, NT], F32).ap()
        h_src = h2T
    else:
        h_src = hT_in
    wg_v = wg.rearrange("(dc p) f -> p dc f", p=128)
    wu_v = wu.rearrange("(dc p) f -> p dc f", p=128)
    wd_v = wd.rearrange("(fc p) m -> p fc m", p=128)

    actT = nc.alloc_sbuf_tensor("actT", [128, FCn, TP], BF16)
    xT = nc.alloc_sbuf_tensor("xT", [128, DC, TP], BF16)
    slots = [nc.alloc_sbuf_tensor(f"ws{i}", [128, 16, 256], BF16) for i in range(NS)]
    NSCR = 6
    scr = [nc.alloc_sbuf_tensor(f"scr{i}", [128, TP], F32) for i in range(NSCR)]
    rstd = nc.alloc_sbuf_tensor("rstd", [128, TP], F32)
    ones = nc.alloc_sbuf_tensor("ones", [128, 128], F32)
    nw_sb = nc.alloc_sbuf_tensor("nw_sb", [128, DC], F32)
    epst = nc.alloc_sbuf_tensor("epst", [128, 1], F32)
    ps = [nc.alloc_psum_tensor(f"ps{i}", [128, 512], F32) for i in range(8)]

    actT_b = k.bufs(FCn, "actT")
    xT_b = k.bufs(DC, "xT")
    scr_b = k.bufs(NSCR, "scr")
    scr_ds = [k.dsem() for _ in range(NSCR)]
    rstd_b = k.buf("rstd")
    ones_b = k.buf("ones")
    eps_b = k.buf("eps")
    nw_b = k.buf("nw")
    nw_ds = k.dsem()
    ps_b = k.bufs(8, "ps")
    ring = Ring(k, slots)
    npass = NT // TP
    h2_b = [[k.buf(f"h2_{p}_{dc}") for dc in range(DC)] for p in range(npass)]
    yT_ds = k.dsem()

    fblocks = []
    f0 = 0
    while f0 < F:
        fw = min(256, F - f0)
        fblocks.append((f0, fw))
        f0 += fw
    dsegs = []
    c0 = 0
    while c0 < FCn:
        n = min(16, FCn - c0)
        dsegs.append((c0, n))
        c0 += n
    NG = D // 256

    for p in range(npass):
        if preproj:
            for dc in range(DC):
                ring.plan(lambda s: s[:, :, 0:128], wo.rearrange("(yc p) m -> p yc m", p=128)[:, :, dc * 128:(dc + 1) * 128])
        for (f0, fw) in fblocks:
            ring.plan(lambda s, fw=fw: s[:, :, 0:fw], wg_v[:, :, f0:f0 + fw])
            ring.plan(lambda s, fw=fw: s[:, :, 0:fw], wu_v[:, :, f0:f0 + fw])
        for gi in range(NG):
            for (c0, n) in dsegs:
                ring.plan(lambda s, n=n: s[:, 0:n, :], wd_v[:, c0:c0 + n, gi * 256:(gi + 1) * 256])

    k.op("dve", lambda e: e.memset(ones[:], 1.0), writes=[ones_b])
    k.op("dve", lambda e: e.memset(epst[:], EPS), writes=[eps_b])
    k.op("sp", lambda e: e.dma_start(out=nw_sb[:], in_=nw), writes=[nw_b], dsem=nw_ds)
    ring.start()
    out_toks = []

    for p in range(npass):
        t0 = p * TP
        tsl = slice(t0, t0 + TP)
        if preproj:
            k.op("sp", lambda e: e.dma_start(out=actT[:, 0:DC, :], in_=yT.rearrange("yc p t -> p yc t")[:, :, tsl]),
                 writes=actT_b[0:DC], dsem=yT_ds)
        for dc in range(DC):
            hi = dc % 2
            ht = scr[hi]
            k.op("sp", lambda e: e.dma_start(out=ht[:], in_=hT_in[dc, :, tsl]), writes=[scr_b[hi]], dsem=scr_ds[hi])
            if preproj:
                slot, sb = ring.get()
                pb = 2 + 2 * (dc % 2)

                def mm(e):
                    for yc in range(DC):
                        for tt in range(NTT):
                            ins = e.matmul(ps[pb + tt][:], lhsT=slot[:, yc, 0:128],
                                           rhs=actT[:, yc, tt * 512:(tt + 1) * 512], start=(yc == 0), stop=(yc == DC - 1))
                    return ins
                k.op("pe", mm, reads=[sb] + actT_b[0:DC], writes=[ps_b[pb + tt] for tt in range(NTT)])
                ring.done()
                for tt in range(NTT):
                    k.op("dve", lambda e: e.tensor_tensor(out=ht[:, tt * 512:(tt + 1) * 512], in0=ps[pb + tt][:],
                                                          in1=ht[:, tt * 512:(tt + 1) * 512], op=ALU.add),
                         reads=[ps_b[pb + tt]], writes=[scr_b[hi]])
                k.op("sp", lambda e: e.dma_start(out=h2T[dc, :, tsl], in_=ht[:]), reads=[scr_b[hi]],
                     writes=[h2_b[p][dc]], dsem=scr_ds[hi])
            si = 2 + dc % 2
            sq = scr[si]
            k.op("act", lambda e: e.activation(out=sq[:], in_=ht[:], func=AF.Square), reads=[scr_b[hi]], writes=[scr_b[si]])

            def mm(e):
                for tt in range(NTT):
                    ins = e.matmul(ps[tt][:], lhsT=ones[:], rhs=sq[:, tt * 512:(tt + 1) * 512], start=(dc == 0),
                                   stop=(dc == DC - 1))
                return ins
            k.op("pe", mm, reads=[scr_b[si], ones_b], writes=[ps_b[tt] for tt in range(NTT)])
        for tt in range(NTT):
            k.op("act", lambda e: e.activation(out=rstd[:, tt * 512:(tt + 1) * 512], in_=ps[tt][:], func=AF.Sqrt,
                                               scale=1.0 / D, bias=epst[:, 0:1]),
                 reads=[ps_b[tt], eps_b], writes=[rstd_b])
        k.op("dve", lambda e: e.reciprocal(out=rstd[:], in_=rstd[:]), reads=[rstd_b], writes=[rstd_b])
        for dc in range(DC):
            hi = dc % 2
            ht = scr[hi]
            rd = [h2_b[p][dc]] if preproj else []
            k.op("sp", lambda e: e.dma_start(out=ht[:], in_=h_src[dc, :, tsl]), reads=rd, writes=[scr_b[hi]],
                 dsem=scr_ds[hi])
            k.op("dve", lambda e: e.scalar_tensor_tensor(out=xT[:, dc, :], in0=ht[:], scalar=nw_sb[:, dc:dc + 1],
                                                         in1=rstd[:], op0=ALU.mult, op1=ALU.mult),
                 reads=[scr_b[hi], rstd_b, nw_b], writes=[xT_b[dc]])
        ci = 0
        for (f0, fw) in fblocks:
            sg_, sgb = ring.get(0)
            su_, sub = ring.get(1)
            for j in range(fw // 128):
                fi = f0 // 128 + j
                par = ci % 2
                ci += 1
                gb = 4 * par
                ub = 4 * par + 2

                def mm(e, w_=None, b0=0):
                    for dc in range(DC):
                        for tt in range(NTT):
                            ins = e.matmul(ps[b0 + tt][:], lhsT=w_[:, dc, j * 128:(j + 1) * 128],
                                           rhs=xT[:, dc, tt * 512:(tt + 1) * 512], start=(dc == 0), stop=(dc == DC - 1))
                    return ins
                k.op("pe", lambda e: mm(e, sg_, gb), reads=[sgb] + xT_b, writes=[ps_b[gb + tt] for tt in range(NTT)])
                k.op("pe", lambda e: mm(e, su_, ub), reads=[sub] + xT_b, writes=[ps_b[ub + tt] for tt in range(NTT)])
                sgi = 4 + par
                sgt = scr[sgi]
                for tt in range(NTT):
                    k.op("act", lambda e: e.activation(out=sgt[:, tt * 512:(tt + 1) * 512], in_=ps[gb + tt][:], func=AF.Silu),
                         reads=[ps_b[gb + tt]], writes=[scr_b[sgi]])
                    k.op("dve", lambda e: e.tensor_tensor(out=actT[:, fi, tt * 512:(tt + 1) * 512],
                                                          in0=sgt[:, tt * 512:(tt + 1) * 512], in1=ps[ub + tt][:], op=ALU.mult),
                         reads=[scr_b[sgi], ps_b[ub + tt]], writes=[actT_b[fi]])
            ring.done()
            ring.done()
        for gi in range(NG):
            base = 4 * (gi % 2)
            for dmi in range(2):
                dmc = gi * 2 + dmi
                hi = dmi
                rd = [h2_b[p][dmc]] if preproj else []
                k.op("sp", lambda e: e.dma_start(out=scr[hi][:], in_=h_src[dmc, :, tsl]), reads=rd, writes=[scr_b[hi]],
                     dsem=scr_ds[hi])
            for (c0, n) in dsegs:
                slot, sb = ring.get()

                def mm(e):
                    for j in range(n):
                        fc = c0 + j
                        for dmi in range(2):
                            for tt in range(NTT):
                                ins = e.matmul(ps[base + 2 * dmi + tt][:], lhsT=slot[:, j, dmi * 128:(dmi + 1) * 128],
                                               rhs=actT[:, fc, tt * 512:(tt + 1) * 512], start=(fc == 0), stop=(fc == FCn - 1))
                    return ins
                k.op("pe", mm, reads=[sb] + actT_b[c0:c0 + n], writes=[ps_b[base + i] for i in range(4)])
                ring.done()
            for dmi in range(2):
                dmc = gi * 2 + dmi
                hi = dmi
                oi = 2 + dmi
                ot = scr[oi]
                for tt in range(NTT):
                    bk = base + 2 * dmi + tt
                    k.op("dve", lambda e: e.scalar_tensor_tensor(out=ot[:, tt * 512:(tt + 1) * 512], in0=ps[bk][:], scalar=0.5,
                                                                 in1=scr[hi][:, tt * 512:(tt + 1) * 512], op0=ALU.mult, op1=ALU.add),
                         reads=[ps_b[bk], scr_b[hi]], writes=[scr_b[oi]])
                tk = k.op("sp", lambda e: e.dma_start(out=hT_out[dmc, :, tsl], in_=ot[:]), reads=[scr_b[oi]], dsem=scr_ds[oi])
                out_toks.append(tk)
    k.finish(out_toks)
    return nc


class TT:
    def __init__(self, k, t, name):
        self.t = t
        self.b = k.buf(name)

    def __getitem__(self, idx):
        return self.t[idx]


def sb(k, name, shape, dt):
    return TT(k, k.nc.alloc_sbuf_tensor(name, list(shape), dt), name)


class PReg:
    _bankbufs = {}

    def __init__(self, k, bank, c0, c1, name):
        self.bank, self.c0, self.c1 = bank, c0, c1
        self.b = k.buf(name)
        key = (id(k), bank.name if hasattr(bank, "name") else id(bank))
        if key not in PReg._bankbufs:
            PReg._bankbufs[key] = k.buf("bank")
        self.b.bank = PReg._bankbufs[key]

    def ap(self, rows=slice(None), a=None, b=None):
        a = self.c0 if a is None else self.c0 + a
        b = self.c1 if b is None else self.c0 + b
        return self.bank[rows, a:b]


def make_consts(k, need_ident=True):
    nc = k.nc
    c = {}
    c["ones"] = sb(k, "c_ones", [128, 128], F32)
    k.op("dve", lambda e: e.memset(c["ones"][:], 1.0), writes=[c["ones"].b])
    c["ident"] = sb(k, "c_ident", [128, 128], F32)
    k.op("pool", lambda e: e.affine_select(out=c["ident"][:], in_=c["ones"][:], pattern=[[-1, 128]],
                                           compare_op=ALU.is_equal, fill=0.0, base=0, channel_multiplier=1),
         reads=[c["ones"].b], writes=[c["ident"].b])
    c["causal"] = sb(k, "c_causal", [128, 128], F32)
    k.op("pool", lambda e: e.affine_select(out=c["causal"][:], in_=c["ones"][:], pattern=[[1, 128]],
                                           compare_op=ALU.is_ge, fill=0.0, base=0, channel_multiplier=-1),
         reads=[c["ones"].b], writes=[c["causal"].b])
    c["eps"] = sb(k, "c_eps", [128, 1], F32)
    k.op("dve", lambda e: e.memset(c["eps"][:], EPS), writes=[c["eps"].b])
    c["one1"] = sb(k, "c_one1", [128, 1], F32)
    k.op("dve", lambda e: e.memset(c["one1"][:], 1.0), writes=[c["one1"].b])
    return c


def load_h_tile(k, hT_src, hTt, t0, TW, ds_h, h_reads=()):
    k.op("sp", lambda e: e.dma_start(out=hTt[:, :, 0:TW], in_=hT_src.rearrange("dc p t -> p dc t")[:, :, t0:t0 + TW]),
         reads=list(h_reads), writes=[hTt.b], dsem=ds_h)


def norm_supertile(k, c, hT_src, nw_sb, hTt, xT, ps_ss, rstd, scrsq, t0, TW, ds_h, h_reads=(), preloaded=False):
    DC = 16
    if not preloaded:
        load_h_tile(k, hT_src, hTt, t0, TW, ds_h, h_reads)
    for dc in range(DC):
        sq = scrsq[dc % 2]
        k.op("act", lambda e: e.activation(out=sq[:, 0:TW], in_=hTt[:, dc, 0:TW], func=AF.Square), reads=[hTt.b],
             writes=[sq.b])
        k.op("pe", lambda e: e.matmul(ps_ss.ap(b=TW), lhsT=c["ones"][:], rhs=sq[:, 0:TW], start=(dc == 0), stop=(dc == DC - 1)),
             reads=[sq.b, c["ones"].b], writes=[ps_ss.b])
    k.op("act", lambda e: e.activation(out=rstd[:, 0:TW], in_=ps_ss.ap(b=TW), func=AF.Sqrt, scale=1.0 / D_MODEL,
                                       bias=c["eps"][:, 0:1]), reads=[ps_ss.b, c["eps"].b], writes=[rstd.b])
    k.op("dve", lambda e: e.reciprocal(out=rstd[:, 0:TW], in_=rstd[:, 0:TW]), reads=[rstd.b], writes=[rstd.b])
    for dc in range(DC):
        k.op("dve", lambda e: e.scalar_tensor_tensor(out=xT[:, dc, 0:TW], in0=hTt[:, dc, 0:TW], scalar=nw_sb[:, dc:dc + 1],
                                                     in1=rstd[:, 0:TW], op0=ALU.mult, op1=ALU.mult),
             reads=[hTt.b, rstd.b, nw_sb.b], writes=[xT.b])


HG_MAX_K = 0.999999


class _Stop(Exception):
    pass


def build_ab(T=8192, layer_j=0, do_ml=2, do_hg=2, stop=99, debug=False):
    TW = 512
    NST = T // TW
    NCOL = 1794
    nc = bass.Bass("TRN2", target_bir_lowering=False)
    k = KB(nc)

    def dram(name, shape, dt=F32, kind="ExternalInput"):
        return nc.dram_tensor(name, list(shape), dt, kind=kind).ap()
    hT = dram("hT", [16, 128, T])
    nw = dram("nw", [128, 16])
    win = dram("win", [2048, NCOL])
    cw = dram("cw", [128, 2, 4])
    cb = dram("cb", [128, 2])
    wq = dram("wq", [256, 256])
    wk = dram("wk", [256, 256])
    gbias = dram("gb", [128, 2])
    mln = dram("mln", [128, 256])
    skp = dram("skp", [128, 256])
    lbl = dram("lbl", [128, 2, 2])
    hgn = dram("hgn", [128, 2, 128])
    yT = dram("yT", [4, 128, T], BF16, kind="ExternalOutput")
    dbg = dram("dbg", [128, 8192], F32, kind="ExternalOutput") if debug else None
    dbg_pos = [0]
    dbg_map = {}
    dbg_ds = k.dsem() if debug else None

    def dump(name, tt, ap, n):
        if not debug or name in dbg_map:
            return
        c0 = dbg_pos[0]
        dbg_pos[0] += n
        dbg_map[name] = (c0, n)
        k.op("sp", lambda e: e.dma_start(out=dbg[:, c0:c0 + n], in_=ap), reads=[tt.b], dsem=dbg_ds)
    nc._dbg_map = dbg_map

    c = make_consts(k)
    ps = [nc.alloc_psum_tensor(f"ps{i}", [128, 512], F32) for i in range(8)]
    acc = [PReg(k, ps[i], 0, 512, f"acc{i}") for i in range(2)]
    ps_ss = PReg(k, ps[2], 0, 512, "ss")
    regT = [PReg(k, ps[2], i * 128, (i + 1) * 128, f"regT{i}") for i in range(4)]
    regA = PReg(k, ps[3], 0, 8, "regA")
    regB = PReg(k, ps[3], 128, 257, "regB")
    regC = PReg(k, ps[3], 384, 512, "regC")
    regND = PReg(k, ps[4], 0, 264, "regND")
    regU = [PReg(k, ps[5 + i], 0, 264, f"regU{i}") for i in range(2)]
    r7A = PReg(k, ps[7], 0, 128, "r7A")
    r7B = PReg(k, ps[7], 128, 256, "r7B")
    r7C = PReg(k, ps[7], 256, 384, "r7C")
    r7D = PReg(k, ps[7], 384, 512, "r7D")

    win_sb = sb(k, "win_sb", [128, 16, NCOL], BF16)
    wds = [k.dsem() for _ in range(4)]
    cuts = [0, 512, 1024, 1536, NCOL]
    win_v = win.rearrange("(dc p) f -> p dc f", p=128)
    wtoks = []
    for i in range(4):
        a, b_ = cuts[i], cuts[i + 1]
        wtoks.append(k.op("pool", lambda e: e.dma_start(out=win_sb[:, :, a:b_], in_=win_v[:, :, a:b_]), writes=[],
                          dsem=wds[i]))
    wq_sb = sb(k, "wq_sb", [128, 2, 256], BF16)
    wk_sb = sb(k, "wk_sb", [128, 2, 256], BF16)
    pds = k.dsem()
    k.op("pool", lambda e: e.dma_start(out=wq_sb[:], in_=wq.rearrange("(d p) e -> p d e", p=128)), writes=[wq_sb.b], dsem=pds)
    k.op("pool", lambda e: e.dma_start(out=wk_sb[:], in_=wk.rearrange("(d p) e -> p d e", p=128)), writes=[wk_sb.b], dsem=k.dsem())

    def ld(name, shape, src):
        t = sb(k, name, shape, F32)
        k.op("sp", lambda e: e.dma_start(out=t[:], in_=src), writes=[t.b], dsem=k.dsem())
        return t
    nw_sb = ld("nw_sb", [128, 16], nw)
    cw_sb = ld("cw_sb", [128, 2, 4], cw)
    cb_sb = ld("cb_sb", [128, 2], cb)
    gb_sb = ld("gb_sb", [128, 2], gbias)
    mln_sb = ld("mln_sb", [128, 256], mln)
    skp_sb = ld("skp_sb", [128, 256], skp)
    lbl_sb = ld("lbl_sb", [128, 2, 2], lbl)
    hgn_sb = ld("hgn_sb", [128, 2, 128], hgn)
    for tkn in wtoks:
        k._wait("pe", tkn)

    oml = sb(k, "oml", [128, 2], F32)
    lbe = sb(k, "lbe", [128, 2, 2], F32)
    lbs = sb(k, "lbs", [128, 2], F32)
    k.op("act", lambda e: e.activation(out=lbe[:], in_=lbl_sb[:], func=AF.Exp), reads=[lbl_sb.b], writes=[lbe.b])
    k.op("dve", lambda e: e.tensor_tensor(out=lbs[:], in0=lbe[:, :, 0], in1=lbe[:, :, 1], op=ALU.add), reads=[lbe.b], writes=[lbs.b])
    k.op("dve", lambda e: e.reciprocal(out=lbs[:], in_=lbs[:]), reads=[lbs.b], writes=[lbs.b])
    for l in range(2):
        k.op("dve", lambda e: e.tensor_tensor(out=lbe[:, :, l], in0=lbe[:, :, l], in1=lbs[:], op=ALU.mult), reads=[lbe.b, lbs.b],
             writes=[lbe.b])
    k.op("dve", lambda e: e.tensor_copy(out=oml[:], in_=lbe[:, :, 0]), reads=[lbe.b], writes=[oml.b])
    for l in range(1, layer_j + 1):
        k.op("dve", lambda e: e.tensor_tensor(out=oml[:], in0=oml[:], in1=lbe[:, :, l], op=ALU.add), reads=[lbe.b, oml.b], writes=[oml.b])
    k.op("dve", lambda e: e.tensor_tensor(out=oml[:], in0=oml[:], in1=lbe[:, :, 0], op=ALU.subtract), reads=[lbe.b, oml.b], writes=[oml.b])
    k.op("dve", lambda e: e.tensor_scalar(out=oml[:], in0=oml[:], scalar1=-1.0, scalar2=1.0, op0=ALU.mult, op1=ALU.add),
         reads=[oml.b], writes=[oml.b])
    nfb = sb(k, "nfb", [128, 1], F32)
    k.op("dve", lambda e: e.tensor_scalar(out=nfb[:], in0=gb_sb[:, 1:2], scalar1=-1.0, scalar2=None, op0=ALU.mult),
         reads=[gb_sb.b], writes=[nfb.b])

    hTt = sb(k, "hTt", [128, 16, TW], F32)
    ds_h = k.dsem()
    xT = sb(k, "xT", [128, 16, TW], BF16)
    rstd = sb(k, "rstd", [128, TW], F32)
    scrsq = [sb(k, f"scrsq{i}", [128, TW], F32) for i in range(2)]
    ubuf = sb(k, "ubuf", [128, 2, TW + 3], F32)
    cacc = sb(k, "cacc", [128, TW], F32)
    cT = sb(k, "cT", [128, 2, TW], F32)
    cTb = sb(k, "cTb", [128, 2, TW], BF16)
    qT = sb(k, "qT", [128, 2, TW], F32)
    qTb = sb(k, "qTb", [128, 2, TW], BF16)
    kTb = sb(k, "kTb", [128, 2, TW], BF16)
    ktok = sb(k, "ktok", [128, 4, 256], F32)
    ctok = sb(k, "ctok", [128, 4, 256], F32)
    vext = sb(k, "vext", [128, 4, 264], BF16)
    osig = sb(k, "osig", [128, 4, 256], F32)
    hv = sb(k, "hv", [128, 4, 2, 128], BF16)
    hgs = sb(k, "hgs", [128, 4, 256], F32)
    ge1 = sb(k, "ge1", [128, 4], F32)
    logf = sb(k, "logf", [128, 4], F32)
    ig = sb(k, "ig", [128, 4], F32)
    lfb = sb(k, "lfb", [128, 128], F32)
    bias_s = sb(k, "bias_s", [128, 1], F32)
    DT = sb(k, "DT", [128, 128], F32)
    Eb = sb(k, "Eb", [128, 128], F32)
    Dm = sb(k, "Dm", [128, 128], F32)
    PT = sb(k, "PT", [128, 128], BF16)
    qs = sb(k, "qs", [128, 2, 128], BF16)
    small = sb(k, "small", [128, 8], F32)
    small2 = sb(k, "small2", [128, 8], F32)
    junk2 = sb(k, "junk2", [128, 128], F32)
    hn = sb(k, "hn", [128, 256], F32)
    junk = sb(k, "junk", [128, 256], F32)
    hm = sb(k, "hm", [128, 256], F32)
    t1 = sb(k, "t1", [128, 256], F32)
    yml = sb(k, "yml", [128, 256], F32)
    ka = sb(k, "ka", [128, 256], BF16)
    Cst = sb(k, "Cst", [128, 2, 264], F32)
    Cb = sb(k, "Cb", [128, 2, 264], BF16)
    ystage = sb(k, "ystage", [128, 4, TW], BF16)
    ds_y = k.dsem()
    resetm = sb(k, "resetm", [128, TW], F32)
    tA = sb(k, "tA", [128, TW], F32)
    tB = sb(k, "tB", [128, TW], F32)
    tC = sb(k, "tC", [128, TW], F32)
    k2 = sb(k, "k2", [128, TW], F32)
    lf1 = sb(k, "lf1", [128, TW], F32)
    lgf = sb(k, "lgf", [128, TW], F32)
    bt = sb(k, "bt", [128, TW], F32)
    brel = sb(k, "brel", [128, TW], F32)
    sqt = sb(k, "sqt", [128, TW], F32)
    kgT = sb(k, "kgT", [128, TW], F32)
    eg8 = sb(k, "eg8", [128, 8], F32)
    qz = [sb(k, f"qz{i}", [128, 4, 128], BF16) for i in range(2)]
    kz = [sb(k, f"kz{i}", [128, 4, 128], BF16) for i in range(2)]
    qbz = [sb(k, f"qbz{i}", [128, 4, 128], BF16) for i in range(2)]
    kgz = [sb(k, f"kgz{i}", [128, 4, 128], BF16) for i in range(2)]
    Am = sb(k, "Am", [128, 128], BF16)
    Sst = [sb(k, f"Sst{i}", [128, 128], F32) for i in range(2)]
    Sb = [[sb(k, f"Sb{i}_{j}", [128, 128], BF16) for j in range(2)] for i in range(2)]
    o_sb = sb(k, "o_sb", [128, 128], F32)
    o2n = sb(k, "o2n", [128, 128], F32)
    yh = sb(k, "yh", [128, 128], F32)

    for t_ in [ubuf, Cst, Cb, Sst[0], Sst[1], Sb[0][0], Sb[0][1], Sb[1][0], Sb[1][1]] + qz + kz + qbz + kgz:
        k.op("dve", lambda e: e.memset(t_[:], 0.0), writes=[t_.b])
    k.op("dve", lambda e: e.memset(vext[:], 1.0), writes=[vext.b])
    k.op("dve", lambda e: e.memset(resetm[:], 1.0), writes=[resetm.b])
    k.op("dve", lambda e: e.memset(resetm[:].rearrange("p (c l) -> p c l", l=64)[:, :, 0:1], 0.0), writes=[resetm.b])

    def v3(t, l=64):
        return t[:].rearrange("p (c l) -> p c l", l=l)

    def v4(t):
        return t[:].rearrange("p (a b l) -> p a b l", b=2, l=64)

    tcnt = [0]

    def transpose_to(dst_ap, dst_b, src_ap, src_b, eng="act"):
        r = regT[tcnt[0] % 4]
        tcnt[0] += 1
        k.op("pe", lambda e: e.transpose(r.ap(), src_ap, c["ident"][:]), reads=[src_b, c["ident"].b], writes=[r.b])
        if eng == "act":
            k.op("act", lambda e: e.copy(out=dst_ap, in_=r.ap()), reads=[r.b], writes=[dst_b])
        else:
            k.op("dve", lambda e: e.tensor_copy(out=dst_ap, in_=r.ap()), reads=[r.b], writes=[dst_b])

    acnt = [0]

    def next_acc():
        a = acc[acnt[0] % 2]
        acnt[0] += 1
        return a

    def inproj_fm(col0):
        a = next_acc()

        def mm(e):
            for dc in range(16):
                ins = e.matmul(a.ap(), lhsT=win_sb[:, dc, col0:col0 + 128], rhs=xT[:, dc, :], start=(dc == 0), stop=(dc == 15))
            return ins
        k.op("pe", mm, reads=[xT.b], writes=[a.b])
        return a

    def inproj_tm(tci, col0, ncol, out_reg=None, oc0=0):
        a = out_reg if out_reg is not None else next_acc()

        def mm(e):
            for dc in range(16):
                ins = e.matmul(a.ap(a=oc0, b=oc0 + ncol), lhsT=xT[:, dc, tci * 128:(tci + 1) * 128],
                               rhs=win_sb[:, dc, col0:col0 + ncol], start=(dc == 0), stop=(dc == 15))
            return ins
        k.op("pe", mm, reads=[xT.b], writes=[a.b])
        return a

    out_toks = []
    def body(st, t0):
        if stop <= 1:
            raise _Stop()
        norm_supertile(k, c, hT, nw_sb, hTt, xT, ps_ss, rstd, scrsq, t0, TW, ds_h)
        if stop <= 2:
            raise _Stop()
        body2(st, t0)

    def body2(st, t0):
        if st > 0:
            k.op("dve", lambda e: e.tensor_copy(out=ubuf[:, :, 0:3], in_=ubuf[:, :, TW:TW + 3]), reads=[ubuf.b], writes=[ubuf.b])
        for ch in range(2):
            a = inproj_fm(ch * 128)
            k.op("act", lambda e: e.copy(out=ubuf[:, ch, 3:3 + TW], in_=a.ap()), reads=[a.b], writes=[ubuf.b])
        if stop <= 2.2:
            raise _Stop()
        for ch in range(2):
            k.op("dve", lambda e: e.tensor_scalar(out=cacc[:], in0=ubuf[:, ch, 0:TW], scalar1=cw_sb[:, ch, 0:1], scalar2=None,
                                                  op0=ALU.mult), reads=[ubuf.b, cw_sb.b], writes=[cacc.b])
            for j in range(1, 4):
                k.op("dve", lambda e: e.scalar_tensor_tensor(out=cacc[:], in0=ubuf[:, ch, j:j + TW], scalar=cw_sb[:, ch, j:j + 1],
                                                             in1=cacc[:], op0=ALU.mult, op1=ALU.add),
                     reads=[ubuf.b, cw_sb.b, cacc.b], writes=[cacc.b])
            k.op("act", lambda e: e.activation(out=cT[:, ch, :], in_=cacc[:], func=AF.Silu, bias=cb_sb[:, ch:ch + 1], scale=1.0),
                 reads=[cacc.b, cb_sb.b], writes=[cT.b])
        k.op("dve", lambda e: e.tensor_copy(out=cTb[:], in_=cT[:]), reads=[cT.b], writes=[cTb.b])
        if stop <= 2.5:
            raise _Stop()
        for e_ in range(2):
            a = next_acc()

            def mm(e):
                for d in range(2):
                    ins = e.matmul(a.ap(), lhsT=wq_sb[:, d, e_ * 128:(e_ + 1) * 128], rhs=cTb[:, d, :], start=(d == 0), stop=(d == 1))
                return ins
            k.op("pe", mm, reads=[wq_sb.b, cTb.b], writes=[a.b])
            k.op("act", lambda e: e.copy(out=qT[:, e_, :], in_=a.ap()), reads=[a.b], writes=[qT.b])
            k.op("dve", lambda e: e.tensor_copy(out=qTb[:, e_, :], in_=qT[:, e_, :]), reads=[qT.b], writes=[qTb.b])
            a = next_acc()

            def mm2(e):
                for d in range(2):
                    ins = e.matmul(a.ap(), lhsT=wk_sb[:, d, e_ * 128:(e_ + 1) * 128], rhs=cTb[:, d, :], start=(d == 0), stop=(d == 1))
                return ins
            k.op("pe", mm2, reads=[wk_sb.b, cTb.b], writes=[a.b])
            k.op("dve", lambda e: e.tensor_scalar(out=kTb[:, e_, :], in0=a.ap(), scalar1=0.0625, scalar2=None, op0=ALU.mult),
                 reads=[a.b], writes=[kTb.b])
        if stop <= 3:
            raise _Stop()
        for tci in range(4):
            tsl = slice(tci * 128, (tci + 1) * 128)
            a = inproj_tm(tci, 768, 512)
            k.op("dve", lambda e: e.tensor_copy(out=vext[:, tci, 0:256], in_=a.ap(b=256)), reads=[a.b], writes=[vext.b])
            k.op("act", lambda e: e.activation(out=osig[:, tci, :], in_=a.ap(a=256, b=512), func=AF.Sigmoid), reads=[a.b],
                 writes=[osig.b])
            a = inproj_tm(tci, 1280, 512)
            k.op("dve", lambda e: e.tensor_copy(out=hv[:, tci, :, :].rearrange("p a b -> p (a b)"), in_=a.ap(b=256)), reads=[a.b],
                 writes=[hv.b])
            k.op("act", lambda e: e.activation(out=hgs[:, tci, :], in_=a.ap(a=256, b=512), func=AF.Silu), reads=[a.b],
                 writes=[hgs.b])
            inproj_tm(tci, 1792, 2, out_reg=regA, oc0=2 * tci)
            a = next_acc()

            def mm3(e):
                for d in range(2):
                    ins = e.matmul(a.ap(b=256), lhsT=cTb[:, d, tsl], rhs=wk_sb[:, d, :], start=(d == 0), stop=(d == 1))
                return ins
            k.op("pe", mm3, reads=[wk_sb.b, cTb.b], writes=[a.b])
            k.op("act", lambda e: e.mul(out=ktok[:, tci, :], in_=a.ap(b=256), mul=0.0625), reads=[a.b], writes=[ktok.b])
            for d in range(2):
                transpose_to(ctok[:, tci, d * 128:(d + 1) * 128], ctok.b, cT[:, d, tsl], cT.b, eng="dve")
        if stop <= 4:
            raise _Stop()
        gv = regA.ap().rearrange("p (a b) -> p a b", b=2)
        k.op("act", lambda e: e.activation(out=ge1[:], in_=gv[:, :, 1], func=AF.Exp, scale=-1.0, bias=nfb[:, 0:1]),
             reads=[regA.b, nfb.b], writes=[ge1.b])
        k.op("act", lambda e: e.activation(out=ge1[:], in_=ge1[:], func=AF.Ln, scale=1.0, bias=c["one1"][:, 0:1]),
             reads=[ge1.b, c["one1"].b], writes=[ge1.b])
        k.op("dve", lambda e: e.tensor_scalar(out=logf[:], in0=ge1[:], scalar1=-1.0, scalar2=None, op0=ALU.mult), reads=[ge1.b],
             writes=[logf.b])
        k.op("dve", lambda e: e.tensor_scalar(out=ig[:], in0=gv[:, :, 0], scalar1=gb_sb[:, 0:1], scalar2=None, op0=ALU.add),
             reads=[regA.b, gb_sb.b], writes=[ig.b])
        def ml_stream():
            for tci in range(4 if do_ml >= 2 else 0):
                tsl = slice(tci * 128, (tci + 1) * 128)
                k.op("dve", lambda e: e.tensor_scalar(out=lfb[:], in0=c["ones"][:], scalar1=logf[:, tci:tci + 1], scalar2=None,
                                                      op0=ALU.mult), reads=[logf.b, c["ones"].b], writes=[lfb.b])

                def mmb(e):
                    e.matmul(regB.ap(b=128), lhsT=lfb[:], rhs=c["causal"][:], start=True, stop=True)
                    return e.matmul(regB.ap(a=128, b=129), lhsT=c["causal"][:], rhs=logf[:, tci:tci + 1], start=True, stop=True)
                k.op("pe", mmb, reads=[lfb.b, c["causal"].b, logf.b], writes=[regB.b])
                k.op("dve", lambda e: e.tensor_tensor(out=bias_s[:], in0=ig[:, tci:tci + 1], in1=regB.ap(a=128, b=129), op=ALU.subtract),
                     reads=[ig.b, regB.b], writes=[bias_s.b])
                k.op("act", lambda e: e.activation(out=DT[:], in_=regB.ap(b=128), func=AF.Exp, bias=bias_s[:, 0:1], scale=1.0),
                     reads=[regB.b, bias_s.b], writes=[DT.b])
                k.op("act", lambda e: e.activation(out=Eb[:], in_=regB.ap(b=128), func=AF.Exp), reads=[regB.b], writes=[Eb.b])
                k.op("dve", lambda e: e.tensor_copy(out=small[:, 0:1], in_=regB.ap(a=127, b=128)), reads=[regB.b], writes=[small.b])
                if stop <= 6.1:
                    raise _Stop()
                k.op("pool", lambda e: e.tensor_tensor(out=Dm[:], in0=DT[:], in1=c["causal"][:], op=ALU.mult),
                     reads=[DT.b, c["causal"].b], writes=[Dm.b])
                if stop <= 6.2:
                    raise _Stop()

                def mms(e):
                    for e_ in range(2):
                        ins = e.matmul(regC.ap(), lhsT=kTb[:, e_, tsl], rhs=qTb[:, e_, tsl], start=(e_ == 0), stop=(e_ == 1))
                    return ins
                k.op("pe", mms, reads=[kTb.b, qTb.b], writes=[regC.b])
                k.op("dve", lambda e: e.tensor_tensor(out=PT[:], in0=regC.ap(), in1=Dm[:], op=ALU.mult), reads=[regC.b, Dm.b],
                     writes=[PT.b])
                for e_ in range(2):
                    k.op("dve", lambda e: e.tensor_tensor(out=qs[:, e_, :], in0=qT[:, e_, tsl], in1=Eb[:], op=ALU.mult),
                         reads=[qT.b, Eb.b], writes=[qs.b])

                def mmnd(e):
                    e.matmul(regND.ap(), lhsT=PT[:], rhs=vext[:, tci, :], start=True, stop=False)
                    e.matmul(regND.ap(), lhsT=qs[:, 0, :], rhs=Cb[:, 0, :], start=False, stop=False)
                    return e.matmul(regND.ap(), lhsT=qs[:, 1, :], rhs=Cb[:, 1, :], start=False, stop=True)
                k.op("pe", mmnd, reads=[PT.b, vext.b, qs.b, Cb.b], writes=[regND.b])
                if debug and st == 0 and tci == 1:
                    dump("Cst", Cst, Cst[:].rearrange("p a b -> p (a b)"), 528)
                    dump("logf", logf, logf[:], 4)
                    dump("ig", ig, ig[:], 4)
                    dump("DT", DT, DT[:], 128)
                    dump("Eb", Eb, Eb[:], 128)
                    dump("ktok1", ktok, ktok[:, 1, :], 256)
                    dump("qT", qT, qT[:, 0, 128:256], 128)
                if stop <= 6.3:
                    raise _Stop()
                k.op("act", lambda e: e.activation(out=small[:, 1:2], in_=regND.ap(a=256, b=257), func=AF.Abs), reads=[regND.b],
                     writes=[small.b])
                k.op("dve", lambda e: e.tensor_scalar(out=small[:, 1:2], in0=small[:, 1:2], scalar1=1.0, scalar2=None, op0=ALU.max),
                     reads=[small.b], writes=[small.b])
                k.op("dve", lambda e: e.reciprocal(out=small[:, 1:2], in_=small[:, 1:2]), reads=[small.b], writes=[small.b])
                k.op("dve", lambda e: e.tensor_scalar(out=hn[:], in0=regND.ap(b=256), scalar1=small[:, 1:2], scalar2=None, op0=ALU.mult),
                     reads=[regND.b, small.b], writes=[hn.b])
                k.op("act", lambda e: e.activation(out=junk[:], in_=hn[:], func=AF.Square, accum_out=small[:, 2:3]), reads=[hn.b],
                     writes=[junk.b, small.b])
                k.op("act", lambda e: e.activation(out=small[:, 3:4], in_=small[:, 2:3], func=AF.Sqrt, scale=1.0 / 256, bias=c["eps"][:, 0:1]),
                     reads=[small.b, c["eps"].b], writes=[small.b])
                k.op("dve", lambda e: e.reciprocal(out=small[:, 3:4], in_=small[:, 3:4]), reads=[small.b], writes=[small.b])
                k.op("dve", lambda e: e.scalar_tensor_tensor(out=hm[:], in0=hn[:], scalar=small[:, 3:4], in1=mln_sb[:], op0=ALU.mult,
                                                             op1=ALU.mult), reads=[hn.b, small.b, mln_sb.b], writes=[hm.b])
                k.op("pool", lambda e: e.tensor_tensor(out=t1[:], in0=ctok[:, tci, :], in1=skp_sb[:], op=ALU.mult),
                     reads=[ctok.b, skp_sb.b], writes=[t1.b])
                k.op("dve", lambda e: e.tensor_tensor(out=t1[:], in0=t1[:], in1=hm[:], op=ALU.add), reads=[t1.b, hm.b], writes=[t1.b])
                k.op("dve", lambda e: e.tensor_tensor(out=yml[:], in0=t1[:], in1=osig[:, tci, :], op=ALU.mult), reads=[t1.b, osig.b],
                     writes=[yml.b])
                for d in range(2):
                    transpose_to(ystage[:, d, tsl], ystage.b, yml[:, d * 128:(d + 1) * 128], yml.b, eng="act")
                if debug and st == 0 and tci == 1:
                    dump("hn", hn, hn[:], 256)
                    dump("small", small, small[:], 8)
                    dump("hm", hm, hm[:], 256)
                    dump("yml", yml, yml[:], 256)
                if stop <= 6.4:
                    raise _Stop()
                k.op("act", lambda e: e.activation(out=small[:, 4:5], in_=bias_s[:], func=AF.Exp, bias=small[:, 0:1], scale=1.0),
                     reads=[bias_s.b, small.b], writes=[small.b])
                k.op("act", lambda e: e.activation(out=small[:, 5:6], in_=small[:, 0:1], func=AF.Exp), reads=[small.b], writes=[small.b])
                if stop <= 6.5:
                    raise _Stop()
                k.op("dve", lambda e: e.tensor_scalar(out=ka[:], in0=ktok[:, tci, :], scalar1=small[:, 4:5], scalar2=None, op0=ALU.mult),
                     reads=[ktok.b, small.b], writes=[ka.b])
                if stop <= 6.6:
                    raise _Stop()
                for kc in range(2):
                    k.op("pe", lambda e: e.matmul(regU[kc].ap(), lhsT=ka[:, kc * 128:(kc + 1) * 128], rhs=vext[:, tci, :], start=True,
                                                  stop=True), reads=[ka.b, vext.b], writes=[regU[kc].b])
                    k.op("dve", lambda e: e.scalar_tensor_tensor(out=Cst[:, kc, :], in0=Cst[:, kc, :], scalar=small[:, 5:6],
                                                                 in1=regU[kc].ap(), op0=ALU.mult, op1=ALU.add),
                         reads=[Cst.b, small.b, regU[kc].b], writes=[Cst.b])
                if stop <= 6.7 and kc == 1:
                    raise _Stop()
                if stop <= 6.8:
                    raise _Stop()
                k.op("act", lambda e: e.copy(out=Cb[:], in_=Cst[:]), reads=[Cst.b], writes=[Cb.b])

        def hg_stream():
            for hd in range(2 if do_hg >= 1 else 0):
                az = inproj_fm(512 + hd * 128)
                aq = inproj_fm(256 + hd * 128)
                k.op("act", lambda e: e.activation(out=tA[:], in_=az.ap(), func=AF.Sigmoid, scale=-1.0), reads=[az.b], writes=[tA.b])
                k.op("act", lambda e: e.activation(out=tB[:], in_=az.ap(), func=AF.Exp, scale=-1.0), reads=[az.b], writes=[tB.b])
                k.op("act", lambda e: e.activation(out=sqt[:], in_=aq.ap(), func=AF.Silu), reads=[aq.b], writes=[sqt.b])
                k.op("dve", lambda e: e.tensor_scalar(out=k2[:], in0=tA[:], scalar1=oml[:, hd:hd + 1], scalar2=None, op0=ALU.mult),
                     reads=[tA.b, oml.b], writes=[k2.b])
                k.op("dve", lambda e: e.tensor_scalar(out=tA[:], in0=k2[:], scalar1=HG_MAX_K, scalar2=None, op0=ALU.min), reads=[k2.b],
                     writes=[tA.b])
                k.op("act", lambda e: e.activation(out=lf1[:], in_=tA[:], func=AF.Ln, scale=-1.0, bias=c["one1"][:, 0:1]),
                     reads=[tA.b, c["one1"].b], writes=[lf1.b])
                k.op("act", lambda e: e.activation(out=tB[:], in_=tB[:], func=AF.Ln, scale=1.0, bias=c["one1"][:, 0:1]),
                     reads=[tB.b, c["one1"].b], writes=[tB.b])
                k.op("dve", lambda e: e.scalar_tensor_tensor(out=lgf[:], in0=tB[:], scalar=-1.0, in1=lf1[:], op0=ALU.mult, op1=ALU.max),
                     reads=[tB.b, lf1.b], writes=[lgf.b])
                k.op("dve", lambda e: e.tensor_tensor_scan(out=bt[:], data0=resetm[:], data1=lgf[:], initial=0.0, op0=ALU.mult,
                                                           op1=ALU.add), reads=[resetm.b, lgf.b], writes=[bt.b])
                k.op("dve", lambda e: e.tensor_tensor(out=v3(brel), in0=v3(bt), in1=v3(bt)[:, :, 31:32].to_broadcast([128, 8, 64]),
                                                      op=ALU.subtract), reads=[bt.b], writes=[brel.b])
                k.op("act", lambda e: e.activation(out=tA[:], in_=brel[:], func=AF.Exp), reads=[brel.b], writes=[tA.b])
                k.op("act", lambda e: e.activation(out=tC[:], in_=brel[:], func=AF.Exp, scale=-1.0), reads=[brel.b], writes=[tC.b])
                for par in range(2):
                    k.op("dve", lambda e: e.tensor_tensor(out=qz[par][:, :, par * 64:(par + 1) * 64], in0=v4(sqt)[:, :, par, :],
                                                          in1=v4(tA)[:, :, par, :], op=ALU.mult), reads=[sqt.b, tA.b], writes=[qz[par].b])
                    k.op("dve", lambda e: e.tensor_tensor(out=kz[par][:, :, par * 64:(par + 1) * 64], in0=v4(k2)[:, :, par, :],
                                                          in1=v4(tC)[:, :, par, :], op=ALU.mult), reads=[k2.b, tC.b], writes=[kz[par].b])
                k.op("act", lambda e: e.activation(out=tA[:], in_=bt[:], func=AF.Exp), reads=[bt.b], writes=[tA.b])
                for par in range(2):
                    k.op("dve", lambda e: e.tensor_tensor(out=qbz[par][:, :, par * 64:(par + 1) * 64], in0=v4(sqt)[:, :, par, :],
                                                          in1=v4(tA)[:, :, par, :], op=ALU.mult), reads=[sqt.b, tA.b], writes=[qbz[par].b])
                k.op("dve", lambda e: e.tensor_tensor(out=v3(tC), in0=v3(bt)[:, :, 63:64].to_broadcast([128, 8, 64]), in1=v3(bt),
                                                      op=ALU.subtract), reads=[bt.b], writes=[tC.b])
                k.op("act", lambda e: e.activation(out=tC[:], in_=tC[:], func=AF.Exp), reads=[tC.b], writes=[tC.b])
                k.op("dve", lambda e: e.tensor_tensor(out=kgT[:], in0=k2[:], in1=tC[:], op=ALU.mult), reads=[k2.b, tC.b], writes=[kgT.b])
                k.op("act", lambda e: e.activation(out=eg8[:], in_=v3(bt)[:, :, 63], func=AF.Exp), reads=[bt.b], writes=[eg8.b])
                for tl in range(4):
                    r = regT[tcnt[0] % 4]
                    tcnt[0] += 1
                    k.op("pe", lambda e: e.transpose(r.ap(), kgT[:, tl * 128:(tl + 1) * 128], c["ident"][:]), reads=[kgT.b, c["ident"].b],
                         writes=[r.b])
                    k.op("act", lambda e: e.copy(out=kgz[0][0:64, tl, :], in_=r.ap(rows=slice(0, 64))), reads=[r.b], writes=[kgz[0].b])
                    k.op("act", lambda e: e.copy(out=kgz[1][64:128, tl, :], in_=r.ap(rows=slice(64, 128))), reads=[r.b], writes=[kgz[1].b])
                S = Sst[hd]
                for tl in range(4 if do_hg >= 2 else 0):
                    def mma(e):
                        e.matmul(r7A.ap(), lhsT=kz[0][:, tl, :], rhs=qz[0][:, tl, :], start=True, stop=False)
                        return e.matmul(r7A.ap(), lhsT=kz[1][:, tl, :], rhs=qz[1][:, tl, :], start=False, stop=True)
                    k.op("pe", mma, reads=[kz[0].b, kz[1].b, qz[0].b, qz[1].b], writes=[r7A.b])
                    k.op("dve", lambda e: e.tensor_tensor(out=Am[:], in0=r7A.ap(), in1=c["causal"][:], op=ALU.mult),
                         reads=[r7A.b, c["causal"].b], writes=[Am.b])
                    k.op("pe", lambda e: e.matmul(r7C.ap(), lhsT=kgz[0][:, tl, :], rhs=hv[:, tl, hd, :], start=True, stop=True),
                         reads=[kgz[0].b, hv.b], writes=[r7C.b])
                    k.op("pe", lambda e: e.matmul(r7D.ap(), lhsT=kgz[1][:, tl, :], rhs=hv[:, tl, hd, :], start=True, stop=True),
                         reads=[kgz[1].b, hv.b], writes=[r7D.b])
                    k.op("dve", lambda e: e.scalar_tensor_tensor(out=S[:], in0=S[:], scalar=eg8[:, 2 * tl:2 * tl + 1], in1=r7C.ap(),
                                                                 op0=ALU.mult, op1=ALU.add), reads=[S.b, eg8.b, r7C.b], writes=[S.b])
                    k.op("act", lambda e: e.copy(out=Sb[hd][1][:], in_=S[:]), reads=[S.b], writes=[Sb[hd][1].b])

                    def mmo(e):
                        e.matmul(r7B.ap(), lhsT=Am[:], rhs=hv[:, tl, hd, :], start=True, stop=False)
                        e.matmul(r7B.ap(), lhsT=qbz[0][:, tl, :], rhs=Sb[hd][0][:], start=False, stop=False)
                        return e.matmul(r7B.ap(), lhsT=qbz[1][:, tl, :], rhs=Sb[hd][1][:], start=False, stop=True)
                    k.op("pe", mmo, reads=[Am.b, hv.b, qbz[0].b, qbz[1].b, Sb[hd][0].b, Sb[hd][1].b], writes=[r7B.b])
                    k.op("dve", lambda e: e.scalar_tensor_tensor(out=S[:], in0=S[:], scalar=eg8[:, 2 * tl + 1:2 * tl + 2], in1=r7D.ap(),
                                                                 op0=ALU.mult, op1=ALU.add), reads=[S.b, eg8.b, r7D.b], writes=[S.b])
                    k.op("act", lambda e: e.copy(out=Sb[hd][0][:], in_=S[:]), reads=[S.b], writes=[Sb[hd][0].b])
                    k.op("act", lambda e: e.copy(out=o_sb[:], in_=r7B.ap()), reads=[r7B.b], writes=[o_sb.b])
                    k.op("act", lambda e: e.activation(out=junk2[:], in_=o_sb[:], func=AF.Square, accum_out=small2[:, 6:7]),
                         reads=[o_sb.b], writes=[junk2.b, small2.b])
                    k.op("act", lambda e: e.activation(out=small2[:, 7:8], in_=small2[:, 6:7], func=AF.Sqrt, scale=1.0 / 128,
                                                       bias=c["eps"][:, 0:1]), reads=[small2.b, c["eps"].b], writes=[small2.b])
                    k.op("dve", lambda e: e.reciprocal(out=small2[:, 7:8], in_=small2[:, 7:8]), reads=[small2.b], writes=[small2.b])
                    k.op("dve", lambda e: e.scalar_tensor_tensor(out=o2n[:], in0=o_sb[:], scalar=small2[:, 7:8], in1=hgn_sb[:, hd, :],
                                                                 op0=ALU.mult, op1=ALU.mult), reads=[o_sb.b, small2.b, hgn_sb.b], writes=[o2n.b])
                    k.op("pool", lambda e: e.tensor_tensor(out=yh[:], in0=o2n[:], in1=hgs[:, tl, hd * 128:(hd + 1) * 128], op=ALU.mult),
                         reads=[o2n.b, hgs.b], writes=[yh.b])
                    transpose_to(ystage[:, 2 + hd, tl * 128:(tl + 1) * 128], ystage.b, yh[:], yh.b, eng="act")

        try:
            interleave(k, [ml_stream, hg_stream])
        except _Stop:
            pass

    for st in range(NST):
        t0 = st * TW
        try:
            body(st, t0)
        except _Stop:
            pass
        tk = k.op("sp", lambda e: e.dma_start(out=yT.rearrange("c p t -> p c t")[:, :, t0:t0 + TW], in_=ystage[:]), reads=[ystage.b],
                  dsem=ds_y)
        out_toks.append(tk)
    k.finish(out_toks)
    return nc


def fm(a):
    T, C = a.shape
    return np.ascontiguousarray(a.T.reshape(C // 128, 128, T))


def unfm(aT):
    n, p, T = aT.shape
    return np.ascontiguousarray(aT.reshape(n * p, T).T)


def pvec(w):
    return np.ascontiguousarray(w.reshape(-1, 128).T)


def rep(w):
    return np.ascontiguousarray(np.broadcast_to(w[None], (128,) + w.shape))


def ab_core_inputs(I, layer, hgp, hT_b):
    j = layer // 2
    h = hgp
    W = I["ab_w_in"][j]
    o_u, o_v, o_o, o_i, o_f, o_hq, o_hf, o_hi, o_hg = 0, 1024, 2048, 3072, 3076, 3080, 4104, 5128, 6152
    sl = lambda o, n, i: W[:, o + i * n:o + (i + 1) * n]
    win = np.concatenate([sl(o_u, 256, h), sl(o_hq, 256, h), sl(o_hf, 256, h), sl(o_v, 256, h), sl(o_o, 256, h),
                          sl(o_hi, 256, h), sl(o_hg, 256, h), W[:, o_i + h:o_i + h + 1], W[:, o_f + h:o_f + h + 1]], axis=1)
    cwf = I["ml_conv_w"][j][:, h * 256:(h + 1) * 256]
    cw = np.ascontiguousarray(cwf.reshape(4, 2, 128).transpose(2, 1, 0))
    cb = np.ascontiguousarray(I["ml_conv_b"][j][h * 256:(h + 1) * 256].reshape(2, 128).T)
    lbl = np.ascontiguousarray(I["hg_lb_logits"][:, h * 256:(h + 1) * 256].reshape(2, 2, 128).transpose(2, 1, 0))
    return {
        "hT": hT_b, "nw": pvec(I["mix_norm"][layer]), "win": np.ascontiguousarray(win), "cw": cw, "cb": cb,
        "wq": np.ascontiguousarray(I["ml_wq"][j][h]), "wk": np.ascontiguousarray(I["ml_wk"][j][h]),
        "gb": rep(np.array([I["ml_i_bias"][j][h], I["ml_f_bias"][j][h]], np.float32)),
        "mln": rep(I["ml_out_norm"][j][h]), "skp": rep(I["ml_skip"][j][h * 256:(h + 1) * 256]),
        "lbl": lbl, "hgn": rep(I["hg_out_norm"][j][2 * h:2 * h + 2]),
    }


import math
from contextlib import ExitStack

NSA_BIG = 200.0
ROPE_INVF = np.power(np.float32(10000.0), -np.arange(64, dtype=np.float32) / 64).astype(np.float32)


def kb_barrier(k):
    toks = [Tok(k.sem[e], k.cnt[e], "E" + e, e) for e in k.engs if k.cnt[e] > 0]
    toks += [Tok(d.h, d.val, d.key, None) for d in k._all_dsems if d.val > 0]
    for e in k.engs:
        for t in toks:
            if t.eng != e:
                k._wait(e, t)


def build_nsa(T=8192, stop=99, debug=False):
    TW = 256
    NST = T // TW
    NT = T // 128
    NCB = (T - 32) // 16 + 1
    NCT = (NCB + 127) // 128
    NCOL = 1292
    scale = 128 ** -0.5
    nc = bass.Bass("TRN2", target_bir_lowering=False)
    k = KB(nc)
    k._all_dsems = []
    _ds = k.dsem

    def dsem2(name=None):
        d = _ds(name)
        k._all_dsems.append(d)
        return d
    k.dsem = dsem2

    def dram(name, shape, dt=F32, kind="ExternalInput"):
        return nc.dram_tensor(name, list(shape), dt, kind=kind).ap()
    hT = dram("hT", [16, 128, T])
    nw = dram("nw", [128, 16])
    win = dram("win", [2048, NCOL])
    qnw = dram("qnw", [128, 512])
    knw = dram("knw", [128, 3, 128])
    posT = dram("posT", [128, 2, 32])
    w1 = dram("w1", [2, 4096, 256])
    b1 = dram("b1", [128, 2, 2])
    w2 = dram("w2", [2, 256, 128])
    gbias = dram("gbias", [128, 12])
    yT = dram("yT", [4, 128, T], BF16, kind="ExternalOutput")
    dbg = dram("dbg", [128, 8192], F32, kind="ExternalOutput") if debug else None
    dbg_pos = [0]
    dbg_map = {}
    nc._dbg_map = dbg_map

    def dump(name, tt, ap, n):
        if not debug or name in dbg_map:
            return
        c0 = dbg_pos[0]
        dbg_pos[0] += n
        dbg_map[name] = (c0, n)
        k.op("sp", lambda e: e.dma_start(out=dbg[:, c0:c0 + n], in_=ap), reads=[tt.b], dsem=k.dsem())

    c = make_consts(k)
    win_v = win.rearrange("(dc p) f -> p dc f", p=128)
    ps = [nc.alloc_psum_tensor(f"ps{i}", [128, 512], F32) for i in range(8)]
    accS = [PReg(k, ps[i], 0, 512, f"accS{i}") for i in range(2)]
    _oset = [PReg(k, ps[2 + h], 0, 130, f"o_{h}") for h in range(4)]
    oreg = [_oset, _oset]
    impT = PReg(k, ps[6], 0, 512, "impT")
    ps_ss = impT
    regT = [PReg(k, ps[7], i * 128, (i + 1) * 128, f"regT{i}") for i in range(4)]
    acnt = [0]
    tcnt = [0]

    def next_acc():
        a = accS[acnt[0] % 2]
        acnt[0] += 1
        return a

    def next_regT():
        r = regT[tcnt[0] % 4]
        tcnt[0] += 1
        return r

    def ld(name, shape, src, dt=F32, eng="sp"):
        t = sb(k, name, shape, dt)
        k.op(eng, lambda e: e.dma_start(out=t[:], in_=src), writes=[t.b], dsem=k.dsem())
        return t

    nw_sb = ld("nw_sb", [128, 16], nw)
    qnw_sb = ld("qnw_sb", [128, 512], qnw)
    knw_sb = ld("knw_sb", [128, 3, 128], knw)
    b1_sb = ld("b1_sb", [128, 2, 2], b1)
    gb_sb = ld("gb_sb", [128, 12], gbias)
    hTt = sb(k, "hTt", [128, 16, TW], F32)
    ds_h = k.dsem()
    xT = sb(k, "xT", [128, 16, TW], BF16)
    xT2 = sb(k, "xT2", [128, 16, TW], BF16)
    hTt2 = sb(k, "hTt2", [128, 16, TW], F32)
    ds_h2 = k.dsem()
    rstd = sb(k, "rstd", [128, TW], F32)
    scrsq = [sb(k, f"scrsq{i}", [128, TW], F32) for i in range(2)]
    kcmpT = sb(k, "kcmpT", [128, NCT * 128], BF16)
    vcmp = sb(k, "vcmp", [128, NCT, 130], BF16)
    cover = sb(k, "cover", [128, NCT, 128], BF16)
    identb = sb(k, "identb", [128, 128], BF16)
    k.op("dve", lambda e: e.tensor_copy(out=identb[:], in_=c["ident"][:]), reads=[c["ident"].b], writes=[identb.b])
    k.op("dve", lambda e: e.memset(kcmpT[:], 0.0), writes=[kcmpT.b])
    k.op("dve", lambda e: e.memset(vcmp[:], 0.0), writes=[vcmp.b])
    k.op("dve", lambda e: e.memset(vcmp[:, :, 128:129], 1.0), writes=[vcmp.b])
    invf = sb(k, "invf", [128, 64], F32)
    for i in range(64):
        k.op("dve", lambda e: e.memset(invf[:, i:i + 1], float(ROPE_INVF[i])), writes=[invf.b])
    pidx_i = sb(k, "pidx_i", [128, 1], I32)
    k.op("pool", lambda e: e.iota(pidx_i[:], pattern=[[0, 1]], base=0, channel_multiplier=1), writes=[pidx_i.b])
    pidx = sb(k, "pidx", [128, 1], F32)
    k.op("dve", lambda e: e.tensor_copy(out=pidx[:], in_=pidx_i[:]), reads=[pidx_i.b], writes=[pidx.b])
    pcol = sb(k, "pcol", [128, 1], F32)
    ang = sb(k, "ang", [128, 64], F32)
    rr = sb(k, "rr", [128, 64], F32)
    rf = sb(k, "rf", [128, 64], F32)
    ri = sb(k, "ri", [128, 64], I32)
    cos_t = sb(k, "cos_t", [128, 64], F32)
    sin_t = sb(k, "sin_t", [128, 64], F32)
    rtmp = sb(k, "rtmp", [128, 4, 64], F32)
    TWO_PI = 2 * math.pi

    def rope_tables(mult, add, cx=None):
        cx = cx or cx0
        k.op("dve", lambda e: e.tensor_scalar(out=cx.pcol[:], in0=pidx[:], scalar1=float(mult), scalar2=float(add), op0=ALU.mult,
                                              op1=ALU.add), reads=[pidx.b], writes=[cx.pcol.b])
        k.op("dve", lambda e: e.tensor_scalar(out=cx.ang[:], in0=invf[:], scalar1=cx.pcol[:, 0:1], scalar2=None, op0=ALU.mult),
             reads=[invf.b, cx.pcol.b], writes=[cx.ang.b])
        for (off, dst) in ((0.0, cx.sin_t), (math.pi / 2, cx.cos_t)):
            k.op("dve", lambda e: e.tensor_scalar(out=cx.rr[:], in0=cx.ang[:], scalar1=off, scalar2=None, op0=ALU.add), reads=[cx.ang.b],
                 writes=[cx.rr.b])
            k.op("dve", lambda e: e.tensor_scalar(out=cx.rf[:], in0=cx.rr[:], scalar1=1.0 / TWO_PI, scalar2=None, op0=ALU.mult),
                 reads=[cx.rr.b], writes=[cx.rf.b])
            k.op("dve", lambda e: e.tensor_copy(out=cx.ri[:], in_=cx.rf[:]), reads=[cx.rf.b], writes=[cx.ri.b])
            k.op("dve", lambda e: e.tensor_copy(out=cx.rf[:], in_=cx.ri[:]), reads=[cx.ri.b], writes=[cx.rf.b])
            k.op("dve", lambda e: e.scalar_tensor_tensor(out=cx.rr[:], in0=cx.rf[:], scalar=-TWO_PI, in1=cx.rr[:], op0=ALU.mult, op1=ALU.add),
                 reads=[cx.rf.b, cx.rr.b], writes=[cx.rr.b])
            k.op("dve", lambda e: e.tensor_scalar(out=cx.rf[:], in0=cx.rr[:], scalar1=math.pi, scalar2=None, op0=ALU.is_gt), reads=[cx.rr.b],
                 writes=[cx.rf.b])
            k.op("dve", lambda e: e.scalar_tensor_tensor(out=cx.rr[:], in0=cx.rf[:], scalar=-TWO_PI, in1=cx.rr[:], op0=ALU.mult, op1=ALU.add),
                 reads=[cx.rf.b, cx.rr.b], writes=[cx.rr.b])
            k.op("act", lambda e: e.activation(out=dst[:], in_=cx.rr[:], func=AF.Sin), reads=[cx.rr.b], writes=[dst.b])

    def apply_rope(dst, src, H, cx=None):
        cx = cx or cx0
        cb = cx.cos_t[:].rearrange("p (o f) -> p o f", o=1).to_broadcast([128, H, 64])
        sbb = cx.sin_t[:].rearrange("p (o f) -> p o f", o=1).to_broadcast([128, H, 64])
        x1, x2 = src[:, 0:H, 0:64], src[:, 0:H, 64:128]
        tm = cx.rtmp[:, 0:H, :]
        k.op("dve", lambda e: e.tensor_tensor(out=tm, in0=x2, in1=sbb, op=ALU.mult), reads=[src.b, cx.sin_t.b], writes=[cx.rtmp.b])
        k.op("dve", lambda e: e.tensor_tensor(out=dst[:, 0:H, 0:64], in0=x1, in1=cb, op=ALU.mult), reads=[src.b, cx.cos_t.b], writes=[dst.b])
        k.op("dve", lambda e: e.tensor_tensor(out=dst[:, 0:H, 0:64], in0=dst[:, 0:H, 0:64], in1=tm, op=ALU.subtract),
             reads=[dst.b, cx.rtmp.b], writes=[dst.b])
        k.op("dve", lambda e: e.tensor_tensor(out=tm, in0=x1, in1=sbb, op=ALU.mult), reads=[src.b, cx.sin_t.b], writes=[cx.rtmp.b])
        k.op("dve", lambda e: e.tensor_tensor(out=dst[:, 0:H, 64:128], in0=x2, in1=cb, op=ALU.mult), reads=[src.b, cx.cos_t.b], writes=[dst.b])
        k.op("dve", lambda e: e.tensor_tensor(out=dst[:, 0:H, 64:128], in0=dst[:, 0:H, 64:128], in1=tm, op=ALU.add),
             reads=[dst.b, cx.rtmp.b], writes=[dst.b])

    small = sb(k, "small", [128, 16], F32)
    junk = sb(k, "junk", [128, 128], F32)
    kn = sb(k, "kn", [128, 4, 128], F32)
    kr = sb(k, "kr", [128, 4, 128], F32)

    def rms_heads(src_reg, col0, H, wfn, post_scale=1.0, nrows=128, cx=None):
        cx = cx or cx0
        rows = slice(0, nrows)
        for h in range(H):
            k.op("act", lambda e: e.activation(out=cx.junk[rows, :], in_=src_reg.ap(rows, col0[h], col0[h] + 128), func=AF.Square,
                                               accum_out=cx.small[rows, h:h + 1]), reads=[src_reg.b], writes=[cx.junk.b, cx.small.b])
        k.op("act", lambda e: e.activation(out=cx.small[rows, 4:4 + H], in_=cx.small[rows, 0:H], func=AF.Sqrt, scale=1.0 / 128,
                                           bias=c["eps"][rows, 0:1]), reads=[cx.small.b, c["eps"].b], writes=[cx.small.b])
        k.op("dve", lambda e: e.reciprocal(out=cx.small[rows, 4:4 + H], in_=cx.small[rows, 4:4 + H]), reads=[cx.small.b], writes=[cx.small.b])
        if post_scale != 1.0:
            k.op("dve", lambda e: e.tensor_scalar(out=cx.small[rows, 4:4 + H], in0=cx.small[rows, 4:4 + H], scalar1=post_scale, scalar2=None,
                                                  op0=ALU.mult), reads=[cx.small.b], writes=[cx.small.b])
        for h in range(H):
            wt, wap = wfn(h)
            k.op("dve", lambda e: e.scalar_tensor_tensor(out=cx.kn[rows, h, :], in0=src_reg.ap(rows, col0[h], col0[h] + 128),
                                                         scalar=cx.small[rows, 4 + h:5 + h], in1=wap, op0=ALU.mult, op1=ALU.mult),
                 reads=[src_reg.b, cx.small.b, wt.b], writes=[cx.kn.b])

    class _Cx:
        pass
    cx0 = _Cx()
    cx0.pcol, cx0.ang, cx0.rr, cx0.rf, cx0.ri, cx0.cos_t, cx0.sin_t, cx0.rtmp = pcol, ang, rr, rf, ri, cos_t, sin_t, rtmp
    cx0.small, cx0.junk, cx0.kn, cx0.kr = small, junk, kn, kr

    def new_cx(alloc, tag):
        cx = _Cx()
        cx.pcol = alloc("pcol" + tag, [128, 1], F32)
        cx.ang = alloc("ang" + tag, [128, 64], F32)
        cx.rr = alloc("rr" + tag, [128, 64], F32)
        cx.rf = alloc("rf" + tag, [128, 64], F32)
        cx.ri = alloc("ri" + tag, [128, 64], I32)
        cx.cos_t = alloc("cos_t" + tag, [128, 64], F32)
        cx.sin_t = alloc("sin_t" + tag, [128, 64], F32)
        cx.rtmp = alloc("rtmp" + tag, [128, 4, 64], F32)
        cx.small = alloc("small" + tag, [128, 16], F32)
        cx.junk = alloc("junk" + tag, [128, 128], F32)
        cx.kn = alloc("kn" + tag, [128, 4, 128], F32)
        cx.kr = alloc("kr" + tag, [128, 4, 128], F32)
        return cx

    out_toks = []
    es = ExitStack()

    def sbs(name, shape, dt):
        return TT(k, es.enter_context(nc.sbuf_tensor(name, list(shape), dt)), name)

    win_a = sbs("win_a", [128, 16, 256], BF16)
    k.op("pool", lambda e: e.dma_start(out=win_a[:], in_=win_v[:, :, 0:256]), writes=[win_a.b], dsem=k.dsem())
    kvT = [sbs(f"kvT{i}", [128, T], BF16) for i in range(2)]
    w1_sb = sbs("w1_sb", [128, 32, 256], BF16)
    w2_sb = sbs("w2_sb", [128, 2, 2, 128], BF16)
    posT_sb = sbs("posT_sb", [128, 2, 32], BF16)
    hsil = sbs("hsil", [128, 2, 128], BF16)
    biasv = sbs("biasv", [128, 2], F32)
    k.op("pool", lambda e: e.dma_start(out=posT_sb[:], in_=posT), writes=[posT_sb.b], dsem=k.dsem())
    for kv in range(2):
        k.op("pool", lambda e: e.dma_start(out=w2_sb[:, kv, :, :], in_=w2[kv].rearrange("(c p) e -> p c e", p=128)), writes=[w2_sb.b],
             dsem=k.dsem())
    xT_dram = nc.dram_tensor("xT_scratch", [NST, 128, 16 * TW], BF16).ap()
    xd_b = k.bufs(NST, "xd")
    ds_xs = k.dsem()
    xTs = [xT, xT2]
    ds_xl = [k.dsem(), k.dsem()]

    def load_xT(st):
        xt = xTs[st % 2]
        k.op("sp", lambda e: e.dma_start(out=xt[:].rearrange("p a b -> p (a b)"), in_=xT_dram[st]), reads=[xd_b[st]], writes=[xt.b],
             dsem=ds_xl[st % 2])

    hbufs = [(hTt, ds_h), (hTt2, ds_h2)]
    load_h_tile(k, hT, hTt, 0, TW, ds_h)
    for st in range(NST):
        t0 = st * TW
        if st + 1 < NST:
            hb, hd = hbufs[(st + 1) % 2]
            load_h_tile(k, hT, hb, t0 + TW, TW, hd)
        hb, hd = hbufs[st % 2]
        norm_supertile(k, c, hT, nw_sb, hb, xT, ps_ss, rstd, scrsq, t0, TW, hd, preloaded=True)
        k.op("sp", lambda e: e.dma_start(out=xT_dram[st], in_=xT[:].rearrange("p a b -> p (a b)")), reads=[xT.b], writes=[xd_b[st]],
             dsem=ds_xs)
        for kv in range(2):
            a = next_acc()

            def mm(e):
                for dc in range(16):
                    ins = e.matmul(a.ap(b=TW), lhsT=win_a[:, dc, kv * 128:(kv + 1) * 128], rhs=xT[:, dc, :], start=(dc == 0), stop=(dc == 15))
                return ins
            k.op("pe", mm, reads=[win_a.b, xT.b], writes=[a.b])
            k.op("act", lambda e: e.copy(out=kvT[kv][:, t0:t0 + TW], in_=a.ap(b=TW)), reads=[a.b], writes=[kvT[kv].b])
    ds_w1 = k.dsem()
    for kv in range(2):
        k.op("pool", lambda e: e.dma_start(out=w1_sb[:], in_=w1[kv].rearrange("(l p) h -> p l h", p=128)), writes=[w1_sb.b], dsem=ds_w1)
        for hc in range(2):
            r = next_regT()

            def mmp(e):
                for l in range(32):
                    ins = e.matmul(r.ap(b=1), lhsT=w1_sb[:, l, hc * 128:(hc + 1) * 128], rhs=posT_sb[:, kv, l:l + 1], start=(l == 0),
                                   stop=(l == 31))
                return ins
            k.op("pe", mmp, reads=[w1_sb.b, posT_sb.b], writes=[r.b])
            k.op("dve", lambda e: e.tensor_tensor(out=biasv[:, hc:hc + 1], in0=r.ap(b=1), in1=b1_sb[:, kv, hc:hc + 1], op=ALU.add),
                 reads=[r.b, b1_sb.b], writes=[biasv.b])
        for nti in range(NCT):
            nn = min(128, NCB - 128 * nti)
            for hc in range(2):
                r = next_regT()

                def mmh(e):
                    for l in range(32):
                        s0 = 16 * 128 * nti + l
                        ins = e.matmul(r.ap(b=nn), lhsT=w1_sb[:, l, hc * 128:(hc + 1) * 128], rhs=kvT[kv][:, s0:s0 + 16 * (nn - 1) + 1:16],
                                       start=(l == 0), stop=(l == 31))
                    return ins
                k.op("pe", mmh, reads=[w1_sb.b, kvT[kv].b], writes=[r.b])
                k.op("act", lambda e: e.activation(out=hsil[:, hc, 0:nn], in_=r.ap(b=nn), func=AF.Silu, bias=biasv[:, hc:hc + 1], scale=1.0),
                     reads=[r.b, biasv.b], writes=[hsil.b])
            r = next_regT()

            def mmo(e):
                for hc in range(2):
                    ins = e.matmul(r.ap(rows=slice(0, nn)), lhsT=hsil[:, hc, 0:nn], rhs=w2_sb[:, kv, hc, :], start=(hc == 0), stop=(hc == 1))
                return ins
            k.op("pe", mmo, reads=[hsil.b, w2_sb.b], writes=[r.b])
            if kv == 0:
                rms_heads(r, [0], 1, lambda h: (knw_sb, knw_sb[0:nn, 0, :]), nrows=nn)
                rope_tables(16.0, 16.0 * 128 * nti + 31.0)
                apply_rope(kr, kn, 1)
                r2 = next_regT()
                k.op("pe", lambda e: e.transpose(r2.ap(b=nn), kr[0:nn, 0, :], c["ident"][0:nn, 0:nn]), reads=[kr.b, c["ident"].b],
                     writes=[r2.b])
                k.op("act", lambda e: e.copy(out=kcmpT[:, nti * 128:nti * 128 + nn], in_=r2.ap(b=nn)), reads=[r2.b], writes=[kcmpT.b])
            else:
                k.op("act", lambda e: e.copy(out=vcmp[0:nn, nti, 0:128], in_=r.ap(rows=slice(0, nn))), reads=[r.b], writes=[vcmp.b])
    if debug:
        dump("kcmpT", kcmpT, kcmpT[:, 0:128], 64) if False else None
    kb_barrier(k)
    es.close()
    es = ExitStack()
    if stop <= 1:
        tk = k.op("sp", lambda e: e.dma_start(out=yT[0, :, 0:NCT * 128], in_=kcmpT[:]), reads=[kcmpT.b], dsem=k.dsem())
        tk2 = k.op("sp", lambda e: e.dma_start(out=yT[1, :, 0:NCT * 130], in_=vcmp[:].rearrange("p a b -> p (a b)")), reads=[vcmp.b],
                   dsem=k.dsem())
        k.finish([tk, tk2])
        return nc

    ksT = sbs("ksT", [128, T], BF16)
    kwT = sbs("kwT", [128, T], BF16)
    vs_e = sbs("vs_e", [128, NT, 130], BF16)
    vw_e = sbs("vw_e", [128, NT, 130], BF16)
    k.op("dve", lambda e: e.memset(vs_e[:, :, 128:130], 1.0), writes=[vs_e.b])
    k.op("dve", lambda e: e.memset(vw_e[:, :, 128:130], 1.0), writes=[vw_e.b])
    esB = ExitStack()
    win_b = TT(k, esB.enter_context(nc.sbuf_tensor("win_b", [128, 16, 512], BF16)), "win_b")
    k.op("pool", lambda e: e.dma_start(out=win_b[:], in_=win_v[:, :, 256:768]), writes=[win_b.b], dsem=k.dsem())
    cxB = [cx0, new_cx(lambda n_, sh, dt: TT(k, esB.enter_context(nc.sbuf_tensor(n_, list(sh), dt)), n_), "_b1")]
    load_xT(0)
    for st in range(NST):
        t0 = st * TW
        if st + 1 < NST:
            load_xT(st + 1)
        xc = xTs[st % 2]
        def tile_stream(tci, cx):
            ti = st * (TW // 128) + tci
            a = next_acc()

            def mm(e):
                for dc in range(16):
                    ins = e.matmul(a.ap(), lhsT=xc[:, dc, tci * 128:(tci + 1) * 128], rhs=win_b[:, dc, :], start=(dc == 0), stop=(dc == 15))
                return ins
            k.op("pe", mm, reads=[win_b.b, xc.b], writes=[a.b])
            k.op("act", lambda e: e.copy(out=vs_e[:, ti, 0:128], in_=a.ap(a=128, b=256)), reads=[a.b], writes=[vs_e.b])
            k.op("act", lambda e: e.copy(out=vw_e[:, ti, 0:128], in_=a.ap(a=384, b=512)), reads=[a.b], writes=[vw_e.b])
            rms_heads(a, [0, 256], 2, lambda h: (knw_sb, knw_sb[:, 1 + h, :]), cx=cx)
            rope_tables(1.0, float(ti * 128), cx=cx)
            apply_rope(cx.kr, cx.kn, 2, cx=cx)
            for h, dstT in ((0, ksT), (1, kwT)):
                r2 = next_regT()
                k.op("pe", lambda e: e.transpose(r2.ap(), cx.kr[:, h, :], c["ident"][:]), reads=[cx.kr.b, c["ident"].b], writes=[r2.b])
                k.op("act", lambda e: e.copy(out=dstT[:, ti * 128:(ti + 1) * 128], in_=r2.ap()), reads=[r2.b], writes=[dstT.b])

        interleave(k, [lambda: tile_stream(0, cxB[0]), lambda: tile_stream(1, cxB[1])])
    kb_barrier(k)
    esB.close()
    if stop <= 2:
        tk = k.op("sp", lambda e: e.dma_start(out=yT[0, :, :], in_=ksT[:]), reads=[ksT.b], dsem=k.dsem())
        tk2 = k.op("sp", lambda e: e.dma_start(out=yT[1, :, :], in_=kwT[:]), reads=[kwT.b], dsem=k.dsem())
        tk3 = k.op("sp", lambda e: e.dma_start(out=yT[2, :, 0:NT * 128].rearrange("p (a b) -> p a b", b=128), in_=vs_e[:, :, 0:128]),
                   reads=[vs_e.b], dsem=k.dsem())
        k.finish([tk, tk2, tk3])
        return nc

    win_q = sbs("win_q", [128, 16, 524], BF16)
    k.op("pool", lambda e: e.dma_start(out=win_q[:], in_=win_v[:, :, 768:1292]), writes=[win_q.b], dsem=k.dsem())
    Esel = sbs("Esel", [128, T], BF16)
    k.op("dve", lambda e: e.memset(Esel[:], 1.0), writes=[Esel.b])
    k.op("pool", lambda e: e.affine_select(out=Esel[:], in_=Esel[:], pattern=[[1, T]], compare_op=ALU.is_ge, fill=0.0, base=0,
                                           channel_multiplier=-64), reads=[Esel.b], writes=[Esel.b])
    k.op("pool", lambda e: e.affine_select(out=Esel[:], in_=Esel[:], pattern=[[-1, T]], compare_op=ALU.is_ge, fill=0.0, base=63,
                                           channel_multiplier=64), reads=[Esel.b], writes=[Esel.b])
    f32a = sbs("f32a", [128, 512], F32)
    f32b = sbs("f32b", [128, 512], F32)
    ones4 = sbs("ones4", [128, 512], F32)
    zeros4 = sbs("zeros4", [128, 512], F32)
    k.op("dve", lambda e: e.memset(ones4[:], 1.0), writes=[ones4.b])
    k.op("dve", lambda e: e.memset(zeros4[:], 0.0), writes=[zeros4.b])
    for nti in range(NCT):
        k.op("pool", lambda e: e.affine_select(out=f32a[:, 0:128], in_=ones4[:, 0:128], pattern=[[64, 128]], compare_op=ALU.is_gt, fill=0.0,
                                               base=64 - 2048 * nti, channel_multiplier=-16), reads=[ones4.b], writes=[f32a.b])
        k.op("pool", lambda e: e.affine_select(out=f32a[:, 0:128], in_=f32a[:, 0:128], pattern=[[-64, 128]], compare_op=ALU.is_gt, fill=0.0,
                                               base=2048 * nti + 32, channel_multiplier=16), reads=[f32a.b], writes=[f32a.b])
        k.op("dve", lambda e: e.tensor_copy(out=cover[:, nti, :], in_=f32a[:, 0:128]), reads=[f32a.b], writes=[cover.b])
    cneg = sbs("cneg", [128, 512], BF16)
    wneg = sbs("wneg", [128, 512], BF16)
    k.op("pool", lambda e: e.affine_select(out=f32a[:].rearrange("p (h j) -> p h j", h=4), in_=zeros4[:].rearrange("p (h j) -> p h j", h=4),
                                           pattern=[[0, 4], [1, 128]], compare_op=ALU.is_ge, fill=-NSA_BIG, base=0, channel_multiplier=-1),
         reads=[zeros4.b], writes=[f32a.b])
    k.op("dve", lambda e: e.tensor_copy(out=cneg[:], in_=f32a[:]), reads=[f32a.b], writes=[cneg.b])
    k.op("pool", lambda e: e.affine_select(out=f32a[:].rearrange("p (h j) -> p h j", h=4), in_=zeros4[:].rearrange("p (h j) -> p h j", h=4),
                                           pattern=[[0, 4], [-1, 128]], compare_op=ALU.is_ge, fill=-NSA_BIG, base=-1, channel_multiplier=1),
         reads=[zeros4.b], writes=[f32a.b])
    k.op("dve", lambda e: e.tensor_copy(out=wneg[:], in_=f32a[:]), reads=[f32a.b], writes=[wneg.b])
    c1e4 = sbs("c1e4", [128, 128], F32)
    k.op("dve", lambda e: e.memset(c1e4[:], 1e4), writes=[c1e4.b])

    gsb = sbs("gsb", [128, 12], F32)
    qn4 = sbs("qn4", [128, 4, 128], F32)
    qr4 = sbs("qr4", [128, 4, 128], F32)
    qT = sbs("qT", [128, 512], BF16)
    Ef = sbs("Ef", [128, 512], F32)
    m01 = sbs("m01", [128, 512], F32)
    Pt = [sbs(f"Pt{i}", [128, 512], BF16) for i in range(2)]
    pcnt = [0]
    impS = sbs("impS", [128, 512], F32)
    imp = sbs("imp", [128, 128], F32)
    bon = sbs("bon", [128, 128], F32)
    impf = sbs("impf", [128, 128], F32)
    val01 = sbs("val01", [128, 128], F32)
    wk_ = sbs("wk_", [128, 128], F32)
    m8 = sbs("m8", [128, 8], F32)
    selm = sbs("selm", [128, 128], F32)
    nmT = sbs("nmT", [128, 512], BF16)
    zt = sbs("zt", [128, 3, 4], F32)
    wgt = sbs("wgt", [128, 4], F32)
    oacc = sbs("oacc", [128, 4, 128], F32)
    ystage = sbs("ystage", [128, 4, TW], BF16)
    ds_y = k.dsem()

    def exp_to_P(a, mask01=None):
        p = Pt[pcnt[0] % 2]
        pcnt[0] += 1
        if mask01 is None:
            k.op("act", lambda e: e.activation(out=p[:], in_=a.ap(), func=AF.Exp), reads=[a.b], writes=[p.b])
        else:
            k.op("act", lambda e: e.activation(out=Ef[:], in_=a.ap(), func=AF.Exp), reads=[a.b], writes=[Ef.b])
            k.op("dve", lambda e: e.tensor_tensor(out=p[:], in0=Ef[:], in1=mask01[:], op=ALU.mult), reads=[Ef.b, mask01.b], writes=[p.b])
        return p

    def pv(p, vt, vidx, oset, first, last):
        def mm4(e):
            for h in range(4):
                ins = e.matmul(oset[h].ap(), lhsT=p[:, h * 128:(h + 1) * 128], rhs=vt[:, vidx, :], start=first, stop=last)
            return ins
        k.op("pe", mm4, reads=[p.b, vt.b], writes=[oset[h].b for h in range(4)])

    def run_pairs(jobs, oset):
        n = len(jobs)
        accs = [None] * n

        def emit_S(i):
            a = next_acc()
            accs[i] = a
            k.op("pe", lambda e: jobs[i]["mm"](e, a), reads=jobs[i]["reads"], writes=[a.b])
        if n:
            emit_S(0)
        for i in range(n):
            if i + 1 < n:
                emit_S(i + 1)
            msk = jobs[i]["pre"]() if "pre" in jobs[i] else None
            p = exp_to_P(accs[i], msk)
            pv(p, jobs[i]["vt"], jobs[i]["vidx"], oset, i == 0, i == n - 1)
            if "post" in jobs[i]:
                jobs[i]["post"](p)

    def combine(oset, br, first, gsb=None):
        for h in range(4):
            k.op("dve", lambda e: e.tensor_scalar(out=zt[:, br, h:h + 1], in0=oset[h].ap(a=128, b=129), scalar1=1e-30, scalar2=None,
                                                  op0=ALU.max), reads=[oset[h].b], writes=[zt.b])
        k.op("dve", lambda e: e.reciprocal(out=zt[:, br, :], in_=zt[:, br, :]), reads=[zt.b], writes=[zt.b])
        k.op("dve", lambda e: e.tensor_tensor(out=wgt[:], in0=zt[:, br, :], in1=gsb[:, br:12:3], op=ALU.mult), reads=[zt.b, gsb.b],
             writes=[wgt.b])
        for h in range(4):
            if first:
                k.op("dve", lambda e: e.tensor_scalar(out=oacc[:, h, :], in0=oset[h].ap(b=128), scalar1=wgt[:, h:h + 1], scalar2=None,
                                                      op0=ALU.mult), reads=[oset[h].b, wgt.b], writes=[oacc.b])
            else:
                k.op("dve", lambda e: e.scalar_tensor_tensor(out=oacc[:, h, :], in0=oset[h].ap(b=128), scalar=wgt[:, h:h + 1],
                                                             in1=oacc[:, h, :], op0=ALU.mult, op1=ALU.add),
                     reads=[oset[h].b, wgt.b, oacc.b], writes=[oacc.b])

    qTs = [qT, sbs("qT_b", [128, 512], BF16)]
    gsbs = [gsb, sbs("gsb_b", [128, 12], F32)]
    qacc = impT

    def q_prep_a(qt):
        st_, tci_ = divmod(qt, TW // 128)
        xc = xTs[st_ % 2]
        tsl_ = slice(tci_ * 128, (tci_ + 1) * 128)
        a = qacc
        g_ = gsbs[qt % 2]

        def mm(e):
            for dc in range(16):
                ins = e.matmul(a.ap(), lhsT=xc[:, dc, tsl_], rhs=win_q[:, dc, 0:512], start=(dc == 0), stop=(dc == 15))
            return ins
        k.op("pe", mm, reads=[win_q.b, xc.b], writes=[a.b])
        rg = next_regT()

        def mmg(e):
            for dc in range(16):
                ins = e.matmul(rg.ap(b=12), lhsT=xc[:, dc, tsl_], rhs=win_q[:, dc, 512:524], start=(dc == 0), stop=(dc == 15))
            return ins
        k.op("pe", mmg, reads=[win_q.b, xc.b], writes=[rg.b])
        k.op("dve", lambda e: e.tensor_tensor(out=g_[:], in0=rg.ap(b=12), in1=gb_sb[:], op=ALU.add), reads=[rg.b, gb_sb.b], writes=[g_.b])
        k.op("act", lambda e: e.activation(out=g_[:], in_=g_[:], func=AF.Sigmoid), reads=[g_.b], writes=[g_.b])
        rms_heads(a, [0, 128, 256, 384], 4, lambda h: (qnw_sb, qnw_sb[:, h * 128:(h + 1) * 128]), post_scale=scale)
        rope_tables(1.0, float(qt * 128))
        apply_rope(qr4, kn, 4)

    def q_prep_b(qt):
        q_ = qTs[qt % 2]
        for h in range(4):
            r2 = next_regT()
            k.op("pe", lambda e: e.transpose(r2.ap(), qr4[:, h, :], c["ident"][:]), reads=[qr4.b, c["ident"].b], writes=[r2.b])
            k.op("act", lambda e: e.copy(out=q_[:, h * 128:(h + 1) * 128], in_=r2.ap()), reads=[r2.b], writes=[q_.b])

    load_xT(0)
    if NST > 1:
        load_xT(1)
    q_prep_a(0)
    q_prep_b(0)
    for st in range(NST):
        t0s = st * TW
        for tci in range(TW // 128):
            qt = st * (TW // 128) + tci
            t0 = qt * 128
            tsl = slice(tci * 128, (tci + 1) * 128)
            qT = qTs[qt % 2]
            gsb = gsbs[qt % 2]
            nmax = (t0 + 127 - 31) // 16
            ntiles = 0 if nmax < 0 else min(NCT, nmax // 128 + 1)
            oc = oreg[0]
            if ntiles == 0:
                k.op("dve", lambda e: e.memset(imp[:], 0.0), writes=[imp.b])
            jobs = []
            for nt in range(ntiles):
                def mmc(e, a, nt=nt):
                    return e.matmul(a.ap(), lhsT=kcmpT[:, nt * 128:(nt + 1) * 128], rhs=qT[:], start=True, stop=True)

                def pre(nt=nt):
                    k.op("pool", lambda e: e.affine_select(out=m01[:].rearrange("p (h j) -> p h j", h=4),
                                                           in_=ones4[:].rearrange("p (h j) -> p h j", h=4), pattern=[[0, 4], [1, 128]],
                                                           compare_op=ALU.is_ge, fill=0.0, base=t0 - 2048 * nt - 31, channel_multiplier=-16),
                         reads=[ones4.b], writes=[m01.b])
                    return m01

                def post(p, nt=nt):
                    k.op("pe", lambda e: e.matmul(impT.ap(), lhsT=cover[:, nt, :], rhs=p[:], start=(nt == 0), stop=(nt == ntiles - 1)),
                         reads=[cover.b, p.b], writes=[impT.b])
                jobs.append(dict(mm=mmc, reads=[kcmpT.b, qT.b], vt=vcmp, vidx=nt, pre=pre, post=post))
            run_pairs(jobs, oc)
            if ntiles > 0:
                combine(oc, 0, True, gsb)
                k.op("act", lambda e: e.copy(out=impS[:], in_=impT.ap()), reads=[impT.b], writes=[impS.b])
                for h in range(4):
                    r2 = next_regT()
                    k.op("pe", lambda e: e.transpose(r2.ap(), impS[:, h * 128:(h + 1) * 128], c["ident"][:]), reads=[impS.b, c["ident"].b],
                         writes=[r2.b])
                    if h == 0:
                        k.op("dve", lambda e: e.tensor_scalar(out=imp[:], in0=r2.ap(), scalar1=zt[:, 0, 0:1], scalar2=None, op0=ALU.mult),
                             reads=[r2.b, zt.b], writes=[imp.b])
                    else:
                        k.op("dve", lambda e: e.scalar_tensor_tensor(out=imp[:], in0=r2.ap(), scalar=zt[:, 0, h:h + 1], in1=imp[:],
                                                                     op0=ALU.mult, op1=ALU.add), reads=[r2.b, zt.b, imp.b], writes=[imp.b])
            else:
                k.op("dve", lambda e: e.memset(oacc[:], 0.0), writes=[oacc.b])
            k.op("pool", lambda e: e.affine_select(out=bon[:], in_=c1e4[:], pattern=[[-64, 128]], compare_op=ALU.is_ge, fill=0.0, base=t0,
                                                   channel_multiplier=1), reads=[c1e4.b], writes=[bon.b])
            k.op("pool", lambda e: e.affine_select(out=bon[:], in_=bon[:], pattern=[[64, 128]], compare_op=ALU.is_ge, fill=0.0,
                                                   base=127 - t0, channel_multiplier=-1), reads=[bon.b], writes=[bon.b])
            k.op("pool", lambda e: e.memset(bon[:, 0:1], 1e4), writes=[bon.b])
            k.op("dve", lambda e: e.tensor_tensor(out=impf[:], in0=imp[:], in1=bon[:], op=ALU.add), reads=[imp.b, bon.b], writes=[impf.b])
            k.op("pool", lambda e: e.affine_select(out=val01[:], in_=ones4[:, 0:128], pattern=[[-64, 128]], compare_op=ALU.is_ge, fill=0.0,
                                                   base=t0, channel_multiplier=1), reads=[ones4.b], writes=[val01.b])
            k.op("dve", lambda e: e.tensor_tensor(out=impf[:], in0=impf[:], in1=val01[:], op=ALU.mult), reads=[impf.b, val01.b],
                 writes=[impf.b])
            k.op("dve", lambda e: e.tensor_scalar(out=val01[:], in0=val01[:], scalar1=1e30, scalar2=-1e30, op0=ALU.mult, op1=ALU.add),
                 reads=[val01.b], writes=[val01.b])
            k.op("dve", lambda e: e.tensor_tensor(out=impf[:], in0=impf[:], in1=val01[:], op=ALU.add), reads=[impf.b, val01.b],
                 writes=[impf.b])
            k.op("dve", lambda e: e.max(out=m8[:], in_=impf[:]), reads=[impf.b], writes=[m8.b])
            k.op("dve", lambda e: e.match_replace(out=wk_[:], in_to_replace=m8[:], in_values=impf[:], imm_value=-3e38), reads=[impf.b, m8.b],
                 writes=[wk_.b])
            k.op("dve", lambda e: e.max(out=m8[:], in_=wk_[:]), reads=[wk_.b], writes=[m8.b])
            k.op("dve", lambda e: e.tensor_scalar(out=selm[:], in0=impf[:], scalar1=m8[:, 7:8], scalar2=None, op0=ALU.is_ge),
                 reads=[impf.b, m8.b], writes=[selm.b])
            k.op("dve", lambda e: e.tensor_scalar(out=selm[:], in0=selm[:], scalar1=-1.0, scalar2=NSA_BIG, op0=ALU.add, op1=ALU.mult),
                 reads=[selm.b], writes=[selm.b])
            owin = oreg[0]
            kts = list(range(max(0, qt - 4), qt + 1))
            jobs = []
            for kt in kts:
                def mmw(e, a, kt=kt):
                    need_mask = (kt == qt) or (kt == qt - 4)
                    ins = e.matmul(a.ap(), lhsT=kwT[:, kt * 128:(kt + 1) * 128], rhs=qT[:], start=True, stop=not need_mask)
                    if kt == qt:
                        ins = e.matmul(a.ap(), lhsT=identb[:], rhs=cneg[:], start=False, stop=True)
                    elif kt == qt - 4:
                        ins = e.matmul(a.ap(), lhsT=identb[:], rhs=wneg[:], start=False, stop=True)
                    return ins
                jobs.append(dict(mm=mmw, reads=[kwT.b, qT.b, identb.b, cneg.b, wneg.b], vt=vw_e, vidx=kt))
            run_pairs(jobs, owin)
            combine(owin, 2, False, gsb)
            r2 = next_regT()
            k.op("pe", lambda e: e.transpose(r2.ap(), selm[:], c["ident"][:]), reads=[selm.b, c["ident"].b], writes=[r2.b])
            k.op("act", lambda e: e.copy(out=nmT[:].rearrange("p (h j) -> p h j", h=4),
                                         in_=r2.ap().rearrange("p (o j) -> p o j", o=1).to_broadcast([128, 4, 128])), reads=[r2.b],
                 writes=[nmT.b])
            if debug and qt == (NT - 1):
                dump("imp", imp, imp[:], 128)
                dump("impf", impf, impf[:], 128)
                dump("selm", selm, selm[:], 128)
            if qt + 1 < NT:
                if (qt + 1) % (TW // 128) == 0 and (qt + 1) // (TW // 128) + 1 < NST:
                    load_xT((qt + 1) // (TW // 128) + 1)
                q_prep_a(qt + 1)
            osel = oreg[1]
            jobs = []
            for kt in range(qt + 1):
                def mms(e, a, kt=kt):
                    e.matmul(a.ap(), lhsT=ksT[:, kt * 128:(kt + 1) * 128], rhs=qT[:], start=True, stop=False)
                    ins = e.matmul(a.ap(), lhsT=Esel[:, kt * 128:(kt + 1) * 128], rhs=nmT[:], start=False, stop=(kt != qt))
                    if kt == qt:
                        ins = e.matmul(a.ap(), lhsT=identb[:], rhs=cneg[:], start=False, stop=True)
                    return ins
                jobs.append(dict(mm=mms, reads=[ksT.b, qT.b, Esel.b, nmT.b, identb.b, cneg.b], vt=vs_e, vidx=kt))
            run_pairs(jobs, osel)
            combine(osel, 1, False, gsb)
            if qt + 1 < NT:
                q_prep_b(qt + 1)
            for h in range(4):
                r2 = next_regT()
                k.op("pe", lambda e: e.transpose(r2.ap(), oacc[:, h, :], c["ident"][:]), reads=[oacc.b, c["ident"].b], writes=[r2.b])
                k.op("act", lambda e: e.copy(out=ystage[:, h, tsl], in_=r2.ap()), reads=[r2.b], writes=[ystage.b])
        tk = k.op("sp", lambda e: e.dma_start(out=yT.rearrange("c p t -> p c t")[:, :, t0s:t0s + TW], in_=ystage[:]), reads=[ystage.b],
                  dsem=ds_y)
        out_toks.append(tk)
    k.finish(out_toks)
    return nc


def nsa_core_inputs(I, layer, g, hT_b):
    j = layer // 2
    W = I["c_w_in"][j]
    o_q, o_kc, o_vc, o_ks, o_vs, o_kw, o_vw, o_gp = 0, 2048, 2560, 3072, 3584, 4096, 4608, 5120
    sl = lambda o: W[:, o + g * 128:o + (g + 1) * 128]
    win = np.concatenate([sl(o_kc), sl(o_vc), sl(o_ks), sl(o_vs), sl(o_kw), sl(o_vw), W[:, g * 512:(g + 1) * 512],
                          W[:, o_gp + 12 * g:o_gp + 12 * (g + 1)]], axis=1)
    posT = np.ascontiguousarray(I["c_cmp_pos"][j].transpose(2, 0, 1))
    b1 = np.ascontiguousarray(I["c_cmp_b1"][j].reshape(2, 2, 128).transpose(2, 0, 1))
    return {
        "hT": hT_b, "nw": pvec(I["mix_norm"][layer]), "win": np.ascontiguousarray(win),
        "qnw": rep(np.tile(I["c_q_norm"][j], 4)), "knw": rep(I["c_k_norm"][j]), "posT": posT,
        "w1": np.ascontiguousarray(I["c_cmp_w1"][j]), "b1": b1, "w2": np.ascontiguousarray(I["c_cmp_w2"][j]),
        "gbias": rep(I["c_gate_bias"][j][12 * g:12 * (g + 1)]),
    }


_PROGS = {}


def _prog(name, fn):
    if name not in _PROGS:
        _PROGS[name] = fn()
    return _PROGS[name]


def _launch(nc, maps):
    res = run_bass_kernel_spmd(nc, maps, core_ids=list(range(8)))
    return res.results


def kernel(**I):
    I = {k_: np.ascontiguousarray(np.asarray(v)) for k_, v in I.items()}
    x = I["x"]
    B, S, D = x.shape
    NTC = S // 4
    hT = [np.ascontiguousarray(x[b].T.reshape(16, 128, S)) for b in range(B)]

    def tok_shard(arrs, c):
        b, q = divmod(c, 4)
        return np.ascontiguousarray(arrs[b][:, :, q * NTC:(q + 1) * NTC])

    def ffn_launch(hT, pre, layer, yT=None, wo=None):
        nc = _prog("ffn_pre" if yT is not None else "ffn", lambda: build_ffn(NT=NTC, preproj=yT is not None))
        maps = []
        for c in range(8):
            m = {"hT": tok_shard(hT, c), "nw": pvec(I[pre + "_norm"][layer]), "wg": I[pre + "_w_gate"][layer],
                 "wu": I[pre + "_w_up"][layer], "wd": I[pre + "_w_down"][layer]}
            if yT is not None:
                m["yT"] = tok_shard(yT, c)
                m["wo"] = wo
            maps.append(m)
        res = _launch(nc, maps)
        out = [np.empty((16, 128, S), np.float32) for _ in range(B)]
        for c in range(8):
            b, q = divmod(c, 4)
            out[b][:, :, q * NTC:(q + 1) * NTC] = res[c]["hT_out"]
        return out

    for layer in range(4):
        j = layer // 2
        hT = ffn_launch(hT, "ffn1", layer)
        yT = [np.empty((16, 128, S), ml_dtypes.bfloat16) for _ in range(B)]
        if layer % 2 == 0:
            nc = _prog(f"ab{j}", lambda: build_ab(T=S, layer_j=j))
            maps = [ab_core_inputs(I, layer, c % 4, hT[c // 4]) for c in range(8)]
            res = _launch(nc, maps)
            for c in range(8):
                b, g = divmod(c, 4)
                y = res[c]["yT"]
                yT[b][2 * g:2 * g + 2] = y[0:2]
                yT[b][8 + 2 * g:8 + 2 * g + 2] = y[2:4]
            wo = I["ab_w_out"][j]
        else:
            nc = _prog("nsa", lambda: build_nsa(T=S))
            maps = [nsa_core_inputs(I, layer, c % 4, hT[c // 4]) for c in range(8)]
            res = _launch(nc, maps)
            for c in range(8):
                b, g = divmod(c, 4)
                yT[b][4 * g:4 * g + 4] = res[c]["yT"]
            wo = I["c_w_out"][j]
        hT = ffn_launch(hT, "ffn2", layer, yT=yT, wo=wo)
    out = np.stack([hT[b].reshape(D, S).T for b in range(B)], axis=0)
    return np.ascontiguousarray(out.astype(np.float32))
```

```python
import numpy as np
import ml_dtypes
import concourse.bass as bass
import concourse.mybir as mybir
from concourse.bass_utils import run_bass_kernel_spmd

F32 = mybir.dt.float32
BF16 = mybir.dt.bfloat16
I32 = mybir.dt.int32
AF = mybir.ActivationFunctionType
ALU = mybir.AluOpType
AX = mybir.AxisListType

D_MODEL = 2048
D_FF = 5504
EPS = 1e-6
SAME_SYNC = True


class Tok:
    __slots__ = ("sem", "val", "key", "eng")

    def __init__(self, sem, val, key, eng):
        self.sem, self.val, self.key, self.eng = sem, val, key, eng


class Buf:
    __slots__ = ("name", "w", "r", "bank")

    def __init__(self, name):
        self.name, self.w, self.r, self.bank = name, None, {}, None


class DSem:
    __slots__ = ("h", "val", "key")

    def __init__(self, h, key):
        self.h, self.val, self.key = h, 0, key


class KB:
    def __init__(self, nc):
        self.nc = nc
        self.engs = dict(pe=nc.tensor, act=nc.scalar, dve=nc.vector, pool=nc.gpsimd, sp=nc.sync)
        self.sem = {e: nc.alloc_semaphore("sem_" + e) for e in self.engs}
        self.cnt = {e: 0 for e in self.engs}
        self.waited = {e: {} for e in self.engs}
        self.nds = 0
        self.out_toks = []
        self.after_op = None

    def buf(self, name=""):
        return Buf(name)

    def bufs(self, n, name=""):
        return [Buf(f"{name}{i}") for i in range(n)]

    def dsem(self, name=None):
        self.nds += 1
        key = f"D{self.nds}"
        return DSem(self.nc.alloc_semaphore(name or key), key)

    def _wait(self, e, tok, raw=False):
        if tok is None:
            return
        if tok.eng == e and not (raw and SAME_SYNC and e != "pe"):
            return
        w = self.waited[e]
        if w.get(tok.key, 0) >= tok.val:
            return
        self.engs[e].wait_ge(tok.sem, tok.val)
        w[tok.key] = tok.val

    def op(self, e, fn, reads=(), writes=(), dsem=None):
        for b in reads:
            self._wait(e, b.w, raw=True)
        for b in writes:
            self._wait(e, b.w)
            for t in b.r.values():
                self._wait(e, t)
        banks = {}
        for b in list(reads) + list(writes):
            if b.bank is not None:
                banks[id(b.bank)] = b.bank
        for bk in banks.values():
            self._wait(e, bk.w)
        ins = fn(self.engs[e])
        if dsem is None:
            self.cnt[e] += 1
            ins.then_inc(self.sem[e], 1)
            tok = Tok(self.sem[e], self.cnt[e], "E" + e, e)
        else:
            dsem.val += 16
            ins.then_inc(dsem.h, 16)
            tok = Tok(dsem.h, dsem.val, dsem.key, None)
        for b in reads:
            b.r[tok.key] = tok
        for b in writes:
            b.w = tok
            b.r = {}
        for bk in banks.values():
            bk.w = tok
        if self.after_op is not None:
            self.after_op()
        return tok

    def finish(self, toks):
        for t in toks:
            self._wait("sp", t)


import threading


def interleave(k, fns):
    n = len(fns)
    cv = threading.Condition()
    state = {"turn": 0, "alive": [True] * n, "err": None}
    tls = threading.local()

    def advance(i):
        for d in range(1, n + 1):
            j = (i + d) % n
            if state["alive"][j]:
                state["turn"] = j
                return
        state["turn"] = -1

    def yield_(i):
        with cv:
            advance(i)
            cv.notify_all()
            while state["turn"] != i:
                cv.wait()

    def runner(i):
        tls.idx = i
        with cv:
            while state["turn"] != i:
                cv.wait()
        try:
            fns[i]()
        except BaseException as e:
            state["err"] = e
        with cv:
            state["alive"][i] = False
            advance(i)
            cv.notify_all()

    old_hook = k.after_op
    k.after_op = lambda: yield_(tls.idx) if getattr(tls, "idx", None) is not None else None
    ths = [threading.Thread(target=runner, args=(i,)) for i in range(n)]
    for t in ths:
        t.start()
    for t in ths:
        t.join()
    k.after_op = old_hook
    if state["err"] is not None:
        raise state["err"]


class Ring:
    def __init__(self, k, slots, name="ws"):
        self.k = k
        self.slots = slots
        self.ns = len(slots)
        self.b = k.bufs(self.ns, name)
        self.ds = [k.dsem() for _ in range(self.ns)]
        self.loads = []
        self.issued = 0
        self.consumed = 0

    def plan(self, dst_fn, src):
        self.loads.append((dst_fn, src))

    def _issue(self):
        if self.issued >= len(self.loads):
            return
        i = self.issued
        s = i % self.ns
        dst_fn, src = self.loads[i]
        slot = self.slots[s]
        self.k.op("pool", lambda e: e.dma_start(out=dst_fn(slot), in_=src), reads=(), writes=[self.b[s]],
                  dsem=self.ds[s])
        self.issued += 1

    def start(self):
        while self.issued < min(self.ns, len(self.loads)):
            self._issue()

    def get(self, off=0):
        i = self.consumed + off
        assert i < self.issued, "ring underflow"
        s = i % self.ns
        return self.slots[s], self.b[s]

    def done(self):
        self.consumed += 1
        self._issue()


def build_ffn(NT=2048, F=D_FF, preproj=False, TP=1024, NS=4):
    D = D_MODEL
    DC = D // 128
    FCn = F // 128
    assert F % 128 == 0 and NT % TP == 0 and TP % 512 == 0
    NTT = TP // 512
    nc = bass.Bass("TRN2", target_bir_lowering=False)
    k = KB(nc)
    hT_in = nc.dram_tensor("hT", [DC, 128, NT], F32, kind="ExternalInput").ap()
    nw = nc.dram_tensor("nw", [128, DC], F32, kind="ExternalInput").ap()
    wg = nc.dram_tensor("wg", [D, F], F32, kind="ExternalInput").ap()
    wu = nc.dram_tensor("wu", [D, F], F32, kind="ExternalInput").ap()
    wd = nc.dram_tensor("wd", [F, D], F32, kind="ExternalInput").ap()
    hT_out = nc.dram_tensor("hT_out", [DC, 128, NT], F32, kind="ExternalOutput").ap()
    if preproj:
        yT = nc.dram_tensor("yT", [DC, 128, NT], BF16, kind="ExternalInput").ap()
        wo = nc.dram_tensor("wo", [D, D], F32, kind="ExternalInput").ap()
        h2T = nc.dram_tensor("h2T", [DC, 128, NT], F32).ap()
        h_src = h2T
    else:
        h_src = hT_in
    wg_v = wg.rearrange("(dc p) f -> p dc f", p=128)
    wu_v = wu.rearrange("(dc p) f -> p dc f", p=128)
    wd_v = wd.rearrange("(fc p) m -> p fc m", p=128)

    actT = nc.alloc_sbuf_tensor("actT", [128, FCn, TP], BF16)
    xT = nc.alloc_sbuf_tensor("xT", [128, DC, TP], BF16)
    slots = [nc.alloc_sbuf_tensor(f"ws{i}", [128, 16, 256], BF16) for i in range(NS)]
    NSCR = 6
    scr = [nc.alloc_sbuf_tensor(f"scr{i}", [128, TP], F32) for i in range(NSCR)]
    rstd = nc.alloc_sbuf_tensor("rstd", [128, TP], F32)
    ones = nc.alloc_sbuf_tensor("ones", [128, 128], F32)
    nw_sb = nc.alloc_sbuf_tensor("nw_sb", [128, DC], F32)
    epst = nc.alloc_sbuf_tensor("epst", [128, 1], F32)
    ps = [nc.alloc_psum_tensor(f"ps{i}", [128, 512], F32) for i in range(8)]

    actT_b = k.bufs(FCn, "actT")
    xT_b = k.bufs(DC, "xT")
    scr_b = k.bufs(NSCR, "scr")
    scr_ds = [k.dsem() for _ in range(NSCR)]
    rstd_b = k.buf("rstd")
    ones_b = k.buf("ones")
    eps_b = k.buf("eps")
    nw_b = k.buf("nw")
    nw_ds = k.dsem()
    ps_b = k.bufs(8, "ps")
    ring = Ring(k, slots)
    npass = NT // TP
    h2_b = [[k.buf(f"h2_{p}_{dc}") for dc in range(DC)] for p in range(npass)]
    yT_ds = k.dsem()

    fblocks = []
    f0 = 0
    while f0 < F:
        fw = min(256, F - f0)
        fblocks.append((f0, fw))
        f0 += fw
    dsegs = []
    c0 = 0
    while c0 < FCn:
        n = min(16, FCn - c0)
        dsegs.append((c0, n))
        c0 += n
    NG = D // 256

    for p in range(npass):
        if preproj:
            for dc in range(DC):
                ring.plan(lambda s: s[:, :, 0:128], wo.rearrange("(yc p) m -> p yc m", p=128)[:, :, dc * 128:(dc + 1) * 128])
        for (f0, fw) in fblocks:
            ring.plan(lambda s, fw=fw: s[:, :, 0:fw], wg_v[:, :, f0:f0 + fw])
            ring.plan(lambda s, fw=fw: s[:, :, 0:fw], wu_v[:, :, f0:f0 + fw])
        for gi in range(NG):
            for (c0, n) in dsegs:
                ring.plan(lambda s, n=n: s[:, 0:n, :], wd_v[:, c0:c0 + n, gi * 256:(gi + 1) * 256])

    k.op("dve", lambda e: e.memset(ones[:], 1.0), writes=[ones_b])
    k.op("dve", lambda e: e.memset(epst[:], EPS), writes=[eps_b])
    k.op("sp", lambda e: e.dma_start(out=nw_sb[:], in_=nw), writes=[nw_b], dsem=nw_ds)
    ring.start()
    out_toks = []

    for p in range(npass):
        t0 = p * TP
        tsl = slice(t0, t0 + TP)
        if preproj:
            k.op("sp", lambda e: e.dma_start(out=actT[:, 0:DC, :], in_=yT.rearrange("yc p t -> p yc t")[:, :, tsl]),
                 writes=actT_b[0:DC], dsem=yT_ds)
        for dc in range(DC):
            hi = dc % 2
            ht = scr[hi]
            k.op("sp", lambda e: e.dma_start(out=ht[:], in_=hT_in[dc, :, tsl]), writes=[scr_b[hi]], dsem=scr_ds[hi])
            if preproj:
                slot, sb = ring.get()
                pb = 2 + 2 * (dc % 2)

                def mm(e):
                    for yc in range(DC):
                        for tt in range(NTT):
                            ins = e.matmul(ps[pb + tt][:], lhsT=slot[:, yc, 0:128],
                                           rhs=actT[:, yc, tt * 512:(tt + 1) * 512], start=(yc == 0), stop=(yc == DC - 1))
                    return ins
                k.op("pe", mm, reads=[sb] + actT_b[0:DC], writes=[ps_b[pb + tt] for tt in range(NTT)])
                ring.done()
                for tt in range(NTT):
                    k.op("dve", lambda e: e.tensor_tensor(out=ht[:, tt * 512:(tt + 1) * 512], in0=ps[pb + tt][:],
                                                          in1=ht[:, tt * 512:(tt + 1) * 512], op=ALU.add),
                         reads=[ps_b[pb + tt]], writes=[scr_b[hi]])
                k.op("sp", lambda e: e.dma_start(out=h2T[dc, :, tsl], in_=ht[:]), reads=[scr_b[hi]],
                     writes=[h2_b[p][dc]], dsem=scr_ds[hi])
            si = 2 + dc % 2
            sq = scr[si]
            k.op("act", lambda e: e.activation(out=sq[:], in_=ht[:], func=AF.Square), reads=[scr_b[hi]], writes=[scr_b[si]])

            def mm(e):
                for tt in range(NTT):
                    ins = e.matmul(ps[tt][:], lhsT=ones[:], rhs=sq[:, tt * 512:(tt + 1) * 512], start=(dc == 0),
                                   stop=(dc == DC - 1))
                return ins
            k.op("pe", mm, reads=[scr_b[si], ones_b], writes=[ps_b[tt] for tt in range(NTT)])
        for tt in range(NTT):
            k.op("act", lambda e: e.activation(out=rstd[:, tt * 512:(tt + 1) * 512], in_=ps[tt][:], func=AF.Sqrt,
                                               scale=1.0 / D, bias=epst[:, 0:1]),
                 reads=[ps_b[tt], eps_b], writes=[rstd_b])
        k.op("dve", lambda e: e.reciprocal(out=rstd[:], in_=rstd[:]), reads=[rstd_b], writes=[rstd_b])
        for dc in range(DC):
            hi = dc % 2
            ht = scr[hi]
            rd = [h2_b[p][dc]] if preproj else []
            k.op("sp", lambda e: e.dma_start(out=ht[:], in_=h_src[dc, :, tsl]), reads=rd, writes=[scr_b[hi]],
                 dsem=scr_ds[hi])
            k.op("dve", lambda e: e.scalar_tensor_tensor(out=xT[:, dc, :], in0=ht[:], scalar=nw_sb[:, dc:dc + 1],
                                                         in1=rstd[:], op0=ALU.mult, op1=ALU.mult),
                 reads=[scr_b[hi], rstd_b, nw_b], writes=[xT_b[dc]])
        ci = 0
        for (f0, fw) in fblocks:
            sg_, sgb = ring.get(0)
            su_, sub = ring.get(1)
            for j in range(fw // 128):
                fi = f0 // 128 + j
                par = ci % 2
                ci += 1
                gb = 4 * par
                ub = 4 * par + 2

                def mm(e, w_=None, b0=0):
                    for dc in range(DC):
                        for tt in range(NTT):
                            ins = e.matmul(ps[b0 + tt][:], lhsT=w_[:, dc, j * 128:(j + 1) * 128],
                                           rhs=xT[:, dc, tt * 512:(tt + 1) * 512], start=(dc == 0), stop=(dc == DC - 1))
                    return ins
                k.op("pe", lambda e: mm(e, sg_, gb), reads=[sgb] + xT_b, writes=[ps_b[gb + tt] for tt in range(NTT)])
                k.op("pe", lambda e: mm(e, su_, ub), reads=[sub] + xT_b, writes=[ps_b[ub + tt] for tt in range(NTT)])
                sgi = 4 + par
                sgt = scr[sgi]
                for tt in range(NTT):
                    k.op("act", lambda e: e.activation(out=sgt[:, tt * 512:(tt + 1) * 512], in_=ps[gb + tt][:], func=AF.Silu),
                         reads=[ps_b[gb + tt]], writes=[scr_b[sgi]])
                    k.op("dve", lambda e: e.tensor_tensor(out=actT[:, fi, tt * 512:(tt + 1) * 512],
                                                          in0=sgt[:, tt * 512:(tt + 1) * 512], in1=ps[ub + tt][:], op=ALU.mult),
                         reads=[scr_b[sgi], ps_b[ub + tt]], writes=[actT_b[fi]])
            ring.done()
            ring.done()
        for gi in range(NG):
            base = 4 * (gi % 2)
            for dmi in range(2):
                dmc = gi * 2 + dmi
                hi = dmi
                rd = [h2_b[p][dmc]] if preproj else []
                k.op("sp", lambda e: e.dma_start(out=scr[hi][:], in_=h_src[dmc, :, tsl]), reads=rd, writes=[scr_b[hi]],
                     dsem=scr_ds[hi])
            for (c0, n) in dsegs:
                slot, sb = ring.get()

                def mm(e):
                    for j in range(n):
                        fc = c0 + j
                        for dmi in range(2):
                            for tt in range(NTT):
                                ins = e.matmul(ps[base + 2 * dmi + tt][:], lhsT=slot[:, j, dmi * 128:(dmi + 1) * 128],
                                               rhs=actT[:, fc, tt * 512:(tt + 1) * 512], start=(fc == 0), stop=(fc == FCn - 1))
                    return ins
                k.op("pe", mm, reads=[sb] + actT_b[c0:c0 + n], writes=[ps_b[base + i] for i in range(4)])
                ring.done()
            for dmi in range(2):
                dmc = gi * 2 + dmi
                hi = dmi
                oi = 2 + dmi
                ot = scr[oi]
                for tt in range(NTT):
                    bk = base + 2 * dmi + tt
                    k.op("dve", lambda e: e.scalar_tensor_tensor(out=ot[:, tt * 512:(tt + 1) * 512], in0=ps[bk][:], scalar=0.5,
                                                                 in1=scr[hi][:, tt * 512:(tt + 1) * 512], op0=ALU.mult, op1=ALU.add),
                         reads=[ps_b[bk], scr_b[hi]], writes=[scr_b[oi]])
                tk = k.op("sp", lambda e: e.dma_start(out=hT_out[dmc, :, tsl], in_=ot[:]), reads=[scr_b[oi]], dsem=scr_ds[oi])
                out_toks.append(tk)
    k.finish(out_toks)
    return nc


class TT:
    def __init__(self, k, t, name):
        self.t = t
        self.b = k.buf(name)

    def __getitem__(self, idx):
        return self.t[idx]


def sb(k, name, shape, dt):
    return TT(k, k.nc.alloc_sbuf_tensor(name, list(shape), dt), name)


class PReg:
    _bankbufs = {}

    def __init__(self, k, bank, c0, c1, name):
        self.bank, self.c0, self.c1 = bank, c0, c1
        self.b = k.buf(name)
        key = (id(k), bank.name if hasattr(bank, "name") else id(bank))
        if key not in PReg._bankbufs:
            PReg._bankbufs[key] = k.buf("bank")
        self.b.bank = PReg._bankbufs[key]

    def ap(self, rows=slice(None), a=None, b=None):
        a = self.c0 if a is None else self.c0 + a
        b = self.c1 if b is None else self.c0 + b
        return self.bank[rows, a:b]


def make_consts(k, need_ident=True):
    nc = k.nc
    c = {}
    c["ones"] = sb(k, "c_ones", [128, 128], F32)
    k.op("dve", lambda e: e.memset(c["ones"][:], 1.0), writes=[c["ones"].b])
    c["ident"] = sb(k, "c_ident", [128, 128], F32)
    k.op("pool", lambda e: e.affine_select(out=c["ident"][:], in_=c["ones"][:], pattern=[[-1, 128]],
                                           compare_op=ALU.is_equal, fill=0.0, base=0, channel_multiplier=1),
         reads=[c["ones"].b], writes=[c["ident"].b])
    c["causal"] = sb(k, "c_causal", [128, 128], F32)
    k.op("pool", lambda e: e.affine_select(out=c["causal"][:], in_=c["ones"][:], pattern=[[1, 128]],
                                           compare_op=ALU.is_ge, fill=0.0, base=0, channel_multiplier=-1),
         reads=[c["ones"].b], writes=[c["causal"].b])
    c["eps"] = sb(k, "c_eps", [128, 1], F32)
    k.op("dve", lambda e: e.memset(c["eps"][:], EPS), writes=[c["eps"].b])
    c["one1"] = sb(k, "c_one1", [128, 1], F32)
    k.op("dve", lambda e: e.memset(c["one1"][:], 1.0), writes=[c["one1"].b])
    return c


def load_h_tile(k, hT_src, hTt, t0, TW, ds_h, h_reads=()):
    k.op("sp", lambda e: e.dma_start(out=hTt[:, :, 0:TW], in_=hT_src.rearrange("dc p t -> p dc t")[:, :, t0:t0 + TW]),
         reads=list(h_reads), writes=[hTt.b], dsem=ds_h)


def norm_supertile(k, c, hT_src, nw_sb, hTt, xT, ps_ss, rstd, scrsq, t0, TW, ds_h, h_reads=(), preloaded=False):
    DC = 16
    if not preloaded:
        load_h_tile(k, hT_src, hTt, t0, TW, ds_h, h_reads)
    for dc in range(DC):
        sq = scrsq[dc % 2]
        k.op("act", lambda e: e.activation(out=sq[:, 0:TW], in_=hTt[:, dc, 0:TW], func=AF.Square), reads=[hTt.b],
             writes=[sq.b])
        k.op("pe", lambda e: e.matmul(ps_ss.ap(b=TW), lhsT=c["ones"][:], rhs=sq[:, 0:TW], start=(dc == 0), stop=(dc == DC - 1)),
             reads=[sq.b, c["ones"].b], writes=[ps_ss.b])
    k.op("act", lambda e: e.activation(out=rstd[:, 0:TW], in_=ps_ss.ap(b=TW), func=AF.Sqrt, scale=1.0 / D_MODEL,
                                       bias=c["eps"][:, 0:1]), reads=[ps_ss.b, c["eps"].b], writes=[rstd.b])
    k.op("dve", lambda e: e.reciprocal(out=rstd[:, 0:TW], in_=rstd[:, 0:TW]), reads=[rstd.b], writes=[rstd.b])
    for dc in range(DC):
        k.op("dve", lambda e: e.scalar_tensor_tensor(out=xT[:, dc, 0:TW], in0=hTt[:, dc, 0:TW], scalar=nw_sb[:, dc:dc + 1],
                                                     in1=rstd[:, 0:TW], op0=ALU.mult, op1=ALU.mult),
             reads=[hTt.b, rstd.b, nw_sb.b], writes=[xT.b])


HG_MAX_K = 0.999999


class _Stop(Exception):
    pass


def build_ab(T=8192, layer_j=0, do_ml=2, do_hg=2, stop=99, debug=False):
    TW = 512
    NST = T // TW
    NCOL = 1794
    nc = bass.Bass("TRN2", target_bir_lowering=False)
    k = KB(nc)

    def dram(name, shape, dt=F32, kind="ExternalInput"):
        return nc.dram_tensor(name, list(shape), dt, kind=kind).ap()
    hT = dram("hT", [16, 128, T])
    nw = dram("nw", [128, 16])
    win = dram("win", [2048, NCOL])
    cw = dram("cw", [128, 2, 4])
    cb = dram("cb", [128, 2])
    wq = dram("wq", [256, 256])
    wk = dram("wk", [256, 256])
    gbias = dram("gb", [128, 2])
    mln = dram("mln", [128, 256])
    skp = dram("skp", [128, 256])
    lbl = dram("lbl", [128, 2, 2])
    hgn = dram("hgn", [128, 2, 128])
    yT = dram("yT", [4, 128, T], BF16, kind="ExternalOutput")
    dbg = dram("dbg", [128, 8192], F32, kind="ExternalOutput") if debug else None
    dbg_pos = [0]
    dbg_map = {}
    dbg_ds = k.dsem() if debug else None

    def dump(name, tt, ap, n):
        if not debug or name in dbg_map:
            return
        c0 = dbg_pos[0]
        dbg_pos[0] += n
        dbg_map[name] = (c0, n)
        k.op("sp", lambda e: e.dma_start(out=dbg[:, c0:c0 + n], in_=ap), reads=[tt.b], dsem=dbg_ds)
    nc._dbg_map = dbg_map

    c = make_consts(k)
    ps = [nc.alloc_psum_tensor(f"ps{i}", [128, 512], F32) for i in range(8)]
    acc = [PReg(k, ps[i], 0, 512, f"acc{i}") for i in range(2)]
    ps_ss = PReg(k, ps[2], 0, 512, "ss")
    regT = [PReg(k, ps[2], i * 128, (i + 1) * 128, f"regT{i}") for i in range(4)]
    regA = PReg(k, ps[3], 0, 8, "regA")
    regB = PReg(k, ps[3], 128, 257, "regB")
    regC = PReg(k, ps[3], 384, 512, "regC")
    regND = PReg(k, ps[4], 0, 264, "regND")
    regU = [PReg(k, ps[5 + i], 0, 264, f"regU{i}") for i in range(2)]
    r7all = PReg(k, ps[7], 0, 512, "r7all")
    r7A = PReg(k, ps[7], 0, 128, "r7A")
    r7B = PReg(k, ps[7], 128, 256, "r7B")
    r7C = PReg(k, ps[7], 256, 384, "r7C")
    r7D = PReg(k, ps[7], 384, 512, "r7D")

    win_sb = sb(k, "win_sb", [128, 16, NCOL], BF16)
    wds = [k.dsem() for _ in range(4)]
    cuts = [0, 512, 1024, 1536, NCOL]
    win_v = win.rearrange("(dc p) f -> p dc f", p=128)
    wtoks = []
    for i in range(4):
        a, b_ = cuts[i], cuts[i + 1]
        wtoks.append(k.op("pool", lambda e: e.dma_start(out=win_sb[:, :, a:b_], in_=win_v[:, :, a:b_]), writes=[],
                          dsem=wds[i]))
    wq_sb = sb(k, "wq_sb", [128, 2, 256], BF16)
    wk_sb = sb(k, "wk_sb", [128, 2, 256], BF16)
    pds = k.dsem()
    k.op("pool", lambda e: e.dma_start(out=wq_sb[:], in_=wq.rearrange("(d p) e -> p d e", p=128)), writes=[wq_sb.b], dsem=pds)
    k.op("pool", lambda e: e.dma_start(out=wk_sb[:], in_=wk.rearrange("(d p) e -> p d e", p=128)), writes=[wk_sb.b], dsem=k.dsem())

    def ld(name, shape, src):
        t = sb(k, name, shape, F32)
        k.op("sp", lambda e: e.dma_start(out=t[:], in_=src), writes=[t.b], dsem=k.dsem())
        return t
    nw_sb = ld("nw_sb", [128, 16], nw)
    cw_sb = ld("cw_sb", [128, 2, 4], cw)
    cb_sb = ld("cb_sb", [128, 2], cb)
    gb_sb = ld("gb_sb", [128, 2], gbias)
    mln_sb = ld("mln_sb", [128, 256], mln)
    skp_sb = ld("skp_sb", [128, 256], skp)
    lbl_sb = ld("lbl_sb", [128, 2, 2], lbl)
    hgn_sb = ld("hgn_sb", [128, 2, 128], hgn)
    for tkn in wtoks:
        k._wait("pe", tkn)

    oml = sb(k, "oml", [128, 2], F32)
    lbe = sb(k, "lbe", [128, 2, 2], F32)
    lbs = sb(k, "lbs", [128, 2], F32)
    k.op("act", lambda e: e.activation(out=lbe[:], in_=lbl_sb[:], func=AF.Exp), reads=[lbl_sb.b], writes=[lbe.b])
    k.op("dve", lambda e: e.tensor_tensor(out=lbs[:], in0=lbe[:, :, 0], in1=lbe[:, :, 1], op=ALU.add), reads=[lbe.b], writes=[lbs.b])
    k.op("dve", lambda e: e.reciprocal(out=lbs[:], in_=lbs[:]), reads=[lbs.b], writes=[lbs.b])
    for l in range(2):
        k.op("dve", lambda e: e.tensor_tensor(out=lbe[:, :, l], in0=lbe[:, :, l], in1=lbs[:], op=ALU.mult), reads=[lbe.b, lbs.b],
             writes=[lbe.b])
    k.op("dve", lambda e: e.tensor_copy(out=oml[:], in_=lbe[:, :, 0]), reads=[lbe.b], writes=[oml.b])
    for l in range(1, layer_j + 1):
        k.op("dve", lambda e: e.tensor_tensor(out=oml[:], in0=oml[:], in1=lbe[:, :, l], op=ALU.add), reads=[lbe.b, oml.b], writes=[oml.b])
    k.op("dve", lambda e: e.tensor_tensor(out=oml[:], in0=oml[:], in1=lbe[:, :, 0], op=ALU.subtract), reads=[lbe.b, oml.b], writes=[oml.b])
    k.op("dve", lambda e: e.tensor_scalar(out=oml[:], in0=oml[:], scalar1=-1.0, scalar2=1.0, op0=ALU.mult, op1=ALU.add),
         reads=[oml.b], writes=[oml.b])
    nfb = sb(k, "nfb", [128, 1], F32)
    k.op("dve", lambda e: e.tensor_scalar(out=nfb[:], in0=gb_sb[:, 1:2], scalar1=-1.0, scalar2=None, op0=ALU.mult),
         reads=[gb_sb.b], writes=[nfb.b])

    hTt = sb(k, "hTt", [128, 16, TW], F32)
    ds_h = k.dsem()
    xT = sb(k, "xT", [128, 16, TW], BF16)
    rstd = sb(k, "rstd", [128, TW], F32)
    scrsq = [sb(k, f"scrsq{i}", [128, TW], F32) for i in range(2)]
    ubuf = sb(k, "ubuf", [128, 2, TW + 3], F32)
    cacc = sb(k, "cacc", [128, TW], F32)
    cT = sb(k, "cT", [128, 2, TW], F32)
    cTb = sb(k, "cTb", [128, 2, TW], BF16)
    qT = sb(k, "qT", [128, 2, TW], F32)
    qTb = sb(k, "qTb", [128, 2, TW], BF16)
    kTb = sb(k, "kTb", [128, 2, TW], BF16)
    ktok = sb(k, "ktok", [128, 4, 256], F32)
    ctok = sb(k, "ctok", [128, 4, 256], F32)
    vext = sb(k, "vext", [128, 4, 264], BF16)
    osig = sb(k, "osig", [128, 4, 256], F32)
    hv = sb(k, "hv", [128, 4, 2, 128], BF16)
    hgs = sb(k, "hgs", [128, 4, 256], F32)
    ge1 = sb(k, "ge1", [128, 4], F32)
    logf = sb(k, "logf", [128, 4], F32)
    ig = sb(k, "ig", [128, 4], F32)
    lfb = sb(k, "lfb", [128, 128], F32)
    bias_s = sb(k, "bias_s", [128, 1], F32)
    DT = sb(k, "DT", [128, 128], F32)
    Eb = sb(k, "Eb", [128, 128], F32)
    Dm = sb(k, "Dm", [128, 128], F32)
    PT = sb(k, "PT", [128, 128], BF16)
    qs = sb(k, "qs", [128, 2, 128], BF16)
    small = sb(k, "small", [128, 8], F32)
    small2 = sb(k, "small2", [128, 8], F32)
    junk2 = sb(k, "junk2", [128, 128], F32)
    hn = sb(k, "hn", [128, 256], F32)
    junk = sb(k, "junk", [128, 256], F32)
    hm = sb(k, "hm", [128, 256], F32)
    t1 = sb(k, "t1", [128, 256], F32)
    yml = sb(k, "yml", [128, 256], F32)
    ka = sb(k, "ka", [128, 256], BF16)
    Cst = sb(k, "Cst", [128, 2, 264], F32)
    Cb = sb(k, "Cb", [128, 2, 264], BF16)
    ystage = sb(k, "ystage", [128, 4, TW], BF16)
    ds_y = k.dsem()
    resetm = sb(k, "resetm", [128, TW], F32)
    tA = sb(k, "tA", [128, TW], F32)
    tB = sb(k, "tB", [128, TW], F32)
    tC = sb(k, "tC", [128, TW], F32)
    k2 = sb(k, "k2", [128, TW], F32)
    lf1 = sb(k, "lf1", [128, TW], F32)
    lgf = sb(k, "lgf", [128, TW], F32)
    bt = sb(k, "bt", [128, TW], F32)
    brel = sb(k, "brel", [128, TW], F32)
    sqt = sb(k, "sqt", [128, TW], F32)
    kgT = sb(k, "kgT", [128, TW], F32)
    eg8 = sb(k, "eg8", [128, 8], F32)
    qz = [sb(k, f"qz{i}", [128, 4, 128], BF16) for i in range(2)]
    kz = [sb(k, f"kz{i}", [128, 4, 128], BF16) for i in range(2)]
    qbz = [sb(k, f"qbz{i}", [128, 4, 128], BF16) for i in range(2)]
    kgz = [sb(k, f"kgz{i}", [128, 4, 128], BF16) for i in range(2)]
    Am = sb(k, "Am", [128, 128], BF16)
    Sst = [sb(k, f"Sst{i}", [128, 128], F32) for i in range(2)]
    Sb = [[sb(k, f"Sb{i}_{j}", [128, 128], BF16) for j in range(2)] for i in range(2)]
    o_sb = sb(k, "o_sb", [128, 128], F32)
    o2n = sb(k, "o2n", [128, 128], F32)
    yh = sb(k, "yh", [128, 128], F32)

    for t_ in [ubuf, Cst, Cb, Sst[0], Sst[1], Sb[0][0], Sb[0][1], Sb[1][0], Sb[1][1]] + qz + kz + qbz + kgz:
        k.op("dve", lambda e: e.memset(t_[:], 0.0), writes=[t_.b])
    k.op("dve", lambda e: e.memset(vext[:], 1.0), writes=[vext.b])
    k.op("dve", lambda e: e.memset(resetm[:], 1.0), writes=[resetm.b])
    k.op("dve", lambda e: e.memset(resetm[:].rearrange("p (c l) -> p c l", l=64)[:, :, 0:1], 0.0), writes=[resetm.b])

    def v3(t, l=64):
        return t[:].rearrange("p (c l) -> p c l", l=l)

    def v4(t):
        return t[:].rearrange("p (a b l) -> p a b l", b=2, l=64)

    tcnt = [0]

    def transpose_to(dst_ap, dst_b, src_ap, src_b, eng="act"):
        r = regT[tcnt[0] % 4]
        tcnt[0] += 1
        k.op("pe", lambda e: e.transpose(r.ap(), src_ap, c["ident"][:]), reads=[src_b, c["ident"].b], writes=[r.b])
        if eng == "act":
            k.op("act", lambda e: e.copy(out=dst_ap, in_=r.ap()), reads=[r.b], writes=[dst_b])
        else:
            k.op("dve", lambda e: e.tensor_copy(out=dst_ap, in_=r.ap()), reads=[r.b], writes=[dst_b])

    acnt = [0]

    def next_acc():
        a = acc[acnt[0] % 2]
        acnt[0] += 1
        return a

    def inproj_fm(col0, dst=None):
        a = dst if dst is not None else next_acc()

        def mm(e):
            for dc in range(16):
                ins = e.matmul(a.ap(), lhsT=win_sb[:, dc, col0:col0 + 128], rhs=xT[:, dc, :], start=(dc == 0), stop=(dc == 15))
            return ins
        k.op("pe", mm, reads=[xT.b], writes=[a.b])
        return a

    def inproj_tm(tci, col0, ncol, out_reg=None, oc0=0):
        a = out_reg if out_reg is not None else next_acc()

        def mm(e):
            for dc in range(16):
                ins = e.matmul(a.ap(a=oc0, b=oc0 + ncol), lhsT=xT[:, dc, tci * 128:(tci + 1) * 128],
                               rhs=win_sb[:, dc, col0:col0 + ncol], start=(dc == 0), stop=(dc == 15))
            return ins
        k.op("pe", mm, reads=[xT.b], writes=[a.b])
        return a

    out_toks = []
    def body(st, t0):
        if stop <= 1:
            raise _Stop()
        norm_supertile(k, c, hT, nw_sb, hTt, xT, ps_ss, rstd, scrsq, t0, TW, ds_h)
        if stop <= 2:
            raise _Stop()
        body2(st, t0)

    def body2(st, t0):
        for tci in range(4):
            tsl = slice(tci * 128, (tci + 1) * 128)
            a = inproj_tm(tci, 768, 512)
            k.op("dve", lambda e: e.tensor_copy(out=vext[:, tci, 0:256], in_=a.ap(b=256)), reads=[a.b], writes=[vext.b])
            k.op("act", lambda e: e.activation(out=osig[:, tci, :], in_=a.ap(a=256, b=512), func=AF.Sigmoid), reads=[a.b],
                 writes=[osig.b])
            a = inproj_tm(tci, 1280, 512)
            k.op("dve", lambda e: e.tensor_copy(out=hv[:, tci, :, :].rearrange("p a b -> p (a b)"), in_=a.ap(b=256)), reads=[a.b],
                 writes=[hv.b])
            k.op("act", lambda e: e.activation(out=hgs[:, tci, :], in_=a.ap(a=256, b=512), func=AF.Silu), reads=[a.b],
                 writes=[hgs.b])
            inproj_tm(tci, 1792, 2, out_reg=regA, oc0=2 * tci)
        def ml_stream():
            if st > 0:
                k.op("dve", lambda e: e.tensor_copy(out=ubuf[:, :, 0:3], in_=ubuf[:, :, TW:TW + 3]), reads=[ubuf.b], writes=[ubuf.b])
            for ch in range(2):
                a = inproj_fm(ch * 128)
                k.op("act", lambda e: e.copy(out=ubuf[:, ch, 3:3 + TW], in_=a.ap()), reads=[a.b], writes=[ubuf.b])
            if stop <= 2.2:
                raise _Stop()
            for ch in range(2):
                k.op("dve", lambda e: e.tensor_scalar(out=cacc[:], in0=ubuf[:, ch, 0:TW], scalar1=cw_sb[:, ch, 0:1], scalar2=None,
                                                      op0=ALU.mult), reads=[ubuf.b, cw_sb.b], writes=[cacc.b])
                for j in range(1, 4):
                    k.op("dve", lambda e: e.scalar_tensor_tensor(out=cacc[:], in0=ubuf[:, ch, j:j + TW], scalar=cw_sb[:, ch, j:j + 1],
                                                                 in1=cacc[:], op0=ALU.mult, op1=ALU.add),
                         reads=[ubuf.b, cw_sb.b, cacc.b], writes=[cacc.b])
                k.op("act", lambda e: e.activation(out=cT[:, ch, :], in_=cacc[:], func=AF.Silu, bias=cb_sb[:, ch:ch + 1], scale=1.0),
                     reads=[cacc.b, cb_sb.b], writes=[cT.b])
            k.op("dve", lambda e: e.tensor_copy(out=cTb[:], in_=cT[:]), reads=[cT.b], writes=[cTb.b])
            if stop <= 2.5:
                raise _Stop()
            for e_ in range(2):
                a = next_acc()

                def mm(e):
                    for d in range(2):
                        ins = e.matmul(a.ap(), lhsT=wq_sb[:, d, e_ * 128:(e_ + 1) * 128], rhs=cTb[:, d, :], start=(d == 0), stop=(d == 1))
                    return ins
                k.op("pe", mm, reads=[wq_sb.b, cTb.b], writes=[a.b])
                k.op("act", lambda e: e.copy(out=qT[:, e_, :], in_=a.ap()), reads=[a.b], writes=[qT.b])
                k.op("dve", lambda e: e.tensor_copy(out=qTb[:, e_, :], in_=qT[:, e_, :]), reads=[qT.b], writes=[qTb.b])
                a = next_acc()

                def mm2(e):
                    for d in range(2):
                        ins = e.matmul(a.ap(), lhsT=wk_sb[:, d, e_ * 128:(e_ + 1) * 128], rhs=cTb[:, d, :], start=(d == 0), stop=(d == 1))
                    return ins
                k.op("pe", mm2, reads=[wk_sb.b, cTb.b], writes=[a.b])
                k.op("dve", lambda e: e.tensor_scalar(out=kTb[:, e_, :], in0=a.ap(), scalar1=0.0625, scalar2=None, op0=ALU.mult),
                     reads=[a.b], writes=[kTb.b])
            if stop <= 3:
                raise _Stop()
            for tci in range(4):
                tsl = slice(tci * 128, (tci + 1) * 128)
                a = next_acc()

                def mm3(e):
                    for d in range(2):
                        ins = e.matmul(a.ap(b=256), lhsT=cTb[:, d, tsl], rhs=wk_sb[:, d, :], start=(d == 0), stop=(d == 1))
                    return ins
                k.op("pe", mm3, reads=[wk_sb.b, cTb.b], writes=[a.b])
                k.op("act", lambda e: e.mul(out=ktok[:, tci, :], in_=a.ap(b=256), mul=0.0625), reads=[a.b], writes=[ktok.b])
                for d in range(2):
                    transpose_to(ctok[:, tci, d * 128:(d + 1) * 128], ctok.b, cT[:, d, tsl], cT.b, eng="dve")
            if stop <= 4:
                raise _Stop()
            gv = regA.ap().rearrange("p (a b) -> p a b", b=2)
            k.op("act", lambda e: e.activation(out=ge1[:], in_=gv[:, :, 1], func=AF.Exp, scale=-1.0, bias=nfb[:, 0:1]),
                 reads=[regA.b, nfb.b], writes=[ge1.b])
            k.op("act", lambda e: e.activation(out=ge1[:], in_=ge1[:], func=AF.Ln, scale=1.0, bias=c["one1"][:, 0:1]),
                 reads=[ge1.b, c["one1"].b], writes=[ge1.b])
            k.op("dve", lambda e: e.tensor_scalar(out=logf[:], in0=ge1[:], scalar1=-1.0, scalar2=None, op0=ALU.mult), reads=[ge1.b],
                 writes=[logf.b])
            k.op("dve", lambda e: e.tensor_scalar(out=ig[:], in0=gv[:, :, 0], scalar1=gb_sb[:, 0:1], scalar2=None, op0=ALU.add),
                 reads=[regA.b, gb_sb.b], writes=[ig.b])
            for tci in range(4 if do_ml >= 2 else 0):
                tsl = slice(tci * 128, (tci + 1) * 128)
                k.op("dve", lambda e: e.tensor_scalar(out=lfb[:], in0=c["ones"][:], scalar1=logf[:, tci:tci + 1], scalar2=None,
                                                      op0=ALU.mult), reads=[logf.b, c["ones"].b], writes=[lfb.b])

                def mmb(e):
                    e.matmul(regB.ap(b=128), lhsT=lfb[:], rhs=c["causal"][:], start=True, stop=True)
                    return e.matmul(regB.ap(a=128, b=129), lhsT=c["causal"][:], rhs=logf[:, tci:tci + 1], start=True, stop=True)
                k.op("pe", mmb, reads=[lfb.b, c["causal"].b, logf.b], writes=[regB.b])
                k.op("dve", lambda e: e.tensor_tensor(out=bias_s[:], in0=ig[:, tci:tci + 1], in1=regB.ap(a=128, b=129), op=ALU.subtract),
                     reads=[ig.b, regB.b], writes=[bias_s.b])
                k.op("act", lambda e: e.activation(out=DT[:], in_=regB.ap(b=128), func=AF.Exp, bias=bias_s[:, 0:1], scale=1.0),
                     reads=[regB.b, bias_s.b], writes=[DT.b])
                k.op("act", lambda e: e.activation(out=Eb[:], in_=regB.ap(b=128), func=AF.Exp), reads=[regB.b], writes=[Eb.b])
                k.op("dve", lambda e: e.tensor_copy(out=small[:, 0:1], in_=regB.ap(a=127, b=128)), reads=[regB.b], writes=[small.b])
                if stop <= 6.1:
                    raise _Stop()
                k.op("pool", lambda e: e.tensor_tensor(out=Dm[:], in0=DT[:], in1=c["causal"][:], op=ALU.mult),
                     reads=[DT.b, c["causal"].b], writes=[Dm.b])
                if stop <= 6.2:
                    raise _Stop()

                def mms(e):
                    for e_ in range(2):
                        ins = e.matmul(regC.ap(), lhsT=kTb[:, e_, tsl], rhs=qTb[:, e_, tsl], start=(e_ == 0), stop=(e_ == 1))
                    return ins
                k.op("pe", mms, reads=[kTb.b, qTb.b], writes=[regC.b])
                k.op("dve", lambda e: e.tensor_tensor(out=PT[:], in0=regC.ap(), in1=Dm[:], op=ALU.mult), reads=[regC.b, Dm.b],
                     writes=[PT.b])
                for e_ in range(2):
                    k.op("dve", lambda e: e.tensor_tensor(out=qs[:, e_, :], in0=qT[:, e_, tsl], in1=Eb[:], op=ALU.mult),
                         reads=[qT.b, Eb.b], writes=[qs.b])

                def mmnd(e):
                    e.matmul(regND.ap(), lhsT=PT[:], rhs=vext[:, tci, :], start=True, stop=False)
                    e.matmul(regND.ap(), lhsT=qs[:, 0, :], rhs=Cb[:, 0, :], start=False, stop=False)
                    return e.matmul(regND.ap(), lhsT=qs[:, 1, :], rhs=Cb[:, 1, :], start=False, stop=True)
                k.op("pe", mmnd, reads=[PT.b, vext.b, qs.b, Cb.b], writes=[regND.b])
                if debug and st == 0 and tci == 1:
                    dump("Cst", Cst, Cst[:].rearrange("p a b -> p (a b)"), 528)
                    dump("logf", logf, logf[:], 4)
                    dump("ig", ig, ig[:], 4)
                    dump("DT", DT, DT[:], 128)
                    dump("Eb", Eb, Eb[:], 128)
                    dump("ktok1", ktok, ktok[:, 1, :], 256)
                    dump("qT", qT, qT[:, 0, 128:256], 128)
                if stop <= 6.3:
                    raise _Stop()
                k.op("act", lambda e: e.activation(out=small[:, 1:2], in_=regND.ap(a=256, b=257), func=AF.Abs), reads=[regND.b],
                     writes=[small.b])
                k.op("dve", lambda e: e.tensor_scalar(out=small[:, 1:2], in0=small[:, 1:2], scalar1=1.0, scalar2=None, op0=ALU.max),
                     reads=[small.b], writes=[small.b])
                k.op("dve", lambda e: e.reciprocal(out=small[:, 1:2], in_=small[:, 1:2]), reads=[small.b], writes=[small.b])
                k.op("dve", lambda e: e.tensor_scalar(out=hn[:], in0=regND.ap(b=256), scalar1=small[:, 1:2], scalar2=None, op0=ALU.mult),
                     reads=[regND.b, small.b], writes=[hn.b])
                k.op("act", lambda e: e.activation(out=junk[:], in_=hn[:], func=AF.Square, accum_out=small[:, 2:3]), reads=[hn.b],
                     writes=[junk.b, small.b])
                k.op("act", lambda e: e.activation(out=small[:, 3:4], in_=small[:, 2:3], func=AF.Sqrt, scale=1.0 / 256, bias=c["eps"][:, 0:1]),
                     reads=[small.b, c["eps"].b], writes=[small.b])
                k.op("dve", lambda e: e.reciprocal(out=small[:, 3:4], in_=small[:, 3:4]), reads=[small.b], writes=[small.b])
                k.op("dve", lambda e: e.scalar_tensor_tensor(out=hm[:], in0=hn[:], scalar=small[:, 3:4], in1=mln_sb[:], op0=ALU.mult,
                                                             op1=ALU.mult), reads=[hn.b, small.b, mln_sb.b], writes=[hm.b])
                k.op("pool", lambda e: e.tensor_tensor(out=t1[:], in0=ctok[:, tci, :], in1=skp_sb[:], op=ALU.mult),
                     reads=[ctok.b, skp_sb.b], writes=[t1.b])
                k.op("dve", lambda e: e.tensor_tensor(out=t1[:], in0=t1[:], in1=hm[:], op=ALU.add), reads=[t1.b, hm.b], writes=[t1.b])
                k.op("dve", lambda e: e.tensor_tensor(out=yml[:], in0=t1[:], in1=osig[:, tci, :], op=ALU.mult), reads=[t1.b, osig.b],
                     writes=[yml.b])
                for d in range(2):
                    transpose_to(ystage[:, d, tsl], ystage.b, yml[:, d * 128:(d + 1) * 128], yml.b, eng="act")
                if debug and st == 0 and tci == 1:
                    dump("hn", hn, hn[:], 256)
                    dump("small", small, small[:], 8)
                    dump("hm", hm, hm[:], 256)
                    dump("yml", yml, yml[:], 256)
                if stop <= 6.4:
                    raise _Stop()
                k.op("act", lambda e: e.activation(out=small[:, 4:5], in_=bias_s[:], func=AF.Exp, bias=small[:, 0:1], scale=1.0),
                     reads=[bias_s.b, small.b], writes=[small.b])
                k.op("act", lambda e: e.activation(out=small[:, 5:6], in_=small[:, 0:1], func=AF.Exp), reads=[small.b], writes=[small.b])
                if stop <= 6.5:
                    raise _Stop()
                k.op("dve", lambda e: e.tensor_scalar(out=ka[:], in0=ktok[:, tci, :], scalar1=small[:, 4:5], scalar2=None, op0=ALU.mult),
                     reads=[ktok.b, small.b], writes=[ka.b])
                if stop <= 6.6:
                    raise _Stop()
                for kc in range(2):
                    k.op("pe", lambda e: e.matmul(regU[kc].ap(), lhsT=ka[:, kc * 128:(kc + 1) * 128], rhs=vext[:, tci, :], start=True,
                                                  stop=True), reads=[ka.b, vext.b], writes=[regU[kc].b])
                    k.op("dve", lambda e: e.scalar_tensor_tensor(out=Cst[:, kc, :], in0=Cst[:, kc, :], scalar=small[:, 5:6],
                                                                 in1=regU[kc].ap(), op0=ALU.mult, op1=ALU.add),
                         reads=[Cst.b, small.b, regU[kc].b], writes=[Cst.b])
                if stop <= 6.7 and kc == 1:
                    raise _Stop()
                if stop <= 6.8:
                    raise _Stop()
                k.op("act", lambda e: e.copy(out=Cb[:], in_=Cst[:]), reads=[Cst.b], writes=[Cb.b])

        def hg_stream():
            for hd in range(2 if do_hg >= 1 else 0):
                az = inproj_fm(512 + hd * 128, dst=r7all)
                k.op("act", lambda e: e.activation(out=tA[:], in_=az.ap(), func=AF.Sigmoid, scale=-1.0), reads=[az.b], writes=[tA.b])
                k.op("act", lambda e: e.activation(out=tB[:], in_=az.ap(), func=AF.Exp, scale=-1.0), reads=[az.b], writes=[tB.b])
                aq = inproj_fm(256 + hd * 128, dst=r7all)
                k.op("act", lambda e: e.activation(out=sqt[:], in_=aq.ap(), func=AF.Silu), reads=[aq.b], writes=[sqt.b])
                k.op("dve", lambda e: e.tensor_scalar(out=k2[:], in0=tA[:], scalar1=oml[:, hd:hd + 1], scalar2=None, op0=ALU.mult),
                     reads=[tA.b, oml.b], writes=[k2.b])
                k.op("dve", lambda e: e.tensor_scalar(out=tA[:], in0=k2[:], scalar1=HG_MAX_K, scalar2=None, op0=ALU.min), reads=[k2.b],
                     writes=[tA.b])
                k.op("act", lambda e: e.activation(out=lf1[:], in_=tA[:], func=AF.Ln, scale=-1.0, bias=c["one1"][:, 0:1]),
                     reads=[tA.b, c["one1"].b], writes=[lf1.b])
                k.op("act", lambda e: e.activation(out=tB[:], in_=tB[:], func=AF.Ln, scale=1.0, bias=c["one1"][:, 0:1]),
                     reads=[tB.b, c["one1"].b], writes=[tB.b])
                k.op("dve", lambda e: e.scalar_tensor_tensor(out=lgf[:], in0=tB[:], scalar=-1.0, in1=lf1[:], op0=ALU.mult, op1=ALU.max),
                     reads=[tB.b, lf1.b], writes=[lgf.b])
                k.op("dve", lambda e: e.tensor_tensor_scan(out=bt[:], data0=resetm[:], data1=lgf[:], initial=0.0, op0=ALU.mult,
                                                           op1=ALU.add), reads=[resetm.b, lgf.b], writes=[bt.b])
                k.op("dve", lambda e: e.tensor_tensor(out=v3(brel), in0=v3(bt), in1=v3(bt)[:, :, 31:32].to_broadcast([128, 8, 64]),
                                                      op=ALU.subtract), reads=[bt.b], writes=[brel.b])
                k.op("act", lambda e: e.activation(out=tA[:], in_=brel[:], func=AF.Exp), reads=[brel.b], writes=[tA.b])
                k.op("act", lambda e: e.activation(out=tC[:], in_=brel[:], func=AF.Exp, scale=-1.0), reads=[brel.b], writes=[tC.b])
                for par in range(2):
                    k.op("dve", lambda e: e.tensor_tensor(out=qz[par][:, :, par * 64:(par + 1) * 64], in0=v4(sqt)[:, :, par, :],
                                                          in1=v4(tA)[:, :, par, :], op=ALU.mult), reads=[sqt.b, tA.b], writes=[qz[par].b])
                    k.op("dve", lambda e: e.tensor_tensor(out=kz[par][:, :, par * 64:(par + 1) * 64], in0=v4(k2)[:, :, par, :],
                                                          in1=v4(tC)[:, :, par, :], op=ALU.mult), reads=[k2.b, tC.b], writes=[kz[par].b])
                k.op("act", lambda e: e.activation(out=tA[:], in_=bt[:], func=AF.Exp), reads=[bt.b], writes=[tA.b])
                for par in range(2):
                    k.op("dve", lambda e: e.tensor_tensor(out=qbz[par][:, :, par * 64:(par + 1) * 64], in0=v4(sqt)[:, :, par, :],
                                                          in1=v4(tA)[:, :, par, :], op=ALU.mult), reads=[sqt.b, tA.b], writes=[qbz[par].b])
                k.op("dve", lambda e: e.tensor_tensor(out=v3(tC), in0=v3(bt)[:, :, 63:64].to_broadcast([128, 8, 64]), in1=v3(bt),
                                                      op=ALU.subtract), reads=[bt.b], writes=[tC.b])
                k.op("act", lambda e: e.activation(out=tC[:], in_=tC[:], func=AF.Exp), reads=[tC.b], writes=[tC.b])
                k.op("dve", lambda e: e.tensor_tensor(out=kgT[:], in0=k2[:], in1=tC[:], op=ALU.mult), reads=[k2.b, tC.b], writes=[kgT.b])
                k.op("act", lambda e: e.activation(out=eg8[:], in_=v3(bt)[:, :, 63], func=AF.Exp), reads=[bt.b], writes=[eg8.b])
                for tl in range(4):
                    r = regT[tcnt[0] % 4]
                    tcnt[0] += 1
                    k.op("pe", lambda e: e.transpose(r.ap(), kgT[:, tl * 128:(tl + 1) * 128], c["ident"][:]), reads=[kgT.b, c["ident"].b],
                         writes=[r.b])
                    k.op("act", lambda e: e.copy(out=kgz[0][0:64, tl, :], in_=r.ap(rows=slice(0, 64))), reads=[r.b], writes=[kgz[0].b])
                    k.op("act", lambda e: e.copy(out=kgz[1][64:128, tl, :], in_=r.ap(rows=slice(64, 128))), reads=[r.b], writes=[kgz[1].b])
                S = Sst[hd]
                for tl in range(4 if do_hg >= 2 else 0):
                    def mma(e):
                        e.matmul(r7A.ap(), lhsT=kz[0][:, tl, :], rhs=qz[0][:, tl, :], start=True, stop=False)
                        return e.matmul(r7A.ap(), lhsT=kz[1][:, tl, :], rhs=qz[1][:, tl, :], start=False, stop=True)
                    k.op("pe", mma, reads=[kz[0].b, kz[1].b, qz[0].b, qz[1].b], writes=[r7A.b])
                    k.op("dve", lambda e: e.tensor_tensor(out=Am[:], in0=r7A.ap(), in1=c["causal"][:], op=ALU.mult),
                         reads=[r7A.b, c["causal"].b], writes=[Am.b])
                    k.op("pe", lambda e: e.matmul(r7C.ap(), lhsT=kgz[0][:, tl, :], rhs=hv[:, tl, hd, :], start=True, stop=True),
                         reads=[kgz[0].b, hv.b], writes=[r7C.b])
                    k.op("pe", lambda e: e.matmul(r7D.ap(), lhsT=kgz[1][:, tl, :], rhs=hv[:, tl, hd, :], start=True, stop=True),
                         reads=[kgz[1].b, hv.b], writes=[r7D.b])
                    k.op("dve", lambda e: e.scalar_tensor_tensor(out=S[:], in0=S[:], scalar=eg8[:, 2 * tl:2 * tl + 1], in1=r7C.ap(),
                                                                 op0=ALU.mult, op1=ALU.add), reads=[S.b, eg8.b, r7C.b], writes=[S.b])
                    k.op("act", lambda e: e.copy(out=Sb[hd][1][:], in_=S[:]), reads=[S.b], writes=[Sb[hd][1].b])

                    def mmo(e):
                        e.matmul(r7B.ap(), lhsT=Am[:], rhs=hv[:, tl, hd, :], start=True, stop=False)
                        e.matmul(r7B.ap(), lhsT=qbz[0][:, tl, :], rhs=Sb[hd][0][:], start=False, stop=False)
                        return e.matmul(r7B.ap(), lhsT=qbz[1][:, tl, :], rhs=Sb[hd][1][:], start=False, stop=True)
                    k.op("pe", mmo, reads=[Am.b, hv.b, qbz[0].b, qbz[1].b, Sb[hd][0].b, Sb[hd][1].b], writes=[r7B.b])
                    k.op("dve", lambda e: e.scalar_tensor_tensor(out=S[:], in0=S[:], scalar=eg8[:, 2 * tl + 1:2 * tl + 2], in1=r7D.ap(),
                                                                 op0=ALU.mult, op1=ALU.add), reads=[S.b, eg8.b, r7D.b], writes=[S.b])
                    k.op("act", lambda e: e.copy(out=Sb[hd][0][:], in_=S[:]), reads=[S.b], writes=[Sb[hd][0].b])
                    k.op("act", lambda e: e.copy(out=o_sb[:], in_=r7B.ap()), reads=[r7B.b], writes=[o_sb.b])
                    k.op("act", lambda e: e.activation(out=junk2[:], in_=o_sb[:], func=AF.Square, accum_out=small2[:, 6:7]),
                         reads=[o_sb.b], writes=[junk2.b, small2.b])
                    k.op("act", lambda e: e.activation(out=small2[:, 7:8], in_=small2[:, 6:7], func=AF.Sqrt, scale=1.0 / 128,
                                                       bias=c["eps"][:, 0:1]), reads=[small2.b, c["eps"].b], writes=[small2.b])
                    k.op("dve", lambda e: e.reciprocal(out=small2[:, 7:8], in_=small2[:, 7:8]), reads=[small2.b], writes=[small2.b])
                    k.op("dve", lambda e: e.scalar_tensor_tensor(out=o2n[:], in0=o_sb[:], scalar=small2[:, 7:8], in1=hgn_sb[:, hd, :],
                                                                 op0=ALU.mult, op1=ALU.mult), reads=[o_sb.b, small2.b, hgn_sb.b], writes=[o2n.b])
                    k.op("pool", lambda e: e.tensor_tensor(out=yh[:], in0=o2n[:], in1=hgs[:, tl, hd * 128:(hd + 1) * 128], op=ALU.mult),
                         reads=[o2n.b, hgs.b], writes=[yh.b])
                    transpose_to(ystage[:, 2 + hd, tl * 128:(tl + 1) * 128], ystage.b, yh[:], yh.b, eng="act")

        try:
            interleave(k, [ml_stream, hg_stream])
        except _Stop:
            pass

    for st in range(NST):
        t0 = st * TW
        try:
            body(st, t0)
        except _Stop:
            pass
        tk = k.op("sp", lambda e: e.dma_start(out=yT.rearrange("c p t -> p c t")[:, :, t0:t0 + TW], in_=ystage[:]), reads=[ystage.b],
                  dsem=ds_y)
        out_toks.append(tk)
    k.finish(out_toks)
    return nc


def fm(a):
    T, C = a.shape
    return np.ascontiguousarray(a.T.reshape(C // 128, 128, T))


def unfm(aT):
    n, p, T = aT.shape
    return np.ascontiguousarray(aT.reshape(n * p, T).T)


def pvec(w):
    return np.ascontiguousarray(w.reshape(-1, 128).T)


def rep(w):
    return np.ascontiguousarray(np.broadcast_to(w[None], (128,) + w.shape))


def ab_core_inputs(I, layer, hgp, hT_b):
    j = layer // 2
    h = hgp
    W = I["ab_w_in"][j]
    o_u, o_v, o_o, o_i, o_f, o_hq, o_hf, o_hi, o_hg = 0, 1024, 2048, 3072, 3076, 3080, 4104, 5128, 6152
    sl = lambda o, n, i: W[:, o + i * n:o + (i + 1) * n]
    win = np.concatenate([sl(o_u, 256, h), sl(o_hq, 256, h), sl(o_hf, 256, h), sl(o_v, 256, h), sl(o_o, 256, h),
                          sl(o_hi, 256, h), sl(o_hg, 256, h), W[:, o_i + h:o_i + h + 1], W[:, o_f + h:o_f + h + 1]], axis=1)
    cwf = I["ml_conv_w"][j][:, h * 256:(h + 1) * 256]
    cw = np.ascontiguousarray(cwf.reshape(4, 2, 128).transpose(2, 1, 0))
    cb = np.ascontiguousarray(I["ml_conv_b"][j][h * 256:(h + 1) * 256].reshape(2, 128).T)
    lbl = np.ascontiguousarray(I["hg_lb_logits"][:, h * 256:(h + 1) * 256].reshape(2, 2, 128).transpose(2, 1, 0))
    return {
        "hT": hT_b, "nw": pvec(I["mix_norm"][layer]), "win": np.ascontiguousarray(win), "cw": cw, "cb": cb,
        "wq": np.ascontiguousarray(I["ml_wq"][j][h]), "wk": np.ascontiguousarray(I["ml_wk"][j][h]),
        "gb": rep(np.array([I["ml_i_bias"][j][h], I["ml_f_bias"][j][h]], np.float32)),
        "mln": rep(I["ml_out_norm"][j][h]), "skp": rep(I["ml_skip"][j][h * 256:(h + 1) * 256]),
        "lbl": lbl, "hgn": rep(I["hg_out_norm"][j][2 * h:2 * h + 2]),
    }


import math
from contextlib import ExitStack

NSA_BIG = 200.0
ROPE_INVF = np.power(np.float32(10000.0), -np.arange(64, dtype=np.float32) / 64).astype(np.float32)


def kb_barrier(k):
    toks = [Tok(k.sem[e], k.cnt[e], "E" + e, e) for e in k.engs if k.cnt[e] > 0]
    toks += [Tok(d.h, d.val, d.key, None) for d in k._all_dsems if d.val > 0]
    for e in k.engs:
        for t in toks:
            if t.eng != e:
                k._wait(e, t)


def build_nsa(T=8192, stop=99, debug=False):
    TW = 256
    NST = T // TW
    NT = T // 128
    NCB = (T - 32) // 16 + 1
    NCT = (NCB + 127) // 128
    NCOL = 1292
    scale = 128 ** -0.5
    nc = bass.Bass("TRN2", target_bir_lowering=False)
    k = KB(nc)
    k._all_dsems = []
    _ds = k.dsem

    def dsem2(name=None):
        d = _ds(name)
        k._all_dsems.append(d)
        return d
    k.dsem = dsem2

    def dram(name, shape, dt=F32, kind="ExternalInput"):
        return nc.dram_tensor(name, list(shape), dt, kind=kind).ap()
    hT = dram("hT", [16, 128, T])
    nw = dram("nw", [128, 16])
    win = dram("win", [2048, NCOL])
    qnw = dram("qnw", [128, 512])
    knw = dram("knw", [128, 3, 128])
    posT = dram("posT", [128, 2, 32])
    w1 = dram("w1", [2, 4096, 256])
    b1 = dram("b1", [128, 2, 2])
    w2 = dram("w2", [2, 256, 128])
    gbias = dram("gbias", [128, 12])
    yT = dram("yT", [4, 128, T], BF16, kind="ExternalOutput")
    dbg = dram("dbg", [128, 8192], F32, kind="ExternalOutput") if debug else None
    dbg_pos = [0]
    dbg_map = {}
    nc._dbg_map = dbg_map

    def dump(name, tt, ap, n):
        if not debug or name in dbg_map:
            return
        c0 = dbg_pos[0]
        dbg_pos[0] += n
        dbg_map[name] = (c0, n)
        k.op("sp", lambda e: e.dma_start(out=dbg[:, c0:c0 + n], in_=ap), reads=[tt.b], dsem=k.dsem())

    c = make_consts(k)
    win_v = win.rearrange("(dc p) f -> p dc f", p=128)
    ps = [nc.alloc_psum_tensor(f"ps{i}", [128, 512], F32) for i in range(8)]
    accS = [PReg(k, ps[i], 0, 512, f"accS{i}") for i in range(2)]
    _oset = [PReg(k, ps[2 + h], 0, 130, f"o_{h}") for h in range(4)]
    oreg = [_oset, _oset]
    impT = PReg(k, ps[6], 0, 512, "impT")
    ps_ss = impT
    regT = [PReg(k, ps[7], i * 128, (i + 1) * 128, f"regT{i}") for i in range(4)]
    acnt = [0]
    tcnt = [0]

    def next_acc():
        a = accS[acnt[0] % 2]
        acnt[0] += 1
        return a

    def next_regT():
        r = regT[tcnt[0] % 4]
        tcnt[0] += 1
        return r

    def ld(name, shape, src, dt=F32, eng="sp"):
        t = sb(k, name, shape, dt)
        k.op(eng, lambda e: e.dma_start(out=t[:], in_=src), writes=[t.b], dsem=k.dsem())
        return t

    nw_sb = ld("nw_sb", [128, 16], nw)
    qnw_sb = ld("qnw_sb", [128, 512], qnw)
    knw_sb = ld("knw_sb", [128, 3, 128], knw)
    b1_sb = ld("b1_sb", [128, 2, 2], b1)
    gb_sb = ld("gb_sb", [128, 12], gbias)
    hTt = sb(k, "hTt", [128, 16, TW], F32)
    ds_h = k.dsem()
    xT = sb(k, "xT", [128, 16, TW], BF16)
    xT2 = sb(k, "xT2", [128, 16, TW], BF16)
    hTt2 = sb(k, "hTt2", [128, 16, TW], F32)
    ds_h2 = k.dsem()
    rstd = sb(k, "rstd", [128, TW], F32)
    scrsq = [sb(k, f"scrsq{i}", [128, TW], F32) for i in range(2)]
    kcmpT = sb(k, "kcmpT", [128, NCT * 128], BF16)
    vcmp = sb(k, "vcmp", [128, NCT, 130], BF16)
    cover = sb(k, "cover", [128, NCT, 128], BF16)
    identb = sb(k, "identb", [128, 128], BF16)
    k.op("dve", lambda e: e.tensor_copy(out=identb[:], in_=c["ident"][:]), reads=[c["ident"].b], writes=[identb.b])
    k.op("dve", lambda e: e.memset(kcmpT[:], 0.0), writes=[kcmpT.b])
    k.op("dve", lambda e: e.memset(vcmp[:], 0.0), writes=[vcmp.b])
    k.op("dve", lambda e: e.memset(vcmp[:, :, 128:129], 1.0), writes=[vcmp.b])
    invf = sb(k, "invf", [128, 64], F32)
    for i in range(64):
        k.op("dve", lambda e: e.memset(invf[:, i:i + 1], float(ROPE_INVF[i])), writes=[invf.b])
    pidx_i = sb(k, "pidx_i", [128, 1], I32)
    k.op("pool", lambda e: e.iota(pidx_i[:], pattern=[[0, 1]], base=0, channel_multiplier=1), writes=[pidx_i.b])
    pidx = sb(k, "pidx", [128, 1], F32)
    k.op("dve", lambda e: e.tensor_copy(out=pidx[:], in_=pidx_i[:]), reads=[pidx_i.b], writes=[pidx.b])
    pcol = sb(k, "pcol", [128, 1], F32)
    ang = sb(k, "ang", [128, 64], F32)
    rr = sb(k, "rr", [128, 64], F32)
    rf = sb(k, "rf", [128, 64], F32)
    ri = sb(k, "ri", [128, 64], I32)
    cos_t = sb(k, "cos_t", [128, 64], F32)
    sin_t = sb(k, "sin_t", [128, 64], F32)
    rtmp = sb(k, "rtmp", [128, 4, 64], F32)
    TWO_PI = 2 * math.pi

    def rope_tables(mult, add, cx=None):
        cx = cx or cx0
        k.op("dve", lambda e: e.tensor_scalar(out=cx.pcol[:], in0=pidx[:], scalar1=float(mult), scalar2=float(add), op0=ALU.mult,
                                              op1=ALU.add), reads=[pidx.b], writes=[cx.pcol.b])
        k.op("dve", lambda e: e.tensor_scalar(out=cx.ang[:], in0=invf[:], scalar1=cx.pcol[:, 0:1], scalar2=None, op0=ALU.mult),
             reads=[invf.b, cx.pcol.b], writes=[cx.ang.b])
        for (off, dst) in ((0.0, cx.sin_t), (math.pi / 2, cx.cos_t)):
            k.op("dve", lambda e: e.tensor_scalar(out=cx.rr[:], in0=cx.ang[:], scalar1=off, scalar2=None, op0=ALU.add), reads=[cx.ang.b],
                 writes=[cx.rr.b])
            k.op("dve", lambda e: e.tensor_scalar(out=cx.rf[:], in0=cx.rr[:], scalar1=1.0 / TWO_PI, scalar2=None, op0=ALU.mult),
                 reads=[cx.rr.b], writes=[cx.rf.b])
            k.op("dve", lambda e: e.tensor_copy(out=cx.ri[:], in_=cx.rf[:]), reads=[cx.rf.b], writes=[cx.ri.b])
            k.op("dve", lambda e: e.tensor_copy(out=cx.rf[:], in_=cx.ri[:]), reads=[cx.ri.b], writes=[cx.rf.b])
            k.op("dve", lambda e: e.scalar_tensor_tensor(out=cx.rr[:], in0=cx.rf[:], scalar=-TWO_PI, in1=cx.rr[:], op0=ALU.mult, op1=ALU.add),
                 reads=[cx.rf.b, cx.rr.b], writes=[cx.rr.b])
            k.op("dve", lambda e: e.tensor_scalar(out=cx.rf[:], in0=cx.rr[:], scalar1=math.pi, scalar2=None, op0=ALU.is_gt), reads=[cx.rr.b],
                 writes=[cx.rf.b])
            k.op("dve", lambda e: e.scalar_tensor_tensor(out=cx.rr[:], in0=cx.rf[:], scalar=-TWO_PI, in1=cx.rr[:], op0=ALU.mult, op1=ALU.add),
                 reads=[cx.rf.b, cx.rr.b], writes=[cx.rr.b])
            k.op("act", lambda e: e.activation(out=dst[:], in_=cx.rr[:], func=AF.Sin), reads=[cx.rr.b], writes=[dst.b])

    def apply_rope(dst, src, H, cx=None):
        cx = cx or cx0
        cb = cx.cos_t[:].rearrange("p (o f) -> p o f", o=1).to_broadcast([128, H, 64])
        sbb = cx.sin_t[:].rearrange("p (o f) -> p o f", o=1).to_broadcast([128, H, 64])
        x1, x2 = src[:, 0:H, 0:64], src[:, 0:H, 64:128]
        tm = cx.rtmp[:, 0:H, :]
        k.op("dve", lambda e: e.tensor_tensor(out=tm, in0=x2, in1=sbb, op=ALU.mult), reads=[src.b, cx.sin_t.b], writes=[cx.rtmp.b])
        k.op("dve", lambda e: e.tensor_tensor(out=dst[:, 0:H, 0:64], in0=x1, in1=cb, op=ALU.mult), reads=[src.b, cx.cos_t.b], writes=[dst.b])
        k.op("dve", lambda e: e.tensor_tensor(out=dst[:, 0:H, 0:64], in0=dst[:, 0:H, 0:64], in1=tm, op=ALU.subtract),
             reads=[dst.b, cx.rtmp.b], writes=[dst.b])
        k.op("dve", lambda e: e.tensor_tensor(out=tm, in0=x1, in1=sbb, op=ALU.mult), reads=[src.b, cx.sin_t.b], writes=[cx.rtmp.b])
        k.op("dve", lambda e: e.tensor_tensor(out=dst[:, 0:H, 64:128], in0=x2, in1=cb, op=ALU.mult), reads=[src.b, cx.cos_t.b], writes=[dst.b])
        k.op("dve", lambda e: e.tensor_tensor(out=dst[:, 0:H, 64:128], in0=dst[:, 0:H, 64:128], in1=tm, op=ALU.add),
             reads=[dst.b, cx.rtmp.b], writes=[dst.b])

    small = sb(k, "small", [128, 16], F32)
    junk = sb(k, "junk", [128, 128], F32)
    kn = sb(k, "kn", [128, 4, 128], F32)
    kr = sb(k, "kr", [128, 4, 128], F32)

    def rms_heads(src_reg, col0, H, wfn, post_scale=1.0, nrows=128, cx=None):
        cx = cx or cx0
        rows = slice(0, nrows)
        for h in range(H):
            k.op("act", lambda e: e.activation(out=cx.junk[rows, :], in_=src_reg.ap(rows, col0[h], col0[h] + 128), func=AF.Square,
                                               accum_out=cx.small[rows, h:h + 1]), reads=[src_reg.b], writes=[cx.junk.b, cx.small.b])
        k.op("act", lambda e: e.activation(out=cx.small[rows, 4:4 + H], in_=cx.small[rows, 0:H], func=AF.Sqrt, scale=1.0 / 128,
                                           bias=c["eps"][rows, 0:1]), reads=[cx.small.b, c["eps"].b], writes=[cx.small.b])
        k.op("dve", lambda e: e.reciprocal(out=cx.small[rows, 4:4 + H], in_=cx.small[rows, 4:4 + H]), reads=[cx.small.b], writes=[cx.small.b])
        if post_scale != 1.0:
            k.op("dve", lambda e: e.tensor_scalar(out=cx.small[rows, 4:4 + H], in0=cx.small[rows, 4:4 + H], scalar1=post_scale, scalar2=None,
                                                  op0=ALU.mult), reads=[cx.small.b], writes=[cx.small.b])
        for h in range(H):
            wt, wap = wfn(h)
            k.op("dve", lambda e: e.scalar_tensor_tensor(out=cx.kn[rows, h, :], in0=src_reg.ap(rows, col0[h], col0[h] + 128),
                                                         scalar=cx.small[rows, 4 + h:5 + h], in1=wap, op0=ALU.mult, op1=ALU.mult),
                 reads=[src_reg.b, cx.small.b, wt.b], writes=[cx.kn.b])

    class _Cx:
        pass
    cx0 = _Cx()
    cx0.pcol, cx0.ang, cx0.rr, cx0.rf, cx0.ri, cx0.cos_t, cx0.sin_t, cx0.rtmp = pcol, ang, rr, rf, ri, cos_t, sin_t, rtmp
    cx0.small, cx0.junk, cx0.kn, cx0.kr = small, junk, kn, kr

    def new_cx(alloc, tag):
        cx = _Cx()
        cx.pcol = alloc("pcol" + tag, [128, 1], F32)
        cx.ang = alloc("ang" + tag, [128, 64], F32)
        cx.rr = alloc("rr" + tag, [128, 64], F32)
        cx.rf = alloc("rf" + tag, [128, 64], F32)
        cx.ri = alloc("ri" + tag, [128, 64], I32)
        cx.cos_t = alloc("cos_t" + tag, [128, 64], F32)
        cx.sin_t = alloc("sin_t" + tag, [128, 64], F32)
        cx.rtmp = alloc("rtmp" + tag, [128, 4, 64], F32)
        cx.small = alloc("small" + tag, [128, 16], F32)
        cx.junk = alloc("junk" + tag, [128, 128], F32)
        cx.kn = alloc("kn" + tag, [128, 4, 128], F32)
        cx.kr = alloc("kr" + tag, [128, 4, 128], F32)
        return cx

    out_toks = []
    es = ExitStack()

    def sbs(name, shape, dt):
        return TT(k, es.enter_context(nc.sbuf_tensor(name, list(shape), dt)), name)

    win_a = sbs("win_a", [128, 16, 256], BF16)
    k.op("pool", lambda e: e.dma_start(out=win_a[:], in_=win_v[:, :, 0:256]), writes=[win_a.b], dsem=k.dsem())
    kvT = [sbs(f"kvT{i}", [128, T], BF16) for i in range(2)]
    w1_sb = sbs("w1_sb", [128, 32, 256], BF16)
    w2_sb = sbs("w2_sb", [128, 2, 2, 128], BF16)
    posT_sb = sbs("posT_sb", [128, 2, 32], BF16)
    hsil = sbs("hsil", [128, 2, 128], BF16)
    biasv = sbs("biasv", [128, 2], F32)
    k.op("pool", lambda e: e.dma_start(out=posT_sb[:], in_=posT), writes=[posT_sb.b], dsem=k.dsem())
    for kv in range(2):
        k.op("pool", lambda e: e.dma_start(out=w2_sb[:, kv, :, :], in_=w2[kv].rearrange("(c p) e -> p c e", p=128)), writes=[w2_sb.b],
             dsem=k.dsem())
    xT_dram = nc.dram_tensor("xT_scratch", [NST, 128, 16 * TW], BF16).ap()
    xd_b = k.bufs(NST, "xd")
    ds_xs = k.dsem()
    xTs = [xT, xT2]
    ds_xl = [k.dsem(), k.dsem()]

    def load_xT(st):
        xt = xTs[st % 2]
        k.op("sp", lambda e: e.dma_start(out=xt[:].rearrange("p a b -> p (a b)"), in_=xT_dram[st]), reads=[xd_b[st]], writes=[xt.b],
             dsem=ds_xl[st % 2])

    hbufs = [(hTt, ds_h), (hTt2, ds_h2)]
    load_h_tile(k, hT, hTt, 0, TW, ds_h)
    for st in range(NST):
        t0 = st * TW
        if st + 1 < NST:
            hb, hd = hbufs[(st + 1) % 2]
            load_h_tile(k, hT, hb, t0 + TW, TW, hd)
        hb, hd = hbufs[st % 2]
        norm_supertile(k, c, hT, nw_sb, hb, xT, ps_ss, rstd, scrsq, t0, TW, hd, preloaded=True)
        k.op("sp", lambda e: e.dma_start(out=xT_dram[st], in_=xT[:].rearrange("p a b -> p (a b)")), reads=[xT.b], writes=[xd_b[st]],
             dsem=ds_xs)
        for kv in range(2):
            a = next_acc()

            def mm(e):
                for dc in range(16):
                    ins = e.matmul(a.ap(b=TW), lhsT=win_a[:, dc, kv * 128:(kv + 1) * 128], rhs=xT[:, dc, :], start=(dc == 0), stop=(dc == 15))
                return ins
            k.op("pe", mm, reads=[win_a.b, xT.b], writes=[a.b])
            k.op("act", lambda e: e.copy(out=kvT[kv][:, t0:t0 + TW], in_=a.ap(b=TW)), reads=[a.b], writes=[kvT[kv].b])
    ds_w1 = k.dsem()
    for kv in range(2):
        k.op("pool", lambda e: e.dma_start(out=w1_sb[:], in_=w1[kv].rearrange("(l p) h -> p l h", p=128)), writes=[w1_sb.b], dsem=ds_w1)
        for hc in range(2):
            r = next_regT()

            def mmp(e):
                for l in range(32):
                    ins = e.matmul(r.ap(b=1), lhsT=w1_sb[:, l, hc * 128:(hc + 1) * 128], rhs=posT_sb[:, kv, l:l + 1], start=(l == 0),
                                   stop=(l == 31))
                return ins
            k.op("pe", mmp, reads=[w1_sb.b, posT_sb.b], writes=[r.b])
            k.op("dve", lambda e: e.tensor_tensor(out=biasv[:, hc:hc + 1], in0=r.ap(b=1), in1=b1_sb[:, kv, hc:hc + 1], op=ALU.add),
                 reads=[r.b, b1_sb.b], writes=[biasv.b])
        for nti in range(NCT):
            nn = min(128, NCB - 128 * nti)
            for hc in range(2):
                r = next_regT()

                def mmh(e):
                    for l in range(32):
                        s0 = 16 * 128 * nti + l
                        ins = e.matmul(r.ap(b=nn), lhsT=w1_sb[:, l, hc * 128:(hc + 1) * 128], rhs=kvT[kv][:, s0:s0 + 16 * (nn - 1) + 1:16],
                                       start=(l == 0), stop=(l == 31))
                    return ins
                k.op("pe", mmh, reads=[w1_sb.b, kvT[kv].b], writes=[r.b])
                k.op("act", lambda e: e.activation(out=hsil[:, hc, 0:nn], in_=r.ap(b=nn), func=AF.Silu, bias=biasv[:, hc:hc + 1], scale=1.0),
                     reads=[r.b, biasv.b], writes=[hsil.b])
            r = next_regT()

            def mmo(e):
                for hc in range(2):
                    ins = e.matmul(r.ap(rows=slice(0, nn)), lhsT=hsil[:, hc, 0:nn], rhs=w2_sb[:, kv, hc, :], start=(hc == 0), stop=(hc == 1))
                return ins
            k.op("pe", mmo, reads=[hsil.b, w2_sb.b], writes=[r.b])
            if kv == 0:
                rms_heads(r, [0], 1, lambda h: (knw_sb, knw_sb[0:nn, 0, :]), nrows=nn)
                rope_tables(16.0, 16.0 * 128 * nti + 31.0)
                apply_rope(kr, kn, 1)
                r2 = next_regT()
                k.op("pe", lambda e: e.transpose(r2.ap(b=nn), kr[0:nn, 0, :], c["ident"][0:nn, 0:nn]), reads=[kr.b, c["ident"].b],
                     writes=[r2.b])
                k.op("act", lambda e: e.copy(out=kcmpT[:, nti * 128:nti * 128 + nn], in_=r2.ap(b=nn)), reads=[r2.b], writes=[kcmpT.b])
            else:
                k.op("act", lambda e: e.copy(out=vcmp[0:nn, nti, 0:128], in_=r.ap(rows=slice(0, nn))), reads=[r.b], writes=[vcmp.b])
    if debug:
        dump("kcmpT", kcmpT, kcmpT[:, 0:128], 64) if False else None
    kb_barrier(k)
    es.close()
    es = ExitStack()
    if stop <= 1:
        tk = k.op("sp", lambda e: e.dma_start(out=yT[0, :, 0:NCT * 128], in_=kcmpT[:]), reads=[kcmpT.b], dsem=k.dsem())
        tk2 = k.op("sp", lambda e: e.dma_start(out=yT[1, :, 0:NCT * 130], in_=vcmp[:].rearrange("p a b -> p (a b)")), reads=[vcmp.b],
                   dsem=k.dsem())
        k.finish([tk, tk2])
        return nc

    ksT = sbs("ksT", [128, T], BF16)
    kwT = sbs("kwT", [128, T], BF16)
    vs_e = sbs("vs_e", [128, NT, 130], BF16)
    vw_e = sbs("vw_e", [128, NT, 130], BF16)
    k.op("dve", lambda e: e.memset(vs_e[:, :, 128:130], 1.0), writes=[vs_e.b])
    k.op("dve", lambda e: e.memset(vw_e[:, :, 128:130], 1.0), writes=[vw_e.b])
    esB = ExitStack()
    win_b = TT(k, esB.enter_context(nc.sbuf_tensor("win_b", [128, 16, 512], BF16)), "win_b")
    k.op("pool", lambda e: e.dma_start(out=win_b[:], in_=win_v[:, :, 256:768]), writes=[win_b.b], dsem=k.dsem())
    cxB = [cx0, new_cx(lambda n_, sh, dt: TT(k, esB.enter_context(nc.sbuf_tensor(n_, list(sh), dt)), n_), "_b1")]
    load_xT(0)
    for st in range(NST):
        t0 = st * TW
        if st + 1 < NST:
            load_xT(st + 1)
        xc = xTs[st % 2]
        def tile_stream(tci, cx):
            ti = st * (TW // 128) + tci
            a = next_acc()

            def mm(e):
                for dc in range(16):
                    ins = e.matmul(a.ap(), lhsT=xc[:, dc, tci * 128:(tci + 1) * 128], rhs=win_b[:, dc, :], start=(dc == 0), stop=(dc == 15))
                return ins
            k.op("pe", mm, reads=[win_b.b, xc.b], writes=[a.b])
            k.op("act", lambda e: e.copy(out=vs_e[:, ti, 0:128], in_=a.ap(a=128, b=256)), reads=[a.b], writes=[vs_e.b])
            k.op("act", lambda e: e.copy(out=vw_e[:, ti, 0:128], in_=a.ap(a=384, b=512)), reads=[a.b], writes=[vw_e.b])
            rms_heads(a, [0, 256], 2, lambda h: (knw_sb, knw_sb[:, 1 + h, :]), cx=cx)
            rope_tables(1.0, float(ti * 128), cx=cx)
            apply_rope(cx.kr, cx.kn, 2, cx=cx)
            for h, dstT in ((0, ksT), (1, kwT)):
                r2 = next_regT()
                k.op("pe", lambda e: e.transpose(r2.ap(), cx.kr[:, h, :], c["ident"][:]), reads=[cx.kr.b, c["ident"].b], writes=[r2.b])
                k.op("act", lambda e: e.copy(out=dstT[:, ti * 128:(ti + 1) * 128], in_=r2.ap()), reads=[r2.b], writes=[dstT.b])

        interleave(k, [lambda: tile_stream(0, cxB[0]), lambda: tile_stream(1, cxB[1])])
    kb_barrier(k)
    esB.close()
    if stop <= 2:
        tk = k.op("sp", lambda e: e.dma_start(out=yT[0, :, :], in_=ksT[:]), reads=[ksT.b], dsem=k.dsem())
        tk2 = k.op("sp", lambda e: e.dma_start(out=yT[1, :, :], in_=kwT[:]), reads=[kwT.b], dsem=k.dsem())
        tk3 = k.op("sp", lambda e: e.dma_start(out=yT[2, :, 0:NT * 128].rearrange("p (a b) -> p a b", b=128), in_=vs_e[:, :, 0:128]),
                   reads=[vs_e.b], dsem=k.dsem())
        k.finish([tk, tk2, tk3])
        return nc

    win_q = sbs("win_q", [128, 16, 524], BF16)
    k.op("pool", lambda e: e.dma_start(out=win_q[:], in_=win_v[:, :, 768:1292]), writes=[win_q.b], dsem=k.dsem())
    Esel = sbs("Esel", [128, T], BF16)
    k.op("dve", lambda e: e.memset(Esel[:], 1.0), writes=[Esel.b])
    k.op("pool", lambda e: e.affine_select(out=Esel[:], in_=Esel[:], pattern=[[1, T]], compare_op=ALU.is_ge, fill=0.0, base=0,
                                           channel_multiplier=-64), reads=[Esel.b], writes=[Esel.b])
    k.op("pool", lambda e: e.affine_select(out=Esel[:], in_=Esel[:], pattern=[[-1, T]], compare_op=ALU.is_ge, fill=0.0, base=63,
                                           channel_multiplier=64), reads=[Esel.b], writes=[Esel.b])
    f32a = sbs("f32a", [128, 512], F32)
    f32b = sbs("f32b", [128, 512], F32)
    ones4 = sbs("ones4", [128, 512], F32)
    zeros4 = sbs("zeros4", [128, 512], F32)
    k.op("dve", lambda e: e.memset(ones4[:], 1.0), writes=[ones4.b])
    k.op("dve", lambda e: e.memset(zeros4[:], 0.0), writes=[zeros4.b])
    for nti in range(NCT):
        k.op("pool", lambda e: e.affine_select(out=f32a[:, 0:128], in_=ones4[:, 0:128], pattern=[[64, 128]], compare_op=ALU.is_gt, fill=0.0,
                                               base=64 - 2048 * nti, channel_multiplier=-16), reads=[ones4.b], writes=[f32a.b])
        k.op("pool", lambda e: e.affine_select(out=f32a[:, 0:128], in_=f32a[:, 0:128], pattern=[[-64, 128]], compare_op=ALU.is_gt, fill=0.0,
                                               base=2048 * nti + 32, channel_multiplier=16), reads=[f32a.b], writes=[f32a.b])
        k.op("dve", lambda e: e.tensor_copy(out=cover[:, nti, :], in_=f32a[:, 0:128]), reads=[f32a.b], writes=[cover.b])
    cneg = sbs("cneg", [128, 512], BF16)
    wneg = sbs("wneg", [128, 512], BF16)
    k.op("pool", lambda e: e.affine_select(out=f32a[:].rearrange("p (h j) -> p h j", h=4), in_=zeros4[:].rearrange("p (h j) -> p h j", h=4),
                                           pattern=[[0, 4], [1, 128]], compare_op=ALU.is_ge, fill=-NSA_BIG, base=0, channel_multiplier=-1),
         reads=[zeros4.b], writes=[f32a.b])
    k.op("dve", lambda e: e.tensor_copy(out=cneg[:], in_=f32a[:]), reads=[f32a.b], writes=[cneg.b])
    k.op("pool", lambda e: e.affine_select(out=f32a[:].rearrange("p (h j) -> p h j", h=4), in_=zeros4[:].rearrange("p (h j) -> p h j", h=4),
                                           pattern=[[0, 4], [-1, 128]], compare_op=ALU.is_ge, fill=-NSA_BIG, base=-1, channel_multiplier=1),
         reads=[zeros4.b], writes=[f32a.b])
    k.op("dve", lambda e: e.tensor_copy(out=wneg[:], in_=f32a[:]), reads=[f32a.b], writes=[wneg.b])
    c1e4 = sbs("c1e4", [128, 128], F32)
    k.op("dve", lambda e: e.memset(c1e4[:], 1e4), writes=[c1e4.b])

    gsb = sbs("gsb", [128, 12], F32)
    qn4 = sbs("qn4", [128, 4, 128], F32)
    qr4 = sbs("qr4", [128, 4, 128], F32)
    qT = sbs("qT", [128, 512], BF16)
    Ef = sbs("Ef", [128, 512], F32)
    m01 = sbs("m01", [128, 512], F32)
    Pt = [sbs(f"Pt{i}", [128, 512], BF16) for i in range(2)]
    pcnt = [0]
    impS = sbs("impS", [128, 512], F32)
    imp = sbs("imp", [128, 128], F32)
    bon = sbs("bon", [128, 128], F32)
    impf = sbs("impf", [128, 128], F32)
    val01 = sbs("val01", [128, 128], F32)
    wk_ = sbs("wk_", [128, 128], F32)
    m8 = sbs("m8", [128, 8], F32)
    selm = sbs("selm", [128, 128], F32)
    nmT = sbs("nmT", [128, 512], BF16)
    zt = sbs("zt", [128, 3, 4], F32)
    wgt = sbs("wgt", [128, 4], F32)
    oacc = sbs("oacc", [128, 4, 128], F32)
    ystage = sbs("ystage", [128, 4, TW], BF16)
    ds_y = k.dsem()

    def exp_to_P(a, mask01=None):
        p = Pt[pcnt[0] % 2]
        pcnt[0] += 1
        if mask01 is None:
            k.op("act", lambda e: e.activation(out=p[:], in_=a.ap(), func=AF.Exp), reads=[a.b], writes=[p.b])
        else:
            k.op("act", lambda e: e.activation(out=Ef[:], in_=a.ap(), func=AF.Exp), reads=[a.b], writes=[Ef.b])
            k.op("dve", lambda e: e.tensor_tensor(out=p[:], in0=Ef[:], in1=mask01[:], op=ALU.mult), reads=[Ef.b, mask01.b], writes=[p.b])
        return p

    def pv(p, vt, vidx, oset, first, last):
        def mm4(e):
            for h in range(4):
                ins = e.matmul(oset[h].ap(), lhsT=p[:, h * 128:(h + 1) * 128], rhs=vt[:, vidx, :], start=first, stop=last)
            return ins
        k.op("pe", mm4, reads=[p.b, vt.b], writes=[oset[h].b for h in range(4)])

    def run_pairs(jobs, oset):
        n = len(jobs)
        accs = [None] * n

        def emit_S(i):
            a = next_acc()
            accs[i] = a
            k.op("pe", lambda e: jobs[i]["mm"](e, a), reads=jobs[i]["reads"], writes=[a.b])
        if n:
            emit_S(0)
        for i in range(n):
            if i + 1 < n:
                emit_S(i + 1)
            msk = jobs[i]["pre"]() if "pre" in jobs[i] else None
            p = exp_to_P(accs[i], msk)
            pv(p, jobs[i]["vt"], jobs[i]["vidx"], oset, i == 0, i == n - 1)
            if "post" in jobs[i]:
                jobs[i]["post"](p)

    def combine(oset, br, first, gsb=None):
        for h in range(4):
            k.op("dve", lambda e: e.tensor_scalar(out=zt[:, br, h:h + 1], in0=oset[h].ap(a=128, b=129), scalar1=1e-30, scalar2=None,
                                                  op0=ALU.max), reads=[oset[h].b], writes=[zt.b])
        k.op("dve", lambda e: e.reciprocal(out=zt[:, br, :], in_=zt[:, br, :]), reads=[zt.b], writes=[zt.b])
        k.op("dve", lambda e: e.tensor_tensor(out=wgt[:], in0=zt[:, br, :], in1=gsb[:, br:12:3], op=ALU.mult), reads=[zt.b, gsb.b],
             writes=[wgt.b])
        for h in range(4):
            if first:
                k.op("dve", lambda e: e.tensor_scalar(out=oacc[:, h, :], in0=oset[h].ap(b=128), scalar1=wgt[:, h:h + 1], scalar2=None,
                                                      op0=ALU.mult), reads=[oset[h].b, wgt.b], writes=[oacc.b])
            else:
                k.op("dve", lambda e: e.scalar_tensor_tensor(out=oacc[:, h, :], in0=oset[h].ap(b=128), scalar=wgt[:, h:h + 1],
                                                             in1=oacc[:, h, :], op0=ALU.mult, op1=ALU.add),
                     reads=[oset[h].b, wgt.b, oacc.b], writes=[oacc.b])

    qTs = [qT, sbs("qT_b", [128, 512], BF16)]
    gsbs = [gsb, sbs("gsb_b", [128, 12], F32)]
    qacc = impT

    def q_prep_a(qt):
        st_, tci_ = divmod(qt, TW // 128)
        xc = xTs[st_ % 2]
        tsl_ = slice(tci_ * 128, (tci_ + 1) * 128)
        a = qacc
        g_ = gsbs[qt % 2]

        def mm(e):
            for dc in range(16):
                ins = e.matmul(a.ap(), lhsT=xc[:, dc, tsl_], rhs=win_q[:, dc, 0:512], start=(dc == 0), stop=(dc == 15))
            return ins
        k.op("pe", mm, reads=[win_q.b, xc.b], writes=[a.b])
        rg = next_regT()

        def mmg(e):
            for dc in range(16):
                ins = e.matmul(rg.ap(b=12), lhsT=xc[:, dc, tsl_], rhs=win_q[:, dc, 512:524], start=(dc == 0), stop=(dc == 15))
            return ins
        k.op("pe", mmg, reads=[win_q.b, xc.b], writes=[rg.b])
        k.op("dve", lambda e: e.tensor_tensor(out=g_[:], in0=rg.ap(b=12), in1=gb_sb[:], op=ALU.add), reads=[rg.b, gb_sb.b], writes=[g_.b])
        k.op("act", lambda e: e.activation(out=g_[:], in_=g_[:], func=AF.Sigmoid), reads=[g_.b], writes=[g_.b])
        rms_heads(a, [0, 128, 256, 384], 4, lambda h: (qnw_sb, qnw_sb[:, h * 128:(h + 1) * 128]), post_scale=scale)
        rope_tables(1.0, float(qt * 128))
        apply_rope(qr4, kn, 4)

    def q_prep_b(qt):
        q_ = qTs[qt % 2]
        for h in range(4):
            r2 = next_regT()
            k.op("pe", lambda e: e.transpose(r2.ap(), qr4[:, h, :], c["ident"][:]), reads=[qr4.b, c["ident"].b], writes=[r2.b])
            k.op("act", lambda e: e.copy(out=q_[:, h * 128:(h + 1) * 128], in_=r2.ap()), reads=[r2.b], writes=[q_.b])

    load_xT(0)
    if NST > 1:
        load_xT(1)
    q_prep_a(0)
    q_prep_b(0)
    for st in range(NST):
        t0s = st * TW
        for tci in range(TW // 128):
            qt = st * (TW // 128) + tci
            t0 = qt * 128
            tsl = slice(tci * 128, (tci + 1) * 128)
            qT = qTs[qt % 2]
            gsb = gsbs[qt % 2]
            nmax = (t0 + 127 - 31) // 16
            ntiles = 0 if nmax < 0 else min(NCT, nmax // 128 + 1)
            oc = oreg[0]
            if ntiles == 0:
                k.op("dve", lambda e: e.memset(imp[:], 0.0), writes=[imp.b])
            jobs = []
            for nt in range(ntiles):
                def mmc(e, a, nt=nt):
                    return e.matmul(a.ap(), lhsT=kcmpT[:, nt * 128:(nt + 1) * 128], rhs=qT[:], start=True, stop=True)

                def pre(nt=nt):
                    k.op("pool", lambda e: e.affine_select(out=m01[:].rearrange("p (h j) -> p h j", h=4),
                                                           in_=ones4[:].rearrange("p (h j) -> p h j", h=4), pattern=[[0, 4], [1, 128]],
                                                           compare_op=ALU.is_ge, fill=0.0, base=t0 - 2048 * nt - 31, channel_multiplier=-16),
                         reads=[ones4.b], writes=[m01.b])
                    return m01

                def post(p, nt=nt):
                    k.op("pe", lambda e: e.matmul(impT.ap(), lhsT=cover[:, nt, :], rhs=p[:], start=(nt == 0), stop=(nt == ntiles - 1)),
                         reads=[cover.b, p.b], writes=[impT.b])
                jobs.append(dict(mm=mmc, reads=[kcmpT.b, qT.b], vt=vcmp, vidx=nt, pre=pre, post=post))
            run_pairs(jobs, oc)
            if ntiles > 0:
                combine(oc, 0, True, gsb)
                k.op("act", lambda e: e.copy(out=impS[:], in_=impT.ap()), reads=[impT.b], writes=[impS.b])
                for h in range(4):
                    r2 = next_regT()
                    k.op("pe", lambda e: e.transpose(r2.ap(), impS[:, h * 128:(h + 1) * 128], c["ident"][:]), reads=[impS.b, c["ident"].b],
                         writes=[r2.b])
                    if h == 0:
                        k.op("dve", lambda e: e.tensor_scalar(out=imp[:], in0=r2.ap(), scalar1=zt[:, 0, 0:1], scalar2=None, op0=ALU.mult),
                             reads=[r2.b, zt.b], writes=[imp.b])
                    else:
                        k.op("dve", lambda e: e.scalar_tensor_tensor(out=imp[:], in0=r2.ap(), scalar=zt[:, 0, h:h + 1], in1=imp[:],
                                                                     op0=ALU.mult, op1=ALU.add), reads=[r2.b, zt.b, imp.b], writes=[imp.b])
            else:
                k.op("dve", lambda e: e.memset(oacc[:], 0.0), writes=[oacc.b])
            k.op("pool", lambda e: e.affine_select(out=bon[:], in_=c1e4[:], pattern=[[-64, 128]], compare_op=ALU.is_ge, fill=0.0, base=t0,
                                                   channel_multiplier=1), reads=[c1e4.b], writes=[bon.b])
            k.op("pool", lambda e: e.affine_select(out=bon[:], in_=bon[:], pattern=[[64, 128]], compare_op=ALU.is_ge, fill=0.0,
                                                   base=127 - t0, channel_multiplier=-1), reads=[bon.b], writes=[bon.b])
            k.op("pool", lambda e: e.memset(bon[:, 0:1], 1e4), writes=[bon.b])
            k.op("dve", lambda e: e.tensor_tensor(out=impf[:], in0=imp[:], in1=bon[:], op=ALU.add), reads=[imp.b, bon.b], writes=[impf.b])
            k.op("pool", lambda e: e.affine_select(out=val01[:], in_=ones4[:, 0:128], pattern=[[-64, 128]], compare_op=ALU.is_ge, fill=0.0,
                                                   base=t0, channel_multiplier=1), reads=[ones4.b], writes=[val01.b])
            k.op("dve", lambda e: e.tensor_tensor(out=impf[:], in0=impf[:], in1=val01[:], op=ALU.mult), reads=[impf.b, val01.b],
                 writes=[impf.b])
            k.op("dve", lambda e: e.tensor_scalar(out=val01[:], in0=val01[:], scalar1=1e30, scalar2=-1e30, op0=ALU.mult, op1=ALU.add),
                 reads=[val01.b], writes=[val01.b])
            k.op("dve", lambda e: e.tensor_tensor(out=impf[:], in0=impf[:], in1=val01[:], op=ALU.add), reads=[impf.b, val01.b],
                 writes=[impf.b])
            k.op("dve", lambda e: e.max(out=m8[:], in_=impf[:]), reads=[impf.b], writes=[m8.b])
            k.op("dve", lambda e: e.match_replace(out=wk_[:], in_to_replace=m8[:], in_values=impf[:], imm_value=-3e38), reads=[impf.b, m8.b],
                 writes=[wk_.b])
            k.op("dve", lambda e: e.max(out=m8[:], in_=wk_[:]), reads=[wk_.b], writes=[m8.b])
            k.op("dve", lambda e: e.tensor_scalar(out=selm[:], in0=impf[:], scalar1=m8[:, 7:8], scalar2=None, op0=ALU.is_ge),
                 reads=[impf.b, m8.b], writes=[selm.b])
            k.op("dve", lambda e: e.tensor_scalar(out=selm[:], in0=selm[:], scalar1=-1.0, scalar2=NSA_BIG, op0=ALU.add, op1=ALU.mult),
                 reads=[selm.b], writes=[selm.b])
            owin = oreg[0]
            kts = list(range(max(0, qt - 4), qt + 1))
            jobs = []
            for kt in kts:
                def mmw(e, a, kt=kt):
                    need_mask = (kt == qt) or (kt == qt - 4)
                    ins = e.matmul(a.ap(), lhsT=kwT[:, kt * 128:(kt + 1) * 128], rhs=qT[:], start=True, stop=not need_mask)
                    if kt == qt:
                        ins = e.matmul(a.ap(), lhsT=identb[:], rhs=cneg[:], start=False, stop=True)
                    elif kt == qt - 4:
                        ins = e.matmul(a.ap(), lhsT=identb[:], rhs=wneg[:], start=False, stop=True)
                    return ins
                jobs.append(dict(mm=mmw, reads=[kwT.b, qT.b, identb.b, cneg.b, wneg.b], vt=vw_e, vidx=kt))
            run_pairs(jobs, owin)
            combine(owin, 2, False, gsb)
            r2 = next_regT()
            k.op("pe", lambda e: e.transpose(r2.ap(), selm[:], c["ident"][:]), reads=[selm.b, c["ident"].b], writes=[r2.b])
            k.op("act", lambda e: e.copy(out=nmT[:].rearrange("p (h j) -> p h j", h=4),
                                         in_=r2.ap().rearrange("p (o j) -> p o j", o=1).to_broadcast([128, 4, 128])), reads=[r2.b],
                 writes=[nmT.b])
            if debug and qt == (NT - 1):
                dump("imp", imp, imp[:], 128)
                dump("impf", impf, impf[:], 128)
                dump("selm", selm, selm[:], 128)
            if qt + 1 < NT:
                if (qt + 1) % (TW // 128) == 0 and (qt + 1) // (TW // 128) + 1 < NST:
                    load_xT((qt + 1) // (TW // 128) + 1)
                q_prep_a(qt + 1)
            osel = oreg[1]
            jobs = []
            for kt in range(qt + 1):
                def mms(e, a, kt=kt):
                    e.matmul(a.ap(), lhsT=ksT[:, kt * 128:(kt + 1) * 128], rhs=qT[:], start=True, stop=False)
                    ins = e.matmul(a.ap(), lhsT=Esel[:, kt * 128:(kt + 1) * 128], rhs=nmT[:], start=False, stop=(kt != qt))
                    if kt == qt:
                        ins = e.matmul(a.ap(), lhsT=identb[:], rhs=cneg[:], start=False, stop=True)
                    return ins
                jobs.append(dict(mm=mms, reads=[ksT.b, qT.b, Esel.b, nmT.b, identb.b, cneg.b], vt=vs_e, vidx=kt))
            run_pairs(jobs, osel)
            combine(osel, 1, False, gsb)
            if qt + 1 < NT:
                q_prep_b(qt + 1)
            for h in range(4):
                r2 = next_regT()
                k.op("pe", lambda e: e.transpose(r2.ap(), oacc[:, h, :], c["ident"][:]), reads=[oacc.b, c["ident"].b], writes=[r2.b])
                k.op("act", lambda e: e.copy(out=ystage[:, h, tsl], in_=r2.ap()), reads=[r2.b], writes=[ystage.b])
        tk = k.op("sp", lambda e: e.dma_start(out=yT.rearrange("c p t -> p c t")[:, :, t0s:t0s + TW], in_=ystage[:]), reads=[ystage.b],
                  dsem=ds_y)
        out_toks.append(tk)
    k.finish(out_toks)
    return nc


def nsa_core_inputs(I, layer, g, hT_b):
    j = layer // 2
    W = I["c_w_in"][j]
    o_q, o_kc, o_vc, o_ks, o_vs, o_kw, o_vw, o_gp = 0, 2048, 2560, 3072, 3584, 4096, 4608, 5120
    sl = lambda o: W[:, o + g * 128:o + (g + 1) * 128]
    win = np.concatenate([sl(o_kc), sl(o_vc), sl(o_ks), sl(o_vs), sl(o_kw), sl(o_vw), W[:, g * 512:(g + 1) * 512],
                          W[:, o_gp + 12 * g:o_gp + 12 * (g + 1)]], axis=1)
    posT = np.ascontiguousarray(I["c_cmp_pos"][j].transpose(2, 0, 1))
    b1 = np.ascontiguousarray(I["c_cmp_b1"][j].reshape(2, 2, 128).transpose(2, 0, 1))
    return {
        "hT": hT_b, "nw": pvec(I["mix_norm"][layer]), "win": np.ascontiguousarray(win),
        "qnw": rep(np.tile(I["c_q_norm"][j], 4)), "knw": rep(I["c_k_norm"][j]), "posT": posT,
        "w1": np.ascontiguousarray(I["c_cmp_w1"][j]), "b1": b1, "w2": np.ascontiguousarray(I["c_cmp_w2"][j]),
        "gbias": rep(I["c_gate_bias"][j][12 * g:12 * (g + 1)]),
    }


_PROGS = {}


def _prog(name, fn):
    if name not in _PROGS:
        _PROGS[name] = fn()
    return _PROGS[name]


def _launch(nc, maps):
    res = run_bass_kernel_spmd(nc, maps, core_ids=list(range(8)))
    return res.results


def kernel(**I):
    I = {k_: np.ascontiguousarray(np.asarray(v)) for k_, v in I.items()}
    x = I["x"]
    B, S, D = x.shape
    NTC = S // 4
    hT = [np.ascontiguousarray(x[b].T.reshape(16, 128, S)) for b in range(B)]

    def tok_shard(arrs, c):
        b, q = divmod(c, 4)
        return np.ascontiguousarray(arrs[b][:, :, q * NTC:(q + 1) * NTC])

    def ffn_launch(hT, pre, layer, yT=None, wo=None):
        nc = _prog("ffn_pre" if yT is not None else "ffn", lambda: build_ffn(NT=NTC, preproj=yT is not None))
        maps = []
        for c in range(8):
            m = {"hT": tok_shard(hT, c), "nw": pvec(I[pre + "_norm"][layer]), "wg": I[pre + "_w_gate"][layer],
                 "wu": I[pre + "_w_up"][layer], "wd": I[pre + "_w_down"][layer]}
            if yT is not None:
                m["yT"] = tok_shard(yT, c)
                m["wo"] = wo
            maps.append(m)
        res = _launch(nc, maps)
        out = [np.empty((16, 128, S), np.float32) for _ in range(B)]
        for c in range(8):
            b, q = divmod(c, 4)
            out[b][:, :, q * NTC:(q + 1) * NTC] = res[c]["hT_out"]
        return out

    for layer in range(4):
        j = layer // 2
        hT = ffn_launch(hT, "ffn1", layer)
        yT = [np.empty((16, 128, S), ml_dtypes.bfloat16) for _ in range(B)]
        if layer % 2 == 0:
            nc = _prog(f"ab{j}", lambda: build_ab(T=S, layer_j=j))
            maps = [ab_core_inputs(I, layer, c % 4, hT[c // 4]) for c in range(8)]
            res = _launch(nc, maps)
            for c in range(8):
                b, g = divmod(c, 4)
                y = res[c]["yT"]
                yT[b][2 * g:2 * g + 2] = y[0:2]
                yT[b][8 + 2 * g:8 + 2 * g + 2] = y[2:4]
            wo = I["ab_w_out"][j]
        else:
            nc = _prog("nsa", lambda: build_nsa(T=S))
            maps = [nsa_core_inputs(I, layer, c % 4, hT[c // 4]) for c in range(8)]
            res = _launch(nc, maps)
            for c in range(8):
                b, g = divmod(c, 4)
                yT[b][4 * g:4 * g + 4] = res[c]["yT"]
            wo = I["c_w_out"][j]
        hT = ffn_launch(hT, "ffn2", layer, yT=yT, wo=wo)
    out = np.stack([hT[b].reshape(D, S).T for b in range(B)], axis=0)
    return np.ascontiguousarray(out.astype(np.float32))
```
